# Optimizing a Trainium2 kernel written in Bass

```python
import jax, jax.numpy as jnp
from jax import lax
import numpy as np

D_MODEL = 1024
BATCH = 2
SEQ = 8192
DEPTH = 4

ATTN_HEADS = 16
HEAD_DIM = 64
ATTN_WIDTH = ATTN_HEADS * HEAD_DIM
IDX_HEADS = 8
IDX_DIM = 64
IDX_SCALE = (IDX_HEADS ** -0.5) * (IDX_DIM ** -0.5)
TOPK_MAX = 256
Q_BLOCK = 128
ROPE_THETA = 10000.0
D_INNER = 2 * D_MODEL
SSM_HEAD_DIM = 64
SSM_HEADS = D_INNER // SSM_HEAD_DIM
SSM_GROUPS = 4
HEADS_PER_GROUP = SSM_HEADS // SSM_GROUPS
D_STATE = 128
SSM_CONV = 4
SSM_CHUNK = 256
XBC_WIDTH = D_INNER + 2 * SSM_GROUPS * D_STATE
D_FF = 2816
FFN_CONV = 3
EPS = 1e-6
SPLITS = (ATTN_WIDTH, ATTN_WIDTH, ATTN_WIDTH, IDX_HEADS * IDX_DIM, IDX_DIM, IDX_HEADS,
          D_INNER, XBC_WIDTH, SSM_HEADS, D_MODEL, D_MODEL)
IN_WIDTH = sum(SPLITS)
SPLIT_POINTS = tuple(int(v) for v in np.cumsum(SPLITS)[:-1])

kernel_name = "hybrid_dsa_ssd_convffn_adaln"


def rms_norm(x, g):
    xf = x.astype(jnp.float32)
    y = xf * lax.rsqrt(jnp.mean(xf * xf, axis=-1, keepdims=True) + EPS)
    return (y * g.astype(jnp.float32)).astype(x.dtype)


def rope_tables(positions, dim):
    inv = 1.0 / (ROPE_THETA ** (jnp.arange(0, dim, 2, dtype=jnp.float32) / dim))
    ang = positions.astype(jnp.float32)[..., None] * inv
    return jnp.cos(ang), jnp.sin(ang)


def apply_rope(x, cos, sin):
    xf = x.astype(jnp.float32)
    x1, x2 = jnp.split(xf, 2, axis=-1)
    c = cos[:, :, None, :]
    s = sin[:, :, None, :]
    return jnp.concatenate([x1 * c - x2 * s, x2 * c + x1 * s], axis=-1).astype(x.dtype)


def causal_dwconv(x, w, b):
    width = w.shape[0]
    y = lax.conv_general_dilated(
        x, w[:, None, :].astype(x.dtype), window_strides=(1,), padding=[(width - 1, 0)],
        dimension_numbers=("NWC", "WIO", "NWC"), feature_group_count=x.shape[-1])
    return y + b.astype(x.dtype)


def dsa_attention(q, k, v, iq, ik, iw, n_keep):
    bsz, seq = q.shape[:2]
    n_blocks = seq // Q_BLOCK
    kpos = jnp.arange(seq)
    scale = HEAD_DIM ** -0.5

    def block(i):
        start = i * Q_BLOCK
        qb = lax.dynamic_slice_in_dim(q, start, Q_BLOCK, axis=1)
        iqb = lax.dynamic_slice_in_dim(iq, start, Q_BLOCK, axis=1)
        iwb = lax.dynamic_slice_in_dim(iw, start, Q_BLOCK, axis=1)
        qpos = start + jnp.arange(Q_BLOCK)
        causal = kpos[None, :] <= qpos[:, None]
        logits = jnp.einsum("bqhd,bsd->bqhs", iqb, ik)
        score = jnp.einsum("bqh,bqhs->bqs", iwb, jax.nn.relu(logits)).astype(jnp.float32) * IDX_SCALE
        score = jnp.where(causal[None], score, -jnp.inf)
        _, sel = lax.top_k(score, n_keep)
        kg = jax.vmap(lambda kb, ib: kb[ib])(k, sel)
        vg = jax.vmap(lambda vb, ib: vb[ib])(v, sel)
        s = jnp.einsum("bqhd,bqkhd->bhqk", qb, kg).astype(jnp.float32) * scale
        valid = sel <= qpos[None, :, None]
        s = jnp.where(valid[:, None], s, -jnp.inf)
        p = jax.nn.softmax(s, axis=-1).astype(v.dtype)
        return jnp.einsum("bhqk,bqkhd->bqhd", p, vg)

    out = lax.map(block, jnp.arange(n_blocks))
    return out.transpose(1, 0, 2, 3, 4).reshape(bsz, seq, ATTN_WIDTH)


def segsum(a):
    t = a.shape[-1]
    cs = jnp.cumsum(a, axis=-1)
    seg = cs[..., :, None] - cs[..., None, :]
    mask = jnp.tril(jnp.ones((t, t), dtype=bool))
    return jnp.where(mask, seg, -jnp.inf)


def ssd_scan(xs, a, bm, cm):
    bsz, seq = xs.shape[:2]
    t = SSM_CHUNK
    pad = (-seq) % t
    if pad:
        padw = lambda z: jnp.pad(z, [(0, 0), (0, pad)] + [(0, 0)] * (z.ndim - 2))
        xs, a, bm, cm = padw(xs), padw(a), padw(bm), padw(cm)
    nc = (seq + pad) // t
    xs_c = xs.reshape(bsz, nc, t, SSM_GROUPS, HEADS_PER_GROUP, SSM_HEAD_DIM)
    bm_c = bm.reshape(bsz, nc, t, SSM_GROUPS, D_STATE)
    cm_c = cm.reshape(bsz, nc, t, SSM_GROUPS, D_STATE)
    a_c = a.reshape(bsz, nc, t, SSM_GROUPS, HEADS_PER_GROUP).transpose(0, 3, 4, 1, 2)
    a_cs = jnp.cumsum(a_c, axis=-1)
    decay = jnp.exp(segsum(a_c))
    cb = jnp.einsum("bclgn,bcsgn->bgcls", cm_c, bm_c)
    y_diag = jnp.einsum("bgcls,bgrcls,bcsgrp->bclgrp", cb, decay, xs_c)
    decay_states = jnp.exp(a_cs[..., -1:] - a_cs)
    states = jnp.einsum("bclgn,bgrcl,bclgrp->bcgrpn", bm_c, decay_states, xs_c)
    states = jnp.concatenate([jnp.zeros_like(states[:, :1]), states], axis=1)
    chunk_decay = jnp.exp(segsum(jnp.pad(a_cs[..., -1], ((0, 0), (0, 0), (0, 0), (1, 0)))))
    prev_states = jnp.einsum("bgrzc,bcgrpn->bzgrpn", chunk_decay, states)[:, :-1]
    y_off = jnp.einsum("bclgn,bcgrpn,bgrcl->bclgrp", cm_c, prev_states, jnp.exp(a_cs))
    y = (y_diag + y_off).reshape(bsz, nc * t, SSM_GROUPS, HEADS_PER_GROUP, SSM_HEAD_DIM)[:, :seq]
    return y.astype(xs.dtype)


def setup_inputs(seed: int = 0) -> dict:
    key = jax.random.key(seed)
    ks = jax.random.split(key, 28)
    f32 = jnp.float32

    def nrm(k, shape, fan_in, gain=1.0):
        return jax.random.normal(k, shape, f32) * (gain * fan_in ** -0.5)

    def gain_vec(k, shape):
        return 1.0 + 0.1 * jax.random.normal(k, shape, f32)

    x = jax.random.normal(ks[0], (BATCH, SEQ, D_MODEL), f32)
    c = jax.random.normal(ks[1], (BATCH, D_MODEL), f32)
    start = jax.random.randint(ks[2], (BATCH, 1), 0, 4096, dtype=jnp.int32)
    positions = start + jnp.arange(SEQ, dtype=jnp.int32)[None, :]
    dt0 = jnp.exp(jax.random.uniform(ks[3], (DEPTH, SSM_HEADS), f32, np.log(1e-3), np.log(1e-1)))
    dt_bias = dt0 + jnp.log(-jnp.expm1(-dt0))
    a_log = jnp.log(jax.random.uniform(ks[4], (DEPTH, SSM_HEADS), f32, 1.0, 16.0))
    return {
        "x": x,
        "c": c,
        "positions": positions,
        "w_ada": nrm(ks[5], (DEPTH, D_MODEL, 6 * D_MODEL), D_MODEL, 0.5),
        "b_ada": 0.01 * jax.random.normal(ks[6], (DEPTH, 6 * D_MODEL), f32),
        "norm1_g": gain_vec(ks[7], (DEPTH, D_MODEL)),
        "w_in": nrm(ks[8], (DEPTH, D_MODEL, IN_WIDTH), D_MODEL),
        "q_norm_g": gain_vec(ks[9], (DEPTH, HEAD_DIM)),
        "k_norm_g": gain_vec(ks[10], (DEPTH, HEAD_DIM)),
        "ssm_conv_w": nrm(ks[11], (DEPTH, SSM_CONV, XBC_WIDTH), SSM_CONV),
        "ssm_conv_b": 0.01 * jax.random.normal(ks[12], (DEPTH, XBC_WIDTH), f32),
        "dt_bias": dt_bias,
        "a_log": a_log,
        "d_skip": gain_vec(ks[13], (DEPTH, SSM_HEADS)),
        "ssm_norm_g": gain_vec(ks[14], (DEPTH, D_INNER)),
        "w_attn_o": nrm(ks[15], (DEPTH, ATTN_WIDTH, D_MODEL), ATTN_WIDTH),
        "w_ssm_o": nrm(ks[16], (DEPTH, D_INNER, D_MODEL), D_INNER),
        "w_out": nrm(ks[17], (DEPTH, D_MODEL, D_MODEL), D_MODEL),
        "norm2_g": gain_vec(ks[18], (DEPTH, D_MODEL)),
        "w_up": nrm(ks[19], (DEPTH, D_MODEL, 2 * D_FF), D_MODEL),
        "ffn_conv_w": nrm(ks[20], (DEPTH, FFN_CONV, 2 * D_FF), FFN_CONV),
        "ffn_conv_b": 0.01 * jax.random.normal(ks[21], (DEPTH, 2 * D_FF), f32),
        "w_down": nrm(ks[22], (DEPTH, D_FF, D_MODEL), D_FF),
    }


def reference(x, c, positions, w_ada, b_ada, norm1_g, w_in, q_norm_g, k_norm_g, ssm_conv_w, ssm_conv_b,
              dt_bias, a_log, d_skip, ssm_norm_g, w_attn_o, w_ssm_o, w_out, norm2_g, w_up, ffn_conv_w,
              ffn_conv_b, w_down):
    bsz, seq, _ = x.shape
    n_keep = min(TOPK_MAX, seq // 4)
    cos, sin = rope_tables(positions, HEAD_DIM)
    c_act = jax.nn.silu(c)

    for l in range(DEPTH):
        mod = c_act @ w_ada[l] + b_ada[l]
        sh1, sc1, g1, sh2, sc2, g2 = [m[:, None, :] for m in jnp.split(mod, 6, axis=-1)]

        h = rms_norm(x, norm1_g[l]) * (1.0 + sc1) + sh1
        proj = h @ w_in[l]
        q, k, v, iq, ik, iw, z, xbc, dt_raw, ga, gm = jnp.split(proj, SPLIT_POINTS, axis=-1)

        q = apply_rope(rms_norm(q.reshape(bsz, seq, ATTN_HEADS, HEAD_DIM), q_norm_g[l]), cos, sin)
        k = apply_rope(rms_norm(k.reshape(bsz, seq, ATTN_HEADS, HEAD_DIM), k_norm_g[l]), cos, sin)
        v = v.reshape(bsz, seq, ATTN_HEADS, HEAD_DIM)
        iq = apply_rope(iq.reshape(bsz, seq, IDX_HEADS, IDX_DIM), cos, sin)
        ik = apply_rope(ik[:, :, None, :], cos, sin)[:, :, 0, :]
        o_attn = dsa_attention(q, k, v, iq, ik, iw, n_keep) @ w_attn_o[l]

        xbc = jax.nn.silu(causal_dwconv(xbc, ssm_conv_w[l], ssm_conv_b[l]))
        xs, bm, cm = jnp.split(xbc, [D_INNER, D_INNER + SSM_GROUPS * D_STATE], axis=-1)
        xs = xs.reshape(bsz, seq, SSM_GROUPS, HEADS_PER_GROUP, SSM_HEAD_DIM)
        bm = bm.reshape(bsz, seq, SSM_GROUPS, D_STATE)
        cm = cm.reshape(bsz, seq, SSM_GROUPS, D_STATE)
        dt = jax.nn.softplus(dt_raw.astype(jnp.float32) + dt_bias[l].astype(jnp.float32))
        a_cont = -jnp.exp(a_log[l].astype(jnp.float32))
        a_dt = (dt * a_cont).reshape(bsz, seq, SSM_GROUPS, HEADS_PER_GROUP)
        dt_g = dt.reshape(bsz, seq, SSM_GROUPS, HEADS_PER_GROUP, 1).astype(xs.dtype)
        y = ssd_scan(xs * dt_g, a_dt, bm, cm)
        y = y + xs * d_skip[l].reshape(SSM_GROUPS, HEADS_PER_GROUP, 1)
        y = y.reshape(bsz, seq, D_INNER) * jax.nn.silu(z)
        y = rms_norm(y.reshape(bsz, seq, SSM_GROUPS, D_INNER // SSM_GROUPS),
                     ssm_norm_g[l].reshape(SSM_GROUPS, D_INNER // SSM_GROUPS)).reshape(bsz, seq, D_INNER)
        o_ssm = y @ w_ssm_o[l]

        mixed = (jax.nn.sigmoid(ga) * o_attn + jax.nn.sigmoid(gm) * o_ssm) @ w_out[l]
        x = x + g1 * mixed

        h = rms_norm(x, norm2_g[l]) * (1.0 + sc2) + sh2
        u = causal_dwconv(h @ w_up[l], ffn_conv_w[l], ffn_conv_b[l])
        val, gate = jnp.split(u, 2, axis=-1)
        x = x + g2 * ((jax.nn.silu(gate) * val) @ w_down[l])

    return x
```

```python
from contextlib import ExitStack
import numpy as np
import ml_dtypes
import concourse.bass as bass
import concourse.mybir as mybir
from concourse.bass_utils import run_bass_kernel_spmd

F32 = mybir.dt.float32
BF16 = mybir.dt.bfloat16
I32 = mybir.dt.int32
FP8 = mybir.dt.float8e5
AF = mybir.ActivationFunctionType
ALU = mybir.AluOpType
AX = mybir.AxisListType

NCORES = 8
GSZ = 4
D = 1024
KC = D // 128
HEADS = 16
HD = 64
IH = 8
TOPK = 256
DI = 2048
SH = 32
SG = 4
NST = 128
XBC = DI + 2 * SG * NST
DFF = 2816
EPS = 1e-6
C_Q, C_K, C_V, C_IQ, C_IK, C_IW, C_Z, C_XBC, C_DT, C_GA, C_GM = (
    0, 1024, 2048, 3072, 3584, 3648, 3656, 5704, 8776, 8808, 9832)
INW = 10856
NEG = -30000.0


class Buf:
    __slots__ = ("name", "w", "r")

    def __init__(self, name):
        self.name = name
        self.w = None
        self.r = {}


class Prog:
    ENGS = ("pe", "act", "dve", "pool", "sp")

    def __init__(self, nc, n_dma=40):
        self.nc = nc
        self.ops = {e: [] for e in self.ENGS}
        self.cnt = {e: 0 for e in self.ENGS}
        self.known = {e: {} for e in self.ENGS}
        self.dma_val = [0] * n_dma
        self.dma_next = 0
        self.cc_val = 0

    def _need(self, eng, k, v):
        if k == eng and eng == "pe":
            return
        kn = self.known[eng]
        if kn.get(k, 0) >= v:
            return
        kn[k] = v
        self.ops[eng].append(("wait", k, v))

    def _deps(self, eng, reads, writes):
        for b in reads:
            if b.w is not None:
                self._need(eng, *b.w)
        for b in writes:
            if b.w is not None:
                self._need(eng, *b.w)
            for k, v in b.r.items():
                self._need(eng, k, v)

    def _mark(self, tok, reads, writes):
        for b in writes:
            b.w = tok
            b.r = {}
        for b in reads:
            if b in writes:
                continue
            if b.r.get(tok[0], 0) < tok[1]:
                b.r[tok[0]] = tok[1]

    def op(self, eng, fn, reads=(), writes=()):
        self._deps(eng, reads, writes)
        self.cnt[eng] += 1
        tok = (eng, self.cnt[eng])
        self.ops[eng].append(("ins", fn, eng, 1, self._where()))
        self._mark(tok, reads, writes)
        return tok

    DEBUG_WHERE = False

    def _where(self):
        if not Prog.DEBUG_WHERE:
            return None
        import traceback
        return [(f.lineno, f.name) for f in traceback.extract_stack(limit=6)[:-2]]

    def dma(self, q, out, in_, reads=(), writes=()):
        i = self.dma_next
        self.dma_next = (i + 1) % len(self.dma_val)
        key = ("dma", i)
        self._deps(q, reads, writes)
        if self.dma_val[i]:
            self._need(q, key, self.dma_val[i])
        self.dma_val[i] += 16
        tok = (key, self.dma_val[i])
        self.ops[q].append(("ins", lambda e, o=out, s=in_: e.dma_start(out=o, in_=s), key, 16))
        self._mark(tok, reads, writes)
        return tok

    def collective(self, fn, reads=(), writes=()):
        self._deps("pool", reads, writes)
        self.cc_val += 1
        tok = ("cc", self.cc_val)
        self.ops["pool"].append(("ins", fn, "cc", 1))
        self._mark(tok, reads, writes)
        return tok

    def barrier(self):
        for e in self.ENGS:
            for f in ("pe", "act", "dve", "pool"):
                if self.cnt[f]:
                    self._need(e, f, self.cnt[f])
            for i, v in enumerate(self.dma_val):
                if v:
                    self._need(e, ("dma", i), v)
            if self.cc_val:
                self._need(e, "cc", self.cc_val)

    def emit(self, stack):
        nc = self.nc
        sems = {}
        for e in ("pe", "act", "dve", "pool"):
            sems[e] = stack.enter_context(nc.semaphore("s_" + e))
        for i in range(len(self.dma_val)):
            sems[("dma", i)] = stack.enter_context(nc.semaphore("d%d" % i))
        sems["cc"] = stack.enter_context(nc.semaphore("s_cc"))
        block = stack.enter_context(nc.Block())

        def mk(name):
            def body(eng):
                for o in self.ops[name]:
                    if o[0] == "wait":
                        eng.wait_ge(sems[o[1]], o[2])
                    else:
                        ins = o[1](eng)
                        ins.then_inc(sems[o[2]], o[3])
                        if Prog.DEBUG_WHERE and len(o) > 4:
                            print("INS", name, getattr(getattr(ins, "ins", None), "name", None), o[4])
            return body

        block.tensor(mk("pe"))
        block.scalar(mk("act"))
        block.vector(mk("dve"))
        block.gpsimd(mk("pool"))
        block.sync(mk("sp"))


class Arena:
    def __init__(self, big, nwords):
        self.big = big
        self.n = nwords
        self.off = 0

    def mark(self):
        return self.off

    def release(self, m):
        self.off = m

    def f32(self, name, cols):
        a = self.off
        self.off += cols
        assert self.off <= self.n, ("SBUF arena overflow", name, self.off, self.n)
        return self.big[:, a:a + cols], Buf(name)

    def bf16(self, name, cols):
        w = (cols + 1) // 2
        a = self.off
        self.off += w
        assert self.off <= self.n, ("SBUF arena overflow", name, self.off, self.n)
        return self.big[:, a:a + w].bitcast(BF16)[:, 0:cols], Buf(name)


def make_consts():
    c = {}
    c["ident"] = np.eye(128, dtype=np.float32)
    c["ones"] = np.ones((128, 128), np.float32)
    bo = np.zeros((128, 128), np.float32)
    bo[:64, :64] = 1.0
    bo[64:, 64:] = 1.0
    c["blockones"] = bo
    rr = np.zeros((128, 128), np.float32)
    for m in range(128):
        if (m % 64) < 32:
            rr[m + 32, m] = -1.0
        else:
            rr[m - 32, m] = 1.0
    c["rrot"] = rr
    tri = (np.arange(128)[:, None] <= np.arange(128)[None, :]).astype(np.float32)
    c["tri"] = tri
    c["causb"] = np.where(np.arange(128)[None, :] <= np.arange(128)[:, None], 0.0, -1e30).astype(np.float32)
    invf = (1.0 / (10000.0 ** (np.arange(0, 64, 2, dtype=np.float32) / 64.0))).astype(np.float32)
    c["invf"] = np.tile(invf, 4)[:, None].astype(np.float32) * np.ones((1, 128), np.float32)
    c["iotaf"] = np.tile(np.arange(512, dtype=np.float32)[None, :], (128, 1))
    c["pidx"] = np.tile(np.arange(128, dtype=np.float32)[:, None], (1, 128))
    sw = np.zeros((128, 128), np.float32)
    for m in range(128):
        sw[(m + 64) % 128, m] = 1.0
    c["swap"] = sw
    names = ["ident", "ones", "blockones", "rrot", "tri", "causb", "invf", "iotaf", "pidx", "swap"]
    return [(n, c[n].shape[1]) for n in names], np.concatenate([c[n] for n in names], axis=1)


CONST_NAMES, CONST_ARR = make_consts()


class Cfg:
    def __init__(self, seq=8192, depth=4, debug=(), stage=99, feed=()):
        self.stage = stage
        self.feed = set(feed)
        self.S = seq
        self.T = seq // GSZ
        self.depth = depth
        self.debug = set(debug)
        self.NT = self.T // 128
        self.NB = self.T // 512
        self.NCH = self.T // 256
        self.nkeep = min(TOPK, seq // 4)


def build_program(cfg):
    T, NT, NB, depth = cfg.T, cfg.NT, cfg.NB, cfg.depth
    nc = bass.Bass("TRN2", target_bir_lowering=False)
    stack = ExitStack()
    P = Prog(nc)
    dram = {}
    dbuf = {}

    dbg_copies = []

    def dten(name, shape, dtype, kind="Internal"):
        if name in cfg.debug:
            if "_g" in name[-4:]:
                dcp = nc.dram_tensor(name + "_dbg", list(shape), dtype, kind="ExternalOutput").ap()
                dram[name + "_dbg"] = dcp
                dbuf[name + "_dbg"] = Buf(name + "_dbg")
                dbg_copies.append(name)
            else:
                kind = "ExternalOutput"
        t = nc.dram_tensor(name, list(shape), dtype, kind=kind).ap()
        dram[name] = t
        dbuf[name] = Buf(name)
        return t

    x_in = dten("x", [T, D], F32, "ExternalInput")
    c_in = dten("c", [KC, 128], F32, "ExternalInput")
    pos_in = dten("positions", [1, T], I32, "ExternalInput")
    consts_in = dten("consts", [128, CONST_ARR.shape[1]], F32, "ExternalInput")
    rank_in = dten("rank", [1, 4], F32, "ExternalInput")
    W = {}
    for nm, shp in (("w_ada", [depth, D, 6 * D]), ("b_ada", [depth, 48, 128]), ("norm1_g", [depth, KC, 128]),
                    ("w_in", [depth, D, INW]), ("q_norm_g", [depth, 1, 64]), ("k_norm_g", [depth, 1, 64]),
                    ("ssm_conv_w", [depth, 4, 24, 128]), ("ssm_conv_b", [depth, 24, 128]),
                    ("dt_bias", [depth, 1, SH]), ("a_log", [depth, 1, SH]), ("d_skip", [depth, 1, SH]),
                    ("ssm_norm_g", [depth, 16, 128]), ("w_attn_o", [depth, D, D]), ("w_ssm_o", [depth, DI, D]),
                    ("w_out", [depth, D, D]), ("norm2_g", [depth, KC, 128]), ("w_up", [depth, D, 2 * DFF]),
                    ("ffn_conv_w", [depth, 3, 44, 128]), ("ffn_conv_b", [depth, 44, 128]),
                    ("w_down", [depth, DFF, D])):
        W[nm] = dten(nm, shp, F32, "ExternalInput")
    y_out = dten("y", [T, D], F32, "ExternalOutput")

    NWORDS = 52224
    big = stack.enter_context(nc.sbuf_tensor("arena", [128, NWORDS], F32))
    A = Arena(big, NWORDS)
    psum = []
    for i in range(8):
        pt = stack.enter_context(nc.psum_tensor("ps%d" % i, [128, 512], F32))
        psum.append((pt[:, :], Buf("ps%d" % i)))
    ps_rr = [0]

    def next_ps():
        i = ps_rr[0]
        ps_rr[0] = (i + 1) % 8
        return psum[i]

    cst, cst_b = A.f32("consts", CONST_ARR.shape[1])
    cv = {}
    o = 0
    for n, wd in CONST_NAMES:
        cv[n] = cst[:, o:o + wd]
        o += wd
    ident_bf, ident_bf_b = A.bf16("ident_bf", 128)

    class RPool:
        def __init__(self, name, n, cols, kind):
            self.t = [(A.f32 if kind == "f32" else A.bf16)("%s%d" % (name, i), cols) for i in range(n)]
            self.i = 0

        def get(self):
            r = self.t[self.i]
            self.i = (self.i + 1) % len(self.t)
            return r


    lp, lp_b = A.f32("lp", 512)
    dtb_bc, dtb_b = A.f32("dtb_bc", SH)
    alog_bc, alog_b = A.f32("alog_bc", SH)
    iw_tm, iw_b = A.f32("iw_tm", NT * 8)
    dt_tm, dt_b = A.f32("dt_tm", NT * SH)
    a_tm, a_b = A.f32("a_tm", NT * SH)
    halfsel, halfsel_b = A.f32("halfsel", 128)
    lfm_pool = RPool("lfm", 3, 128, "f32")
    WSTG = 2048
    wst_pool = RPool("wst", 2, WSTG, "f32")
    wbf_pool = RPool("wbf", 2, WSTG, "bf16")
    stg_pool = RPool("stg", 8, 512, "f32")
    sbf_pool = RPool("sbf", 4, 512, "bf16")
    epsT, epsT_b = A.f32("epsT", 2)
    oneT, oneT_b = A.f32("oneT", 2)
    rk, rk_b = A.f32("rk", 8)
    prevsel, prevsel_b = A.f32("prevsel", 4)
    class NS:
        pass
    ns = NS()
    m_pers = A.mark()
    xT, xT_b = A.f32("xT", KC * T)
    xT3 = xT.rearrange("p (k t) -> p k t", t=T)
    m_x = A.mark()

    P.dma("sp", cst, consts_in, reads=[dbuf["consts"]], writes=[cst_b])
    P.op("dve", lambda e: e.tensor_copy(out=ident_bf, in_=cv["ident"]), reads=[cst_b], writes=[ident_bf_b])

    def transpose_f32(dst, dst_b, src, src_b, rows, cols, evac="act"):
        pt, pb = next_ps()
        P.op("pe", lambda e: e.transpose(out=pt[0:cols, 0:rows], in_=src, identity=cv["ident"][0:rows, 0:rows]),
             reads=[src_b, cst_b], writes=[pb])
        if evac == "act":
            P.op("act", lambda e: e.copy(out=dst, in_=pt[0:cols, 0:rows]), reads=[pb], writes=[dst_b])
        else:
            P.op("dve", lambda e: e.tensor_copy(out=dst, in_=pt[0:cols, 0:rows]), reads=[pb], writes=[dst_b])

    def load_fm(dst, dst_b, src_ap, src_name, nrow):
        m = A.mark()
        tmp, tmp_b = A.f32("lfm_tmp", 128)
        P.dma("sp", tmp[0:nrow, :], src_ap, reads=[dbuf[src_name]], writes=[tmp_b])
        transpose_f32(dst, dst_b, tmp[0:nrow, :], tmp_b, nrow, 128)
        A.release(m)
        return tmp_b

    m0 = A.mark()
    xtoks = [A.f32("xtok%d" % i, D) for i in range(2)]
    for i in range(NT):
        xt, xt_b = xtoks[i % 2]
        P.dma("sp", xt, x_in[i * 128:(i + 1) * 128, :], reads=[dbuf["x"]], writes=[xt_b])
        for k in range(KC):
            pt, pb = next_ps()
            P.op("pe", lambda e, pt=pt, xt=xt, k=k: e.transpose(out=pt[:, 0:128], in_=xt[:, k * 128:(k + 1) * 128],
                                                               identity=cv["ident"]),
                 reads=[xt_b, cst_b], writes=[pb])
            eng = "act" if k % 2 == 0 else "dve"
            if eng == "act":
                P.op("act", lambda e, pt=pt, k=k, i=i: e.copy(out=xT3[:, k, i * 128:(i + 1) * 128], in_=pt[:, 0:128]),
                     reads=[pb], writes=[xT_b])
            else:
                P.op("dve", lambda e, pt=pt, k=k, i=i: e.tensor_copy(out=xT3[:, k, i * 128:(i + 1) * 128], in_=pt[:, 0:128]),
                     reads=[pb], writes=[xT_b])
    P.barrier()
    A.release(m0)


    def ACT(out, in_, func, reads, writes, **kw):
        P.op("act", lambda e: e.activation(out=out, in_=in_, func=func, **kw), reads, writes)

    def TS(eng, out, in0, s1, op0, reads, writes, s2=None, op1=None, **kw):
        if op1 is None:
            P.op(eng, lambda e: e.tensor_scalar(out=out, in0=in0, scalar1=s1, scalar2=None, op0=op0, **kw), reads, writes)
        else:
            P.op(eng, lambda e: e.tensor_scalar(out=out, in0=in0, scalar1=s1, scalar2=s2, op0=op0, op1=op1, **kw),
                 reads, writes)

    def TT(eng, out, in0, in1, op, reads, writes):
        P.op(eng, lambda e: e.tensor_tensor(out=out, in0=in0, in1=in1, op=op), reads, writes)

    def STT(out, in0, scalar, in1, op0, op1, reads, writes):
        P.op("dve", lambda e: e.scalar_tensor_tensor(out=out, in0=in0, scalar=scalar, in1=in1, op0=op0, op1=op1),
             reads, writes)

    def MM(out, lhsT, rhs, start, stop, reads, writes):
        P.op("pe", lambda e: e.matmul(out, lhsT, rhs, start=start, stop=stop), reads, writes)

    def CP(eng, out, in_, reads, writes):
        if eng == "act":
            P.op("act", lambda e: e.copy(out=out, in_=in_), reads, writes)
        else:
            P.op(eng, lambda e: e.tensor_copy(out=out, in_=in_), reads, writes)

    def bcast_ap(ap2d_row, nparts, ncols, offset_elems=0):
        return bass.AP(ap2d_row.tensor, ap2d_row.offset + offset_elems, [[0, nparts], [1, ncols]])

    qT_d = dten("qT_d", [8 * 128, T], BF16)
    groups = [[0, 1, 2, 3], [4, 5, 6, 7]]

    class GatherSet:
        def __init__(self, name, rows, cols, dtype, esz):
            rpc = rows
            while rpc * cols * esz > (1 << 20):
                rpc //= 2
            assert rows % rpc == 0
            self.name, self.rows, self.cols, self.rpc, self.n = name, rows, cols, rpc, rows // rpc
            self.loc = [dten("%s_loc%d" % (name, c), [rpc, cols], dtype) for c in range(self.n)]
            self.g = [dten("%s_g%d" % (name, c), [GSZ * rpc, cols], dtype) for c in range(self.n)]

        def loc_rows(self, r0, r1):
            c = r0 // self.rpc
            assert (r1 - 1) // self.rpc == c
            return self.loc[c][r0 - c * self.rpc:r1 - c * self.rpc, :], "%s_loc%d" % (self.name, c)

        def g_rows(self, rank, r0, r1):
            c = r0 // self.rpc
            assert (r1 - 1) // self.rpc == c
            base = rank * self.rpc - c * self.rpc
            return self.g[c][base + r0:base + r1, :], "%s_g%d" % (self.name, c)

        def gather(self):
            for c in range(self.n):
                ln, gn = "%s_loc%d" % (self.name, c), "%s_g%d" % (self.name, c)
                P.collective(lambda e, ln=ln, gn=gn: e.collective_compute("AllGather", ALU.bypass, replica_groups=groups,
                                                                          ins=[dram[ln]], outs=[dram[gn]]),
                             reads=[dbuf[ln]], writes=[dbuf[gn]])

    kT_gs = GatherSet("kT", 8 * 128, T, BF16, 2)
    v_gs = GatherSet("v", T, 2048, BF16, 2)
    iqT_d = dten("iqT_d", [4 * 128, T], BF16)
    ik_gs = GatherSet("ik", 128, T, BF16, 2)
    zT_d = dten("zT_d", [16 * 128, T], BF16)
    xbc_raw = dten("xbc_raw", [24 * 128, T], F32)
    halo_gs = GatherSet("halo", 128, 24 * 4, F32, 4)
    gT_d = dten("gT_d", [16 * 128, T], F32)
    dbg_small = dten("dbg_small", [128, NT * 80], F32)

    rope_d = dten("rope_d", [2 * 128, T], F32)
    m0 = A.mark()
    C4, C4_b = A.f32("C4", T)
    S4, S4_b = A.f32("S4", T)
    posi, posi_b = A.f32("posi", T)
    posi_i = posi.bitcast(I32)
    ang, ang_b = A.f32("ang", T)
    t1, t1_b = A.f32("rt1", T)
    t2, t2_b = A.f32("rt2", T)
    t2_i = t2.bitcast(I32)
    P.dma("sp", posi_i, bcast_ap(pos_in, 128, T), reads=[dbuf["positions"]], writes=[posi_b])
    CP("dve", ang, posi_i, [posi_b], [ang_b])
    TS("dve", ang, ang, cv["invf"][:, 0:1], ALU.mult, [ang_b, cst_b], [ang_b])
    TWO_PI = 2.0 * np.pi
    C1 = 6.28125
    C2 = TWO_PI - C1
    for dst, dst_b, shift in ((S4, S4_b, 0.0), (C4, C4_b, np.pi / 2.0)):
        TS("dve", t1, ang, shift, ALU.add, [ang_b], [t1_b], s2=1.0 / TWO_PI, op1=ALU.mult)
        CP("dve", t2_i, t1, [t1_b], [t2_b])
        CP("dve", t1, t2_i, [t2_b], [t1_b])
        TS("dve", t2, ang, shift, ALU.add, [ang_b], [t2_b])
        STT(t2, t1, -C1, t2, ALU.mult, ALU.add, [t1_b, t2_b], [t2_b])
        STT(t2, t1, -C2, t2, ALU.mult, ALU.add, [t1_b, t2_b], [t2_b])
        TS("dve", t1, t2, float(np.pi), ALU.is_gt, [t2_b], [t1_b])
        STT(t2, t1, -TWO_PI, t2, ALU.mult, ALU.add, [t1_b, t2_b], [t2_b])
        TS("dve", t1, t2, float(-np.pi), ALU.is_lt, [t2_b], [t1_b])
        STT(t2, t1, TWO_PI, t2, ALU.mult, ALU.add, [t1_b, t2_b], [t2_b])
        TS("dve", t2, t2, float(np.pi), ALU.min, [t2_b], [t2_b], s2=float(-np.pi), op1=ALU.max)
        ACT(dst, t2, AF.Sin, [t2_b], [dst_b])
    P.dma("sp", rope_d[0:128, :], C4, reads=[C4_b], writes=[dbuf["rope_d"]])
    P.dma("sp", rope_d[128:256, :], S4, reads=[S4_b], writes=[dbuf["rope_d"]])
    P.barrier()
    A.release(m0)

    o_ = [0]

    def lp_alloc(n):
        a = o_[0]
        o_[0] += n
        assert o_[0] <= 512
        return lp[:, a:a + n]

    b_adaT = lp_alloc(48)
    modT = lp_alloc(48)
    n1gT = lp_alloc(8)
    n2gT = lp_alloc(8)
    scwT = lp_alloc(96)
    scbT = lp_alloc(24)
    sngT = lp_alloc(16)
    fcwT = lp_alloc(132)
    fcbT = lp_alloc(44)
    qg2 = lp_alloc(1)
    kg2 = lp_alloc(1)
    A1 = lp_alloc(8)
    A2 = lp_alloc(8)
    cact2 = lp_alloc(16)
    dskT = lp_alloc(16)
    cact2_3 = cact2.rearrange("p (k two) -> p k two", two=2)
    iw3 = iw_tm.rearrange("p (i h) -> p i h", h=8)
    dt3 = dt_tm.rearrange("p (i h) -> p i h", h=SH)
    a3 = a_tm.rearrange("p (i h) -> p i h", h=SH)

    m0 = A.mark()
    tmp, tmp_b = A.f32("ctmp", 128)
    P.dma("sp", tmp[0:KC, :], c_in, reads=[dbuf["c"]], writes=[tmp_b])
    pt, pb = next_ps()
    P.op("pe", lambda e, pt=pt: e.transpose(out=pt[:, 0:KC], in_=tmp[0:KC, :], identity=cv["ident"][0:KC, 0:KC]),
         reads=[tmp_b, cst_b], writes=[pb])
    ACT(cact2_3[:, :, 0], pt[:, 0:KC], AF.Silu, [pb], [lp_b])
    ACT(cact2_3[:, :, 1], pt[:, 0:KC], AF.Silu, [pb], [lp_b])
    P.barrier()
    A.release(m0)

    def load_fm_rows(dst, src_ap, src_name, nrow):
        tmp_, tmpb_ = lfm_pool.get()
        P.dma("sp", tmp_[0:nrow, :], src_ap, reads=[dbuf[src_name]], writes=[tmpb_])
        pt_, pb_ = next_ps()
        P.op("pe", lambda e: e.transpose(out=pt_[:, 0:nrow], in_=tmp_[0:nrow, :], identity=cv["ident"][0:nrow, 0:nrow]),
             reads=[tmpb_, cst_b], writes=[pb_])
        CP("dve", dst, pt_[:, 0:nrow], [pb_], [lp_b])

    def layer_params(l):
        load_fm_rows(b_adaT, W["b_ada"][l], "b_ada", 48)
        load_fm_rows(n1gT, W["norm1_g"][l], "norm1_g", KC)
        load_fm_rows(n2gT, W["norm2_g"][l], "norm2_g", KC)
        for tap in range(4):
            load_fm_rows(scwT[:, tap * 24:(tap + 1) * 24], W["ssm_conv_w"][l, tap], "ssm_conv_w", 24)
        load_fm_rows(scbT, W["ssm_conv_b"][l], "ssm_conv_b", 24)
        load_fm_rows(sngT, W["ssm_norm_g"][l], "ssm_norm_g", 16)
        for tap in range(3):
            load_fm_rows(fcwT[:, tap * 44:(tap + 1) * 44], W["ffn_conv_w"][l, tap], "ffn_conv_w", 44)
        load_fm_rows(fcbT, W["ffn_conv_b"][l], "ffn_conv_b", 44)
        for dst, nm in ((qg2, "q_norm_g"), (kg2, "k_norm_g")):
            tmp_, tmpb_ = lfm_pool.get()
            P.dma("sp", tmp_[0:1, 0:64], W[nm][l], reads=[dbuf[nm]], writes=[tmpb_])
            P.dma("sp", tmp_[0:1, 64:128], W[nm][l], reads=[dbuf[nm]], writes=[tmpb_])
            pt_, pb_ = next_ps()
            P.op("pe", lambda e, pt_=pt_, tmp_=tmp_: e.transpose(out=pt_[:, 0:1], in_=tmp_[0:1, :],
                                                                 identity=cv["ident"][0:1, 0:1]),
                 reads=[tmpb_, cst_b], writes=[pb_])
            CP("dve", dst, pt_[:, 0:1], [pb_], [lp_b])
        tmp_, tmpb_ = lfm_pool.get()
        P.dma("sp", tmp_[0:16, 0:2], W["d_skip"][l].rearrange("o (c two) -> (o c) two", two=2),
              reads=[dbuf["d_skip"]], writes=[tmpb_])
        pt_, pb_ = next_ps()
        P.op("pe", lambda e: e.transpose(out=pt_[0:2, 0:16], in_=tmp_[0:16, 0:2], identity=cv["ident"][0:16, 0:16]),
             reads=[tmpb_, cst_b], writes=[pb_])
        tmp2_, tmp2b_ = lfm_pool.get()
        CP("dve", tmp2_[0:2, 0:16], pt_[0:2, 0:16], [pb_], [tmp2b_])
        pt2_, pb2_ = next_ps()
        MM(pt2_[:, 0:16], halfsel[0:2, :], tmp2_[0:2, 0:16], True, True, [tmp2b_, halfsel_b], [pb2_])
        CP("dve", dskT, pt2_[:, 0:16], [pb2_], [lp_b])
        P.dma("sp", dtb_bc, bcast_ap(W["dt_bias"][l], 128, SH), reads=[dbuf["dt_bias"]], writes=[dtb_b])
        P.dma("sp", alog_bc, bcast_ap(W["a_log"][l], 128, SH), reads=[dbuf["a_log"]], writes=[alog_b])
        ACT(alog_bc, alog_bc, AF.Exp, [alog_b], [alog_b])

    P.dma("sp", halfsel[0:1, :], consts_in[0:1, 256:384], reads=[dbuf["consts"]], writes=[halfsel_b])
    P.dma("sp", halfsel[1:2, :], consts_in[64:65, 256:384], reads=[dbuf["consts"]], writes=[halfsel_b])


    cast_rr = [0]

    def load_w(wname, l, col0, ncols, kc, row0=0, to_bf16=True, dst=None):
        assert kc * ncols <= WSTG
        st, st_b = wst_pool.get()
        st3 = st[:, 0:kc * ncols].rearrange("p (k c) -> p k c", c=ncols)
        src = W[wname][l][row0:row0 + kc * 128, col0:col0 + ncols].rearrange("(k p) c -> p k c", p=128)
        P.dma("sp", st3, src, reads=[dbuf[wname]], writes=[st_b])
        if not to_bf16:
            return st3, st_b
        if dst is not None:
            CP("pool", dst[0], st3, [st_b], [dst[1]])
            return dst
        wb, wb_b = wbf_pool.get()
        wb3 = wb[:, 0:kc * ncols].rearrange("p (k c) -> p k c", c=ncols)
        CP("pool", wb[:, 0:kc * ncols], st[:, 0:kc * ncols], [st_b], [wb_b])
        return wb3, wb_b

    def load_w_resident(name, wname, l, kc, ncols):
        wr, wr_b = A.bf16(name, kc * ncols)
        wr3 = wr.rearrange("p (k c) -> p k c", c=ncols)
        kstep = max(1, WSTG // ncols) if ncols <= WSTG else 1
        cstep = min(ncols, WSTG)
        for k0 in range(0, kc, kstep):
            kk = min(kstep, kc - k0)
            for c0 in range(0, ncols, cstep):
                cc = min(cstep, ncols - c0)
                load_w(wname, l, c0, cc, kk, row0=k0 * 128, dst=(wr3[:, k0:k0 + kk, c0:c0 + cc], wr_b))
        return wr3, wr_b


    def store(dst_ap, dname, src_ap, src_b):
        P.dma("pool", dst_ap, src_ap, reads=[src_b], writes=[dbuf[dname]])

    def compute_mod(l):
        for cb in range(24):
            w3, w_b = load_w("w_ada", l, cb * 256, 256, KC, to_bf16=False)
            pt, pb = next_ps()
            for m in range(2):
                for k in range(KC):
                    MM(pt[:, 2 * m:2 * m + 2], w3[:, k, m * 128:(m + 1) * 128], cact2_3[:, k, :], k == 0, k == KC - 1,
                       [w_b, lp_b], [pb])
            ptv = pt[:, 0:4].rearrange("p (m two) -> p m two", two=2)[:, :, 0]
            TT("dve", modT[:, cb * 2:cb * 2 + 2], ptv, b_adaT[:, cb * 2:cb * 2 + 2], ALU.add, [pb, lp_b], [lp_b])
        STT(A1, modT[:, 8:16], 1.0, n1gT, ALU.add, ALU.mult, [lp_b], [lp_b])
        STT(A2, modT[:, 32:40], 1.0, n2gT, ALU.add, ALU.mult, [lp_b], [lp_b])

    def norm_mod(Avec, Bvec):
        for tb in range(NB):
            sl = slice(tb * 512, (tb + 1) * 512)
            pt, pb = next_ps()
            for k in range(KC):
                sq, sq_b = stg_pool.get()
                ACT(sq, xT3[:, k, sl], AF.Square, [xT_b], [sq_b])
                MM(pt, cv["ones"], sq, k == 0, k == KC - 1, [sq_b, cst_b], [pb])
            rs, rs_b = stg_pool.get()
            ACT(rs, pt, AF.Sqrt, [pb], [rs_b], bias=epsT[:, 0:1], scale=1.0 / D)
            P.op("dve", lambda e, rs=rs: e.reciprocal(out=rs, in_=rs), [rs_b], [rs_b])
            for k in range(KC):
                tq, tq_b = stg_pool.get()
                TT("dve", tq, xT3[:, k, sl], rs, ALU.mult, [xT_b, rs_b], [tq_b])
                TS("dve", ns.hT3[:, k, sl], tq, Avec[:, k:k + 1], ALU.mult, [tq_b, lp_b], [ns.hT_b],
                   s2=Bvec[:, k:k + 1], op1=ALU.add)

    P.op("dve", lambda e: e.memset(epsT, EPS), [], [epsT_b])

    def proj_fm(wname, l, col0, ncols, rhs3, rhs_b, kc, epilogue, sub0=0, row0=0, blk=512):
        per = max(128, (WSTG // kc) // 128 * 128)
        per = min(per, blk)
        c = 0
        while c < ncols:
            n = min(per, ncols - c)
            w3, w_b = load_w(wname, l, col0 + c, n, kc, row0=row0)
            for m in range(n // 128):
                for tb in range(NB):
                    pt, pb = next_ps()
                    for k in range(kc):
                        MM(pt, w3[:, k, m * 128:(m + 1) * 128], rhs3[:, k, tb * 512:(tb + 1) * 512], k == 0, k == kc - 1,
                           [w_b, rhs_b], [pb])
                    epilogue(pt, pb, sub0 + (c // 128) + m, tb)
            c += n

    def dst_rows(dname, r0, r1):
        if isinstance(dname, GatherSet):
            return dname.loc_rows(r0, r1)
        return dram[dname][r0:r1, :], dname

    def rope_epilogue(src_sb, src_b, tb, dst_dram, dname, row):
        sl = slice(tb * 512, (tb + 1) * 512)
        pr, prb = next_ps()
        MM(pr, cv["rrot"], src_sb, True, True, [src_b, cst_b], [prb])
        u1, u1_b = stg_pool.get()
        TT("pool", u1, src_sb, ns.C4[:, sl], ALU.mult, [src_b, ns.C4_b], [u1_b])
        u2, u2_b = stg_pool.get()
        TT("dve", u2, pr, ns.S4[:, sl], ALU.mult, [prb, ns.S4_b], [u2_b])
        ob, ob_b = sbf_pool.get()
        TT("dve", ob, u1, u2, ALU.add, [u1_b, u2_b], [ob_b])
        dap, dn = dst_rows(dname, row * 128, (row + 1) * 128)
        store(dap[:, sl], dn, ob, ob_b)

    def qk_epilogue(gvec, dname):
        def ep(pt, pb, sub, tb):
            sq, sq_b = stg_pool.get()
            ACT(sq, pt, AF.Square, [pb], [sq_b])
            p2, p2b = next_ps()
            MM(p2, cv["blockones"], sq, True, True, [sq_b, cst_b], [p2b])
            rs, rs_b = stg_pool.get()
            ACT(rs, p2, AF.Sqrt, [p2b], [rs_b], bias=epsT[:, 0:1], scale=1.0 / HD)
            P.op("dve", lambda e, rs=rs: e.reciprocal(out=rs, in_=rs), [rs_b], [rs_b])
            qn, qn_b = stg_pool.get()
            STT(qn, pt, gvec, rs, ALU.mult, ALU.mult, [pb, lp_b, rs_b], [qn_b])
            rope_epilogue(qn, qn_b, tb, None, dname, sub)
        return ep

    def iq_epilogue(dname, nsub_real):
        def ep(pt, pb, sub, tb):
            qn, qn_b = stg_pool.get()
            CP("act", qn, pt, [pb], [qn_b])
            rope_epilogue(qn, qn_b, tb, None, dname, sub)
        return ep

    def z_epilogue(pt, pb, sub, tb):
        ob, ob_b = sbf_pool.get()
        ACT(ob, pt, AF.Silu, [pb], [ob_b])
        store(zT_d[sub * 128:(sub + 1) * 128, tb * 512:(tb + 1) * 512], "zT_d", ob, ob_b)

    def xbc_epilogue(pt, pb, sub, tb):
        o32, o32_b = stg_pool.get()
        CP("act" if (sub + tb) % 2 == 0 else "dve", o32, pt, [pb], [o32_b])
        store(xbc_raw[sub * 128:(sub + 1) * 128, tb * 512:(tb + 1) * 512], "xbc_raw", o32, o32_b)
        if tb == NB - 1:
            dap, dn = halo_gs.loc_rows(0, 128)
            store(dap[:, sub * 4:sub * 4 + 3], dn, o32[:, 509:512], o32_b)

    def gate_epilogue(pt, pb, sub, tb):
        o32, o32_b = stg_pool.get()
        ACT(o32, pt, AF.Sigmoid, [pb], [o32_b])
        store(gT_d[sub * 128:(sub + 1) * 128, tb * 512:(tb + 1) * 512], "gT_d", o32, o32_b)


    def v_projection(l):
        for qd in range(4):
            w3, w_b = load_w("w_in", l, C_V + qd * 256, 256, KC)
            for i in range(NT):
                va, va_b = ns.vaug[i % 2]
                va4 = va.rearrange("p (pr two c) -> p pr two c", two=2, c=128)
                pt, pb = next_ps()
                for k in range(KC):
                    MM(pt[:, 0:256], ns.hT3[:, k, i * 128:(i + 1) * 128], w3[:, k, :], k == 0, k == KC - 1, [ns.hT_b, w_b], [pb])
                pt4 = pt[:, 0:256].rearrange("p (pr two c) -> p pr two c", two=2, c=64)
                CP("act", va4[:, :, 0, 0:64], pt4[:, :, 0, :], [pb], [va_b])
                CP("dve", va4[:, :, 1, 64:128], pt4[:, :, 1, :], [pb], [va_b])
                dap, dn = v_gs.loc_rows(i * 128, (i + 1) * 128)
                store(dap[:, qd * 512:(qd + 1) * 512], dn, va, va_b)

    def small_projection(l):
        st, st_b = wst_pool.get()
        st3 = st[:, 0:KC * 40].rearrange("p (k c) -> p k c", c=40)
        P.dma("sp", st3[:, :, 0:32], W["w_in"][l][:, C_DT:C_DT + 32].rearrange("(k p) c -> p k c", p=128),
              reads=[dbuf["w_in"]], writes=[st_b])
        P.dma("sp", st3[:, :, 32:40], W["w_in"][l][:, C_IW:C_IW + 8].rearrange("(k p) c -> p k c", p=128),
              reads=[dbuf["w_in"]], writes=[st_b])
        wb, wb_b = wbf_pool.get()
        wb3 = wb[:, 0:KC * 40].rearrange("p (k c) -> p k c", c=40)
        CP("pool", wb[:, 0:KC * 40], st[:, 0:KC * 40], [st_b], [wb_b])
        for i in range(NT):
            pt, pb = next_ps()
            for k in range(KC):
                MM(pt[:, 0:40], ns.hT3[:, k, i * 128:(i + 1) * 128], wb3[:, k, :], k == 0, k == KC - 1, [ns.hT_b, wb_b], [pb])
            xx, xx_b = stg_pool.get()
            CP("dve", xx[:, 64:104], pt[:, 0:40], [pb], [xx_b])
            CP("dve", iw3[:, i, :], xx[:, 96:104], [xx_b], [iw_b])
            TT("dve", xx[:, 0:32], xx[:, 64:96], dtb_bc, ALU.add, [xx_b, dtb_b], [xx_b])
            STT(xx[:, 32:64], xx[:, 0:32], -1.0, xx[:, 0:32], ALU.mult, ALU.max, [xx_b], [xx_b])
            ACT(xx[:, 32:64], xx[:, 32:64], AF.Exp, [xx_b], [xx_b], scale=-1.0)
            ACT(xx[:, 32:64], xx[:, 32:64], AF.Ln, [xx_b], [xx_b], bias=oneT[:, 0:1], scale=1.0)
            STT(dt3[:, i, :], xx[:, 0:32], 0.0, xx[:, 32:64], ALU.max, ALU.add, [xx_b], [dt_b])
            STT(a3[:, i, :], dt3[:, i, :], -1.0, alog_bc, ALU.mult, ALU.mult, [dt_b, alog_b], [a_b])

    P.op("dve", lambda e: e.memset(oneT, 1.0), [], [oneT_b])

    def alloc_hT():
        hT, ns.hT_b = A.bf16("hT", KC * T)
        ns.hT3 = hT.rearrange("p (k t) -> p k t", t=T)

    def phase_A(l):
        if cfg.stage < 1:
            return
        alloc_hT()
        ns.C4, ns.C4_b = A.f32("C4", T)
        ns.S4, ns.S4_b = A.f32("S4", T)
        P.dma("sp", ns.C4, rope_d[0:128, :], reads=[dbuf["rope_d"]], writes=[ns.C4_b])
        P.dma("sp", ns.S4, rope_d[128:256, :], reads=[dbuf["rope_d"]], writes=[ns.S4_b])
        ns.vaug = [A.bf16("vaug%d" % i, 512) for i in range(2)]
        for va, va_b in ns.vaug:
            P.op("pool", lambda e, va=va: e.memset(va, 1.0), [], [va_b])
        layer_params(l)
        if cfg.stage < 2:
            return
        compute_mod(l)
        if cfg.stage < 3:
            return
        norm_mod(A1, modT[:, 0:8])
        if cfg.stage < 4:
            return
        proj_fm("w_in", l, C_Q, 1024, ns.hT3, ns.hT_b, KC, qk_epilogue(qg2[:, 0:1], "qT_d"))
        if cfg.stage < 5:
            return
        proj_fm("w_in", l, C_K, 1024, ns.hT3, ns.hT_b, KC, qk_epilogue(kg2[:, 0:1], kT_gs))
        v_projection(l)
        proj_fm("w_in", l, C_IQ, 512, ns.hT3, ns.hT_b, KC, iq_epilogue("iqT_d", 4))
        st, st_b = wst_pool.get()
        st3 = st[:, 0:KC * 128].rearrange("p (k c) -> p k c", c=128)
        for hf in range(2):
            P.dma("sp", st3[:, :, hf * 64:(hf + 1) * 64],
                  W["w_in"][l][:, C_IK:C_IK + 64].rearrange("(k p) c -> p k c", p=128),
                  reads=[dbuf["w_in"]], writes=[st_b])
        wb, wb_b = wbf_pool.get()
        wb3 = wb[:, 0:KC * 128].rearrange("p (k c) -> p k c", c=128)
        CP("pool", wb[:, 0:KC * 128], st[:, 0:KC * 128], [st_b], [wb_b])
        ikep = iq_epilogue(ik_gs, 1)
        for tb in range(NB):
            pt, pb = next_ps()
            for k in range(KC):
                MM(pt, wb3[:, k, :], ns.hT3[:, k, tb * 512:(tb + 1) * 512], k == 0, k == KC - 1, [wb_b, ns.hT_b], [pb])
            ikep(pt, pb, 0, tb)
        if cfg.stage < 6:
            return
        small_projection(l)
        if cfg.stage < 7:
            return
        proj_fm("w_in", l, C_Z, 2048, ns.hT3, ns.hT_b, KC, z_epilogue)
        proj_fm("w_in", l, C_XBC, 3072, ns.hT3, ns.hT_b, KC, xbc_epilogue)
        proj_fm("w_in", l, C_GA, 2048, ns.hT3, ns.hT_b, KC, gate_epilogue)
        if "dbg_small" in cfg.debug:
            dbg3 = dbg_small.rearrange("p (i c) -> p i c", c=80)
            store(dbg3[:, :, 0:8], "dbg_small", iw3, iw_b)
            store(dbg3[:, :, 8:40], "dbg_small", dt3, dt_b)
            store(dbg3[:, :, 40:72], "dbg_small", a3, a_b)
        if cfg.stage < 8:
            return
        kT_gs.gather()
        v_gs.gather()
        ik_gs.gather()
        halo_gs.gather()


    attnT_d = dten("attnT_d", [8 * 128, T], BF16, "ExternalInput" if "attnT_d" in cfg.feed else "Internal")
    ynT_d = dten("ynT_d", [16 * 128, T], BF16, "ExternalInput" if "ynT_d" in cfg.feed else "Internal")
    aT_d = dten("aT_d", [22 * 128, T], BF16)
    uhalo_gs = GatherSet("uhalo", 128, 44 * 2, F32, 4)

    mixT_d = dten("mixT_d", [8 * 128, T], BF16)

    def phase_D(l):
        m0_ = A.mark()
        wao3, wao_b = load_w_resident("wao", "w_attn_o", l, 8, D)
        wso3, wso_b = load_w_resident("wso", "w_ssm_o", l, 16, D)
        at, at_b = A.bf16("attn_blk", 8 * 512)
        at3 = at.rearrange("p (k t) -> p k t", t=512)
        yn, yn_b = A.bf16("yn_blk", 16 * 512)
        yn3 = yn.rearrange("p (k t) -> p k t", t=512)
        gpool = RPool("gate", 4, 512, "f32")
        for tb in range(NB):
            sl = slice(tb * 512, (tb + 1) * 512)
            P.dma("sp", at3, attnT_d.rearrange("(k p) t -> p k t", p=128)[:, :, sl], reads=[dbuf["attnT_d"]], writes=[at_b])
            P.dma("sp", yn3, ynT_d.rearrange("(k p) t -> p k t", p=128)[:, :, sl], reads=[dbuf["ynT_d"]], writes=[yn_b])
            for m in range(8):
                ga, ga_b = gpool.get()
                gm_, gm_b = gpool.get()
                P.dma("sp", ga, gT_d[m * 128:(m + 1) * 128, sl], reads=[dbuf["gT_d"]], writes=[ga_b])
                P.dma("sp", gm_, gT_d[(8 + m) * 128:(9 + m) * 128, sl], reads=[dbuf["gT_d"]], writes=[gm_b])
                pa, pab = next_ps()
                for k in range(8):
                    MM(pa, wao3[:, k, m * 128:(m + 1) * 128], at3[:, k, :], k == 0, k == 7, [wao_b, at_b], [pab])
                psm, psb = next_ps()
                for k in range(16):
                    MM(psm, wso3[:, k, m * 128:(m + 1) * 128], yn3[:, k, :], k == 0, k == 15, [wso_b, yn_b], [psb])
                t1_, t1b_ = stg_pool.get()
                TT("dve", t1_, pa, ga, ALU.mult, [pab, ga_b], [t1b_])
                t2_, t2b_ = stg_pool.get()
                TT("dve", t2_, psm, gm_, ALU.mult, [psb, gm_b], [t2b_])
                ob, ob_b = sbf_pool.get()
                TT("pool", ob, t1_, t2_, ALU.add, [t1b_, t2b_], [ob_b])
                store(mixT_d[m * 128:(m + 1) * 128, sl], "mixT_d", ob, ob_b)
        P.barrier()
        A.release(m0_)
        m0_ = A.mark()
        wout3, wout_b = load_w_resident("wout", "w_out", l, 8, D)
        mxs = [A.bf16("mix_blk%d" % i, 8 * 512) for i in range(2)]
        for tb in range(NB):
            sl = slice(tb * 512, (tb + 1) * 512)
            mx, mx_b = mxs[tb % 2]
            mx3 = mx.rearrange("p (k t) -> p k t", t=512)
            P.dma("sp", mx3, mixT_d.rearrange("(k p) t -> p k t", p=128)[:, :, sl], reads=[dbuf["mixT_d"]], writes=[mx_b])
            for m2 in range(8):
                po, pob = next_ps()
                for k in range(8):
                    MM(po, wout3[:, k, m2 * 128:(m2 + 1) * 128], mx3[:, k, :], k == 0, k == 7, [wout_b, mx_b], [pob])
                STT(xT3[:, m2, sl], po, modT[:, 16 + m2:17 + m2], xT3[:, m2, sl], ALU.mult, ALU.add, [pob, lp_b, xT_b], [xT_b])
        P.barrier()
        A.release(m0_)

    def phase_EF(l, rank_sel):
        m0_ = A.mark()
        alloc_hT()
        norm_mod(A2, modT[:, 24:32])
        uh, uh_b = A.f32("uhalo_sb", 44 * 2)
        uh3 = uh.rearrange("p (m two) -> p m two", two=2)
        for c0 in range(0, 2 * DFF, 256):
            w3, w_b = load_w("w_up", l, c0, 256, KC)
            pt, pb = next_ps()
            for m in range(2):
                for k in range(KC):
                    MM(pt[:, 2 * m:2 * m + 2], w3[:, k, m * 128:(m + 1) * 128], ns.hT3[:, k, T - 2:T], k == 0, k == KC - 1,
                       [w_b, ns.hT_b], [pb])
            CP("dve", uh3[:, (c0 // 128):(c0 // 128) + 2, :], pt[:, 0:4].rearrange("p (m two) -> p m two", two=2), [pb], [uh_b])
        dap, dn = uhalo_gs.loc_rows(0, 128)
        store(dap, dn, uh, uh_b)
        uhalo_gs.gather()
        pv, pv_b = A.f32("uprev", 44 * 2)
        pv3 = pv.rearrange("p (m two) -> p m two", two=2)
        load_prev_rows(uhalo_gs, pv3, pv_b, 44, 2)
        ub = [A.f32("ubuf%d" % i, 516) for i in range(4)]
        for m in range(22):
            wv3, wv_b = load_w("w_up", l, m * 128, 128, KC)
            wg3, wg_b = load_w("w_up", l, DFF + m * 128, 128, KC)
            conv_out = []
            for tb in range(NB):
                sl = slice(tb * 512, (tb + 1) * 512)
                res = []
                for which, (w3, w_b, sub) in enumerate(((wv3, wv_b, m), (wg3, wg_b, 22 + m))):
                    pt, pb = next_ps()
                    for k in range(KC):
                        MM(pt, w3[:, k, :], ns.hT3[:, k, sl], k == 0, k == KC - 1, [w_b, ns.hT_b], [pb])
                    u, u_b = ub[(tb % 2) * 2 + which]
                    if tb == 0:
                        CP("dve", u[:, 0:2], pv3[:, sub, :], [pv_b], [u_b])
                    else:
                        up, up_b = ub[((tb - 1) % 2) * 2 + which]
                        CP("dve", u[:, 0:2], up[:, 512:514], [up_b], [u_b])
                    CP("act", u[:, 2:514], pt, [pb], [u_b])
                    c_, c_b = stg_pool.get()
                    TS("dve", c_, u[:, 0:512], fcwT[:, sub:sub + 1], ALU.mult, [u_b, lp_b], [c_b],
                       s2=fcbT[:, sub:sub + 1], op1=ALU.add)
                    STT(c_, u[:, 1:513], fcwT[:, 44 + sub:45 + sub], c_, ALU.mult, ALU.add, [u_b, lp_b, c_b], [c_b])
                    STT(c_, u[:, 2:514], fcwT[:, 88 + sub:89 + sub], c_, ALU.mult, ALU.add, [u_b, lp_b, c_b], [c_b])
                    res.append((c_, c_b))
                (cv_, cvb_), (cg_, cgb_) = res
                sg, sg_b = stg_pool.get()
                ACT(sg, cg_, AF.Silu, [cgb_], [sg_b])
                ob, ob_b = sbf_pool.get()
                TT("pool", ob, sg, cv_, ALU.mult, [sg_b, cvb_], [ob_b])
                store(aT_d[m * 128:(m + 1) * 128, sl], "aT_d", ob, ob_b)
        P.barrier()
        A.release(m0_)
        m0_ = A.mark()
        wd3, wd_b = load_w_resident("wdown", "w_down", l, 22, D)
        ab = [A.bf16("a_blk%d" % i, 22 * 512) for i in range(1)]
        for tb in range(NB):
            sl = slice(tb * 512, (tb + 1) * 512)
            a_, a_b_ = ab[0]
            a3_ = a_.rearrange("p (k t) -> p k t", t=512)
            P.dma("sp", a3_, aT_d.rearrange("(k p) t -> p k t", p=128)[:, :, sl], reads=[dbuf["aT_d"]], writes=[a_b_])
            for m2 in range(8):
                po, pob = next_ps()
                for k in range(22):
                    MM(po, wd3[:, k, m2 * 128:(m2 + 1) * 128], a3_[:, k, :], k == 0, k == 21, [wd_b, a_b_], [pob])
                STT(xT3[:, m2, sl], po, modT[:, 40 + m2:41 + m2], xT3[:, m2, sl], ALU.mult, ALU.add, [pob, lp_b, xT_b], [xT_b])
        P.barrier()
        A.release(m0_)


    x_save = dten("x_save", [8 * 128, T], F32)

    def spill_x():
        P.dma("sp", x_save.rearrange("(k p) t -> p k t", p=128), xT3, reads=[xT_b], writes=[dbuf["x_save"]])
        P.barrier()

    def restore_x():
        P.barrier()
        P.dma("sp", xT3, x_save.rearrange("(k p) t -> p k t", p=128), reads=[dbuf["x_save"]], writes=[xT_b])

    NKEY = GSZ * T
    NKT = NKEY // 128
    NKB = NKEY // 512
    NIT = 23
    LO0 = -4096.0
    BIGP = float(2.0 ** 100)
    NSPL = (int(NKEY * 0.41) // 512) * 512
    NACT = NKEY - NSPL
    SCALE = float(HD ** -0.5)

    def phase_B(l):
        m0_ = A.mark()
        score, score_b = A.f32("score", NKEY)
        mb, mbA_b = A.bf16("mb", NKEY)
        mbB_b = Buf("mbB")
        mbT_raw, mbT_b = A.f32("mbT", NKT * 512 // 4)
        mbT = mbT_raw.bitcast(FP8)
        mbT3 = mbT.rearrange("p (k t) -> p k t", t=512)
        ikT, ikT_b = A.bf16("ikT", NKEY)
        ikT3 = ikT.rearrange("p (r t) -> p r t", t=T)
        iqb, iqb_b = A.bf16("iq_blk", 4 * 512)
        iqb3 = iqb.rearrange("p (k t) -> p k t", t=512)
        qb_, qb_b = A.bf16("q_blk", 8 * 512)
        qb3 = qb_.rearrange("p (k t) -> p k t", t=512)
        kst = [A.bf16("kst%d" % i, 2 * 512) for i in range(2)]
        vst = [A.bf16("vst%d" % i, 4 * 512) for i in range(2)]
        ppool = RPool("pT", 3, 512, "bf16")
        ab_, ab_b = A.bf16("attn_o_blk", 8 * 512)
        ab3 = ab_.rearrange("p (k t) -> p k t", t=512)
        negd0, negd0_b = A.f32("negd0", 512)
        sm, sm_b = A.f32("bis_small", 16)
        md, md_b = A.f32("bis_mid", 2)
        cn, cnt_b = A.f32("bis_cnt", 2)
        sa, sacc_b = A.f32("bis_sacc", 2)
        Wk, Wk_b = A.f32("bis_w", NIT)
        lo = sm[:, 0:1]
        mid = md[:, 0:1]
        nmid = md[:, 1:2]
        cnt = cn[:, 0:1]
        sacc = sa[:, 0:1]
        tot = sm[:, 5:6]
        ge = sm[:, 6:7]
        hi0 = sm[:, 7:8]
        qp0 = sm[:, 8:9]
        STT(qp0, rk[:, 0:1], float(T), cv["pidx"][:, 0:1], ALU.mult, ALU.add, [rk_b, cst_b], [sm_b])
        TS("dve", negd0, cv["iotaf"], qp0, ALU.subtract, [cst_b, sm_b], [negd0_b], s2=-BIGP, op1=ALU.mult)
        P.dma("sp", ikT3, ik_gs.g[0].rearrange("(r p) t -> p r t", p=128), reads=[dbuf["ik_g0"]], writes=[ikT_b])
        for qb in range(NB):
            qsl = slice(qb * 512, (qb + 1) * 512)
            P.dma("sp", iqb3, iqT_d.rearrange("(k p) t -> p k t", p=128)[:, :, qsl], reads=[dbuf["iqT_d"]], writes=[iqb_b])
            P.dma("sp", qb3, qT_d.rearrange("(k p) t -> p k t", p=128)[:, :, qsl], reads=[dbuf["qT_d"]], writes=[qb_b])
            for qs in range(4):
                qt = qb * 4 + qs
                for kb in range(NKB):
                    ksl = slice(kb * 512, (kb + 1) * 512)
                    madd, madd_b = stg_pool.get()
                    TS("dve", madd, negd0, float((kb * 512 - qt * 128) * (-BIGP)), ALU.add, [negd0_b], [madd_b],
                       s2=0.0, op1=ALU.min)
                    for h in range(IH):
                        hb = (h % 2) * 64
                        pt, pb = next_ps()
                        MM(pt, iqb3[hb:hb + 64, h // 2, qs * 128:(qs + 1) * 128], ikT[hb:hb + 64, ksl], True, True,
                           [iqb_b, ikT_b], [pb])
                        rl, rl_b = stg_pool.get()
                        ACT(rl, pt, AF.Relu, [pb], [rl_b])
                        if h == 0:
                            STT(score[:, ksl], rl, iw3[:, qt, h:h + 1], madd, ALU.mult, ALU.add, [rl_b, iw_b, madd_b], [score_b])
                        else:
                            STT(score[:, ksl], rl, iw3[:, qt, h:h + 1], score[:, ksl], ALU.mult, ALU.add,
                                [rl_b, iw_b, score_b], [score_b])
                P.op("dve", lambda e: e.tensor_reduce(out=hi0, in_=score, axis=AX.X, op=ALU.max), [score_b], [sm_b])
                TS("dve", tot, hi0, 1.0 - LO0, ALU.add, [sm_b], [sm_b])
                for k in range(NIT):
                    TS("dve", Wk[:, k:k + 1], tot, float(2.0 ** -(k + 1)), ALU.mult, [sm_b], [Wk_b])
                P.op("dve", lambda e: e.memset(lo, LO0), [], [sm_b])
                for k in range(NIT):
                    TT("dve", mid, lo, Wk[:, k:k + 1], ALU.add, [sm_b, Wk_b], [md_b])
                    TS("dve", nmid, mid, -1.0, ALU.mult, [md_b], [md_b])
                    ACT(mb[:, NSPL:NKEY], score[:, NSPL:NKEY], AF.Sign, [score_b, md_b], [mbB_b, sacc_b], bias=nmid, scale=1.0,
                        accum_out=sacc)
                    TS("dve", mb[:, 0:NSPL], score[:, 0:NSPL], mid, ALU.is_ge, [score_b, md_b], [mbA_b, cnt_b],
                       s2=0.0, op1=ALU.add, accum_out=cnt)
                    STT(tot, sacc, 0.5, cnt, ALU.mult, ALU.add, [sacc_b, cnt_b], [sm_b])
                    TS("dve", ge, tot, float(cfg.nkeep - NACT / 2.0), ALU.is_ge, [sm_b], [sm_b])
                    STT(lo, ge, Wk[:, k:k + 1], lo, ALU.mult, ALU.add, [sm_b, Wk_b], [sm_b])
                TS("dve", mb, score, lo, ALU.is_lt, [score_b, sm_b], [mbA_b, mbB_b], s2=NEG, op1=ALU.mult)
                for k4 in range(NKT // 4):
                    pt, pb = next_ps()
                    ptb = pt.bitcast(BF16)
                    for u in range(4):
                        kt = k4 * 4 + u
                        P.op("pe", lambda e, ptb=ptb, u=u, kt=kt: e.transpose(out=ptb[:, u * 128:(u + 1) * 128],
                                                                             in_=mb[:, kt * 128:(kt + 1) * 128],
                                                                             identity=ident_bf),
                             [mbA_b, mbB_b, ident_bf_b], [pb])
                    dst_ = mbT3[:, k4 * 4:(k4 + 1) * 4, qs * 128:(qs + 1) * 128]
                    src_ = ptb[:, 0:512].rearrange("p (u t) -> p u t", t=128)
                    if k4 % 2 == 0:
                        P.op("act", lambda e, dst_=dst_, src_=src_: e.activation(out=dst_, in_=src_, func=AF.Copy, saturate=False),
                             [pb], [mbT_b])
                    else:
                        P.op("dve", lambda e, dst_=dst_, src_=src_: e.tensor_copy(out=dst_, in_=src_, saturate=False),
                             [pb], [mbT_b])
            for hg in range(4):
                accs = [psum[i_] for i_ in range(4)]
                for k4 in range(NKT // 4):
                    kk, kk_b = kst[k4 % 2]
                    kk3 = kk.rearrange("p (pr t) -> p pr t", t=512)
                    vv, vv_b = vst[k4 % 2]
                    vv3 = vv.rearrange("p (u c) -> p u c", c=512)
                    r_ = (k4 * 512) // T
                    off = (k4 * 512) % T
                    for pr in range(2):
                        gap, gn = kT_gs.g_rows(r_, (hg * 2 + pr) * 128, (hg * 2 + pr + 1) * 128)
                        P.dma("sp", kk3[:, pr, :], gap[:, off:off + 512], reads=[dbuf[gn]], writes=[kk_b])
                    gap, gn = v_gs.g_rows(r_, off, off + 512) if v_gs.rpc >= 512 else (None, None)
                    if gap is not None:
                        P.dma("sp", vv3, gap.rearrange("(u p) c -> p u c", p=128)[:, :, hg * 512:(hg + 1) * 512],
                              reads=[dbuf[gn]], writes=[vv_b])
                    else:
                        for u in range(4):
                            gap, gn = v_gs.g_rows(r_, off + u * 128, off + (u + 1) * 128)
                            P.dma("sp", vv3[:, u, :], gap[:, hg * 512:(hg + 1) * 512], reads=[dbuf[gn]], writes=[vv_b])
                    for u in range(4):
                        kt = k4 * 4 + u
                        for hh in range(4):
                            h = hg * 4 + hh
                            hb = (h % 2) * 64
                            pt, pb = next_ps_s()
                            MM(pt, kk3[hb:hb + 64, hh // 2, u * 128:(u + 1) * 128], qb3[hb:hb + 64, h // 2, :], True, False,
                               [kk_b, qb_b], [pb])
                            MM(pt, ident_bf, mbT3[:, kt, :], False, True, [ident_bf_b, mbT_b], [pb])
                            pT, pT_b = ppool.get()
                            ACT(pT, pt, AF.Exp, [pb], [pT_b], scale=SCALE)
                            acc, acc_b = accs[hh]
                            MM(acc, vv3[:, u, hh * 128:(hh + 1) * 128], pT, kt == 0, kt == NKT - 1, [vv_b, pT_b], [acc_b])
                for hh in range(4):
                    h = hg * 4 + hh
                    hb = (h % 2) * 64
                    acc, acc_b = accs[hh]
                    osb, osb_b = stg_pool.get()
                    CP("act", osb, acc, [acc_b], [osb_b])
                    pw, pw_b = next_ps_s()
                    MM(pw, cv["swap"], osb, True, True, [cst_b, osb_b], [pw_b])
                    rc, rc_b = stg_pool.get()
                    P.op("dve", lambda e, rc=rc, pw=pw: e.reciprocal(out=rc, in_=pw), [pw_b], [rc_b])
                    TT("dve", ab3[hb:hb + 64, h // 2, :], osb[hb:hb + 64, :], rc[hb:hb + 64, :], ALU.mult, [osb_b, rc_b], [ab_b])
            store(attnT_d.rearrange("(k p) t -> p k t", p=128)[:, :, qsl], "attnT_d", ab3, ab_b)
        P.barrier()
        A.release(m0_)

    ps_s_rr = [0]

    def next_ps_s():
        i = ps_s_rr[0]
        ps_s_rr[0] = (i + 1) % 4
        return psum[4 + i]


    NCH = T // 256
    st_gs = GatherSet("ssdF", 128, 2048, F32, 4)
    dd_gs = GatherSet("ssdD", 128, SH, F32, 4)

    def bcast_last(ap, n):
        return bass.AP(ap.tensor, ap.offset, [list(x) for x in ap.ap] + [[0, n]])

    def phase_C(l):
        m0_ = A.mark()
        rwp = RPool("rw", 3, 260, "f32")
        xsT, xsT_b = A.f32("xsT", 16 * 256)
        xsT3 = xsT.rearrange("p (m t) -> p m t", t=256)
        BT, BT_b = A.bf16("BT", 4 * 256)
        BT3 = BT.rearrange("p (g t) -> p g t", t=256)
        CT, CT_b = A.bf16("CT", 4 * 256)
        CT3 = CT.rearrange("p (g t) -> p g t", t=256)
        xdt, xdt_b = A.bf16("xdt", 2 * 2048)
        xdt3 = xdt.rearrange("p (i c) -> p i c", c=2048)
        xdd, xdd_b = A.bf16("xdd", 2 * 2048)
        xdd3 = xdd.rearrange("p (i c) -> p i c", c=2048)
        Btm, Btm_b = A.bf16("Btm", 2 * 512)
        Btm4 = Btm.rearrange("p (i g n) -> p i g n", g=4, n=128)
        acs, acs_b = A.f32("acs_tm", 2 * SH)
        acs3 = acs.rearrange("p (i h) -> p i h", h=SH)
        totb, totb_b = A.f32("tot_bc", SH)
        etot, etot_b = A.f32("etot", SH)
        ds_, ds_b = A.f32("ds", 2 * SH)
        ds3 = ds_.rearrange("p (i h) -> p i h", h=SH)
        Sst, Sst_b = A.f32("Sstate", 2048)
        Sbf, Sbf_b = A.bf16("Sstate_bf", 2048)
        Dacc, Dacc_b = A.f32("Dacc", SH)
        cbm, cbm_b = A.f32("CBm", 4 * 384)
        cbm3 = cbm.rearrange("p (g t) -> p g t", t=384)
        triL, triL_b = A.f32("triL", 512)
        yg, yg_b = A.f32("yg", 16 * 256)
        yg3 = yg.rearrange("p (m t) -> p m t", t=256)
        prev, prev_b = A.f32("xbc_prev", 24 * 4)
        prev3 = prev.rearrange("p (m c) -> p m c", c=4)
        zp = RPool("zt", 2, 256, "bf16")
        CP("dve", triL[:, 0:128], cv["tri"], [cst_b], [triL_b])
        CP("dve", triL[:, 128:256], cv["ones"], [cst_b], [triL_b])
        P.op("dve", lambda e: e.memset(triL[:, 256:384], 0.0), [], [triL_b])
        CP("dve", triL[:, 384:512], cv["tri"], [cst_b], [triL_b])
        load_prev_rows(halo_gs, prev3, prev_b, 24, 4)

        def conv_block(blk, c, out_ap, out_b, eng_out="act"):
            rw, rw_b = rwp.get()
            if c == 0:
                P.dma("sp", rw[:, 3:259], xbc_raw[blk * 128:(blk + 1) * 128, 0:256], reads=[dbuf["xbc_raw"]], writes=[rw_b])
                CP("pool", rw[:, 0:3], prev3[:, blk, 0:3], [prev_b], [rw_b])
            else:
                P.dma("sp", rw[:, 0:259], xbc_raw[blk * 128:(blk + 1) * 128, c * 256 - 3:c * 256 + 256],
                      reads=[dbuf["xbc_raw"]], writes=[rw_b])
            ac, ac_b = stg_pool.get()
            TS("dve", ac[:, 0:256], rw[:, 0:256], scwT[:, blk:blk + 1], ALU.mult, [rw_b, lp_b], [ac_b],
               s2=scbT[:, blk:blk + 1], op1=ALU.add)
            for tap in range(1, 4):
                STT(ac[:, 0:256], rw[:, tap:tap + 256], scwT[:, tap * 24 + blk:tap * 24 + blk + 1], ac[:, 0:256], ALU.mult, ALU.add,
                    [rw_b, lp_b, ac_b], [ac_b])
            ACT(out_ap, ac[:, 0:256], AF.Silu, [ac_b], [out_b])

        def ssd_pass(compute_y):
            for c in range(NCH):
                csl = slice(c * 256, (c + 1) * 256)
                for blk in range(16):
                    conv_block(blk, c, xsT3[:, blk, :], xsT_b)
                for g in range(4):
                    conv_block(16 + g, c, BT3[:, g, :], BT_b)
                if compute_y:
                    for g in range(4):
                        conv_block(20 + g, c, CT3[:, g, :], CT_b)
                for i in range(2):
                    ti = c * 2 + i
                    for bank in range(4):
                        pt, pb = next_ps()
                        for u in range(4):
                            blk = bank * 4 + u
                            P.op("pe", lambda e, pt=pt, u=u, blk=blk, i=i: e.transpose(
                                out=pt[:, u * 128:(u + 1) * 128], in_=xsT3[:, blk, i * 128:(i + 1) * 128], identity=cv["ident"]),
                                [xsT_b, cst_b], [pb])
                        TT("dve", xdt3[:, i, bank * 512:(bank + 1) * 512].rearrange("p (h q) -> p h q", q=64),
                           pt.rearrange("p (h q) -> p h q", q=64), bcast_last(dt3[:, ti, bank * 8:(bank + 1) * 8], 64), ALU.mult,
                           [pb, dt_b], [xdt_b])
                    pt, pb = next_ps()
                    ptb = pt.bitcast(BF16)
                    for g in range(4):
                        P.op("pe", lambda e, ptb=ptb, g=g, i=i: e.transpose(out=ptb[:, g * 128:(g + 1) * 128],
                                                                           in_=BT3[:, g, i * 128:(i + 1) * 128], identity=ident_bf),
                             [BT_b, ident_bf_b], [pb])
                    CP("act", Btm4[:, i, :, :], ptb[:, 0:512].rearrange("p (g n) -> p g n", n=128), [pb], [Btm_b])
                a0 = a3[:, c * 2, :]
                a1 = a3[:, c * 2 + 1, :]
                pt, pb = next_ps()
                MM(pt[:, 0:32], cv["tri"], a0, True, True, [cst_b, a_b], [pb])
                MM(pt[:, 32:64], cv["ones"], a0, True, False, [cst_b, a_b], [pb])
                MM(pt[:, 32:64], cv["tri"], a1, False, True, [cst_b, a_b], [pb])
                MM(pt[:, 64:96], cv["ones"], a0, True, False, [cst_b, a_b], [pb])
                MM(pt[:, 64:96], cv["ones"], a1, False, True, [cst_b, a_b], [pb])
                CP("dve", acs, pt[:, 0:64], [pb], [acs_b])
                CP("dve", totb, pt[:, 64:96], [pb], [totb_b])
                ACT(etot, totb, AF.Exp, [totb_b], [etot_b])
                TT("dve", Dacc, Dacc, totb, ALU.add, [Dacc_b, totb_b], [Dacc_b])
                for i in range(2):
                    TT("dve", ds3[:, i, :], totb, acs3[:, i, :], ALU.subtract, [totb_b, acs_b], [ds_b])
                ACT(ds_, ds_, AF.Exp, [ds_b], [ds_b])
                for i in range(2):
                    TT("pool", xdd3[:, i, :].rearrange("p (h q) -> p h q", q=64), xdt3[:, i, :].rearrange("p (h q) -> p h q", q=64),
                       bcast_last(ds3[:, i, :], 64), ALU.mult, [xdt_b, ds_b], [xdd_b])
                if compute_y:
                    CP("dve", Sbf, Sst, [Sst_b], [Sbf_b])
                    for g in range(4):
                        pt, pb = next_ps()
                        MM(pt[:, 0:256], BT3[:, g, 0:128], CT3[:, g, :], True, True, [BT_b, CT_b], [pb])
                        MM(pt[:, 256:384], BT3[:, g, 128:256], CT3[:, g, 128:256], True, True, [BT_b, CT_b], [pb])
                        TT("dve", cbm3[:, g, 0:128], pt[:, 0:128], cv["tri"], ALU.mult, [pb, cst_b], [cbm_b])
                        CP("dve", cbm3[:, g, 128:256], pt[:, 128:256], [pb], [cbm_b])
                        TT("dve", cbm3[:, g, 256:384], pt[:, 256:384], cv["tri"], ALU.mult, [pb, cst_b], [cbm_b])
                    for pr in range(16):
                        py, pyb = next_ps()
                        for hh in range(2):
                            h = pr * 2 + hh
                            g = h // 8
                            hb = hh * 64
                            r0, r0_b = stg_pool.get()
                            TS("dve", r0[:, 0:256], triL[:, 0:256], a0[:, h:h + 1], ALU.mult, [triL_b, a_b], [r0_b])
                            TS("pool", r0[:, 256:512], triL[:, 256:512], a1[:, h:h + 1], ALU.mult, [triL_b, a_b], [r0_b])
                            pbc, pbcb = next_ps()
                            MM(pbc[:, 0:256], cv["ones"], r0[:, 0:256], True, False, [cst_b, r0_b], [pbcb])
                            MM(pbc[:, 0:256], cv["ones"], r0[:, 256:512], False, True, [cst_b, r0_b], [pbcb])
                            d_, d_b = stg_pool.get()
                            TS("dve", d_[:, 0:256], pbc[:, 0:256], acs3[:, 0, h:h + 1], ALU.subtract, [pbcb, acs_b], [d_b],
                               s2=0.0, op1=ALU.min)
                            TS("dve", d_[:, 256:384], pbc[:, 128:256], acs3[:, 1, h:h + 1], ALU.subtract, [pbcb, acs_b], [d_b],
                               s2=0.0, op1=ALU.min)
                            ACT(d_[:, 0:384], d_[:, 0:384], AF.Exp, [d_b], [d_b])
                            G_, G_b = sbf_pool.get()
                            TT("dve", G_[:, 0:384], d_[:, 0:384], cbm3[:, g, :], ALU.mult, [d_b, cbm_b], [G_b])
                            eb, eb_b = stg_pool.get()
                            ACT(eb[:, 0:256], pbc[:, 0:256], AF.Exp, [pbcb], [eb_b])
                            Ce, Ce_b = sbf_pool.get()
                            TT("pool", Ce[:, 0:256], eb[:, 0:256], CT3[:, g, :], ALU.mult, [eb_b, CT_b], [Ce_b])
                            MM(py[hb:hb + 64, 0:256], xdt3[:, 0, h * 64:(h + 1) * 64], G_[:, 0:256], True, False, [xdt_b, G_b], [pyb])
                            MM(py[hb:hb + 64, 128:256], xdt3[:, 1, h * 64:(h + 1) * 64], G_[:, 256:384], False, False,
                               [xdt_b, G_b], [pyb])
                            MM(py[hb:hb + 64, 0:256], Sbf[:, h * 64:(h + 1) * 64], Ce[:, 0:256], False, True, [Sbf_b, Ce_b], [pyb])
                        zt, zt_b = zp.get()
                        P.dma("sp", zt, zT_d[pr * 128:(pr + 1) * 128, csl], reads=[dbuf["zT_d"]], writes=[zt_b])
                        yv, yv_b = stg_pool.get()
                        STT(yv[:, 0:256], xsT3[:, pr, :], dskT[:, pr:pr + 1], py[:, 0:256], ALU.mult, ALU.add, [xsT_b, lp_b, pyb], [yv_b])
                        TT("pool", yg3[:, pr, :], yv[:, 0:256], zt, ALU.mult, [yv_b, zt_b], [yg_b])
                    for g in range(4):
                        pss, pssb = next_ps()
                        for q in range(4):
                            sq, sq_b = stg_pool.get()
                            ACT(sq[:, 0:256], yg3[:, g * 4 + q, :], AF.Square, [yg_b], [sq_b])
                            MM(pss[:, 0:256], cv["ones"], sq[:, 0:256], q == 0, q == 3, [cst_b, sq_b], [pssb])
                        rs, rs_b = stg_pool.get()
                        ACT(rs[:, 0:256], pss[:, 0:256], AF.Sqrt, [pssb], [rs_b], bias=epsT[:, 0:1], scale=1.0 / 512.0)
                        P.op("dve", lambda e, rs=rs: e.reciprocal(out=rs[:, 0:256], in_=rs[:, 0:256]), [rs_b], [rs_b])
                        for q in range(4):
                            pr = g * 4 + q
                            ob, ob_b = sbf_pool.get()
                            STT(ob[:, 0:256], yg3[:, pr, :], sngT[:, pr:pr + 1], rs[:, 0:256], ALU.mult, ALU.mult,
                                [yg_b, lp_b, rs_b], [ob_b])
                            store(ynT_d[pr * 128:(pr + 1) * 128, csl], "ynT_d", ob[:, 0:256], ob_b)
                for g in range(4):
                    pst, pstb = next_ps()
                    for i in range(2):
                        MM(pst, Btm4[:, i, g, :], xdd3[:, i, g * 512:(g + 1) * 512], i == 0, i == 1, [Btm_b, xdd_b], [pstb])
                    sg3 = Sst[:, g * 512:(g + 1) * 512].rearrange("p (h q) -> p h q", q=64)
                    TT("dve", sg3, sg3, bcast_last(etot[:, g * 8:(g + 1) * 8], 64), ALU.mult, [Sst_b, etot_b], [Sst_b])
                    TT("dve", Sst[:, g * 512:(g + 1) * 512], Sst[:, g * 512:(g + 1) * 512], pst, ALU.add, [Sst_b, pstb], [Sst_b])

        P.op("dve", lambda e: e.memset(Sst, 0.0), [], [Sst_b])
        P.op("dve", lambda e: e.memset(Dacc, 0.0), [], [Dacc_b])
        ssd_pass(False)
        dap, dn = st_gs.loc_rows(0, 128)
        store(dap, dn, Sst, Sst_b)
        dap, dn = dd_gs.loc_rows(0, 128)
        store(dap, dn, Dacc, Dacc_b)
        st_gs.gather()
        dd_gs.gather()
        Dr = []
        for r in range(GSZ - 1):
            t_, tb_ = A.f32("Dr%d" % r, SH)
            gap, gn = dd_gs.g_rows(r, 0, 128)
            P.dma("sp", t_, gap, reads=[dbuf[gn]], writes=[tb_])
            Dr.append((t_, tb_))
        lt, lt_b = A.f32("ltflag", 4)
        for m in range(GSZ - 1):
            TS("dve", lt[:, m:m + 1], rk[:, 0:1], float(m), ALU.is_gt, [rk_b], [lt_b])
        P.op("dve", lambda e: e.memset(Sst, 0.0), [], [Sst_b])
        Fr, Fr_b = A.f32("Fr", 2048)
        for r in range(GSZ - 1):
            wr, wr_b = A.f32("wr%d" % r, SH)
            P.op("dve", lambda e, wr=wr: e.memset(wr, 0.0), [], [wr_b])
            for m in range(r + 1, GSZ - 1):
                STT(wr, Dr[m][0], lt[:, m:m + 1], wr, ALU.mult, ALU.add, [Dr[m][1], lt_b, wr_b], [wr_b])
            ACT(wr, wr, AF.Exp, [wr_b], [wr_b])
            TS("dve", wr, wr, lt[:, r:r + 1], ALU.mult, [wr_b, lt_b], [wr_b])
            gap, gn = st_gs.g_rows(r, 0, 128)
            P.dma("sp", Fr, gap, reads=[dbuf[gn]], writes=[Fr_b])
            F3 = Fr.rearrange("p (h q) -> p h q", q=64)
            TT("dve", F3, F3, bcast_last(wr, 64), ALU.mult, [Fr_b, wr_b], [Fr_b])
            TT("dve", Sst, Sst, Fr, ALU.add, [Sst_b, Fr_b], [Sst_b])
        ssd_pass(True)
        P.barrier()
        A.release(m0_)

    P.dma("sp", rk[:, 0:4], bcast_ap(rank_in, 128, 4), reads=[dbuf["rank"]], writes=[rk_b])
    for r in range(GSZ):
        TS("dve", prevsel[:, r:r + 1], rk[:, 0:1], float(r + 1), ALU.is_equal, [rk_b], [prevsel_b])

    def load_prev_rows(gs, dst3, dst_b, nsub, ncols):
        first = True
        for r in range(GSZ - 1):
            tmp_, tmpb_ = A.f32("prevtmp%d" % r, nsub * ncols)
            tmp3 = tmp_.rearrange("p (m c) -> p m c", c=ncols)
            gap, gn = gs.g_rows(r, 0, 128)
            P.dma("sp", tmp_, gap, reads=[dbuf[gn]], writes=[tmpb_])
            if first:
                TS("dve", dst3, tmp3, prevsel[:, r:r + 1], ALU.mult, [tmpb_, prevsel_b], [dst_b])
                first = False
            else:
                STT(dst3, tmp3, prevsel[:, r:r + 1], dst3, ALU.mult, ALU.add, [tmpb_, prevsel_b, dst_b], [dst_b])

    for l in range(depth):
        A.release(m_x)
        phase_A(l)
        P.barrier()
        A.release(m_x)
        if cfg.stage >= 30:
            spill_x()
            A.release(m_pers)
            phase_B(l)
            A.release(m_pers)
            if cfg.stage >= 40:
                phase_C(l)
            A.release(m_x)
            restore_x()
        if cfg.stage >= 20:
            phase_D(l)
        if cfg.stage >= 21:
            phase_EF(l, None)
    A.release(m_x)
    for name in dbg_copies:
        rows = dram[name].shape[0]
        for r0 in range(0, rows, 512):
            r1 = min(rows, r0 + 512)
            P.dma("sp", dram[name + "_dbg"][r0:r1, :], dram[name][r0:r1, :], reads=[dbuf[name]], writes=[dbuf[name + "_dbg"]])
    P.barrier()

    m0 = A.mark()
    outs = [A.f32("otok%d" % i, D) for i in range(2)]
    for i in range(NT):
        ot, ot_b = outs[i % 2]
        for k in range(KC):
            pt, pb = next_ps()
            P.op("pe", lambda e, pt=pt, k=k, i=i: e.transpose(out=pt[:, 0:128], in_=xT3[:, k, i * 128:(i + 1) * 128],
                                                             identity=cv["ident"]),
                 reads=[xT_b, cst_b], writes=[pb])
            if k % 2 == 0:
                P.op("act", lambda e, pt=pt, k=k, ot=ot: e.copy(out=ot[:, k * 128:(k + 1) * 128], in_=pt[:, 0:128]),
                     reads=[pb], writes=[ot_b])
            else:
                P.op("dve", lambda e, pt=pt, k=k, ot=ot: e.tensor_copy(out=ot[:, k * 128:(k + 1) * 128], in_=pt[:, 0:128]),
                     reads=[pb], writes=[ot_b])
        P.dma("sp", y_out[i * 128:(i + 1) * 128, :], ot, reads=[ot_b], writes=[dbuf["y"]])
    P.barrier()
    A.release(m0)

    P.emit(stack)
    stack.close()
    return nc


def make_in_maps(cfg, inputs):
    T = cfg.T
    depth = cfg.depth
    maps = []
    f = lambda a: np.ascontiguousarray(np.asarray(a, dtype=np.float32))
    shared = {
        "consts": CONST_ARR,
        "w_ada": f(inputs["w_ada"]), "b_ada": f(inputs["b_ada"]).reshape(depth, 48, 128),
        "norm1_g": f(inputs["norm1_g"]).reshape(depth, KC, 128), "w_in": f(inputs["w_in"]),
        "q_norm_g": f(inputs["q_norm_g"]).reshape(depth, 1, 64), "k_norm_g": f(inputs["k_norm_g"]).reshape(depth, 1, 64),
        "ssm_conv_w": f(inputs["ssm_conv_w"]).reshape(depth, 4, 24, 128),
        "ssm_conv_b": f(inputs["ssm_conv_b"]).reshape(depth, 24, 128),
        "dt_bias": f(inputs["dt_bias"]).reshape(depth, 1, SH), "a_log": f(inputs["a_log"]).reshape(depth, 1, SH),
        "d_skip": f(inputs["d_skip"]).reshape(depth, 1, SH), "ssm_norm_g": f(inputs["ssm_norm_g"]).reshape(depth, 16, 128),
        "w_attn_o": f(inputs["w_attn_o"]), "w_ssm_o": f(inputs["w_ssm_o"]), "w_out": f(inputs["w_out"]),
        "norm2_g": f(inputs["norm2_g"]).reshape(depth, KC, 128), "w_up": f(inputs["w_up"]),
        "ffn_conv_w": f(inputs["ffn_conv_w"]).reshape(depth, 3, 44, 128),
        "ffn_conv_b": f(inputs["ffn_conv_b"]).reshape(depth, 44, 128), "w_down": f(inputs["w_down"]),
    }
    x = f(inputs["x"])
    c = f(inputs["c"])
    pos = np.ascontiguousarray(np.asarray(inputs["positions"], dtype=np.int32))
    for r in range(NCORES):
        b, j = divmod(r, GSZ)
        m = dict(shared)
        m["x"] = np.ascontiguousarray(x[b, j * T:(j + 1) * T, :])
        m["c"] = np.ascontiguousarray(c[b].reshape(KC, 128))
        m["positions"] = np.ascontiguousarray(pos[b, j * T:(j + 1) * T].reshape(1, T))
        rk = np.zeros((1, 4), np.float32)
        rk[0, 0] = j
        m["rank"] = rk
        for k_, v_ in (getattr(cfg, "feed_data", None) or {}).items():
            m[k_] = v_[r]
        maps.append(m)
    return maps


_CACHE = {}


def run(cfg, inputs):
    key = (cfg.S, cfg.depth, tuple(sorted(cfg.debug)), cfg.stage, tuple(sorted(cfg.feed)))
    if key not in _CACHE:
        _CACHE[key] = build_program(cfg)
    nc = _CACHE[key]
    maps = make_in_maps(cfg, inputs)
    res = run_bass_kernel_spmd(nc, maps, core_ids=list(range(NCORES)))
    return res.results


def kernel(**inputs):
    cfg = Cfg(seq=int(np.asarray(inputs["x"]).shape[1]), depth=int(np.asarray(inputs["w_in"]).shape[0]))
    results = run(cfg, inputs)
    B = np.asarray(inputs["x"]).shape[0]
    out = np.zeros((B, cfg.S, D), np.float32)
    for r in range(NCORES):
        b, j = divmod(r, GSZ)
        out[b, j * cfg.T:(j + 1) * cfg.T, :] = results[r]["y"]
    return out
```

```python
from contextlib import ExitStack
import numpy as np
import ml_dtypes
import concourse.bass as bass
import concourse.mybir as mybir
from concourse.bass_utils import run_bass_kernel_spmd

F32 = mybir.dt.float32
BF16 = mybir.dt.bfloat16
I32 = mybir.dt.int32
FP8 = mybir.dt.float8e5
AF = mybir.ActivationFunctionType
ALU = mybir.AluOpType
AX = mybir.AxisListType

NCORES = 8
GSZ = 4
D = 1024
KC = D // 128
HEADS = 16
HD = 64
IH = 8
TOPK = 256
DI = 2048
SH = 32
SG = 4
NST = 128
XBC = DI + 2 * SG * NST
DFF = 2816
EPS = 1e-6
C_Q, C_K, C_V, C_IQ, C_IK, C_IW, C_Z, C_XBC, C_DT, C_GA, C_GM = (
    0, 1024, 2048, 3072, 3584, 3648, 3656, 5704, 8776, 8808, 9832)
INW = 10856
NEG = -30000.0


class Buf:
    __slots__ = ("name", "w", "r")

    def __init__(self, name):
        self.name = name
        self.w = None
        self.r = {}


class Prog:
    ENGS = ("pe", "act", "dve", "pool", "sp")

    def __init__(self, nc, n_dma=40):
        self.nc = nc
        self.ops = {e: [] for e in self.ENGS}
        self.cnt = {e: 0 for e in self.ENGS}
        self.known = {e: {} for e in self.ENGS}
        self.dma_val = [0] * n_dma
        self.dma_next = 0
        self.cc_val = 0

    def _need(self, eng, k, v):
        if k == eng and eng == "pe":
            return
        kn = self.known[eng]
        if kn.get(k, 0) >= v:
            return
        kn[k] = v
        self.ops[eng].append(("wait", k, v))

    def _deps(self, eng, reads, writes):
        for b in reads:
            if b.w is not None:
                self._need(eng, *b.w)
        for b in writes:
            if b.w is not None:
                self._need(eng, *b.w)
            for k, v in b.r.items():
                self._need(eng, k, v)

    def _mark(self, tok, reads, writes):
        for b in writes:
            b.w = tok
            b.r = {}
        for b in reads:
            if b in writes:
                continue
            if b.r.get(tok[0], 0) < tok[1]:
                b.r[tok[0]] = tok[1]

    def op(self, eng, fn, reads=(), writes=()):
        self._deps(eng, reads, writes)
        self.cnt[eng] += 1
        tok = (eng, self.cnt[eng])
        self.ops[eng].append(("ins", fn, eng, 1, self._where()))
        self._mark(tok, reads, writes)
        return tok

    DEBUG_WHERE = False

    def _where(self):
        if not Prog.DEBUG_WHERE:
            return None
        import traceback
        return [(f.lineno, f.name) for f in traceback.extract_stack(limit=6)[:-2]]

    def dma(self, q, out, in_, reads=(), writes=()):
        i = self.dma_next
        self.dma_next = (i + 1) % len(self.dma_val)
        key = ("dma", i)
        self._deps(q, reads, writes)
        if self.dma_val[i]:
            self._need(q, key, self.dma_val[i])
        self.dma_val[i] += 16
        tok = (key, self.dma_val[i])
        self.ops[q].append(("ins", lambda e, o=out, s=in_: e.dma_start(out=o, in_=s), key, 16))
        self._mark(tok, reads, writes)
        return tok

    def collective(self, fn, reads=(), writes=()):
        self._deps("pool", reads, writes)
        self.cc_val += 1
        tok = ("cc", self.cc_val)
        self.ops["pool"].append(("ins", fn, "cc", 1))
        self._mark(tok, reads, writes)
        return tok

    def barrier(self):
        for e in self.ENGS:
            for f in ("pe", "act", "dve", "pool"):
                if self.cnt[f]:
                    self._need(e, f, self.cnt[f])
            for i, v in enumerate(self.dma_val):
                if v:
                    self._need(e, ("dma", i), v)
            if self.cc_val:
                self._need(e, "cc", self.cc_val)

    def emit(self, stack):
        nc = self.nc
        sems = {}
        for e in ("pe", "act", "dve", "pool"):
            sems[e] = stack.enter_context(nc.semaphore("s_" + e))
        for i in range(len(self.dma_val)):
            sems[("dma", i)] = stack.enter_context(nc.semaphore("d%d" % i))
        sems["cc"] = stack.enter_context(nc.semaphore("s_cc"))
        block = stack.enter_context(nc.Block())

        def mk(name):
            def body(eng):
                for o in self.ops[name]:
                    if o[0] == "wait":
                        eng.wait_ge(sems[o[1]], o[2])
                    else:
                        ins = o[1](eng)
                        ins.then_inc(sems[o[2]], o[3])
                        if Prog.DEBUG_WHERE and len(o) > 4:
                            print("INS", name, getattr(getattr(ins, "ins", None), "name", None), o[4])
            return body

        block.tensor(mk("pe"))
        block.scalar(mk("act"))
        block.vector(mk("dve"))
        block.gpsimd(mk("pool"))
        block.sync(mk("sp"))


class Arena:
    def __init__(self, big, nwords):
        self.big = big
        self.n = nwords
        self.off = 0

    def mark(self):
        return self.off

    def release(self, m):
        self.off = m

    def f32(self, name, cols):
        a = self.off
        self.off += cols
        assert self.off <= self.n, ("SBUF arena overflow", name, self.off, self.n)
        return self.big[:, a:a + cols], Buf(name)

    def bf16(self, name, cols):
        w = (cols + 1) // 2
        a = self.off
        self.off += w
        assert self.off <= self.n, ("SBUF arena overflow", name, self.off, self.n)
        return self.big[:, a:a + w].bitcast(BF16)[:, 0:cols], Buf(name)


def make_consts():
    c = {}
    c["ident"] = np.eye(128, dtype=np.float32)
    c["ones"] = np.ones((128, 128), np.float32)
    bo = np.zeros((128, 128), np.float32)
    bo[:64, :64] = 1.0
    bo[64:, 64:] = 1.0
    c["blockones"] = bo
    rr = np.zeros((128, 128), np.float32)
    for m in range(128):
        if (m % 64) < 32:
            rr[m + 32, m] = -1.0
        else:
            rr[m - 32, m] = 1.0
    c["rrot"] = rr
    tri = (np.arange(128)[:, None] <= np.arange(128)[None, :]).astype(np.float32)
    c["tri"] = tri
    c["causb"] = np.where(np.arange(128)[None, :] <= np.arange(128)[:, None], 0.0, -1e30).astype(np.float32)
    invf = (1.0 / (10000.0 ** (np.arange(0, 64, 2, dtype=np.float32) / 64.0))).astype(np.float32)
    c["invf"] = np.tile(invf, 4)[:, None].astype(np.float32) * np.ones((1, 128), np.float32)
    c["iotaf"] = np.tile(np.arange(512, dtype=np.float32)[None, :], (128, 1))
    c["pidx"] = np.tile(np.arange(128, dtype=np.float32)[:, None], (1, 128))
    sw = np.zeros((128, 128), np.float32)
    for m in range(128):
        sw[(m + 64) % 128, m] = 1.0
    c["swap"] = sw
    names = ["ident", "ones", "blockones", "rrot", "tri", "causb", "invf", "iotaf", "pidx", "swap"]
    return [(n, c[n].shape[1]) for n in names], np.concatenate([c[n] for n in names], axis=1)


CONST_NAMES, CONST_ARR = make_consts()


class Cfg:
    def __init__(self, seq=8192, depth=4, debug=(), stage=99, feed=()):
        self.stage = stage
        self.feed = set(feed)
        self.S = seq
        self.T = seq // GSZ
        self.depth = depth
        self.debug = set(debug)
        self.NT = self.T // 128
        self.NB = self.T // 512
        self.NCH = self.T // 256
        self.nkeep = min(TOPK, seq // 4)


def build_program(cfg):
    T, NT, NB, depth = cfg.T, cfg.NT, cfg.NB, cfg.depth
    nc = bass.Bass("TRN2", target_bir_lowering=False)
    stack = ExitStack()
    P = Prog(nc)
    dram = {}
    dbuf = {}

    dbg_copies = []

    def dten(name, shape, dtype, kind="Internal"):
        if name in cfg.debug:
            if "_g" in name[-4:]:
                dcp = nc.dram_tensor(name + "_dbg", list(shape), dtype, kind="ExternalOutput").ap()
                dram[name + "_dbg"] = dcp
                dbuf[name + "_dbg"] = Buf(name + "_dbg")
                dbg_copies.append(name)
            else:
                kind = "ExternalOutput"
        t = nc.dram_tensor(name, list(shape), dtype, kind=kind).ap()
        dram[name] = t
        dbuf[name] = Buf(name)
        return t

    x_in = dten("x", [T, D], F32, "ExternalInput")
    c_in = dten("c", [KC, 128], F32, "ExternalInput")
    pos_in = dten("positions", [1, T], I32, "ExternalInput")
    consts_in = dten("consts", [128, CONST_ARR.shape[1]], F32, "ExternalInput")
    rank_in = dten("rank", [1, 4], F32, "ExternalInput")
    W = {}
    for nm, shp in (("w_ada", [depth, D, 6 * D]), ("b_ada", [depth, 48, 128]), ("norm1_g", [depth, KC, 128]),
                    ("w_in", [depth, D, INW]), ("q_norm_g", [depth, 1, 64]), ("k_norm_g", [depth, 1, 64]),
                    ("ssm_conv_w", [depth, 4, 24, 128]), ("ssm_conv_b", [depth, 24, 128]),
                    ("dt_bias", [depth, 1, SH]), ("a_log", [depth, 1, SH]), ("d_skip", [depth, 1, SH]),
                    ("ssm_norm_g", [depth, 16, 128]), ("w_attn_o", [depth, D, D]), ("w_ssm_o", [depth, DI, D]),
                    ("w_out", [depth, D, D]), ("norm2_g", [depth, KC, 128]), ("w_up", [depth, D, 2 * DFF]),
                    ("ffn_conv_w", [depth, 3, 44, 128]), ("ffn_conv_b", [depth, 44, 128]),
                    ("w_down", [depth, DFF, D])):
        W[nm] = dten(nm, shp, F32, "ExternalInput")
    y_out = dten("y", [T, D], F32, "ExternalOutput")

    NWORDS = 52224
    big = stack.enter_context(nc.sbuf_tensor("arena", [128, NWORDS], F32))
    A = Arena(big, NWORDS)
    psum = []
    for i in range(8):
        pt = stack.enter_context(nc.psum_tensor("ps%d" % i, [128, 512], F32))
        psum.append((pt[:, :], Buf("ps%d" % i)))
    ps_rr = [0]

    def next_ps():
        i = ps_rr[0]
        ps_rr[0] = (i + 1) % 8
        return psum[i]

    cst, cst_b = A.f32("consts", CONST_ARR.shape[1])
    cv = {}
    o = 0
    for n, wd in CONST_NAMES:
        cv[n] = cst[:, o:o + wd]
        o += wd
    ident_bf, ident_bf_b = A.bf16("ident_bf", 128)

    class RPool:
        def __init__(self, name, n, cols, kind):
            self.t = [(A.f32 if kind == "f32" else A.bf16)("%s%d" % (name, i), cols) for i in range(n)]
            self.i = 0

        def get(self):
            r = self.t[self.i]
            self.i = (self.i + 1) % len(self.t)
            return r


    lp, lp_b = A.f32("lp", 512)
    dtb_bc, dtb_b = A.f32("dtb_bc", SH)
    alog_bc, alog_b = A.f32("alog_bc", SH)
    iw_tm, iw_b = A.f32("iw_tm", NT * 8)
    dt_tm, dt_b = A.f32("dt_tm", NT * SH)
    a_tm, a_b = A.f32("a_tm", NT * SH)
    halfsel, halfsel_b = A.f32("halfsel", 128)
    lfm_pool = RPool("lfm", 3, 128, "f32")
    WSTG = 2048
    wst_pool = RPool("wst", 2, WSTG, "f32")
    wbf_pool = RPool("wbf", 2, WSTG, "bf16")
    stg_pool = RPool("stg", 8, 512, "f32")
    sbf_pool = RPool("sbf", 4, 512, "bf16")
    epsT, epsT_b = A.f32("epsT", 2)
    oneT, oneT_b = A.f32("oneT", 2)
    rk, rk_b = A.f32("rk", 8)
    prevsel, prevsel_b = A.f32("prevsel", 4)
    class NS:
        pass
    ns = NS()
    m_pers = A.mark()
    xT, xT_b = A.f32("xT", KC * T)
    xT3 = xT.rearrange("p (k t) -> p k t", t=T)
    m_x = A.mark()

    P.dma("sp", cst, consts_in, reads=[dbuf["consts"]], writes=[cst_b])
    P.op("dve", lambda e: e.tensor_copy(out=ident_bf, in_=cv["ident"]), reads=[cst_b], writes=[ident_bf_b])

    def transpose_f32(dst, dst_b, src, src_b, rows, cols, evac="act"):
        pt, pb = next_ps()
        P.op("pe", lambda e: e.transpose(out=pt[0:cols, 0:rows], in_=src, identity=cv["ident"][0:rows, 0:rows]),
             reads=[src_b, cst_b], writes=[pb])
        if evac == "act":
            P.op("act", lambda e: e.copy(out=dst, in_=pt[0:cols, 0:rows]), reads=[pb], writes=[dst_b])
        else:
            P.op("dve", lambda e: e.tensor_copy(out=dst, in_=pt[0:cols, 0:rows]), reads=[pb], writes=[dst_b])

    def load_fm(dst, dst_b, src_ap, src_name, nrow):
        m = A.mark()
        tmp, tmp_b = A.f32("lfm_tmp", 128)
        P.dma("sp", tmp[0:nrow, :], src_ap, reads=[dbuf[src_name]], writes=[tmp_b])
        transpose_f32(dst, dst_b, tmp[0:nrow, :], tmp_b, nrow, 128)
        A.release(m)
        return tmp_b

    m0 = A.mark()
    xtoks = [A.f32("xtok%d" % i, D) for i in range(2)]
    for i in range(NT):
        xt, xt_b = xtoks[i % 2]
        P.dma("sp", xt, x_in[i * 128:(i + 1) * 128, :], reads=[dbuf["x"]], writes=[xt_b])
        for k in range(KC):
            pt, pb = next_ps()
            P.op("pe", lambda e, pt=pt, xt=xt, k=k: e.transpose(out=pt[:, 0:128], in_=xt[:, k * 128:(k + 1) * 128],
                                                               identity=cv["ident"]),
                 reads=[xt_b, cst_b], writes=[pb])
            eng = "act" if k % 2 == 0 else "dve"
            if eng == "act":
                P.op("act", lambda e, pt=pt, k=k, i=i: e.copy(out=xT3[:, k, i * 128:(i + 1) * 128], in_=pt[:, 0:128]),
                     reads=[pb], writes=[xT_b])
            else:
                P.op("dve", lambda e, pt=pt, k=k, i=i: e.tensor_copy(out=xT3[:, k, i * 128:(i + 1) * 128], in_=pt[:, 0:128]),
                     reads=[pb], writes=[xT_b])
    P.barrier()
    A.release(m0)


    def ACT(out, in_, func, reads, writes, **kw):
        P.op("act", lambda e: e.activation(out=out, in_=in_, func=func, **kw), reads, writes)

    def TS(eng, out, in0, s1, op0, reads, writes, s2=None, op1=None, **kw):
        if op1 is None:
            P.op(eng, lambda e: e.tensor_scalar(out=out, in0=in0, scalar1=s1, scalar2=None, op0=op0, **kw), reads, writes)
        else:
            P.op(eng, lambda e: e.tensor_scalar(out=out, in0=in0, scalar1=s1, scalar2=s2, op0=op0, op1=op1, **kw),
                 reads, writes)

    def TT(eng, out, in0, in1, op, reads, writes):
        P.op(eng, lambda e: e.tensor_tensor(out=out, in0=in0, in1=in1, op=op), reads, writes)

    def STT(out, in0, scalar, in1, op0, op1, reads, writes):
        P.op("dve", lambda e: e.scalar_tensor_tensor(out=out, in0=in0, scalar=scalar, in1=in1, op0=op0, op1=op1),
             reads, writes)

    def MM(out, lhsT, rhs, start, stop, reads, writes):
        P.op("pe", lambda e: e.matmul(out, lhsT, rhs, start=start, stop=stop), reads, writes)

    def CP(eng, out, in_, reads, writes):
        if eng == "act":
            P.op("act", lambda e: e.copy(out=out, in_=in_), reads, writes)
        else:
            P.op(eng, lambda e: e.tensor_copy(out=out, in_=in_), reads, writes)

    def bcast_ap(ap2d_row, nparts, ncols, offset_elems=0):
        return bass.AP(ap2d_row.tensor, ap2d_row.offset + offset_elems, [[0, nparts], [1, ncols]])

    qT_d = dten("qT_d", [8 * 128, T], BF16)
    groups = [[0, 1, 2, 3], [4, 5, 6, 7]]

    class GatherSet:
        def __init__(self, name, rows, cols, dtype, esz):
            rpc = rows
            while rpc * cols * esz > (1 << 20):
                rpc //= 2
            assert rows % rpc == 0
            self.name, self.rows, self.cols, self.rpc, self.n = name, rows, cols, rpc, rows // rpc
            self.loc = [dten("%s_loc%d" % (name, c), [rpc, cols], dtype) for c in range(self.n)]
            self.g = [dten("%s_g%d" % (name, c), [GSZ * rpc, cols], dtype) for c in range(self.n)]

        def loc_rows(self, r0, r1):
            c = r0 // self.rpc
            assert (r1 - 1) // self.rpc == c
            return self.loc[c][r0 - c * self.rpc:r1 - c * self.rpc, :], "%s_loc%d" % (self.name, c)

        def g_rows(self, rank, r0, r1):
            c = r0 // self.rpc
            assert (r1 - 1) // self.rpc == c
            base = rank * self.rpc - c * self.rpc
            return self.g[c][base + r0:base + r1, :], "%s_g%d" % (self.name, c)

        def gather(self):
            for c in range(self.n):
                ln, gn = "%s_loc%d" % (self.name, c), "%s_g%d" % (self.name, c)
                P.collective(lambda e, ln=ln, gn=gn: e.collective_compute("AllGather", ALU.bypass, replica_groups=groups,
                                                                          ins=[dram[ln]], outs=[dram[gn]]),
                             reads=[dbuf[ln]], writes=[dbuf[gn]])

    kT_gs = GatherSet("kT", 8 * 128, T, BF16, 2)
    v_gs = GatherSet("v", T, 2048, BF16, 2)
    iqT_d = dten("iqT_d", [4 * 128, T], BF16)
    ik_gs = GatherSet("ik", 128, T, BF16, 2)
    zT_d = dten("zT_d", [16 * 128, T], BF16)
    xbc_raw = dten("xbc_raw", [24 * 128, T], F32)
    halo_gs = GatherSet("halo", 128, 24 * 4, F32, 4)
    gT_d = dten("gT_d", [16 * 128, T], F32)
    dbg_small = dten("dbg_small", [128, NT * 80], F32)

    rope_d = dten("rope_d", [2 * 128, T], F32)
    m0 = A.mark()
    C4, C4_b = A.f32("C4", T)
    S4, S4_b = A.f32("S4", T)
    posi, posi_b = A.f32("posi", T)
    posi_i = posi.bitcast(I32)
    ang, ang_b = A.f32("ang", T)
    t1, t1_b = A.f32("rt1", T)
    t2, t2_b = A.f32("rt2", T)
    t2_i = t2.bitcast(I32)
    P.dma("sp", posi_i, bcast_ap(pos_in, 128, T), reads=[dbuf["positions"]], writes=[posi_b])
    CP("dve", ang, posi_i, [posi_b], [ang_b])
    TS("dve", ang, ang, cv["invf"][:, 0:1], ALU.mult, [ang_b, cst_b], [ang_b])
    TWO_PI = 2.0 * np.pi
    C1 = 6.28125
    C2 = TWO_PI - C1
    for dst, dst_b, shift in ((S4, S4_b, 0.0), (C4, C4_b, np.pi / 2.0)):
        TS("dve", t1, ang, shift, ALU.add, [ang_b], [t1_b], s2=1.0 / TWO_PI, op1=ALU.mult)
        CP("dve", t2_i, t1, [t1_b], [t2_b])
        CP("dve", t1, t2_i, [t2_b], [t1_b])
        TS("dve", t2, ang, shift, ALU.add, [ang_b], [t2_b])
        STT(t2, t1, -C1, t2, ALU.mult, ALU.add, [t1_b, t2_b], [t2_b])
        STT(t2, t1, -C2, t2, ALU.mult, ALU.add, [t1_b, t2_b], [t2_b])
        TS("dve", t1, t2, float(np.pi), ALU.is_gt, [t2_b], [t1_b])
        STT(t2, t1, -TWO_PI, t2, ALU.mult, ALU.add, [t1_b, t2_b], [t2_b])
        TS("dve", t1, t2, float(-np.pi), ALU.is_lt, [t2_b], [t1_b])
        STT(t2, t1, TWO_PI, t2, ALU.mult, ALU.add, [t1_b, t2_b], [t2_b])
        TS("dve", t2, t2, float(np.pi), ALU.min, [t2_b], [t2_b], s2=float(-np.pi), op1=ALU.max)
        ACT(dst, t2, AF.Sin, [t2_b], [dst_b])
    P.dma("sp", rope_d[0:128, :], C4, reads=[C4_b], writes=[dbuf["rope_d"]])
    P.dma("sp", rope_d[128:256, :], S4, reads=[S4_b], writes=[dbuf["rope_d"]])
    P.barrier()
    A.release(m0)

    o_ = [0]

    def lp_alloc(n):
        a = o_[0]
        o_[0] += n
        assert o_[0] <= 512
        return lp[:, a:a + n]

    b_adaT = lp_alloc(48)
    modT = lp_alloc(48)
    n1gT = lp_alloc(8)
    n2gT = lp_alloc(8)
    scwT = lp_alloc(96)
    scbT = lp_alloc(24)
    sngT = lp_alloc(16)
    fcwT = lp_alloc(132)
    fcbT = lp_alloc(44)
    qg2 = lp_alloc(1)
    kg2 = lp_alloc(1)
    A1 = lp_alloc(8)
    A2 = lp_alloc(8)
    cact2 = lp_alloc(16)
    dskT = lp_alloc(16)
    cact2_3 = cact2.rearrange("p (k two) -> p k two", two=2)
    iw3 = iw_tm.rearrange("p (i h) -> p i h", h=8)
    dt3 = dt_tm.rearrange("p (i h) -> p i h", h=SH)
    a3 = a_tm.rearrange("p (i h) -> p i h", h=SH)

    m0 = A.mark()
    tmp, tmp_b = A.f32("ctmp", 128)
    P.dma("sp", tmp[0:KC, :], c_in, reads=[dbuf["c"]], writes=[tmp_b])
    pt, pb = next_ps()
    P.op("pe", lambda e, pt=pt: e.transpose(out=pt[:, 0:KC], in_=tmp[0:KC, :], identity=cv["ident"][0:KC, 0:KC]),
         reads=[tmp_b, cst_b], writes=[pb])
    ACT(cact2_3[:, :, 0], pt[:, 0:KC], AF.Silu, [pb], [lp_b])
    ACT(cact2_3[:, :, 1], pt[:, 0:KC], AF.Silu, [pb], [lp_b])
    P.barrier()
    A.release(m0)

    def load_fm_rows(dst, src_ap, src_name, nrow):
        tmp_, tmpb_ = lfm_pool.get()
        P.dma("sp", tmp_[0:nrow, :], src_ap, reads=[dbuf[src_name]], writes=[tmpb_])
        pt_, pb_ = next_ps()
        P.op("pe", lambda e: e.transpose(out=pt_[:, 0:nrow], in_=tmp_[0:nrow, :], identity=cv["ident"][0:nrow, 0:nrow]),
             reads=[tmpb_, cst_b], writes=[pb_])
        CP("dve", dst, pt_[:, 0:nrow], [pb_], [lp_b])

    def layer_params(l):
        load_fm_rows(b_adaT, W["b_ada"][l], "b_ada", 48)
        load_fm_rows(n1gT, W["norm1_g"][l], "norm1_g", KC)
        load_fm_rows(n2gT, W["norm2_g"][l], "norm2_g", KC)
        for tap in range(4):
            load_fm_rows(scwT[:, tap * 24:(tap + 1) * 24], W["ssm_conv_w"][l, tap], "ssm_conv_w", 24)
        load_fm_rows(scbT, W["ssm_conv_b"][l], "ssm_conv_b", 24)
        load_fm_rows(sngT, W["ssm_norm_g"][l], "ssm_norm_g", 16)
        for tap in range(3):
            load_fm_rows(fcwT[:, tap * 44:(tap + 1) * 44], W["ffn_conv_w"][l, tap], "ffn_conv_w", 44)
        load_fm_rows(fcbT, W["ffn_conv_b"][l], "ffn_conv_b", 44)
        for dst, nm in ((qg2, "q_norm_g"), (kg2, "k_norm_g")):
            tmp_, tmpb_ = lfm_pool.get()
            P.dma("sp", tmp_[0:1, 0:64], W[nm][l], reads=[dbuf[nm]], writes=[tmpb_])
            P.dma("sp", tmp_[0:1, 64:128], W[nm][l], reads=[dbuf[nm]], writes=[tmpb_])
            pt_, pb_ = next_ps()
            P.op("pe", lambda e, pt_=pt_, tmp_=tmp_: e.transpose(out=pt_[:, 0:1], in_=tmp_[0:1, :],
                                                                 identity=cv["ident"][0:1, 0:1]),
                 reads=[tmpb_, cst_b], writes=[pb_])
            CP("dve", dst, pt_[:, 0:1], [pb_], [lp_b])
        tmp_, tmpb_ = lfm_pool.get()
        P.dma("sp", tmp_[0:16, 0:2], W["d_skip"][l].rearrange("o (c two) -> (o c) two", two=2),
              reads=[dbuf["d_skip"]], writes=[tmpb_])
        pt_, pb_ = next_ps()
        P.op("pe", lambda e: e.transpose(out=pt_[0:2, 0:16], in_=tmp_[0:16, 0:2], identity=cv["ident"][0:16, 0:16]),
             reads=[tmpb_, cst_b], writes=[pb_])
        tmp2_, tmp2b_ = lfm_pool.get()
        CP("dve", tmp2_[0:2, 0:16], pt_[0:2, 0:16], [pb_], [tmp2b_])
        pt2_, pb2_ = next_ps()
        MM(pt2_[:, 0:16], halfsel[0:2, :], tmp2_[0:2, 0:16], True, True, [tmp2b_, halfsel_b], [pb2_])
        CP("dve", dskT, pt2_[:, 0:16], [pb2_], [lp_b])
        P.dma("sp", dtb_bc, bcast_ap(W["dt_bias"][l], 128, SH), reads=[dbuf["dt_bias"]], writes=[dtb_b])
        P.dma("sp", alog_bc, bcast_ap(W["a_log"][l], 128, SH), reads=[dbuf["a_log"]], writes=[alog_b])
        ACT(alog_bc, alog_bc, AF.Exp, [alog_b], [alog_b])

    P.dma("sp", halfsel[0:1, :], consts_in[0:1, 256:384], reads=[dbuf["consts"]], writes=[halfsel_b])
    P.dma("sp", halfsel[1:2, :], consts_in[64:65, 256:384], reads=[dbuf["consts"]], writes=[halfsel_b])


    cast_rr = [0]

    def load_w(wname, l, col0, ncols, kc, row0=0, to_bf16=True, dst=None):
        assert kc * ncols <= WSTG
        st, st_b = wst_pool.get()
        st3 = st[:, 0:kc * ncols].rearrange("p (k c) -> p k c", c=ncols)
        src = W[wname][l][row0:row0 + kc * 128, col0:col0 + ncols].rearrange("(k p) c -> p k c", p=128)
        P.dma("sp", st3, src, reads=[dbuf[wname]], writes=[st_b])
        if not to_bf16:
            return st3, st_b
        if dst is not None:
            CP("pool", dst[0], st3, [st_b], [dst[1]])
            return dst
        wb, wb_b = wbf_pool.get()
        wb3 = wb[:, 0:kc * ncols].rearrange("p (k c) -> p k c", c=ncols)
        CP("pool", wb[:, 0:kc * ncols], st[:, 0:kc * ncols], [st_b], [wb_b])
        return wb3, wb_b

    def load_w_resident(name, wname, l, kc, ncols):
        wr, wr_b = A.bf16(name, kc * ncols)
        wr3 = wr.rearrange("p (k c) -> p k c", c=ncols)
        kstep = max(1, WSTG // ncols) if ncols <= WSTG else 1
        cstep = min(ncols, WSTG)
        for k0 in range(0, kc, kstep):
            kk = min(kstep, kc - k0)
            for c0 in range(0, ncols, cstep):
                cc = min(cstep, ncols - c0)
                load_w(wname, l, c0, cc, kk, row0=k0 * 128, dst=(wr3[:, k0:k0 + kk, c0:c0 + cc], wr_b))
        return wr3, wr_b


    def store(dst_ap, dname, src_ap, src_b):
        P.dma("pool", dst_ap, src_ap, reads=[src_b], writes=[dbuf[dname]])

    def compute_mod(l):
        for cb in range(24):
            w3, w_b = load_w("w_ada", l, cb * 256, 256, KC, to_bf16=False)
            pt, pb = next_ps()
            for m in range(2):
                for k in range(KC):
                    MM(pt[:, 2 * m:2 * m + 2], w3[:, k, m * 128:(m + 1) * 128], cact2_3[:, k, :], k == 0, k == KC - 1,
                       [w_b, lp_b], [pb])
            ptv = pt[:, 0:4].rearrange("p (m two) -> p m two", two=2)[:, :, 0]
            TT("dve", modT[:, cb * 2:cb * 2 + 2], ptv, b_adaT[:, cb * 2:cb * 2 + 2], ALU.add, [pb, lp_b], [lp_b])
        STT(A1, modT[:, 8:16], 1.0, n1gT, ALU.add, ALU.mult, [lp_b], [lp_b])
        STT(A2, modT[:, 32:40], 1.0, n2gT, ALU.add, ALU.mult, [lp_b], [lp_b])

    def norm_mod(Avec, Bvec):
        for tb in range(NB):
            sl = slice(tb * 512, (tb + 1) * 512)
            pt, pb = next_ps()
            for k in range(KC):
                sq, sq_b = stg_pool.get()
                ACT(sq, xT3[:, k, sl], AF.Square, [xT_b], [sq_b])
                MM(pt, cv["ones"], sq, k == 0, k == KC - 1, [sq_b, cst_b], [pb])
            rs, rs_b = stg_pool.get()
            ACT(rs, pt, AF.Sqrt, [pb], [rs_b], bias=epsT[:, 0:1], scale=1.0 / D)
            P.op("dve", lambda e, rs=rs: e.reciprocal(out=rs, in_=rs), [rs_b], [rs_b])
            for k in range(KC):
                tq, tq_b = stg_pool.get()
                TT("dve", tq, xT3[:, k, sl], rs, ALU.mult, [xT_b, rs_b], [tq_b])
                TS("dve", ns.hT3[:, k, sl], tq, Avec[:, k:k + 1], ALU.mult, [tq_b, lp_b], [ns.hT_b],
                   s2=Bvec[:, k:k + 1], op1=ALU.add)

    P.op("dve", lambda e: e.memset(epsT, EPS), [], [epsT_b])

    def proj_fm(wname, l, col0, ncols, rhs3, rhs_b, kc, epilogue, sub0=0, row0=0, blk=512):
        per = max(128, (WSTG // kc) // 128 * 128)
        per = min(per, blk)
        c = 0
        while c < ncols:
            n = min(per, ncols - c)
            w3, w_b = load_w(wname, l, col0 + c, n, kc, row0=row0)
            for m in range(n // 128):
                for tb in range(NB):
                    pt, pb = next_ps()
                    for k in range(kc):
                        MM(pt, w3[:, k, m * 128:(m + 1) * 128], rhs3[:, k, tb * 512:(tb + 1) * 512], k == 0, k == kc - 1,
                           [w_b, rhs_b], [pb])
                    epilogue(pt, pb, sub0 + (c // 128) + m, tb)
            c += n

    def dst_rows(dname, r0, r1):
        if isinstance(dname, GatherSet):
            return dname.loc_rows(r0, r1)
        return dram[dname][r0:r1, :], dname

    def rope_epilogue(src_sb, src_b, tb, dst_dram, dname, row):
        sl = slice(tb * 512, (tb + 1) * 512)
        pr, prb = next_ps()
        MM(pr, cv["rrot"], src_sb, True, True, [src_b, cst_b], [prb])
        u1, u1_b = stg_pool.get()
        TT("pool", u1, src_sb, ns.C4[:, sl], ALU.mult, [src_b, ns.C4_b], [u1_b])
        u2, u2_b = stg_pool.get()
        TT("dve", u2, pr, ns.S4[:, sl], ALU.mult, [prb, ns.S4_b], [u2_b])
        ob, ob_b = sbf_pool.get()
        TT("dve", ob, u1, u2, ALU.add, [u1_b, u2_b], [ob_b])
        dap, dn = dst_rows(dname, row * 128, (row + 1) * 128)
        store(dap[:, sl], dn, ob, ob_b)

    def qk_epilogue(gvec, dname):
        def ep(pt, pb, sub, tb):
            sq, sq_b = stg_pool.get()
            ACT(sq, pt, AF.Square, [pb], [sq_b])
            p2, p2b = next_ps()
            MM(p2, cv["blockones"], sq, True, True, [sq_b, cst_b], [p2b])
            rs, rs_b = stg_pool.get()
            ACT(rs, p2, AF.Sqrt, [p2b], [rs_b], bias=epsT[:, 0:1], scale=1.0 / HD)
            P.op("dve", lambda e, rs=rs: e.reciprocal(out=rs, in_=rs), [rs_b], [rs_b])
            qn, qn_b = stg_pool.get()
            STT(qn, pt, gvec, rs, ALU.mult, ALU.mult, [pb, lp_b, rs_b], [qn_b])
            rope_epilogue(qn, qn_b, tb, None, dname, sub)
        return ep

    def iq_epilogue(dname, nsub_real):
        def ep(pt, pb, sub, tb):
            qn, qn_b = stg_pool.get()
            CP("act", qn, pt, [pb], [qn_b])
            rope_epilogue(qn, qn_b, tb, None, dname, sub)
        return ep

    def z_epilogue(pt, pb, sub, tb):
        ob, ob_b = sbf_pool.get()
        ACT(ob, pt, AF.Silu, [pb], [ob_b])
        store(zT_d[sub * 128:(sub + 1) * 128, tb * 512:(tb + 1) * 512], "zT_d", ob, ob_b)

    def xbc_epilogue(pt, pb, sub, tb):
        o32, o32_b = stg_pool.get()
        CP("act" if (sub + tb) % 2 == 0 else "dve", o32, pt, [pb], [o32_b])
        store(xbc_raw[sub * 128:(sub + 1) * 128, tb * 512:(tb + 1) * 512], "xbc_raw", o32, o32_b)
        if tb == NB - 1:
            dap, dn = halo_gs.loc_rows(0, 128)
            store(dap[:, sub * 4:sub * 4 + 3], dn, o32[:, 509:512], o32_b)

    def gate_epilogue(pt, pb, sub, tb):
        o32, o32_b = stg_pool.get()
        ACT(o32, pt, AF.Sigmoid, [pb], [o32_b])
        store(gT_d[sub * 128:(sub + 1) * 128, tb * 512:(tb + 1) * 512], "gT_d", o32, o32_b)


    def v_projection(l):
        for qd in range(4):
            w3, w_b = load_w("w_in", l, C_V + qd * 256, 256, KC)
            for i in range(NT):
                va, va_b = ns.vaug[i % 2]
                va4 = va.rearrange("p (pr two c) -> p pr two c", two=2, c=128)
                pt, pb = next_ps()
                for k in range(KC):
                    MM(pt[:, 0:256], ns.hT3[:, k, i * 128:(i + 1) * 128], w3[:, k, :], k == 0, k == KC - 1, [ns.hT_b, w_b], [pb])
                pt4 = pt[:, 0:256].rearrange("p (pr two c) -> p pr two c", two=2, c=64)
                CP("act", va4[:, :, 0, 0:64], pt4[:, :, 0, :], [pb], [va_b])
                CP("dve", va4[:, :, 1, 64:128], pt4[:, :, 1, :], [pb], [va_b])
                dap, dn = v_gs.loc_rows(i * 128, (i + 1) * 128)
                store(dap[:, qd * 512:(qd + 1) * 512], dn, va, va_b)

    def small_projection(l):
        st, st_b = wst_pool.get()
        st3 = st[:, 0:KC * 40].rearrange("p (k c) -> p k c", c=40)
        P.dma("sp", st3[:, :, 0:32], W["w_in"][l][:, C_DT:C_DT + 32].rearrange("(k p) c -> p k c", p=128),
              reads=[dbuf["w_in"]], writes=[st_b])
        P.dma("sp", st3[:, :, 32:40], W["w_in"][l][:, C_IW:C_IW + 8].rearrange("(k p) c -> p k c", p=128),
              reads=[dbuf["w_in"]], writes=[st_b])
        wb, wb_b = wbf_pool.get()
        wb3 = wb[:, 0:KC * 40].rearrange("p (k c) -> p k c", c=40)
        CP("pool", wb[:, 0:KC * 40], st[:, 0:KC * 40], [st_b], [wb_b])
        for i in range(NT):
            pt, pb = next_ps()
            for k in range(KC):
                MM(pt[:, 0:40], ns.hT3[:, k, i * 128:(i + 1) * 128], wb3[:, k, :], k == 0, k == KC - 1, [ns.hT_b, wb_b], [pb])
            xx, xx_b = stg_pool.get()
            CP("dve", xx[:, 64:104], pt[:, 0:40], [pb], [xx_b])
            CP("dve", iw3[:, i, :], xx[:, 96:104], [xx_b], [iw_b])
            TT("dve", xx[:, 0:32], xx[:, 64:96], dtb_bc, ALU.add, [xx_b, dtb_b], [xx_b])
            STT(xx[:, 32:64], xx[:, 0:32], -1.0, xx[:, 0:32], ALU.mult, ALU.max, [xx_b], [xx_b])
            ACT(xx[:, 32:64], xx[:, 32:64], AF.Exp, [xx_b], [xx_b], scale=-1.0)
            ACT(xx[:, 32:64], xx[:, 32:64], AF.Ln, [xx_b], [xx_b], bias=oneT[:, 0:1], scale=1.0)
            STT(dt3[:, i, :], xx[:, 0:32], 0.0, xx[:, 32:64], ALU.max, ALU.add, [xx_b], [dt_b])
            STT(a3[:, i, :], dt3[:, i, :], -1.0, alog_bc, ALU.mult, ALU.mult, [dt_b, alog_b], [a_b])

    P.op("dve", lambda e: e.memset(oneT, 1.0), [], [oneT_b])

    def alloc_hT():
        hT, ns.hT_b = A.bf16("hT", KC * T)
        ns.hT3 = hT.rearrange("p (k t) -> p k t", t=T)

    def phase_A(l):
        if cfg.stage < 1:
            return
        alloc_hT()
        ns.C4, ns.C4_b = A.f32("C4", T)
        ns.S4, ns.S4_b = A.f32("S4", T)
        P.dma("sp", ns.C4, rope_d[0:128, :], reads=[dbuf["rope_d"]], writes=[ns.C4_b])
        P.dma("sp", ns.S4, rope_d[128:256, :], reads=[dbuf["rope_d"]], writes=[ns.S4_b])
        ns.vaug = [A.bf16("vaug%d" % i, 512) for i in range(2)]
        for va, va_b in ns.vaug:
            P.op("pool", lambda e, va=va: e.memset(va, 1.0), [], [va_b])
        layer_params(l)
        if cfg.stage < 2:
            return
        compute_mod(l)
        if cfg.stage < 3:
            return
        norm_mod(A1, modT[:, 0:8])
        if cfg.stage < 4:
            return
        proj_fm("w_in", l, C_Q, 1024, ns.hT3, ns.hT_b, KC, qk_epilogue(qg2[:, 0:1], "qT_d"))
        if cfg.stage < 5:
            return
        proj_fm("w_in", l, C_K, 1024, ns.hT3, ns.hT_b, KC, qk_epilogue(kg2[:, 0:1], kT_gs))
        v_projection(l)
        proj_fm("w_in", l, C_IQ, 512, ns.hT3, ns.hT_b, KC, iq_epilogue("iqT_d", 4))
        st, st_b = wst_pool.get()
        st3 = st[:, 0:KC * 128].rearrange("p (k c) -> p k c", c=128)
        for hf in range(2):
            P.dma("sp", st3[:, :, hf * 64:(hf + 1) * 64],
                  W["w_in"][l][:, C_IK:C_IK + 64].rearrange("(k p) c -> p k c", p=128),
                  reads=[dbuf["w_in"]], writes=[st_b])
        wb, wb_b = wbf_pool.get()
        wb3 = wb[:, 0:KC * 128].rearrange("p (k c) -> p k c", c=128)
        CP("pool", wb[:, 0:KC * 128], st[:, 0:KC * 128], [st_b], [wb_b])
        ikep = iq_epilogue(ik_gs, 1)
        for tb in range(NB):
            pt, pb = next_ps()
            for k in range(KC):
                MM(pt, wb3[:, k, :], ns.hT3[:, k, tb * 512:(tb + 1) * 512], k == 0, k == KC - 1, [wb_b, ns.hT_b], [pb])
            ikep(pt, pb, 0, tb)
        if cfg.stage < 6:
            return
        small_projection(l)
        if cfg.stage < 7:
            return
        proj_fm("w_in", l, C_Z, 2048, ns.hT3, ns.hT_b, KC, z_epilogue)
        proj_fm("w_in", l, C_XBC, 3072, ns.hT3, ns.hT_b, KC, xbc_epilogue)
        proj_fm("w_in", l, C_GA, 2048, ns.hT3, ns.hT_b, KC, gate_epilogue)
        if "dbg_small" in cfg.debug:
            dbg3 = dbg_small.rearrange("p (i c) -> p i c", c=80)
            store(dbg3[:, :, 0:8], "dbg_small", iw3, iw_b)
            store(dbg3[:, :, 8:40], "dbg_small", dt3, dt_b)
            store(dbg3[:, :, 40:72], "dbg_small", a3, a_b)
        if cfg.stage < 8:
            return
        kT_gs.gather()
        v_gs.gather()
        ik_gs.gather()
        halo_gs.gather()


    attnT_d = dten("attnT_d", [8 * 128, T], BF16, "ExternalInput" if "attnT_d" in cfg.feed else "Internal")
    ynT_d = dten("ynT_d", [16 * 128, T], BF16, "ExternalInput" if "ynT_d" in cfg.feed else "Internal")
    aT_d = dten("aT_d", [22 * 128, T], BF16)
    uhalo_gs = GatherSet("uhalo", 128, 44 * 2, F32, 4)

    mixT_d = dten("mixT_d", [8 * 128, T], BF16)

    def phase_D(l):
        m0_ = A.mark()
        wao3, wao_b = load_w_resident("wao", "w_attn_o", l, 8, D)
        wso3, wso_b = load_w_resident("wso", "w_ssm_o", l, 16, D)
        at, at_b = A.bf16("attn_blk", 8 * 512)
        at3 = at.rearrange("p (k t) -> p k t", t=512)
        yn, yn_b = A.bf16("yn_blk", 16 * 512)
        yn3 = yn.rearrange("p (k t) -> p k t", t=512)
        gpool = RPool("gate", 4, 512, "f32")
        for tb in range(NB):
            sl = slice(tb * 512, (tb + 1) * 512)
            P.dma("sp", at3, attnT_d.rearrange("(k p) t -> p k t", p=128)[:, :, sl], reads=[dbuf["attnT_d"]], writes=[at_b])
            P.dma("sp", yn3, ynT_d.rearrange("(k p) t -> p k t", p=128)[:, :, sl], reads=[dbuf["ynT_d"]], writes=[yn_b])
            for m in range(8):
                ga, ga_b = gpool.get()
                gm_, gm_b = gpool.get()
                P.dma("sp", ga, gT_d[m * 128:(m + 1) * 128, sl], reads=[dbuf["gT_d"]], writes=[ga_b])
                P.dma("sp", gm_, gT_d[(8 + m) * 128:(9 + m) * 128, sl], reads=[dbuf["gT_d"]], writes=[gm_b])
                pa, pab = next_ps()
                for k in range(8):
                    MM(pa, wao3[:, k, m * 128:(m + 1) * 128], at3[:, k, :], k == 0, k == 7, [wao_b, at_b], [pab])
                psm, psb = next_ps()
                for k in range(16):
                    MM(psm, wso3[:, k, m * 128:(m + 1) * 128], yn3[:, k, :], k == 0, k == 15, [wso_b, yn_b], [psb])
                t1_, t1b_ = stg_pool.get()
                TT("dve", t1_, pa, ga, ALU.mult, [pab, ga_b], [t1b_])
                t2_, t2b_ = stg_pool.get()
                TT("dve", t2_, psm, gm_, ALU.mult, [psb, gm_b], [t2b_])
                ob, ob_b = sbf_pool.get()
                TT("pool", ob, t1_, t2_, ALU.add, [t1b_, t2b_], [ob_b])
                store(mixT_d[m * 128:(m + 1) * 128, sl], "mixT_d", ob, ob_b)
        P.barrier()
        A.release(m0_)
        m0_ = A.mark()
        wout3, wout_b = load_w_resident("wout", "w_out", l, 8, D)
        mxs = [A.bf16("mix_blk%d" % i, 8 * 512) for i in range(2)]
        for tb in range(NB):
            sl = slice(tb * 512, (tb + 1) * 512)
            mx, mx_b = mxs[tb % 2]
            mx3 = mx.rearrange("p (k t) -> p k t", t=512)
            P.dma("sp", mx3, mixT_d.rearrange("(k p) t -> p k t", p=128)[:, :, sl], reads=[dbuf["mixT_d"]], writes=[mx_b])
            for m2 in range(8):
                po, pob = next_ps()
                for k in range(8):
                    MM(po, wout3[:, k, m2 * 128:(m2 + 1) * 128], mx3[:, k, :], k == 0, k == 7, [wout_b, mx_b], [pob])
                STT(xT3[:, m2, sl], po, modT[:, 16 + m2:17 + m2], xT3[:, m2, sl], ALU.mult, ALU.add, [pob, lp_b, xT_b], [xT_b])
        P.barrier()
        A.release(m0_)

    def phase_EF(l, rank_sel):
        m0_ = A.mark()
        alloc_hT()
        norm_mod(A2, modT[:, 24:32])
        uh, uh_b = A.f32("uhalo_sb", 44 * 2)
        uh3 = uh.rearrange("p (m two) -> p m two", two=2)
        for c0 in range(0, 2 * DFF, 256):
            w3, w_b = load_w("w_up", l, c0, 256, KC)
            pt, pb = next_ps()
            for m in range(2):
                for k in range(KC):
                    MM(pt[:, 2 * m:2 * m + 2], w3[:, k, m * 128:(m + 1) * 128], ns.hT3[:, k, T - 2:T], k == 0, k == KC - 1,
                       [w_b, ns.hT_b], [pb])
            CP("dve", uh3[:, (c0 // 128):(c0 // 128) + 2, :], pt[:, 0:4].rearrange("p (m two) -> p m two", two=2), [pb], [uh_b])
        dap, dn = uhalo_gs.loc_rows(0, 128)
        store(dap, dn, uh, uh_b)
        uhalo_gs.gather()
        pv, pv_b = A.f32("uprev", 44 * 2)
        pv3 = pv.rearrange("p (m two) -> p m two", two=2)
        load_prev_rows(uhalo_gs, pv3, pv_b, 44, 2)
        ub = [A.f32("ubuf%d" % i, 516) for i in range(4)]
        for m in range(22):
            wv3, wv_b = load_w("w_up", l, m * 128, 128, KC)
            wg3, wg_b = load_w("w_up", l, DFF + m * 128, 128, KC)
            conv_out = []
            for tb in range(NB):
                sl = slice(tb * 512, (tb + 1) * 512)
                res = []
                for which, (w3, w_b, sub) in enumerate(((wv3, wv_b, m), (wg3, wg_b, 22 + m))):
                    pt, pb = next_ps()
                    for k in range(KC):
                        MM(pt, w3[:, k, :], ns.hT3[:, k, sl], k == 0, k == KC - 1, [w_b, ns.hT_b], [pb])
                    u, u_b = ub[(tb % 2) * 2 + which]
                    if tb == 0:
                        CP("dve", u[:, 0:2], pv3[:, sub, :], [pv_b], [u_b])
                    else:
                        up, up_b = ub[((tb - 1) % 2) * 2 + which]
                        CP("dve", u[:, 0:2], up[:, 512:514], [up_b], [u_b])
                    CP("act", u[:, 2:514], pt, [pb], [u_b])
                    c_, c_b = stg_pool.get()
                    TS("dve", c_, u[:, 0:512], fcwT[:, sub:sub + 1], ALU.mult, [u_b, lp_b], [c_b],
                       s2=fcbT[:, sub:sub + 1], op1=ALU.add)
                    STT(c_, u[:, 1:513], fcwT[:, 44 + sub:45 + sub], c_, ALU.mult, ALU.add, [u_b, lp_b, c_b], [c_b])
                    STT(c_, u[:, 2:514], fcwT[:, 88 + sub:89 + sub], c_, ALU.mult, ALU.add, [u_b, lp_b, c_b], [c_b])
                    res.append((c_, c_b))
                (cv_, cvb_), (cg_, cgb_) = res
                sg, sg_b = stg_pool.get()
                ACT(sg, cg_, AF.Silu, [cgb_], [sg_b])
                ob, ob_b = sbf_pool.get()
                TT("pool", ob, sg, cv_, ALU.mult, [sg_b, cvb_], [ob_b])
                store(aT_d[m * 128:(m + 1) * 128, sl], "aT_d", ob, ob_b)
        P.barrier()
        A.release(m0_)
        m0_ = A.mark()
        wd3, wd_b = load_w_resident("wdown", "w_down", l, 22, D)
        ab = [A.bf16("a_blk%d" % i, 22 * 512) for i in range(1)]
        for tb in range(NB):
            sl = slice(tb * 512, (tb + 1) * 512)
            a_, a_b_ = ab[0]
            a3_ = a_.rearrange("p (k t) -> p k t", t=512)
            P.dma("sp", a3_, aT_d.rearrange("(k p) t -> p k t", p=128)[:, :, sl], reads=[dbuf["aT_d"]], writes=[a_b_])
            for m2 in range(8):
                po, pob = next_ps()
                for k in range(22):
                    MM(po, wd3[:, k, m2 * 128:(m2 + 1) * 128], a3_[:, k, :], k == 0, k == 21, [wd_b, a_b_], [pob])
                STT(xT3[:, m2, sl], po, modT[:, 40 + m2:41 + m2], xT3[:, m2, sl], ALU.mult, ALU.add, [pob, lp_b, xT_b], [xT_b])
        P.barrier()
        A.release(m0_)


    x_save = dten("x_save", [8 * 128, T], F32)

    def spill_x():
        P.dma("sp", x_save.rearrange("(k p) t -> p k t", p=128), xT3, reads=[xT_b], writes=[dbuf["x_save"]])
        P.barrier()

    def restore_x():
        P.barrier()
        P.dma("sp", xT3, x_save.rearrange("(k p) t -> p k t", p=128), reads=[dbuf["x_save"]], writes=[xT_b])

    NKEY = GSZ * T
    NKT = NKEY // 128
    NKB = NKEY // 512
    NIT = 23
    LO0 = -4096.0
    BIGP = float(2.0 ** 100)
    NSPL = (int(NKEY * 0.41) // 512) * 512
    NACT = NKEY - NSPL
    SCALE = float(HD ** -0.5)

    def phase_B(l):
        m0_ = A.mark()
        score, score_b = A.f32("score", NKEY)
        mb, mbA_b = A.bf16("mb", NKEY)
        mbB_b = Buf("mbB")
        mbT_raw, mbT_b = A.f32("mbT", NKT * 512 // 4)
        mbT = mbT_raw.bitcast(FP8)
        mbT3 = mbT.rearrange("p (k t) -> p k t", t=512)
        ikT, ikT_b = A.bf16("ikT", NKEY)
        ikT3 = ikT.rearrange("p (r t) -> p r t", t=T)
        iqb, iqb_b = A.bf16("iq_blk", 4 * 512)
        iqb3 = iqb.rearrange("p (k t) -> p k t", t=512)
        qb_, qb_b = A.bf16("q_blk", 8 * 512)
        qb3 = qb_.rearrange("p (k t) -> p k t", t=512)
        kst = [A.bf16("kst%d" % i, 2 * 512) for i in range(3)]
        vst = [A.bf16("vst%d" % i, 4 * 512) for i in range(3)]
        ppool = RPool("pT", 4, 512, "bf16")
        ab_, ab_b = A.bf16("attn_o_blk", 8 * 512)
        ab3 = ab_.rearrange("p (k t) -> p k t", t=512)
        negd0, negd0_b = A.f32("negd0", 512)
        sm, sm_b = A.f32("bis_small", 16)
        md, md_b = A.f32("bis_mid", 2)
        cn, cnt_b = A.f32("bis_cnt", 2)
        sa, sacc_b = A.f32("bis_sacc", 2)
        Wk, Wk_b = A.f32("bis_w", NIT)
        lo = sm[:, 0:1]
        mid = md[:, 0:1]
        nmid = md[:, 1:2]
        cnt = cn[:, 0:1]
        sacc = sa[:, 0:1]
        tot = sm[:, 5:6]
        ge = sm[:, 6:7]
        hi0 = sm[:, 7:8]
        qp0 = sm[:, 8:9]
        STT(qp0, rk[:, 0:1], float(T), cv["pidx"][:, 0:1], ALU.mult, ALU.add, [rk_b, cst_b], [sm_b])
        TS("dve", negd0, cv["iotaf"], qp0, ALU.subtract, [cst_b, sm_b], [negd0_b], s2=-BIGP, op1=ALU.mult)
        P.dma("sp", ikT3, ik_gs.g[0].rearrange("(r p) t -> p r t", p=128), reads=[dbuf["ik_g0"]], writes=[ikT_b])
        for qb in range(NB):
            qsl = slice(qb * 512, (qb + 1) * 512)
            P.dma("sp", iqb3, iqT_d.rearrange("(k p) t -> p k t", p=128)[:, :, qsl], reads=[dbuf["iqT_d"]], writes=[iqb_b])
            P.dma("sp", qb3, qT_d.rearrange("(k p) t -> p k t", p=128)[:, :, qsl], reads=[dbuf["qT_d"]], writes=[qb_b])
            for qs in range(4):
                qt = qb * 4 + qs
                for kb in range(NKB):
                    ksl = slice(kb * 512, (kb + 1) * 512)
                    madd, madd_b = stg_pool.get()
                    TS("dve", madd, negd0, float((kb * 512 - qt * 128) * (-BIGP)), ALU.add, [negd0_b], [madd_b],
                       s2=0.0, op1=ALU.min)
                    for h in range(IH):
                        hb = (h % 2) * 64
                        pt, pb = next_ps()
                        MM(pt, iqb3[hb:hb + 64, h // 2, qs * 128:(qs + 1) * 128], ikT[hb:hb + 64, ksl], True, True,
                           [iqb_b, ikT_b], [pb])
                        rl, rl_b = stg_pool.get()
                        ACT(rl, pt, AF.Relu, [pb], [rl_b])
                        if h == 0:
                            STT(score[:, ksl], rl, iw3[:, qt, h:h + 1], madd, ALU.mult, ALU.add, [rl_b, iw_b, madd_b], [score_b])
                        else:
                            STT(score[:, ksl], rl, iw3[:, qt, h:h + 1], score[:, ksl], ALU.mult, ALU.add,
                                [rl_b, iw_b, score_b], [score_b])
                P.op("dve", lambda e: e.tensor_reduce(out=hi0, in_=score, axis=AX.X, op=ALU.max), [score_b], [sm_b])
                TS("dve", tot, hi0, 1.0 - LO0, ALU.add, [sm_b], [sm_b])
                for k in range(NIT):
                    TS("dve", Wk[:, k:k + 1], tot, float(2.0 ** -(k + 1)), ALU.mult, [sm_b], [Wk_b])
                P.op("dve", lambda e: e.memset(lo, LO0), [], [sm_b])
                for k in range(NIT):
                    TT("dve", mid, lo, Wk[:, k:k + 1], ALU.add, [sm_b, Wk_b], [md_b])
                    TS("dve", nmid, mid, -1.0, ALU.mult, [md_b], [md_b])
                    ACT(mb[:, NSPL:NKEY], score[:, NSPL:NKEY], AF.Sign, [score_b, md_b], [mbB_b, sacc_b], bias=nmid, scale=1.0,
                        accum_out=sacc)
                    TS("dve", mb[:, 0:NSPL], score[:, 0:NSPL], mid, ALU.is_ge, [score_b, md_b], [mbA_b, cnt_b],
                       s2=0.0, op1=ALU.add, accum_out=cnt)
                    STT(tot, sacc, 0.5, cnt, ALU.mult, ALU.add, [sacc_b, cnt_b], [sm_b])
                    TS("dve", ge, tot, float(cfg.nkeep - NACT / 2.0), ALU.is_ge, [sm_b], [sm_b])
                    STT(lo, ge, Wk[:, k:k + 1], lo, ALU.mult, ALU.add, [sm_b, Wk_b], [sm_b])
                TS("dve", mb, score, lo, ALU.is_lt, [score_b, sm_b], [mbA_b, mbB_b], s2=NEG, op1=ALU.mult)
                for k4 in range(NKT // 4):
                    pt, pb = next_ps()
                    ptb = pt.bitcast(BF16)
                    for u in range(4):
                        kt = k4 * 4 + u
                        P.op("pe", lambda e, ptb=ptb, u=u, kt=kt: e.transpose(out=ptb[:, u * 128:(u + 1) * 128],
                                                                             in_=mb[:, kt * 128:(kt + 1) * 128],
                                                                             identity=ident_bf),
                             [mbA_b, mbB_b, ident_bf_b], [pb])
                    dst_ = mbT3[:, k4 * 4:(k4 + 1) * 4, qs * 128:(qs + 1) * 128]
                    src_ = ptb[:, 0:512].rearrange("p (u t) -> p u t", t=128)
                    if k4 % 2 == 0:
                        P.op("act", lambda e, dst_=dst_, src_=src_: e.activation(out=dst_, in_=src_, func=AF.Copy, saturate=False),
                             [pb], [mbT_b])
                    else:
                        P.op("dve", lambda e, dst_=dst_, src_=src_: e.tensor_copy(out=dst_, in_=src_, saturate=False),
                             [pb], [mbT_b])
            for hg in range(4):
                accs = [psum[i_] for i_ in range(4)]
                nk4 = NKT // 4
                bufs = {}

                def issue_kv(k4, hg=hg, bufs=bufs):
                    kk, kk_b = kst[k4 % 3]
                    kk3 = kk.rearrange("p (pr t) -> p pr t", t=512)
                    vv, vv_b = vst[k4 % 3]
                    vv3 = vv.rearrange("p (u c) -> p u c", c=512)
                    r_ = (k4 * 512) // T
                    off = (k4 * 512) % T
                    for pr in range(2):
                        gap, gn = kT_gs.g_rows(r_, (hg * 2 + pr) * 128, (hg * 2 + pr + 1) * 128)
                        P.dma("sp", kk3[:, pr, :], gap[:, off:off + 512], reads=[dbuf[gn]], writes=[kk_b])
                    for u in range(4):
                        gap, gn = v_gs.g_rows(r_, off + u * 128, off + (u + 1) * 128)
                        P.dma("sp", vv3[:, u, :], gap[:, hg * 512:(hg + 1) * 512], reads=[dbuf[gn]], writes=[vv_b])
                    bufs[k4] = (kk3, kk_b, vv3, vv_b)

                steps = [(k4, u, hh) for k4 in range(nk4) for u in range(4) for hh in range(4)]
                LA = 2
                pend = {}
                issue_kv(0)
                for si in range(len(steps) + LA):
                    if si < len(steps):
                        k4, u, hh = steps[si]
                        if u == 0 and hh == 0 and k4 + 1 < nk4:
                            issue_kv(k4 + 1)
                        kk3, kk_b, vv3, vv_b = bufs[k4]
                        kt = k4 * 4 + u
                        h = hg * 4 + hh
                        hb = (h % 2) * 64
                        pt, pb = next_ps_s()
                        MM(pt, kk3[hb:hb + 64, hh // 2, u * 128:(u + 1) * 128], qb3[hb:hb + 64, h // 2, :], True, False,
                           [kk_b, qb_b], [pb])
                        MM(pt, ident_bf, mbT3[:, kt, :], False, True, [ident_bf_b, mbT_b], [pb])
                        pend[si] = (pt, pb)
                    ti = si - LA
                    if ti >= 0:
                        k4, u, hh = steps[ti]
                        kk3, kk_b, vv3, vv_b = bufs[k4]
                        kt = k4 * 4 + u
                        pt, pb = pend.pop(ti)
                        pT, pT_b = ppool.get()
                        ACT(pT, pt, AF.Exp, [pb], [pT_b], scale=SCALE)
                        acc, acc_b = accs[hh]
                        MM(acc, vv3[:, u, hh * 128:(hh + 1) * 128], pT, kt == 0, kt == NKT - 1, [vv_b, pT_b], [acc_b])
                for hh in range(4):
                    h = hg * 4 + hh
                    hb = (h % 2) * 64
                    acc, acc_b = accs[hh]
                    osb, osb_b = stg_pool.get()
                    CP("act", osb, acc, [acc_b], [osb_b])
                    pw, pw_b = next_ps_s()
                    MM(pw, cv["swap"], osb, True, True, [cst_b, osb_b], [pw_b])
                    rc, rc_b = stg_pool.get()
                    P.op("dve", lambda e, rc=rc, pw=pw: e.reciprocal(out=rc, in_=pw), [pw_b], [rc_b])
                    TT("dve", ab3[hb:hb + 64, h // 2, :], osb[hb:hb + 64, :], rc[hb:hb + 64, :], ALU.mult, [osb_b, rc_b], [ab_b])
            store(attnT_d.rearrange("(k p) t -> p k t", p=128)[:, :, qsl], "attnT_d", ab3, ab_b)
        print("phase B arena top", A.off, "of", A.n)
        P.barrier()
        A.release(m0_)

    ps_s_rr = [0]

    def next_ps_s():
        i = ps_s_rr[0]
        ps_s_rr[0] = (i + 1) % 4
        return psum[4 + i]


    NCH = T // 256
    st_gs = GatherSet("ssdF", 128, 2048, F32, 4)
    dd_gs = GatherSet("ssdD", 128, SH, F32, 4)

    def bcast_last(ap, n):
        return bass.AP(ap.tensor, ap.offset, [list(x) for x in ap.ap] + [[0, n]])

    def phase_C(l):
        m0_ = A.mark()
        rwp = RPool("rw", 3, 260, "f32")
        xsT, xsT_b = A.f32("xsT", 16 * 256)
        xsT3 = xsT.rearrange("p (m t) -> p m t", t=256)
        BT, BT_b = A.bf16("BT", 4 * 256)
        BT3 = BT.rearrange("p (g t) -> p g t", t=256)
        CT, CT_b = A.bf16("CT", 4 * 256)
        CT3 = CT.rearrange("p (g t) -> p g t", t=256)
        xdt, xdt_b = A.bf16("xdt", 2 * 2048)
        xdt3 = xdt.rearrange("p (i c) -> p i c", c=2048)
        xdd, xdd_b = A.bf16("xdd", 2 * 2048)
        xdd3 = xdd.rearrange("p (i c) -> p i c", c=2048)
        Btm, Btm_b = A.bf16("Btm", 2 * 512)
        Btm4 = Btm.rearrange("p (i g n) -> p i g n", g=4, n=128)
        acs, acs_b = A.f32("acs_tm", 2 * SH)
        acs3 = acs.rearrange("p (i h) -> p i h", h=SH)
        totb, totb_b = A.f32("tot_bc", SH)
        etot, etot_b = A.f32("etot", SH)
        ds_, ds_b = A.f32("ds", 2 * SH)
        ds3 = ds_.rearrange("p (i h) -> p i h", h=SH)
        Sst, Sst_b = A.f32("Sstate", 2048)
        Sbf, Sbf_b = A.bf16("Sstate_bf", 2048)
        Dacc, Dacc_b = A.f32("Dacc", SH)
        cbm, cbm_b = A.f32("CBm", 4 * 384)
        cbm3 = cbm.rearrange("p (g t) -> p g t", t=384)
        triL, triL_b = A.f32("triL", 512)
        yg, yg_b = A.f32("yg", 16 * 256)
        yg3 = yg.rearrange("p (m t) -> p m t", t=256)
        prev, prev_b = A.f32("xbc_prev", 24 * 4)
        prev3 = prev.rearrange("p (m c) -> p m c", c=4)
        zp = RPool("zt", 3, 256, "bf16")
        r0p = RPool("c_r0", 6, 512, "f32")
        dpp = RPool("c_d", 6, 384, "f32")
        ebp = RPool("c_eb", 6, 256, "f32")
        gpp = RPool("c_G", 6, 384, "bf16")
        cep = RPool("c_Ce", 6, 256, "bf16")
        print("phase C arena top", A.off, "of", A.n)
        CP("dve", triL[:, 0:128], cv["tri"], [cst_b], [triL_b])
        CP("dve", triL[:, 128:256], cv["ones"], [cst_b], [triL_b])
        P.op("dve", lambda e: e.memset(triL[:, 256:384], 0.0), [], [triL_b])
        CP("dve", triL[:, 384:512], cv["tri"], [cst_b], [triL_b])
        load_prev_rows(halo_gs, prev3, prev_b, 24, 4)

        def conv_block(blk, c, out_ap, out_b, eng_out="act"):
            rw, rw_b = rwp.get()
            if c == 0:
                P.dma("sp", rw[:, 3:259], xbc_raw[blk * 128:(blk + 1) * 128, 0:256], reads=[dbuf["xbc_raw"]], writes=[rw_b])
                CP("pool", rw[:, 0:3], prev3[:, blk, 0:3], [prev_b], [rw_b])
            else:
                P.dma("sp", rw[:, 0:259], xbc_raw[blk * 128:(blk + 1) * 128, c * 256 - 3:c * 256 + 256],
                      reads=[dbuf["xbc_raw"]], writes=[rw_b])
            ac, ac_b = stg_pool.get()
            TS("dve", ac[:, 0:256], rw[:, 0:256], scwT[:, blk:blk + 1], ALU.mult, [rw_b, lp_b], [ac_b],
               s2=scbT[:, blk:blk + 1], op1=ALU.add)
            for tap in range(1, 4):
                STT(ac[:, 0:256], rw[:, tap:tap + 256], scwT[:, tap * 24 + blk:tap * 24 + blk + 1], ac[:, 0:256], ALU.mult, ALU.add,
                    [rw_b, lp_b, ac_b], [ac_b])
            ACT(out_ap, ac[:, 0:256], AF.Silu, [ac_b], [out_b])

        def ssd_pass(compute_y):
            for c in range(NCH):
                csl = slice(c * 256, (c + 1) * 256)
                for blk in range(16):
                    conv_block(blk, c, xsT3[:, blk, :], xsT_b)
                for g in range(4):
                    conv_block(16 + g, c, BT3[:, g, :], BT_b)
                if compute_y:
                    for g in range(4):
                        conv_block(20 + g, c, CT3[:, g, :], CT_b)
                for i in range(2):
                    ti = c * 2 + i
                    for bank in range(4):
                        pt, pb = next_ps()
                        for u in range(4):
                            blk = bank * 4 + u
                            P.op("pe", lambda e, pt=pt, u=u, blk=blk, i=i: e.transpose(
                                out=pt[:, u * 128:(u + 1) * 128], in_=xsT3[:, blk, i * 128:(i + 1) * 128], identity=cv["ident"]),
                                [xsT_b, cst_b], [pb])
                        TT("dve", xdt3[:, i, bank * 512:(bank + 1) * 512].rearrange("p (h q) -> p h q", q=64),
                           pt.rearrange("p (h q) -> p h q", q=64), bcast_last(dt3[:, ti, bank * 8:(bank + 1) * 8], 64), ALU.mult,
                           [pb, dt_b], [xdt_b])
                    pt, pb = next_ps()
                    ptb = pt.bitcast(BF16)
                    for g in range(4):
                        P.op("pe", lambda e, ptb=ptb, g=g, i=i: e.transpose(out=ptb[:, g * 128:(g + 1) * 128],
                                                                           in_=BT3[:, g, i * 128:(i + 1) * 128], identity=ident_bf),
                             [BT_b, ident_bf_b], [pb])
                    CP("act", Btm4[:, i, :, :], ptb[:, 0:512].rearrange("p (g n) -> p g n", n=128), [pb], [Btm_b])
                a0 = a3[:, c * 2, :]
                a1 = a3[:, c * 2 + 1, :]
                pt, pb = next_ps()
                MM(pt[:, 0:32], cv["tri"], a0, True, True, [cst_b, a_b], [pb])
                MM(pt[:, 32:64], cv["ones"], a0, True, False, [cst_b, a_b], [pb])
                MM(pt[:, 32:64], cv["tri"], a1, False, True, [cst_b, a_b], [pb])
                MM(pt[:, 64:96], cv["ones"], a0, True, False, [cst_b, a_b], [pb])
                MM(pt[:, 64:96], cv["ones"], a1, False, True, [cst_b, a_b], [pb])
                CP("dve", acs, pt[:, 0:64], [pb], [acs_b])
                CP("dve", totb, pt[:, 64:96], [pb], [totb_b])
                ACT(etot, totb, AF.Exp, [totb_b], [etot_b])
                TT("dve", Dacc, Dacc, totb, ALU.add, [Dacc_b, totb_b], [Dacc_b])
                for i in range(2):
                    TT("dve", ds3[:, i, :], totb, acs3[:, i, :], ALU.subtract, [totb_b, acs_b], [ds_b])
                ACT(ds_, ds_, AF.Exp, [ds_b], [ds_b])
                for i in range(2):
                    TT("pool", xdd3[:, i, :].rearrange("p (h q) -> p h q", q=64), xdt3[:, i, :].rearrange("p (h q) -> p h q", q=64),
                       bcast_last(ds3[:, i, :], 64), ALU.mult, [xdt_b, ds_b], [xdd_b])
                if compute_y:
                    CP("dve", Sbf, Sst, [Sst_b], [Sbf_b])
                    for g in range(4):
                        pt, pb = next_ps()
                        MM(pt[:, 0:256], BT3[:, g, 0:128], CT3[:, g, :], True, True, [BT_b, CT_b], [pb])
                        MM(pt[:, 256:384], BT3[:, g, 128:256], CT3[:, g, 128:256], True, True, [BT_b, CT_b], [pb])
                        TT("dve", cbm3[:, g, 0:128], pt[:, 0:128], cv["tri"], ALU.mult, [pb, cst_b], [cbm_b])
                        CP("dve", cbm3[:, g, 128:256], pt[:, 128:256], [pb], [cbm_b])
                        TT("dve", cbm3[:, g, 256:384], pt[:, 256:384], cv["tri"], ALU.mult, [pb, cst_b], [cbm_b])
                    for pr in range(16):
                        py, pyb = next_ps()
                        for hh in range(2):
                            h = pr * 2 + hh
                            g = h // 8
                            hb = hh * 64
                            r0, r0_b = stg_pool.get()
                            TS("dve", r0[:, 0:256], triL[:, 0:256], a0[:, h:h + 1], ALU.mult, [triL_b, a_b], [r0_b])
                            TS("pool", r0[:, 256:512], triL[:, 256:512], a1[:, h:h + 1], ALU.mult, [triL_b, a_b], [r0_b])
                            pbc, pbcb = next_ps()
                            MM(pbc[:, 0:256], cv["ones"], r0[:, 0:256], True, False, [cst_b, r0_b], [pbcb])
                            MM(pbc[:, 0:256], cv["ones"], r0[:, 256:512], False, True, [cst_b, r0_b], [pbcb])
                            d_, d_b = stg_pool.get()
                            TS("dve", d_[:, 0:256], pbc[:, 0:256], acs3[:, 0, h:h + 1], ALU.subtract, [pbcb, acs_b], [d_b],
                               s2=0.0, op1=ALU.min)
                            TS("dve", d_[:, 256:384], pbc[:, 128:256], acs3[:, 1, h:h + 1], ALU.subtract, [pbcb, acs_b], [d_b],
                               s2=0.0, op1=ALU.min)
                            ACT(d_[:, 0:384], d_[:, 0:384], AF.Exp, [d_b], [d_b])
                            G_, G_b = sbf_pool.get()
                            TT("dve", G_[:, 0:384], d_[:, 0:384], cbm3[:, g, :], ALU.mult, [d_b, cbm_b], [G_b])
                            eb, eb_b = stg_pool.get()
                            ACT(eb[:, 0:256], pbc[:, 0:256], AF.Exp, [pbcb], [eb_b])
                            Ce, Ce_b = sbf_pool.get()
                            TT("pool", Ce[:, 0:256], eb[:, 0:256], CT3[:, g, :], ALU.mult, [eb_b, CT_b], [Ce_b])
                            MM(py[hb:hb + 64, 0:256], xdt3[:, 0, h * 64:(h + 1) * 64], G_[:, 0:256], True, False, [xdt_b, G_b], [pyb])
                            MM(py[hb:hb + 64, 128:256], xdt3[:, 1, h * 64:(h + 1) * 64], G_[:, 256:384], False, False,
                               [xdt_b, G_b], [pyb])
                            MM(py[hb:hb + 64, 0:256], Sbf[:, h * 64:(h + 1) * 64], Ce[:, 0:256], False, True, [Sbf_b, Ce_b], [pyb])
                        zt, zt_b = zp.get()
                        P.dma("sp", zt, zT_d[pr * 128:(pr + 1) * 128, csl], reads=[dbuf["zT_d"]], writes=[zt_b])
                        yv, yv_b = stg_pool.get()
                        STT(yv[:, 0:256], xsT3[:, pr, :], dskT[:, pr:pr + 1], py[:, 0:256], ALU.mult, ALU.add, [xsT_b, lp_b, pyb], [yv_b])
                        TT("pool", yg3[:, pr, :], yv[:, 0:256], zt, ALU.mult, [yv_b, zt_b], [yg_b])
                    for g in range(4):
                        pss, pssb = next_ps()
                        for q in range(4):
                            sq, sq_b = stg_pool.get()
                            ACT(sq[:, 0:256], yg3[:, g * 4 + q, :], AF.Square, [yg_b], [sq_b])
                            MM(pss[:, 0:256], cv["ones"], sq[:, 0:256], q == 0, q == 3, [cst_b, sq_b], [pssb])
                        rs, rs_b = stg_pool.get()
                        ACT(rs[:, 0:256], pss[:, 0:256], AF.Sqrt, [pssb], [rs_b], bias=epsT[:, 0:1], scale=1.0 / 512.0)
                        P.op("dve", lambda e, rs=rs: e.reciprocal(out=rs[:, 0:256], in_=rs[:, 0:256]), [rs_b], [rs_b])
                        for q in range(4):
                            pr = g * 4 + q
                            ob, ob_b = sbf_pool.get()
                            STT(ob[:, 0:256], yg3[:, pr, :], sngT[:, pr:pr + 1], rs[:, 0:256], ALU.mult, ALU.mult,
                                [yg_b, lp_b, rs_b], [ob_b])
                            store(ynT_d[pr * 128:(pr + 1) * 128, csl], "ynT_d", ob[:, 0:256], ob_b)
                for g in range(4):
                    pst, pstb = next_ps()
                    for i in range(2):
                        MM(pst, Btm4[:, i, g, :], xdd3[:, i, g * 512:(g + 1) * 512], i == 0, i == 1, [Btm_b, xdd_b], [pstb])
                    sg3 = Sst[:, g * 512:(g + 1) * 512].rearrange("p (h q) -> p h q", q=64)
                    TT("dve", sg3, sg3, bcast_last(etot[:, g * 8:(g + 1) * 8], 64), ALU.mult, [Sst_b, etot_b], [Sst_b])
                    TT("dve", Sst[:, g * 512:(g + 1) * 512], Sst[:, g * 512:(g + 1) * 512], pst, ALU.add, [Sst_b, pstb], [Sst_b])

        P.op("dve", lambda e: e.memset(Sst, 0.0), [], [Sst_b])
        P.op("dve", lambda e: e.memset(Dacc, 0.0), [], [Dacc_b])
        ssd_pass(False)
        dap, dn = st_gs.loc_rows(0, 128)
        store(dap, dn, Sst, Sst_b)
        dap, dn = dd_gs.loc_rows(0, 128)
        store(dap, dn, Dacc, Dacc_b)
        st_gs.gather()
        dd_gs.gather()
        Dr = []
        for r in range(GSZ - 1):
            t_, tb_ = A.f32("Dr%d" % r, SH)
            gap, gn = dd_gs.g_rows(r, 0, 128)
            P.dma("sp", t_, gap, reads=[dbuf[gn]], writes=[tb_])
            Dr.append((t_, tb_))
        lt, lt_b = A.f32("ltflag", 4)
        for m in range(GSZ - 1):
            TS("dve", lt[:, m:m + 1], rk[:, 0:1], float(m), ALU.is_gt, [rk_b], [lt_b])
        P.op("dve", lambda e: e.memset(Sst, 0.0), [], [Sst_b])
        Fr, Fr_b = A.f32("Fr", 2048)
        for r in range(GSZ - 1):
            wr, wr_b = A.f32("wr%d" % r, SH)
            P.op("dve", lambda e, wr=wr: e.memset(wr, 0.0), [], [wr_b])
            for m in range(r + 1, GSZ - 1):
                STT(wr, Dr[m][0], lt[:, m:m + 1], wr, ALU.mult, ALU.add, [Dr[m][1], lt_b, wr_b], [wr_b])
            ACT(wr, wr, AF.Exp, [wr_b], [wr_b])
            TS("dve", wr, wr, lt[:, r:r + 1], ALU.mult, [wr_b, lt_b], [wr_b])
            gap, gn = st_gs.g_rows(r, 0, 128)
            P.dma("sp", Fr, gap, reads=[dbuf[gn]], writes=[Fr_b])
            F3 = Fr.rearrange("p (h q) -> p h q", q=64)
            TT("dve", F3, F3, bcast_last(wr, 64), ALU.mult, [Fr_b, wr_b], [Fr_b])
            TT("dve", Sst, Sst, Fr, ALU.add, [Sst_b, Fr_b], [Sst_b])
        ssd_pass(True)
        P.barrier()
        A.release(m0_)

    P.dma("sp", rk[:, 0:4], bcast_ap(rank_in, 128, 4), reads=[dbuf["rank"]], writes=[rk_b])
    for r in range(GSZ):
        TS("dve", prevsel[:, r:r + 1], rk[:, 0:1], float(r + 1), ALU.is_equal, [rk_b], [prevsel_b])

    def load_prev_rows(gs, dst3, dst_b, nsub, ncols):
        first = True
        for r in range(GSZ - 1):
            tmp_, tmpb_ = A.f32("prevtmp%d" % r, nsub * ncols)
            tmp3 = tmp_.rearrange("p (m c) -> p m c", c=ncols)
            gap, gn = gs.g_rows(r, 0, 128)
            P.dma("sp", tmp_, gap, reads=[dbuf[gn]], writes=[tmpb_])
            if first:
                TS("dve", dst3, tmp3, prevsel[:, r:r + 1], ALU.mult, [tmpb_, prevsel_b], [dst_b])
                first = False
            else:
                STT(dst3, tmp3, prevsel[:, r:r + 1], dst3, ALU.mult, ALU.add, [tmpb_, prevsel_b, dst_b], [dst_b])

    for l in range(depth):
        A.release(m_x)
        phase_A(l)
        P.barrier()
        A.release(m_x)
        if cfg.stage >= 30:
            spill_x()
            A.release(m_pers)
            phase_B(l)
            A.release(m_pers)
            if cfg.stage >= 40:
                phase_C(l)
            A.release(m_x)
            restore_x()
        if cfg.stage >= 20:
            phase_D(l)
        if cfg.stage >= 21:
            phase_EF(l, None)
    A.release(m_x)
    for name in dbg_copies:
        rows = dram[name].shape[0]
        for r0 in range(0, rows, 512):
            r1 = min(rows, r0 + 512)
            P.dma("sp", dram[name + "_dbg"][r0:r1, :], dram[name][r0:r1, :], reads=[dbuf[name]], writes=[dbuf[name + "_dbg"]])
    P.barrier()

    m0 = A.mark()
    outs = [A.f32("otok%d" % i, D) for i in range(2)]
    for i in range(NT):
        ot, ot_b = outs[i % 2]
        for k in range(KC):
            pt, pb = next_ps()
            P.op("pe", lambda e, pt=pt, k=k, i=i: e.transpose(out=pt[:, 0:128], in_=xT3[:, k, i * 128:(i + 1) * 128],
                                                             identity=cv["ident"]),
                 reads=[xT_b, cst_b], writes=[pb])
            if k % 2 == 0:
                P.op("act", lambda e, pt=pt, k=k, ot=ot: e.copy(out=ot[:, k * 128:(k + 1) * 128], in_=pt[:, 0:128]),
                     reads=[pb], writes=[ot_b])
            else:
                P.op("dve", lambda e, pt=pt, k=k, ot=ot: e.tensor_copy(out=ot[:, k * 128:(k + 1) * 128], in_=pt[:, 0:128]),
                     reads=[pb], writes=[ot_b])
        P.dma("sp", y_out[i * 128:(i + 1) * 128, :], ot, reads=[ot_b], writes=[dbuf["y"]])
    P.barrier()
    A.release(m0)

    P.emit(stack)
    stack.close()
    return nc


def make_in_maps(cfg, inputs):
    T = cfg.T
    depth = cfg.depth
    maps = []
    f = lambda a: np.ascontiguousarray(np.asarray(a, dtype=np.float32))
    shared = {
        "consts": CONST_ARR,
        "w_ada": f(inputs["w_ada"]), "b_ada": f(inputs["b_ada"]).reshape(depth, 48, 128),
        "norm1_g": f(inputs["norm1_g"]).reshape(depth, KC, 128), "w_in": f(inputs["w_in"]),
        "q_norm_g": f(inputs["q_norm_g"]).reshape(depth, 1, 64), "k_norm_g": f(inputs["k_norm_g"]).reshape(depth, 1, 64),
        "ssm_conv_w": f(inputs["ssm_conv_w"]).reshape(depth, 4, 24, 128),
        "ssm_conv_b": f(inputs["ssm_conv_b"]).reshape(depth, 24, 128),
        "dt_bias": f(inputs["dt_bias"]).reshape(depth, 1, SH), "a_log": f(inputs["a_log"]).reshape(depth, 1, SH),
        "d_skip": f(inputs["d_skip"]).reshape(depth, 1, SH), "ssm_norm_g": f(inputs["ssm_norm_g"]).reshape(depth, 16, 128),
        "w_attn_o": f(inputs["w_attn_o"]), "w_ssm_o": f(inputs["w_ssm_o"]), "w_out": f(inputs["w_out"]),
        "norm2_g": f(inputs["norm2_g"]).reshape(depth, KC, 128), "w_up": f(inputs["w_up"]),
        "ffn_conv_w": f(inputs["ffn_conv_w"]).reshape(depth, 3, 44, 128),
        "ffn_conv_b": f(inputs["ffn_conv_b"]).reshape(depth, 44, 128), "w_down": f(inputs["w_down"]),
    }
    x = f(inputs["x"])
    c = f(inputs["c"])
    pos = np.ascontiguousarray(np.asarray(inputs["positions"], dtype=np.int32))
    for r in range(NCORES):
        b, j = divmod(r, GSZ)
        m = dict(shared)
        m["x"] = np.ascontiguousarray(x[b, j * T:(j + 1) * T, :])
        m["c"] = np.ascontiguousarray(c[b].reshape(KC, 128))
        m["positions"] = np.ascontiguousarray(pos[b, j * T:(j + 1) * T].reshape(1, T))
        rk = np.zeros((1, 4), np.float32)
        rk[0, 0] = j
        m["rank"] = rk
        for k_, v_ in (getattr(cfg, "feed_data", None) or {}).items():
            m[k_] = v_[r]
        maps.append(m)
    return maps


_CACHE = {}


def run(cfg, inputs):
    key = (cfg.S, cfg.depth, tuple(sorted(cfg.debug)), cfg.stage, tuple(sorted(cfg.feed)))
    if key not in _CACHE:
        _CACHE[key] = build_program(cfg)
    nc = _CACHE[key]
    maps = make_in_maps(cfg, inputs)
    res = run_bass_kernel_spmd(nc, maps, core_ids=list(range(NCORES)))
    return res.results


def kernel(**inputs):
    cfg = Cfg(seq=int(np.asarray(inputs["x"]).shape[1]), depth=int(np.asarray(inputs["w_in"]).shape[0]))
    results = run(cfg, inputs)
    B = np.asarray(inputs["x"]).shape[0]
    out = np.zeros((B, cfg.S, D), np.float32)
    for r in range(NCORES):
        b, j = divmod(r, GSZ)
        out[b, j * cfg.T:(j + 1) * cfg.T, :] = results[r]["y"]
    return out
```

```python
from contextlib import ExitStack
import numpy as np
import ml_dtypes
import concourse.bass as bass
import concourse.mybir as mybir
from concourse.bass_utils import run_bass_kernel_spmd

F32 = mybir.dt.float32
BF16 = mybir.dt.bfloat16
I32 = mybir.dt.int32
FP8 = mybir.dt.float8e5
AF = mybir.ActivationFunctionType
ALU = mybir.AluOpType
AX = mybir.AxisListType

NCORES = 8
GSZ = 4
D = 1024
KC = D // 128
HEADS = 16
HD = 64
IH = 8
TOPK = 256
DI = 2048
SH = 32
SG = 4
NST = 128
XBC = DI + 2 * SG * NST
DFF = 2816
EPS = 1e-6
C_Q, C_K, C_V, C_IQ, C_IK, C_IW, C_Z, C_XBC, C_DT, C_GA, C_GM = (
    0, 1024, 2048, 3072, 3584, 3648, 3656, 5704, 8776, 8808, 9832)
INW = 10856
NEG = -30000.0


class Buf:
    __slots__ = ("name", "w", "r")

    def __init__(self, name):
        self.name = name
        self.w = None
        self.r = {}


class Prog:
    ENGS = ("pe", "act", "dve", "pool", "sp")

    def __init__(self, nc, n_dma=40):
        self.nc = nc
        self.ops = {e: [] for e in self.ENGS}
        self.cnt = {e: 0 for e in self.ENGS}
        self.known = {e: {} for e in self.ENGS}
        self.dma_val = [0] * n_dma
        self.dma_next = 0
        self.cc_val = 0

    def _need(self, eng, k, v):
        if k == eng and eng == "pe":
            return
        kn = self.known[eng]
        if kn.get(k, 0) >= v:
            return
        kn[k] = v
        self.ops[eng].append(("wait", k, v))

    def _deps(self, eng, reads, writes):
        for b in reads:
            if b.w is not None:
                self._need(eng, *b.w)
        for b in writes:
            if b.w is not None:
                self._need(eng, *b.w)
            for k, v in b.r.items():
                self._need(eng, k, v)

    def _mark(self, tok, reads, writes):
        for b in writes:
            b.w = tok
            b.r = {}
        for b in reads:
            if b in writes:
                continue
            if b.r.get(tok[0], 0) < tok[1]:
                b.r[tok[0]] = tok[1]

    def op(self, eng, fn, reads=(), writes=()):
        self._deps(eng, reads, writes)
        self.cnt[eng] += 1
        tok = (eng, self.cnt[eng])
        self.ops[eng].append(("ins", fn, eng, 1, self._where()))
        self._mark(tok, reads, writes)
        return tok

    DEBUG_WHERE = False

    def _where(self):
        if not Prog.DEBUG_WHERE:
            return None
        import traceback
        return [(f.lineno, f.name) for f in traceback.extract_stack(limit=6)[:-2]]

    def dma(self, q, out, in_, reads=(), writes=()):
        i = self.dma_next
        self.dma_next = (i + 1) % len(self.dma_val)
        key = ("dma", i)
        self._deps(q, reads, writes)
        if self.dma_val[i]:
            self._need(q, key, self.dma_val[i])
        self.dma_val[i] += 16
        tok = (key, self.dma_val[i])
        self.ops[q].append(("ins", lambda e, o=out, s=in_: e.dma_start(out=o, in_=s), key, 16))
        self._mark(tok, reads, writes)
        return tok

    def collective(self, fn, reads=(), writes=()):
        self._deps("pool", reads, writes)
        self.cc_val += 1
        tok = ("cc", self.cc_val)
        self.ops["pool"].append(("ins", fn, "cc", 1))
        self._mark(tok, reads, writes)
        return tok

    def barrier(self):
        for e in self.ENGS:
            for f in ("pe", "act", "dve", "pool"):
                if self.cnt[f]:
                    self._need(e, f, self.cnt[f])
            for i, v in enumerate(self.dma_val):
                if v:
                    self._need(e, ("dma", i), v)
            if self.cc_val:
                self._need(e, "cc", self.cc_val)

    def emit(self, stack):
        nc = self.nc
        sems = {}
        for e in ("pe", "act", "dve", "pool"):
            sems[e] = stack.enter_context(nc.semaphore("s_" + e))
        for i in range(len(self.dma_val)):
            sems[("dma", i)] = stack.enter_context(nc.semaphore("d%d" % i))
        sems["cc"] = stack.enter_context(nc.semaphore("s_cc"))
        block = stack.enter_context(nc.Block())

        def mk(name):
            def body(eng):
                for o in self.ops[name]:
                    if o[0] == "wait":
                        eng.wait_ge(sems[o[1]], o[2])
                    else:
                        ins = o[1](eng)
                        ins.then_inc(sems[o[2]], o[3])
                        if Prog.DEBUG_WHERE and len(o) > 4:
                            print("INS", name, getattr(getattr(ins, "ins", None), "name", None), o[4])
            return body

        block.tensor(mk("pe"))
        block.scalar(mk("act"))
        block.vector(mk("dve"))
        block.gpsimd(mk("pool"))
        block.sync(mk("sp"))


class Arena:
    def __init__(self, big, nwords):
        self.big = big
        self.n = nwords
        self.off = 0

    def mark(self):
        return self.off

    def release(self, m):
        self.off = m

    def f32(self, name, cols):
        a = self.off
        self.off += cols
        assert self.off <= self.n, ("SBUF arena overflow", name, self.off, self.n)
        return self.big[:, a:a + cols], Buf(name)

    def bf16(self, name, cols):
        w = (cols + 1) // 2
        a = self.off
        self.off += w
        assert self.off <= self.n, ("SBUF arena overflow", name, self.off, self.n)
        return self.big[:, a:a + w].bitcast(BF16)[:, 0:cols], Buf(name)


def make_consts():
    c = {}
    c["ident"] = np.eye(128, dtype=np.float32)
    c["ones"] = np.ones((128, 128), np.float32)
    bo = np.zeros((128, 128), np.float32)
    bo[:64, :64] = 1.0
    bo[64:, 64:] = 1.0
    c["blockones"] = bo
    rr = np.zeros((128, 128), np.float32)
    for m in range(128):
        if (m % 64) < 32:
            rr[m + 32, m] = -1.0
        else:
            rr[m - 32, m] = 1.0
    c["rrot"] = rr
    tri = (np.arange(128)[:, None] <= np.arange(128)[None, :]).astype(np.float32)
    c["tri"] = tri
    c["causb"] = np.where(np.arange(128)[None, :] <= np.arange(128)[:, None], 0.0, -1e30).astype(np.float32)
    invf = (1.0 / (10000.0 ** (np.arange(0, 64, 2, dtype=np.float32) / 64.0))).astype(np.float32)
    c["invf"] = np.tile(invf, 4)[:, None].astype(np.float32) * np.ones((1, 128), np.float32)
    c["iotaf"] = np.tile(np.arange(512, dtype=np.float32)[None, :], (128, 1))
    c["pidx"] = np.tile(np.arange(128, dtype=np.float32)[:, None], (1, 128))
    sw = np.zeros((128, 128), np.float32)
    for m in range(128):
        sw[(m + 64) % 128, m] = 1.0
    c["swap"] = sw
    names = ["ident", "ones", "blockones", "rrot", "tri", "causb", "invf", "iotaf", "pidx", "swap"]
    return [(n, c[n].shape[1]) for n in names], np.concatenate([c[n] for n in names], axis=1)


CONST_NAMES, CONST_ARR = make_consts()


class Cfg:
    def __init__(self, seq=8192, depth=4, debug=(), stage=99, feed=()):
        self.stage = stage
        self.feed = set(feed)
        self.S = seq
        self.T = seq // GSZ
        self.depth = depth
        self.debug = set(debug)
        self.NT = self.T // 128
        self.NB = self.T // 512
        self.NCH = self.T // 256
        self.nkeep = min(TOPK, seq // 4)


def build_program(cfg):
    T, NT, NB, depth = cfg.T, cfg.NT, cfg.NB, cfg.depth
    nc = bass.Bass("TRN2", target_bir_lowering=False)
    stack = ExitStack()
    P = Prog(nc)
    dram = {}
    dbuf = {}

    dbg_copies = []

    def dten(name, shape, dtype, kind="Internal"):
        if name in cfg.debug:
            if "_g" in name[-4:]:
                dcp = nc.dram_tensor(name + "_dbg", list(shape), dtype, kind="ExternalOutput").ap()
                dram[name + "_dbg"] = dcp
                dbuf[name + "_dbg"] = Buf(name + "_dbg")
                dbg_copies.append(name)
            else:
                kind = "ExternalOutput"
        t = nc.dram_tensor(name, list(shape), dtype, kind=kind).ap()
        dram[name] = t
        dbuf[name] = Buf(name)
        return t

    x_in = dten("x", [T, D], F32, "ExternalInput")
    c_in = dten("c", [KC, 128], F32, "ExternalInput")
    pos_in = dten("positions", [1, T], I32, "ExternalInput")
    consts_in = dten("consts", [128, CONST_ARR.shape[1]], F32, "ExternalInput")
    rank_in = dten("rank", [1, 4], F32, "ExternalInput")
    W = {}
    for nm, shp in (("w_ada", [depth, D, 6 * D]), ("b_ada", [depth, 48, 128]), ("norm1_g", [depth, KC, 128]),
                    ("w_in", [depth, D, INW]), ("q_norm_g", [depth, 1, 64]), ("k_norm_g", [depth, 1, 64]),
                    ("ssm_conv_w", [depth, 4, 24, 128]), ("ssm_conv_b", [depth, 24, 128]),
                    ("dt_bias", [depth, 1, SH]), ("a_log", [depth, 1, SH]), ("d_skip", [depth, 1, SH]),
                    ("ssm_norm_g", [depth, 16, 128]), ("w_attn_o", [depth, D, D]), ("w_ssm_o", [depth, DI, D]),
                    ("w_out", [depth, D, D]), ("norm2_g", [depth, KC, 128]), ("w_up", [depth, D, 2 * DFF]),
                    ("ffn_conv_w", [depth, 3, 44, 128]), ("ffn_conv_b", [depth, 44, 128]),
                    ("w_down", [depth, DFF, D])):
        W[nm] = dten(nm, shp, F32, "ExternalInput")
    y_out = dten("y", [T, D], F32, "ExternalOutput")

    NWORDS = 52224
    big = stack.enter_context(nc.sbuf_tensor("arena", [128, NWORDS], F32))
    A = Arena(big, NWORDS)
    psum = []
    for i in range(8):
        pt = stack.enter_context(nc.psum_tensor("ps%d" % i, [128, 512], F32))
        psum.append((pt[:, :], Buf("ps%d" % i)))
    ps_rr = [0]

    def next_ps():
        i = ps_rr[0]
        ps_rr[0] = (i + 1) % 8
        return psum[i]

    cst, cst_b = A.f32("consts", CONST_ARR.shape[1])
    cv = {}
    o = 0
    for n, wd in CONST_NAMES:
        cv[n] = cst[:, o:o + wd]
        o += wd
    ident_bf, ident_bf_b = A.bf16("ident_bf", 128)

    class RPool:
        def __init__(self, name, n, cols, kind):
            self.t = [(A.f32 if kind == "f32" else A.bf16)("%s%d" % (name, i), cols) for i in range(n)]
            self.i = 0

        def get(self):
            r = self.t[self.i]
            self.i = (self.i + 1) % len(self.t)
            return r


    lp, lp_b = A.f32("lp", 512)
    dtb_bc, dtb_b = A.f32("dtb_bc", SH)
    alog_bc, alog_b = A.f32("alog_bc", SH)
    iw_tm, iw_b = A.f32("iw_tm", NT * 8)
    dt_tm, dt_b = A.f32("dt_tm", NT * SH)
    a_tm, a_b = A.f32("a_tm", NT * SH)
    halfsel, halfsel_b = A.f32("halfsel", 128)
    lfm_pool = RPool("lfm", 3, 128, "f32")
    WSTG = 2048
    wst_pool = RPool("wst", 2, WSTG, "f32")
    wbf_pool = RPool("wbf", 2, WSTG, "bf16")
    stg_pool = RPool("stg", 8, 512, "f32")
    sbf_pool = RPool("sbf", 4, 512, "bf16")
    epsT, epsT_b = A.f32("epsT", 2)
    oneT, oneT_b = A.f32("oneT", 2)
    rk, rk_b = A.f32("rk", 8)
    prevsel, prevsel_b = A.f32("prevsel", 4)
    class NS:
        pass
    ns = NS()
    m_pers = A.mark()
    xT, xT_b = A.f32("xT", KC * T)
    xT3 = xT.rearrange("p (k t) -> p k t", t=T)
    m_x = A.mark()

    P.dma("sp", cst, consts_in, reads=[dbuf["consts"]], writes=[cst_b])
    P.op("dve", lambda e: e.tensor_copy(out=ident_bf, in_=cv["ident"]), reads=[cst_b], writes=[ident_bf_b])

    def transpose_f32(dst, dst_b, src, src_b, rows, cols, evac="act"):
        pt, pb = next_ps()
        P.op("pe", lambda e: e.transpose(out=pt[0:cols, 0:rows], in_=src, identity=cv["ident"][0:rows, 0:rows]),
             reads=[src_b, cst_b], writes=[pb])
        if evac == "act":
            P.op("act", lambda e: e.copy(out=dst, in_=pt[0:cols, 0:rows]), reads=[pb], writes=[dst_b])
        else:
            P.op("dve", lambda e: e.tensor_copy(out=dst, in_=pt[0:cols, 0:rows]), reads=[pb], writes=[dst_b])

    def load_fm(dst, dst_b, src_ap, src_name, nrow):
        m = A.mark()
        tmp, tmp_b = A.f32("lfm_tmp", 128)
        P.dma("sp", tmp[0:nrow, :], src_ap, reads=[dbuf[src_name]], writes=[tmp_b])
        transpose_f32(dst, dst_b, tmp[0:nrow, :], tmp_b, nrow, 128)
        A.release(m)
        return tmp_b

    m0 = A.mark()
    xtoks = [A.f32("xtok%d" % i, D) for i in range(2)]
    for i in range(NT):
        xt, xt_b = xtoks[i % 2]
        P.dma("sp", xt, x_in[i * 128:(i + 1) * 128, :], reads=[dbuf["x"]], writes=[xt_b])
        for k in range(KC):
            pt, pb = next_ps()
            P.op("pe", lambda e, pt=pt, xt=xt, k=k: e.transpose(out=pt[:, 0:128], in_=xt[:, k * 128:(k + 1) * 128],
                                                               identity=cv["ident"]),
                 reads=[xt_b, cst_b], writes=[pb])
            eng = "act" if k % 2 == 0 else "dve"
            if eng == "act":
                P.op("act", lambda e, pt=pt, k=k, i=i: e.copy(out=xT3[:, k, i * 128:(i + 1) * 128], in_=pt[:, 0:128]),
                     reads=[pb], writes=[xT_b])
            else:
                P.op("dve", lambda e, pt=pt, k=k, i=i: e.tensor_copy(out=xT3[:, k, i * 128:(i + 1) * 128], in_=pt[:, 0:128]),
                     reads=[pb], writes=[xT_b])
    P.barrier()
    A.release(m0)


    def ACT(out, in_, func, reads, writes, **kw):
        P.op("act", lambda e: e.activation(out=out, in_=in_, func=func, **kw), reads, writes)

    def TS(eng, out, in0, s1, op0, reads, writes, s2=None, op1=None, **kw):
        if op1 is None:
            P.op(eng, lambda e: e.tensor_scalar(out=out, in0=in0, scalar1=s1, scalar2=None, op0=op0, **kw), reads, writes)
        else:
            P.op(eng, lambda e: e.tensor_scalar(out=out, in0=in0, scalar1=s1, scalar2=s2, op0=op0, op1=op1, **kw),
                 reads, writes)

    def TT(eng, out, in0, in1, op, reads, writes):
        P.op(eng, lambda e: e.tensor_tensor(out=out, in0=in0, in1=in1, op=op), reads, writes)

    def STT(out, in0, scalar, in1, op0, op1, reads, writes):
        P.op("dve", lambda e: e.scalar_tensor_tensor(out=out, in0=in0, scalar=scalar, in1=in1, op0=op0, op1=op1),
             reads, writes)

    def MM(out, lhsT, rhs, start, stop, reads, writes):
        P.op("pe", lambda e: e.matmul(out, lhsT, rhs, start=start, stop=stop), reads, writes)

    def CP(eng, out, in_, reads, writes):
        if eng == "act":
            P.op("act", lambda e: e.copy(out=out, in_=in_), reads, writes)
        else:
            P.op(eng, lambda e: e.tensor_copy(out=out, in_=in_), reads, writes)

    def bcast_ap(ap2d_row, nparts, ncols, offset_elems=0):
        return bass.AP(ap2d_row.tensor, ap2d_row.offset + offset_elems, [[0, nparts], [1, ncols]])

    qT_d = dten("qT_d", [8 * 128, T], BF16)
    groups = [[0, 1, 2, 3], [4, 5, 6, 7]]

    class GatherSet:
        def __init__(self, name, rows, cols, dtype, esz):
            rpc = rows
            while rpc * cols * esz > (1 << 20):
                rpc //= 2
            assert rows % rpc == 0
            self.name, self.rows, self.cols, self.rpc, self.n = name, rows, cols, rpc, rows // rpc
            self.loc = [dten("%s_loc%d" % (name, c), [rpc, cols], dtype) for c in range(self.n)]
            self.g = [dten("%s_g%d" % (name, c), [GSZ * rpc, cols], dtype) for c in range(self.n)]

        def loc_rows(self, r0, r1):
            c = r0 // self.rpc
            assert (r1 - 1) // self.rpc == c
            return self.loc[c][r0 - c * self.rpc:r1 - c * self.rpc, :], "%s_loc%d" % (self.name, c)

        def g_rows(self, rank, r0, r1):
            c = r0 // self.rpc
            assert (r1 - 1) // self.rpc == c
            base = rank * self.rpc - c * self.rpc
            return self.g[c][base + r0:base + r1, :], "%s_g%d" % (self.name, c)

        def gather(self):
            for c in range(self.n):
                ln, gn = "%s_loc%d" % (self.name, c), "%s_g%d" % (self.name, c)
                P.collective(lambda e, ln=ln, gn=gn: e.collective_compute("AllGather", ALU.bypass, replica_groups=groups,
                                                                          ins=[dram[ln]], outs=[dram[gn]]),
                             reads=[dbuf[ln]], writes=[dbuf[gn]])

    kT_gs = GatherSet("kT", 8 * 128, T, BF16, 2)
    v_gs = GatherSet("v", T, 2048, BF16, 2)
    iqT_d = dten("iqT_d", [4 * 128, T], BF16)
    ik_gs = GatherSet("ik", 128, T, BF16, 2)
    zT_d = dten("zT_d", [16 * 128, T], BF16)
    xbc_raw = dten("xbc_raw", [24 * 128, T], F32)
    halo_gs = GatherSet("halo", 128, 24 * 4, F32, 4)
    gT_d = dten("gT_d", [16 * 128, T], F32)
    dbg_small = dten("dbg_small", [128, NT * 80], F32)

    rope_d = dten("rope_d", [2 * 128, T], F32)
    m0 = A.mark()
    C4, C4_b = A.f32("C4", T)
    S4, S4_b = A.f32("S4", T)
    posi, posi_b = A.f32("posi", T)
    posi_i = posi.bitcast(I32)
    ang, ang_b = A.f32("ang", T)
    t1, t1_b = A.f32("rt1", T)
    t2, t2_b = A.f32("rt2", T)
    t2_i = t2.bitcast(I32)
    P.dma("sp", posi_i, bcast_ap(pos_in, 128, T), reads=[dbuf["positions"]], writes=[posi_b])
    CP("dve", ang, posi_i, [posi_b], [ang_b])
    TS("dve", ang, ang, cv["invf"][:, 0:1], ALU.mult, [ang_b, cst_b], [ang_b])
    TWO_PI = 2.0 * np.pi
    C1 = 6.28125
    C2 = TWO_PI - C1
    for dst, dst_b, shift in ((S4, S4_b, 0.0), (C4, C4_b, np.pi / 2.0)):
        TS("dve", t1, ang, shift, ALU.add, [ang_b], [t1_b], s2=1.0 / TWO_PI, op1=ALU.mult)
        CP("dve", t2_i, t1, [t1_b], [t2_b])
        CP("dve", t1, t2_i, [t2_b], [t1_b])
        TS("dve", t2, ang, shift, ALU.add, [ang_b], [t2_b])
        STT(t2, t1, -C1, t2, ALU.mult, ALU.add, [t1_b, t2_b], [t2_b])
        STT(t2, t1, -C2, t2, ALU.mult, ALU.add, [t1_b, t2_b], [t2_b])
        TS("dve", t1, t2, float(np.pi), ALU.is_gt, [t2_b], [t1_b])
        STT(t2, t1, -TWO_PI, t2, ALU.mult, ALU.add, [t1_b, t2_b], [t2_b])
        TS("dve", t1, t2, float(-np.pi), ALU.is_lt, [t2_b], [t1_b])
        STT(t2, t1, TWO_PI, t2, ALU.mult, ALU.add, [t1_b, t2_b], [t2_b])
        TS("dve", t2, t2, float(np.pi), ALU.min, [t2_b], [t2_b], s2=float(-np.pi), op1=ALU.max)
        ACT(dst, t2, AF.Sin, [t2_b], [dst_b])
    P.dma("sp", rope_d[0:128, :], C4, reads=[C4_b], writes=[dbuf["rope_d"]])
    P.dma("sp", rope_d[128:256, :], S4, reads=[S4_b], writes=[dbuf["rope_d"]])
    P.barrier()
    A.release(m0)

    o_ = [0]

    def lp_alloc(n):
        a = o_[0]
        o_[0] += n
        assert o_[0] <= 512
        return lp[:, a:a + n]

    b_adaT = lp_alloc(48)
    modT = lp_alloc(48)
    n1gT = lp_alloc(8)
    n2gT = lp_alloc(8)
    scwT = lp_alloc(96)
    scbT = lp_alloc(24)
    sngT = lp_alloc(16)
    fcwT = lp_alloc(132)
    fcbT = lp_alloc(44)
    qg2 = lp_alloc(1)
    kg2 = lp_alloc(1)
    A1 = lp_alloc(8)
    A2 = lp_alloc(8)
    cact2 = lp_alloc(16)
    dskT = lp_alloc(16)
    cact2_3 = cact2.rearrange("p (k two) -> p k two", two=2)
    iw3 = iw_tm.rearrange("p (i h) -> p i h", h=8)
    dt3 = dt_tm.rearrange("p (i h) -> p i h", h=SH)
    a3 = a_tm.rearrange("p (i h) -> p i h", h=SH)

    m0 = A.mark()
    tmp, tmp_b = A.f32("ctmp", 128)
    P.dma("sp", tmp[0:KC, :], c_in, reads=[dbuf["c"]], writes=[tmp_b])
    pt, pb = next_ps()
    P.op("pe", lambda e, pt=pt: e.transpose(out=pt[:, 0:KC], in_=tmp[0:KC, :], identity=cv["ident"][0:KC, 0:KC]),
         reads=[tmp_b, cst_b], writes=[pb])
    ACT(cact2_3[:, :, 0], pt[:, 0:KC], AF.Silu, [pb], [lp_b])
    ACT(cact2_3[:, :, 1], pt[:, 0:KC], AF.Silu, [pb], [lp_b])
    P.barrier()
    A.release(m0)

    def load_fm_rows(dst, src_ap, src_name, nrow):
        tmp_, tmpb_ = lfm_pool.get()
        P.dma("sp", tmp_[0:nrow, :], src_ap, reads=[dbuf[src_name]], writes=[tmpb_])
        pt_, pb_ = next_ps()
        P.op("pe", lambda e: e.transpose(out=pt_[:, 0:nrow], in_=tmp_[0:nrow, :], identity=cv["ident"][0:nrow, 0:nrow]),
             reads=[tmpb_, cst_b], writes=[pb_])
        CP("dve", dst, pt_[:, 0:nrow], [pb_], [lp_b])

    def layer_params(l):
        load_fm_rows(b_adaT, W["b_ada"][l], "b_ada", 48)
        load_fm_rows(n1gT, W["norm1_g"][l], "norm1_g", KC)
        load_fm_rows(n2gT, W["norm2_g"][l], "norm2_g", KC)
        for tap in range(4):
            load_fm_rows(scwT[:, tap * 24:(tap + 1) * 24], W["ssm_conv_w"][l, tap], "ssm_conv_w", 24)
        load_fm_rows(scbT, W["ssm_conv_b"][l], "ssm_conv_b", 24)
        load_fm_rows(sngT, W["ssm_norm_g"][l], "ssm_norm_g", 16)
        for tap in range(3):
            load_fm_rows(fcwT[:, tap * 44:(tap + 1) * 44], W["ffn_conv_w"][l, tap], "ffn_conv_w", 44)
        load_fm_rows(fcbT, W["ffn_conv_b"][l], "ffn_conv_b", 44)
        for dst, nm in ((qg2, "q_norm_g"), (kg2, "k_norm_g")):
            tmp_, tmpb_ = lfm_pool.get()
            P.dma("sp", tmp_[0:1, 0:64], W[nm][l], reads=[dbuf[nm]], writes=[tmpb_])
            P.dma("sp", tmp_[0:1, 64:128], W[nm][l], reads=[dbuf[nm]], writes=[tmpb_])
            pt_, pb_ = next_ps()
            P.op("pe", lambda e, pt_=pt_, tmp_=tmp_: e.transpose(out=pt_[:, 0:1], in_=tmp_[0:1, :],
                                                                 identity=cv["ident"][0:1, 0:1]),
                 reads=[tmpb_, cst_b], writes=[pb_])
            CP("dve", dst, pt_[:, 0:1], [pb_], [lp_b])
        tmp_, tmpb_ = lfm_pool.get()
        P.dma("sp", tmp_[0:16, 0:2], W["d_skip"][l].rearrange("o (c two) -> (o c) two", two=2),
              reads=[dbuf["d_skip"]], writes=[tmpb_])
        pt_, pb_ = next_ps()
        P.op("pe", lambda e: e.transpose(out=pt_[0:2, 0:16], in_=tmp_[0:16, 0:2], identity=cv["ident"][0:16, 0:16]),
             reads=[tmpb_, cst_b], writes=[pb_])
        tmp2_, tmp2b_ = lfm_pool.get()
        CP("dve", tmp2_[0:2, 0:16], pt_[0:2, 0:16], [pb_], [tmp2b_])
        pt2_, pb2_ = next_ps()
        MM(pt2_[:, 0:16], halfsel[0:2, :], tmp2_[0:2, 0:16], True, True, [tmp2b_, halfsel_b], [pb2_])
        CP("dve", dskT, pt2_[:, 0:16], [pb2_], [lp_b])
        P.dma("sp", dtb_bc, bcast_ap(W["dt_bias"][l], 128, SH), reads=[dbuf["dt_bias"]], writes=[dtb_b])
        P.dma("sp", alog_bc, bcast_ap(W["a_log"][l], 128, SH), reads=[dbuf["a_log"]], writes=[alog_b])
        ACT(alog_bc, alog_bc, AF.Exp, [alog_b], [alog_b])

    P.dma("sp", halfsel[0:1, :], consts_in[0:1, 256:384], reads=[dbuf["consts"]], writes=[halfsel_b])
    P.dma("sp", halfsel[1:2, :], consts_in[64:65, 256:384], reads=[dbuf["consts"]], writes=[halfsel_b])


    cast_rr = [0]

    def load_w(wname, l, col0, ncols, kc, row0=0, to_bf16=True, dst=None):
        assert kc * ncols <= WSTG
        st, st_b = wst_pool.get()
        st3 = st[:, 0:kc * ncols].rearrange("p (k c) -> p k c", c=ncols)
        src = W[wname][l][row0:row0 + kc * 128, col0:col0 + ncols].rearrange("(k p) c -> p k c", p=128)
        P.dma("sp", st3, src, reads=[dbuf[wname]], writes=[st_b])
        if not to_bf16:
            return st3, st_b
        if dst is not None:
            CP("pool", dst[0], st3, [st_b], [dst[1]])
            return dst
        wb, wb_b = wbf_pool.get()
        wb3 = wb[:, 0:kc * ncols].rearrange("p (k c) -> p k c", c=ncols)
        CP("pool", wb[:, 0:kc * ncols], st[:, 0:kc * ncols], [st_b], [wb_b])
        return wb3, wb_b

    def load_w_resident(name, wname, l, kc, ncols):
        wr, wr_b = A.bf16(name, kc * ncols)
        wr3 = wr.rearrange("p (k c) -> p k c", c=ncols)
        kstep = max(1, WSTG // ncols) if ncols <= WSTG else 1
        cstep = min(ncols, WSTG)
        for k0 in range(0, kc, kstep):
            kk = min(kstep, kc - k0)
            for c0 in range(0, ncols, cstep):
                cc = min(cstep, ncols - c0)
                load_w(wname, l, c0, cc, kk, row0=k0 * 128, dst=(wr3[:, k0:k0 + kk, c0:c0 + cc], wr_b))
        return wr3, wr_b


    def store(dst_ap, dname, src_ap, src_b):
        P.dma("pool", dst_ap, src_ap, reads=[src_b], writes=[dbuf[dname]])

    def compute_mod(l):
        for cb in range(24):
            w3, w_b = load_w("w_ada", l, cb * 256, 256, KC, to_bf16=False)
            pt, pb = next_ps()
            for m in range(2):
                for k in range(KC):
                    MM(pt[:, 2 * m:2 * m + 2], w3[:, k, m * 128:(m + 1) * 128], cact2_3[:, k, :], k == 0, k == KC - 1,
                       [w_b, lp_b], [pb])
            ptv = pt[:, 0:4].rearrange("p (m two) -> p m two", two=2)[:, :, 0]
            TT("dve", modT[:, cb * 2:cb * 2 + 2], ptv, b_adaT[:, cb * 2:cb * 2 + 2], ALU.add, [pb, lp_b], [lp_b])
        STT(A1, modT[:, 8:16], 1.0, n1gT, ALU.add, ALU.mult, [lp_b], [lp_b])
        STT(A2, modT[:, 32:40], 1.0, n2gT, ALU.add, ALU.mult, [lp_b], [lp_b])

    def norm_mod(Avec, Bvec):
        for tb in range(NB):
            sl = slice(tb * 512, (tb + 1) * 512)
            pt, pb = next_ps()
            for k in range(KC):
                sq, sq_b = stg_pool.get()
                ACT(sq, xT3[:, k, sl], AF.Square, [xT_b], [sq_b])
                MM(pt, cv["ones"], sq, k == 0, k == KC - 1, [sq_b, cst_b], [pb])
            rs, rs_b = stg_pool.get()
            ACT(rs, pt, AF.Sqrt, [pb], [rs_b], bias=epsT[:, 0:1], scale=1.0 / D)
            P.op("dve", lambda e, rs=rs: e.reciprocal(out=rs, in_=rs), [rs_b], [rs_b])
            for k in range(KC):
                tq, tq_b = stg_pool.get()
                TT("dve", tq, xT3[:, k, sl], rs, ALU.mult, [xT_b, rs_b], [tq_b])
                TS("dve", ns.hT3[:, k, sl], tq, Avec[:, k:k + 1], ALU.mult, [tq_b, lp_b], [ns.hT_b],
                   s2=Bvec[:, k:k + 1], op1=ALU.add)

    P.op("dve", lambda e: e.memset(epsT, EPS), [], [epsT_b])

    def proj_fm(wname, l, col0, ncols, rhs3, rhs_b, kc, epilogue, sub0=0, row0=0, blk=512):
        per = max(128, (WSTG // kc) // 128 * 128)
        per = min(per, blk)
        c = 0
        while c < ncols:
            n = min(per, ncols - c)
            w3, w_b = load_w(wname, l, col0 + c, n, kc, row0=row0)
            for m in range(n // 128):
                for tb in range(NB):
                    pt, pb = next_ps()
                    for k in range(kc):
                        MM(pt, w3[:, k, m * 128:(m + 1) * 128], rhs3[:, k, tb * 512:(tb + 1) * 512], k == 0, k == kc - 1,
                           [w_b, rhs_b], [pb])
                    epilogue(pt, pb, sub0 + (c // 128) + m, tb)
            c += n

    def dst_rows(dname, r0, r1):
        if isinstance(dname, GatherSet):
            return dname.loc_rows(r0, r1)
        return dram[dname][r0:r1, :], dname

    def rope_epilogue(src_sb, src_b, tb, dst_dram, dname, row):
        sl = slice(tb * 512, (tb + 1) * 512)
        pr, prb = next_ps()
        MM(pr, cv["rrot"], src_sb, True, True, [src_b, cst_b], [prb])
        u1, u1_b = stg_pool.get()
        TT("pool", u1, src_sb, ns.C4[:, sl], ALU.mult, [src_b, ns.C4_b], [u1_b])
        u2, u2_b = stg_pool.get()
        TT("dve", u2, pr, ns.S4[:, sl], ALU.mult, [prb, ns.S4_b], [u2_b])
        ob, ob_b = sbf_pool.get()
        TT("dve", ob, u1, u2, ALU.add, [u1_b, u2_b], [ob_b])
        dap, dn = dst_rows(dname, row * 128, (row + 1) * 128)
        store(dap[:, sl], dn, ob, ob_b)

    def qk_epilogue(gvec, dname):
        def ep(pt, pb, sub, tb):
            sq, sq_b = stg_pool.get()
            ACT(sq, pt, AF.Square, [pb], [sq_b])
            p2, p2b = next_ps()
            MM(p2, cv["blockones"], sq, True, True, [sq_b, cst_b], [p2b])
            rs, rs_b = stg_pool.get()
            ACT(rs, p2, AF.Sqrt, [p2b], [rs_b], bias=epsT[:, 0:1], scale=1.0 / HD)
            P.op("dve", lambda e, rs=rs: e.reciprocal(out=rs, in_=rs), [rs_b], [rs_b])
            qn, qn_b = stg_pool.get()
            STT(qn, pt, gvec, rs, ALU.mult, ALU.mult, [pb, lp_b, rs_b], [qn_b])
            rope_epilogue(qn, qn_b, tb, None, dname, sub)
        return ep

    def iq_epilogue(dname, nsub_real):
        def ep(pt, pb, sub, tb):
            qn, qn_b = stg_pool.get()
            CP("act", qn, pt, [pb], [qn_b])
            rope_epilogue(qn, qn_b, tb, None, dname, sub)
        return ep

    def z_epilogue(pt, pb, sub, tb):
        ob, ob_b = sbf_pool.get()
        ACT(ob, pt, AF.Silu, [pb], [ob_b])
        store(zT_d[sub * 128:(sub + 1) * 128, tb * 512:(tb + 1) * 512], "zT_d", ob, ob_b)

    def xbc_epilogue(pt, pb, sub, tb):
        o32, o32_b = stg_pool.get()
        CP("act" if (sub + tb) % 2 == 0 else "dve", o32, pt, [pb], [o32_b])
        store(xbc_raw[sub * 128:(sub + 1) * 128, tb * 512:(tb + 1) * 512], "xbc_raw", o32, o32_b)
        if tb == NB - 1:
            dap, dn = halo_gs.loc_rows(0, 128)
            store(dap[:, sub * 4:sub * 4 + 3], dn, o32[:, 509:512], o32_b)

    def gate_epilogue(pt, pb, sub, tb):
        o32, o32_b = stg_pool.get()
        ACT(o32, pt, AF.Sigmoid, [pb], [o32_b])
        store(gT_d[sub * 128:(sub + 1) * 128, tb * 512:(tb + 1) * 512], "gT_d", o32, o32_b)


    def v_projection(l):
        for qd in range(4):
            w3, w_b = load_w("w_in", l, C_V + qd * 256, 256, KC)
            for i in range(NT):
                va, va_b = ns.vaug[i % 2]
                va4 = va.rearrange("p (pr two c) -> p pr two c", two=2, c=128)
                pt, pb = next_ps()
                for k in range(KC):
                    MM(pt[:, 0:256], ns.hT3[:, k, i * 128:(i + 1) * 128], w3[:, k, :], k == 0, k == KC - 1, [ns.hT_b, w_b], [pb])
                pt4 = pt[:, 0:256].rearrange("p (pr two c) -> p pr two c", two=2, c=64)
                ev_ = "act" if i % 2 == 0 else "dve"
                CP(ev_, va4[:, :, 0, 0:64], pt4[:, :, 0, :], [pb], [va_b])
                CP(ev_, va4[:, :, 1, 64:128], pt4[:, :, 1, :], [pb], [va_b])
                dap, dn = v_gs.loc_rows(i * 128, (i + 1) * 128)
                store(dap[:, qd * 512:(qd + 1) * 512], dn, va, va_b)

    def small_projection(l):
        st, st_b = wst_pool.get()
        st3 = st[:, 0:KC * 40].rearrange("p (k c) -> p k c", c=40)
        P.dma("sp", st3[:, :, 0:32], W["w_in"][l][:, C_DT:C_DT + 32].rearrange("(k p) c -> p k c", p=128),
              reads=[dbuf["w_in"]], writes=[st_b])
        P.dma("sp", st3[:, :, 32:40], W["w_in"][l][:, C_IW:C_IW + 8].rearrange("(k p) c -> p k c", p=128),
              reads=[dbuf["w_in"]], writes=[st_b])
        wb, wb_b = wbf_pool.get()
        wb3 = wb[:, 0:KC * 40].rearrange("p (k c) -> p k c", c=40)
        CP("pool", wb[:, 0:KC * 40], st[:, 0:KC * 40], [st_b], [wb_b])
        for i in range(NT):
            pt, pb = next_ps()
            for k in range(KC):
                MM(pt[:, 0:40], ns.hT3[:, k, i * 128:(i + 1) * 128], wb3[:, k, :], k == 0, k == KC - 1, [ns.hT_b, wb_b], [pb])
            xx, xx_b = stg_pool.get()
            CP("dve", xx[:, 64:104], pt[:, 0:40], [pb], [xx_b])
            CP("dve", iw3[:, i, :], xx[:, 96:104], [xx_b], [iw_b])
            TT("dve", xx[:, 0:32], xx[:, 64:96], dtb_bc, ALU.add, [xx_b, dtb_b], [xx_b])
            STT(xx[:, 32:64], xx[:, 0:32], -1.0, xx[:, 0:32], ALU.mult, ALU.max, [xx_b], [xx_b])
            ACT(xx[:, 32:64], xx[:, 32:64], AF.Exp, [xx_b], [xx_b], scale=-1.0)
            ACT(xx[:, 32:64], xx[:, 32:64], AF.Ln, [xx_b], [xx_b], bias=oneT[:, 0:1], scale=1.0)
            STT(dt3[:, i, :], xx[:, 0:32], 0.0, xx[:, 32:64], ALU.max, ALU.add, [xx_b], [dt_b])
            STT(a3[:, i, :], dt3[:, i, :], -1.0, alog_bc, ALU.mult, ALU.mult, [dt_b, alog_b], [a_b])

    P.op("dve", lambda e: e.memset(oneT, 1.0), [], [oneT_b])

    def alloc_hT():
        hT, ns.hT_b = A.bf16("hT", KC * T)
        ns.hT3 = hT.rearrange("p (k t) -> p k t", t=T)

    def phase_A(l):
        if cfg.stage < 1:
            return
        alloc_hT()
        ns.C4, ns.C4_b = A.f32("C4", T)
        ns.S4, ns.S4_b = A.f32("S4", T)
        P.dma("sp", ns.C4, rope_d[0:128, :], reads=[dbuf["rope_d"]], writes=[ns.C4_b])
        P.dma("sp", ns.S4, rope_d[128:256, :], reads=[dbuf["rope_d"]], writes=[ns.S4_b])
        ns.vaug = [A.bf16("vaug%d" % i, 512) for i in range(2)]
        for va, va_b in ns.vaug:
            P.op("pool", lambda e, va=va: e.memset(va, 1.0), [], [va_b])
        layer_params(l)
        if cfg.stage < 2:
            return
        compute_mod(l)
        if cfg.stage < 3:
            return
        norm_mod(A1, modT[:, 0:8])
        if cfg.stage < 4:
            return
        proj_fm("w_in", l, C_Q, 1024, ns.hT3, ns.hT_b, KC, qk_epilogue(qg2[:, 0:1], "qT_d"))
        if cfg.stage < 5:
            return
        proj_fm("w_in", l, C_K, 1024, ns.hT3, ns.hT_b, KC, qk_epilogue(kg2[:, 0:1], kT_gs))
        v_projection(l)
        proj_fm("w_in", l, C_IQ, 512, ns.hT3, ns.hT_b, KC, iq_epilogue("iqT_d", 4))
        st, st_b = wst_pool.get()
        st3 = st[:, 0:KC * 128].rearrange("p (k c) -> p k c", c=128)
        for hf in range(2):
            P.dma("sp", st3[:, :, hf * 64:(hf + 1) * 64],
                  W["w_in"][l][:, C_IK:C_IK + 64].rearrange("(k p) c -> p k c", p=128),
                  reads=[dbuf["w_in"]], writes=[st_b])
        wb, wb_b = wbf_pool.get()
        wb3 = wb[:, 0:KC * 128].rearrange("p (k c) -> p k c", c=128)
        CP("pool", wb[:, 0:KC * 128], st[:, 0:KC * 128], [st_b], [wb_b])
        ikep = iq_epilogue(ik_gs, 1)
        for tb in range(NB):
            pt, pb = next_ps()
            for k in range(KC):
                MM(pt, wb3[:, k, :], ns.hT3[:, k, tb * 512:(tb + 1) * 512], k == 0, k == KC - 1, [wb_b, ns.hT_b], [pb])
            ikep(pt, pb, 0, tb)
        if cfg.stage < 6:
            return
        small_projection(l)
        if cfg.stage < 7:
            return
        proj_fm("w_in", l, C_Z, 2048, ns.hT3, ns.hT_b, KC, z_epilogue)
        proj_fm("w_in", l, C_XBC, 3072, ns.hT3, ns.hT_b, KC, xbc_epilogue)
        proj_fm("w_in", l, C_GA, 2048, ns.hT3, ns.hT_b, KC, gate_epilogue)
        if "dbg_small" in cfg.debug:
            dbg3 = dbg_small.rearrange("p (i c) -> p i c", c=80)
            store(dbg3[:, :, 0:8], "dbg_small", iw3, iw_b)
            store(dbg3[:, :, 8:40], "dbg_small", dt3, dt_b)
            store(dbg3[:, :, 40:72], "dbg_small", a3, a_b)
        if cfg.stage < 8:
            return
        kT_gs.gather()
        v_gs.gather()
        ik_gs.gather()
        halo_gs.gather()


    attnT_d = dten("attnT_d", [8 * 128, T], BF16, "ExternalInput" if "attnT_d" in cfg.feed else "Internal")
    ynT_d = dten("ynT_d", [16 * 128, T], BF16, "ExternalInput" if "ynT_d" in cfg.feed else "Internal")
    aT_d = dten("aT_d", [22 * 128, T], BF16)
    uhalo_gs = GatherSet("uhalo", 128, 44 * 2, F32, 4)

    mixT_d = dten("mixT_d", [8 * 128, T], BF16)

    def phase_D(l):
        m0_ = A.mark()
        wao3, wao_b = load_w_resident("wao", "w_attn_o", l, 8, D)
        wso3, wso_b = load_w_resident("wso", "w_ssm_o", l, 16, D)
        at, at_b = A.bf16("attn_blk", 8 * 512)
        at3 = at.rearrange("p (k t) -> p k t", t=512)
        yn, yn_b = A.bf16("yn_blk", 16 * 512)
        yn3 = yn.rearrange("p (k t) -> p k t", t=512)
        gpool = RPool("gate", 4, 512, "f32")
        for tb in range(NB):
            sl = slice(tb * 512, (tb + 1) * 512)
            P.dma("sp", at3, attnT_d.rearrange("(k p) t -> p k t", p=128)[:, :, sl], reads=[dbuf["attnT_d"]], writes=[at_b])
            P.dma("sp", yn3, ynT_d.rearrange("(k p) t -> p k t", p=128)[:, :, sl], reads=[dbuf["ynT_d"]], writes=[yn_b])
            for m in range(8):
                ga, ga_b = gpool.get()
                gm_, gm_b = gpool.get()
                P.dma("sp", ga, gT_d[m * 128:(m + 1) * 128, sl], reads=[dbuf["gT_d"]], writes=[ga_b])
                P.dma("sp", gm_, gT_d[(8 + m) * 128:(9 + m) * 128, sl], reads=[dbuf["gT_d"]], writes=[gm_b])
                pa, pab = next_ps()
                for k in range(8):
                    MM(pa, wao3[:, k, m * 128:(m + 1) * 128], at3[:, k, :], k == 0, k == 7, [wao_b, at_b], [pab])
                psm, psb = next_ps()
                for k in range(16):
                    MM(psm, wso3[:, k, m * 128:(m + 1) * 128], yn3[:, k, :], k == 0, k == 15, [wso_b, yn_b], [psb])
                t1_, t1b_ = stg_pool.get()
                TT("dve", t1_, pa, ga, ALU.mult, [pab, ga_b], [t1b_])
                t2_, t2b_ = stg_pool.get()
                TT("dve", t2_, psm, gm_, ALU.mult, [psb, gm_b], [t2b_])
                ob, ob_b = sbf_pool.get()
                TT("pool", ob, t1_, t2_, ALU.add, [t1b_, t2b_], [ob_b])
                store(mixT_d[m * 128:(m + 1) * 128, sl], "mixT_d", ob, ob_b)
        P.barrier()
        A.release(m0_)
        m0_ = A.mark()
        wout3, wout_b = load_w_resident("wout", "w_out", l, 8, D)
        mxs = [A.bf16("mix_blk%d" % i, 8 * 512) for i in range(2)]
        for tb in range(NB):
            sl = slice(tb * 512, (tb + 1) * 512)
            mx, mx_b = mxs[tb % 2]
            mx3 = mx.rearrange("p (k t) -> p k t", t=512)
            P.dma("sp", mx3, mixT_d.rearrange("(k p) t -> p k t", p=128)[:, :, sl], reads=[dbuf["mixT_d"]], writes=[mx_b])
            for m2 in range(8):
                po, pob = next_ps()
                for k in range(8):
                    MM(po, wout3[:, k, m2 * 128:(m2 + 1) * 128], mx3[:, k, :], k == 0, k == 7, [wout_b, mx_b], [pob])
                STT(xT3[:, m2, sl], po, modT[:, 16 + m2:17 + m2], xT3[:, m2, sl], ALU.mult, ALU.add, [pob, lp_b, xT_b], [xT_b])
        P.barrier()
        A.release(m0_)

    def phase_EF(l, rank_sel):
        m0_ = A.mark()
        alloc_hT()
        norm_mod(A2, modT[:, 24:32])
        uh, uh_b = A.f32("uhalo_sb", 44 * 2)
        uh3 = uh.rearrange("p (m two) -> p m two", two=2)
        for c0 in range(0, 2 * DFF, 256):
            w3, w_b = load_w("w_up", l, c0, 256, KC)
            pt, pb = next_ps()
            for m in range(2):
                for k in range(KC):
                    MM(pt[:, 2 * m:2 * m + 2], w3[:, k, m * 128:(m + 1) * 128], ns.hT3[:, k, T - 2:T], k == 0, k == KC - 1,
                       [w_b, ns.hT_b], [pb])
            CP("dve", uh3[:, (c0 // 128):(c0 // 128) + 2, :], pt[:, 0:4].rearrange("p (m two) -> p m two", two=2), [pb], [uh_b])
        dap, dn = uhalo_gs.loc_rows(0, 128)
        store(dap, dn, uh, uh_b)
        uhalo_gs.gather()
        pv, pv_b = A.f32("uprev", 44 * 2)
        pv3 = pv.rearrange("p (m two) -> p m two", two=2)
        load_prev_rows(uhalo_gs, pv3, pv_b, 44, 2)
        ub = [A.f32("ubuf%d" % i, 516) for i in range(4)]
        for m in range(22):
            wv3, wv_b = load_w("w_up", l, m * 128, 128, KC)
            wg3, wg_b = load_w("w_up", l, DFF + m * 128, 128, KC)
            conv_out = []
            for tb in range(NB):
                sl = slice(tb * 512, (tb + 1) * 512)
                res = []
                for which, (w3, w_b, sub) in enumerate(((wv3, wv_b, m), (wg3, wg_b, 22 + m))):
                    pt, pb = next_ps()
                    for k in range(KC):
                        MM(pt, w3[:, k, :], ns.hT3[:, k, sl], k == 0, k == KC - 1, [w_b, ns.hT_b], [pb])
                    u, u_b = ub[(tb % 2) * 2 + which]
                    if tb == 0:
                        CP("dve", u[:, 0:2], pv3[:, sub, :], [pv_b], [u_b])
                    else:
                        up, up_b = ub[((tb - 1) % 2) * 2 + which]
                        CP("dve", u[:, 0:2], up[:, 512:514], [up_b], [u_b])
                    CP("act", u[:, 2:514], pt, [pb], [u_b])
                    c_, c_b = stg_pool.get()
                    TS("dve", c_, u[:, 0:512], fcwT[:, sub:sub + 1], ALU.mult, [u_b, lp_b], [c_b],
                       s2=fcbT[:, sub:sub + 1], op1=ALU.add)
                    STT(c_, u[:, 1:513], fcwT[:, 44 + sub:45 + sub], c_, ALU.mult, ALU.add, [u_b, lp_b, c_b], [c_b])
                    STT(c_, u[:, 2:514], fcwT[:, 88 + sub:89 + sub], c_, ALU.mult, ALU.add, [u_b, lp_b, c_b], [c_b])
                    res.append((c_, c_b))
                (cv_, cvb_), (cg_, cgb_) = res
                sg, sg_b = stg_pool.get()
                ACT(sg, cg_, AF.Silu, [cgb_], [sg_b])
                ob, ob_b = sbf_pool.get()
                TT("pool", ob, sg, cv_, ALU.mult, [sg_b, cvb_], [ob_b])
                store(aT_d[m * 128:(m + 1) * 128, sl], "aT_d", ob, ob_b)
        P.barrier()
        A.release(m0_)
        m0_ = A.mark()
        wd3, wd_b = load_w_resident("wdown", "w_down", l, 22, D)
        ab = [A.bf16("a_blk%d" % i, 22 * 512) for i in range(1)]
        for tb in range(NB):
            sl = slice(tb * 512, (tb + 1) * 512)
            a_, a_b_ = ab[0]
            a3_ = a_.rearrange("p (k t) -> p k t", t=512)
            P.dma("sp", a3_, aT_d.rearrange("(k p) t -> p k t", p=128)[:, :, sl], reads=[dbuf["aT_d"]], writes=[a_b_])
            for m2 in range(8):
                po, pob = next_ps()
                for k in range(22):
                    MM(po, wd3[:, k, m2 * 128:(m2 + 1) * 128], a3_[:, k, :], k == 0, k == 21, [wd_b, a_b_], [pob])
                STT(xT3[:, m2, sl], po, modT[:, 40 + m2:41 + m2], xT3[:, m2, sl], ALU.mult, ALU.add, [pob, lp_b, xT_b], [xT_b])
        P.barrier()
        A.release(m0_)


    x_save = dten("x_save", [8 * 128, T], F32)

    def spill_x():
        P.dma("sp", x_save.rearrange("(k p) t -> p k t", p=128), xT3, reads=[xT_b], writes=[dbuf["x_save"]])
        P.barrier()

    def restore_x():
        P.barrier()
        P.dma("sp", xT3, x_save.rearrange("(k p) t -> p k t", p=128), reads=[dbuf["x_save"]], writes=[xT_b])

    NKEY = GSZ * T
    NKT = NKEY // 128
    NKB = NKEY // 512
    NIT = 23
    LO0 = -4096.0
    BIGP = float(2.0 ** 100)
    NSPL = (int(NKEY * 0.41) // 512) * 512
    NACT = NKEY - NSPL
    SCALE = float(HD ** -0.5)

    def phase_B(l):
        m0_ = A.mark()
        score, score_b = A.f32("score", NKEY)
        mb, mbA_b = A.bf16("mb", NKEY)
        mbB_b = Buf("mbB")
        mbT_raw, mbT_b = A.f32("mbT", NKT * 512 // 4)
        mbT = mbT_raw.bitcast(FP8)
        mbT3 = mbT.rearrange("p (k t) -> p k t", t=512)
        ikT, ikT_b = A.bf16("ikT", NKEY)
        ikT3 = ikT.rearrange("p (r t) -> p r t", t=T)
        iqb, iqb_b = A.bf16("iq_blk", 4 * 512)
        iqb3 = iqb.rearrange("p (k t) -> p k t", t=512)
        qb_, qb_b = A.bf16("q_blk", 8 * 512)
        qb3 = qb_.rearrange("p (k t) -> p k t", t=512)
        kst = [A.bf16("kst%d" % i, 2 * 512) for i in range(3)]
        vst = [A.bf16("vst%d" % i, 4 * 512) for i in range(3)]
        ppool = RPool("pT", 4, 512, "bf16")
        ab_, ab_b = A.bf16("attn_o_blk", 8 * 512)
        ab3 = ab_.rearrange("p (k t) -> p k t", t=512)
        negd0, negd0_b = A.f32("negd0", 512)
        sm, sm_b = A.f32("bis_small", 16)
        md, md_b = A.f32("bis_mid", 2)
        cn, cnt_b = A.f32("bis_cnt", 2)
        sa, sacc_b = A.f32("bis_sacc", 2)
        Wk, Wk_b = A.f32("bis_w", NIT)
        lo = sm[:, 0:1]
        mid = md[:, 0:1]
        nmid = md[:, 1:2]
        cnt = cn[:, 0:1]
        sacc = sa[:, 0:1]
        tot = sm[:, 5:6]
        ge = sm[:, 6:7]
        hi0 = sm[:, 7:8]
        qp0 = sm[:, 8:9]
        STT(qp0, rk[:, 0:1], float(T), cv["pidx"][:, 0:1], ALU.mult, ALU.add, [rk_b, cst_b], [sm_b])
        TS("dve", negd0, cv["iotaf"], qp0, ALU.subtract, [cst_b, sm_b], [negd0_b], s2=-BIGP, op1=ALU.mult)
        P.dma("sp", ikT3, ik_gs.g[0].rearrange("(r p) t -> p r t", p=128), reads=[dbuf["ik_g0"]], writes=[ikT_b])
        for qb in range(NB):
            qsl = slice(qb * 512, (qb + 1) * 512)
            P.dma("sp", iqb3, iqT_d.rearrange("(k p) t -> p k t", p=128)[:, :, qsl], reads=[dbuf["iqT_d"]], writes=[iqb_b])
            P.dma("sp", qb3, qT_d.rearrange("(k p) t -> p k t", p=128)[:, :, qsl], reads=[dbuf["qT_d"]], writes=[qb_b])
            for qs in range(4):
                qt = qb * 4 + qs
                for kb in range(NKB):
                    ksl = slice(kb * 512, (kb + 1) * 512)
                    madd, madd_b = stg_pool.get()
                    TS("dve", madd, negd0, float((kb * 512 - qt * 128) * (-BIGP)), ALU.add, [negd0_b], [madd_b],
                       s2=0.0, op1=ALU.min)
                    for h in range(IH):
                        hb = (h % 2) * 64
                        pt, pb = next_ps()
                        MM(pt, iqb3[hb:hb + 64, h // 2, qs * 128:(qs + 1) * 128], ikT[hb:hb + 64, ksl], True, True,
                           [iqb_b, ikT_b], [pb])
                        rl, rl_b = stg_pool.get()
                        ACT(rl, pt, AF.Relu, [pb], [rl_b])
                        if h == 0:
                            STT(score[:, ksl], rl, iw3[:, qt, h:h + 1], madd, ALU.mult, ALU.add, [rl_b, iw_b, madd_b], [score_b])
                        else:
                            STT(score[:, ksl], rl, iw3[:, qt, h:h + 1], score[:, ksl], ALU.mult, ALU.add,
                                [rl_b, iw_b, score_b], [score_b])
                P.op("dve", lambda e: e.tensor_reduce(out=hi0, in_=score, axis=AX.X, op=ALU.max), [score_b], [sm_b])
                TS("dve", tot, hi0, 1.0 - LO0, ALU.add, [sm_b], [sm_b])
                for k in range(NIT):
                    TS("dve", Wk[:, k:k + 1], tot, float(2.0 ** -(k + 1)), ALU.mult, [sm_b], [Wk_b])
                P.op("dve", lambda e: e.memset(lo, LO0), [], [sm_b])
                for k in range(NIT):
                    TT("dve", mid, lo, Wk[:, k:k + 1], ALU.add, [sm_b, Wk_b], [md_b])
                    TS("dve", nmid, mid, -1.0, ALU.mult, [md_b], [md_b])
                    ACT(mb[:, NSPL:NKEY], score[:, NSPL:NKEY], AF.Sign, [score_b, md_b], [mbB_b, sacc_b], bias=nmid, scale=1.0,
                        accum_out=sacc)
                    TS("dve", mb[:, 0:NSPL], score[:, 0:NSPL], mid, ALU.is_ge, [score_b, md_b], [mbA_b, cnt_b],
                       s2=0.0, op1=ALU.add, accum_out=cnt)
                    STT(tot, sacc, 0.5, cnt, ALU.mult, ALU.add, [sacc_b, cnt_b], [sm_b])
                    TS("dve", ge, tot, float(cfg.nkeep - NACT / 2.0), ALU.is_ge, [sm_b], [sm_b])
                    STT(lo, ge, Wk[:, k:k + 1], lo, ALU.mult, ALU.add, [sm_b, Wk_b], [sm_b])
                TS("dve", mb, score, lo, ALU.is_ge, [score_b, sm_b], [mbA_b, mbB_b])
                for k4 in range(NKT // 4):
                    pt, pb = next_ps()
                    ptb = pt.bitcast(BF16)
                    for u in range(4):
                        kt = k4 * 4 + u
                        P.op("pe", lambda e, ptb=ptb, u=u, kt=kt: e.transpose(out=ptb[:, u * 128:(u + 1) * 128],
                                                                             in_=mb[:, kt * 128:(kt + 1) * 128],
                                                                             identity=ident_bf),
                             [mbA_b, mbB_b, ident_bf_b], [pb])
                    dst_ = mbT3[:, k4 * 4:(k4 + 1) * 4, qs * 128:(qs + 1) * 128]
                    src_ = ptb[:, 0:512].rearrange("p (u t) -> p u t", t=128)
                    if k4 % 2 == 0:
                        P.op("act", lambda e, dst_=dst_, src_=src_: e.activation(out=dst_, in_=src_, func=AF.Copy, saturate=False),
                             [pb], [mbT_b])
                    else:
                        P.op("dve", lambda e, dst_=dst_, src_=src_: e.tensor_copy(out=dst_, in_=src_, saturate=False),
                             [pb], [mbT_b])
            for hg in range(4):
                accs = [psum[i_] for i_ in range(4)]
                nk4 = NKT // 4
                bufs = {}

                def issue_kv(k4, hg=hg, bufs=bufs):
                    kk, kk_b = kst[k4 % 3]
                    kk3 = kk.rearrange("p (pr t) -> p pr t", t=512)
                    vv, vv_b = vst[k4 % 3]
                    vv3 = vv.rearrange("p (u c) -> p u c", c=512)
                    r_ = (k4 * 512) // T
                    off = (k4 * 512) % T
                    for pr in range(2):
                        gap, gn = kT_gs.g_rows(r_, (hg * 2 + pr) * 128, (hg * 2 + pr + 1) * 128)
                        P.dma("sp", kk3[:, pr, :], gap[:, off:off + 512], reads=[dbuf[gn]], writes=[kk_b])
                    for u in range(4):
                        gap, gn = v_gs.g_rows(r_, off + u * 128, off + (u + 1) * 128)
                        P.dma("sp", vv3[:, u, :], gap[:, hg * 512:(hg + 1) * 512], reads=[dbuf[gn]], writes=[vv_b])
                    bufs[k4] = (kk3, kk_b, vv3, vv_b)

                steps = [(k4, u, hh) for k4 in range(nk4) for u in range(4) for hh in range(4)]
                LA = 3
                pend = {}
                issue_kv(0)
                for si in range(len(steps) + LA):
                    if si < len(steps):
                        k4, u, hh = steps[si]
                        if u == 0 and hh == 0 and k4 + 1 < nk4:
                            issue_kv(k4 + 1)
                        kk3, kk_b, vv3, vv_b = bufs[k4]
                        kt = k4 * 4 + u
                        h = hg * 4 + hh
                        hb = (h % 2) * 64
                        pt, pb = next_ps_s()
                        MM(pt, kk3[hb:hb + 64, hh // 2, u * 128:(u + 1) * 128], qb3[hb:hb + 64, h // 2, :], True, True,
                           [kk_b, qb_b], [pb])
                        pend[si] = (pt, pb)
                    ti = si - LA
                    if ti >= 0:
                        k4, u, hh = steps[ti]
                        kk3, kk_b, vv3, vv_b = bufs[k4]
                        kt = k4 * 4 + u
                        pt, pb = pend.pop(ti)
                        pT, pT_b = ppool.get()
                        ACT(pT, pt, AF.Exp, [pb], [pT_b], scale=SCALE)
                        TT("dve", pT, pT, mbT3[:, kt, :], ALU.mult, [pT_b, mbT_b], [pT_b])
                        acc, acc_b = accs[hh]
                        MM(acc, vv3[:, u, hh * 128:(hh + 1) * 128], pT, kt == 0, kt == NKT - 1, [vv_b, pT_b], [acc_b])
                for hh in range(4):
                    h = hg * 4 + hh
                    hb = (h % 2) * 64
                    acc, acc_b = accs[hh]
                    osb, osb_b = stg_pool.get()
                    CP("act", osb, acc, [acc_b], [osb_b])
                    pw, pw_b = next_ps_s()
                    MM(pw, cv["swap"], osb, True, True, [cst_b, osb_b], [pw_b])
                    rc, rc_b = stg_pool.get()
                    P.op("dve", lambda e, rc=rc, pw=pw: e.reciprocal(out=rc, in_=pw), [pw_b], [rc_b])
                    TT("dve", ab3[hb:hb + 64, h // 2, :], osb[hb:hb + 64, :], rc[hb:hb + 64, :], ALU.mult, [osb_b, rc_b], [ab_b])
            store(attnT_d.rearrange("(k p) t -> p k t", p=128)[:, :, qsl], "attnT_d", ab3, ab_b)
        print("phase B arena top", A.off, "of", A.n)
        P.barrier()
        A.release(m0_)

    ps_s_rr = [0]

    def next_ps_s():
        i = ps_s_rr[0]
        ps_s_rr[0] = (i + 1) % 4
        return psum[4 + i]


    NCH = T // 256
    st_gs = GatherSet("ssdF", 128, 2048, F32, 4)
    dd_gs = GatherSet("ssdD", 128, SH, F32, 4)

    def bcast_last(ap, n):
        return bass.AP(ap.tensor, ap.offset, [list(x) for x in ap.ap] + [[0, n]])

    def phase_C(l):
        m0_ = A.mark()
        rwp = RPool("rw", 3, 260, "f32")
        xsT, xsT_b = A.f32("xsT", 16 * 256)
        xsT3 = xsT.rearrange("p (m t) -> p m t", t=256)
        BT, BT_b = A.bf16("BT", 4 * 256)
        BT3 = BT.rearrange("p (g t) -> p g t", t=256)
        CT, CT_b = A.bf16("CT", 4 * 256)
        CT3 = CT.rearrange("p (g t) -> p g t", t=256)
        xdt, xdt_b = A.bf16("xdt", 2 * 2048)
        xdt3 = xdt.rearrange("p (i c) -> p i c", c=2048)
        xdd, xdd_b = A.bf16("xdd", 2 * 2048)
        xdd3 = xdd.rearrange("p (i c) -> p i c", c=2048)
        Btm, Btm_b = A.bf16("Btm", 2 * 512)
        Btm4 = Btm.rearrange("p (i g n) -> p i g n", g=4, n=128)
        acs, acs_b = A.f32("acs_tm", 2 * SH)
        acs3 = acs.rearrange("p (i h) -> p i h", h=SH)
        totb, totb_b = A.f32("tot_bc", SH)
        etot, etot_b = A.f32("etot", SH)
        ds_, ds_b = A.f32("ds", 2 * SH)
        ds3 = ds_.rearrange("p (i h) -> p i h", h=SH)
        Sst, Sst_b = A.f32("Sstate", 2048)
        Sbf, Sbf_b = A.bf16("Sstate_bf", 2048)
        Dacc, Dacc_b = A.f32("Dacc", SH)
        cbm, cbm_b = A.f32("CBm", 4 * 384)
        cbm3 = cbm.rearrange("p (g t) -> p g t", t=384)
        triL, triL_b = A.f32("triL", 512)
        yg, yg_b = A.f32("yg", 16 * 256)
        yg3 = yg.rearrange("p (m t) -> p m t", t=256)
        prev, prev_b = A.f32("xbc_prev", 24 * 4)
        prev3 = prev.rearrange("p (m c) -> p m c", c=4)
        zp = RPool("zt", 3, 256, "bf16")
        r0p = RPool("c_r0", 6, 512, "f32")
        dpp = RPool("c_d", 8, 384, "f32")
        ebp = RPool("c_eb", 8, 256, "f32")
        bcp = RPool("c_bc", 6, 256, "f32")
        gpp = RPool("c_G", 6, 384, "bf16")
        cep = RPool("c_Ce", 6, 256, "bf16")
        print("phase C arena top", A.off, "of", A.n)
        CP("dve", triL[:, 0:128], cv["tri"], [cst_b], [triL_b])
        CP("dve", triL[:, 128:256], cv["ones"], [cst_b], [triL_b])
        P.op("dve", lambda e: e.memset(triL[:, 256:384], 0.0), [], [triL_b])
        CP("dve", triL[:, 384:512], cv["tri"], [cst_b], [triL_b])
        load_prev_rows(halo_gs, prev3, prev_b, 24, 4)

        def conv_block(blk, c, out_ap, out_b, eng_out="act"):
            rw, rw_b = rwp.get()
            if c == 0:
                P.dma("sp", rw[:, 3:259], xbc_raw[blk * 128:(blk + 1) * 128, 0:256], reads=[dbuf["xbc_raw"]], writes=[rw_b])
                CP("pool", rw[:, 0:3], prev3[:, blk, 0:3], [prev_b], [rw_b])
            else:
                P.dma("sp", rw[:, 0:259], xbc_raw[blk * 128:(blk + 1) * 128, c * 256 - 3:c * 256 + 256],
                      reads=[dbuf["xbc_raw"]], writes=[rw_b])
            ac, ac_b = stg_pool.get()
            TS("dve", ac[:, 0:256], rw[:, 0:256], scwT[:, blk:blk + 1], ALU.mult, [rw_b, lp_b], [ac_b],
               s2=scbT[:, blk:blk + 1], op1=ALU.add)
            for tap in range(1, 4):
                STT(ac[:, 0:256], rw[:, tap:tap + 256], scwT[:, tap * 24 + blk:tap * 24 + blk + 1], ac[:, 0:256], ALU.mult, ALU.add,
                    [rw_b, lp_b, ac_b], [ac_b])
            ACT(out_ap, ac[:, 0:256], AF.Silu, [ac_b], [out_b])

        def ssd_pass(compute_y):
            for c in range(NCH):
                csl = slice(c * 256, (c + 1) * 256)
                for blk in range(16):
                    conv_block(blk, c, xsT3[:, blk, :], xsT_b)
                for g in range(4):
                    conv_block(16 + g, c, BT3[:, g, :], BT_b)
                if compute_y:
                    for g in range(4):
                        conv_block(20 + g, c, CT3[:, g, :], CT_b)
                for i in range(2):
                    ti = c * 2 + i
                    for bank in range(4):
                        pt, pb = next_ps()
                        for u in range(4):
                            blk = bank * 4 + u
                            P.op("pe", lambda e, pt=pt, u=u, blk=blk, i=i: e.transpose(
                                out=pt[:, u * 128:(u + 1) * 128], in_=xsT3[:, blk, i * 128:(i + 1) * 128], identity=cv["ident"]),
                                [xsT_b, cst_b], [pb])
                        TT("dve", xdt3[:, i, bank * 512:(bank + 1) * 512].rearrange("p (h q) -> p h q", q=64),
                           pt.rearrange("p (h q) -> p h q", q=64), bcast_last(dt3[:, ti, bank * 8:(bank + 1) * 8], 64), ALU.mult,
                           [pb, dt_b], [xdt_b])
                    pt, pb = next_ps()
                    ptb = pt.bitcast(BF16)
                    for g in range(4):
                        P.op("pe", lambda e, ptb=ptb, g=g, i=i: e.transpose(out=ptb[:, g * 128:(g + 1) * 128],
                                                                           in_=BT3[:, g, i * 128:(i + 1) * 128], identity=ident_bf),
                             [BT_b, ident_bf_b], [pb])
                    CP("act", Btm4[:, i, :, :], ptb[:, 0:512].rearrange("p (g n) -> p g n", n=128), [pb], [Btm_b])
                a0 = a3[:, c * 2, :]
                a1 = a3[:, c * 2 + 1, :]
                pt, pb = next_ps()
                MM(pt[:, 0:32], cv["tri"], a0, True, True, [cst_b, a_b], [pb])
                MM(pt[:, 32:64], cv["ones"], a0, True, False, [cst_b, a_b], [pb])
                MM(pt[:, 32:64], cv["tri"], a1, False, True, [cst_b, a_b], [pb])
                MM(pt[:, 64:96], cv["ones"], a0, True, False, [cst_b, a_b], [pb])
                MM(pt[:, 64:96], cv["ones"], a1, False, True, [cst_b, a_b], [pb])
                CP("dve", acs, pt[:, 0:64], [pb], [acs_b])
                CP("dve", totb, pt[:, 64:96], [pb], [totb_b])
                ACT(etot, totb, AF.Exp, [totb_b], [etot_b])
                TT("dve", Dacc, Dacc, totb, ALU.add, [Dacc_b, totb_b], [Dacc_b])
                for i in range(2):
                    TT("dve", ds3[:, i, :], totb, acs3[:, i, :], ALU.subtract, [totb_b, acs_b], [ds_b])
                ACT(ds_, ds_, AF.Exp, [ds_b], [ds_b])
                for i in range(2):
                    TT("pool", xdd3[:, i, :].rearrange("p (h q) -> p h q", q=64), xdt3[:, i, :].rearrange("p (h q) -> p h q", q=64),
                       bcast_last(ds3[:, i, :], 64), ALU.mult, [xdt_b, ds_b], [xdd_b])
                if compute_y:
                    CP("dve", Sbf, Sst, [Sst_b], [Sbf_b])
                    for g in range(4):
                        pt, pb = next_ps()
                        MM(pt[:, 0:256], BT3[:, g, 0:128], CT3[:, g, :], True, True, [BT_b, CT_b], [pb])
                        MM(pt[:, 256:384], BT3[:, g, 128:256], CT3[:, g, 128:256], True, True, [BT_b, CT_b], [pb])
                        TT("dve", cbm3[:, g, 0:128], pt[:, 0:128], cv["tri"], ALU.mult, [pb, cst_b], [cbm_b])
                        CP("dve", cbm3[:, g, 128:256], pt[:, 128:256], [pb], [cbm_b])
                        TT("dve", cbm3[:, g, 256:384], pt[:, 256:384], cv["tri"], ALU.mult, [pb, cst_b], [cbm_b])
                    NBQ = 8

                    def front(bq):
                        st_ = []
                        for hq in range(4):
                            h = bq * 4 + hq
                            r0, r0_b = r0p.get()
                            TS("dve", r0[:, 0:256], triL[:, 0:256], a0[:, h:h + 1], ALU.mult, [triL_b, a_b], [r0_b])
                            TS("pool", r0[:, 256:512], triL[:, 256:512], a1[:, h:h + 1], ALU.mult, [triL_b, a_b], [r0_b])
                            st_.append([h, r0, r0_b])
                        for e_ in st_:
                            h, r0, r0_b = e_
                            pbc, pbcb = next_ps()
                            MM(pbc[:, 0:256], cv["ones"], r0[:, 0:256], True, False, [cst_b, r0_b], [pbcb])
                            MM(pbc[:, 0:256], cv["ones"], r0[:, 256:512], False, True, [cst_b, r0_b], [pbcb])
                            e_ += [pbc, pbcb]
                        for e_ in st_:
                            h, r0, r0_b, pbc, pbcb = e_
                            d_, d_b = dpp.get()
                            bcs, bcs_b = bcp.get()
                            CP("act", bcs, pbc[:, 0:256], [pbcb], [bcs_b])
                            TS("dve", d_[:, 0:256], bcs[:, 0:256], acs3[:, 0, h:h + 1], ALU.subtract, [bcs_b, acs_b], [d_b],
                               s2=0.0, op1=ALU.min)
                            TS("dve", d_[:, 256:384], bcs[:, 128:256], acs3[:, 1, h:h + 1], ALU.subtract, [bcs_b, acs_b], [d_b],
                               s2=0.0, op1=ALU.min)
                            eb, eb_b = ebp.get()
                            ACT(eb, bcs, AF.Exp, [bcs_b], [eb_b])
                            e_ += [d_, d_b, eb, eb_b]
                        return st_

                    def back(bq, st_):
                        for e_ in st_:
                            h, r0, r0_b, pbc, pbcb, d_, d_b, eb, eb_b = e_
                            g = h // 8
                            ACT(d_, d_, AF.Exp, [d_b], [d_b])
                            Ce, Ce_b = cep.get()
                            TT("pool", Ce, eb, CT3[:, g, :], ALU.mult, [eb_b, CT_b], [Ce_b])
                            e_ += [Ce, Ce_b]
                        for e_ in st_:
                            h, d_, d_b = e_[0], e_[5], e_[6]
                            g = h // 8
                            G_, G_b = gpp.get()
                            TT("dve", G_, d_, cbm3[:, g, :], ALU.mult, [d_b, cbm_b], [G_b])
                            e_ += [G_, G_b]
                        for pq in range(2):
                            pr = bq * 2 + pq
                            py, pyb = next_ps()
                            for hh in range(2):
                                e_ = st_[pq * 2 + hh]
                                h, Ce, Ce_b, G_, G_b = e_[0], e_[9], e_[10], e_[11], e_[12]
                                hb = hh * 64
                                MM(py[hb:hb + 64, 0:256], xdt3[:, 0, h * 64:(h + 1) * 64], G_[:, 0:256], True, False, [xdt_b, G_b], [pyb])
                                MM(py[hb:hb + 64, 128:256], xdt3[:, 1, h * 64:(h + 1) * 64], G_[:, 256:384], False, False,
                                   [xdt_b, G_b], [pyb])
                                MM(py[hb:hb + 64, 0:256], Sbf[:, h * 64:(h + 1) * 64], Ce, False, True, [Sbf_b, Ce_b], [pyb])
                            zt, zt_b = zp.get()
                            P.dma("sp", zt, zT_d[pr * 128:(pr + 1) * 128, csl], reads=[dbuf["zT_d"]], writes=[zt_b])
                            yv, yv_b = stg_pool.get()
                            STT(yv[:, 0:256], xsT3[:, pr, :], dskT[:, pr:pr + 1], py[:, 0:256], ALU.mult, ALU.add,
                                [xsT_b, lp_b, pyb], [yv_b])
                            TT("pool", yg3[:, pr, :], yv[:, 0:256], zt, ALU.mult, [yv_b, zt_b], [yg_b])

                    prev_st = None
                    for bq in range(NBQ + 1):
                        cur = front(bq) if bq < NBQ else None
                        if prev_st is not None:
                            back(bq - 1, prev_st)
                        prev_st = cur
                    for g in range(4):
                        pss, pssb = next_ps()
                        for q in range(4):
                            sq, sq_b = stg_pool.get()
                            ACT(sq[:, 0:256], yg3[:, g * 4 + q, :], AF.Square, [yg_b], [sq_b])
                            MM(pss[:, 0:256], cv["ones"], sq[:, 0:256], q == 0, q == 3, [cst_b, sq_b], [pssb])
                        rs, rs_b = stg_pool.get()
                        ACT(rs[:, 0:256], pss[:, 0:256], AF.Sqrt, [pssb], [rs_b], bias=epsT[:, 0:1], scale=1.0 / 512.0)
                        P.op("dve", lambda e, rs=rs: e.reciprocal(out=rs[:, 0:256], in_=rs[:, 0:256]), [rs_b], [rs_b])
                        for q in range(4):
                            pr = g * 4 + q
                            ob, ob_b = sbf_pool.get()
                            STT(ob[:, 0:256], yg3[:, pr, :], sngT[:, pr:pr + 1], rs[:, 0:256], ALU.mult, ALU.mult,
                                [yg_b, lp_b, rs_b], [ob_b])
                            store(ynT_d[pr * 128:(pr + 1) * 128, csl], "ynT_d", ob[:, 0:256], ob_b)
                for g in range(4):
                    pst, pstb = next_ps()
                    for i in range(2):
                        MM(pst, Btm4[:, i, g, :], xdd3[:, i, g * 512:(g + 1) * 512], i == 0, i == 1, [Btm_b, xdd_b], [pstb])
                    sg3 = Sst[:, g * 512:(g + 1) * 512].rearrange("p (h q) -> p h q", q=64)
                    TT("dve", sg3, sg3, bcast_last(etot[:, g * 8:(g + 1) * 8], 64), ALU.mult, [Sst_b, etot_b], [Sst_b])
                    TT("dve", Sst[:, g * 512:(g + 1) * 512], Sst[:, g * 512:(g + 1) * 512], pst, ALU.add, [Sst_b, pstb], [Sst_b])

        P.op("dve", lambda e: e.memset(Sst, 0.0), [], [Sst_b])
        P.op("dve", lambda e: e.memset(Dacc, 0.0), [], [Dacc_b])
        ssd_pass(False)
        dap, dn = st_gs.loc_rows(0, 128)
        store(dap, dn, Sst, Sst_b)
        dap, dn = dd_gs.loc_rows(0, 128)
        store(dap, dn, Dacc, Dacc_b)
        st_gs.gather()
        dd_gs.gather()
        Dr = []
        for r in range(GSZ - 1):
            t_, tb_ = A.f32("Dr%d" % r, SH)
            gap, gn = dd_gs.g_rows(r, 0, 128)
            P.dma("sp", t_, gap, reads=[dbuf[gn]], writes=[tb_])
            Dr.append((t_, tb_))
        lt, lt_b = A.f32("ltflag", 4)
        for m in range(GSZ - 1):
            TS("dve", lt[:, m:m + 1], rk[:, 0:1], float(m), ALU.is_gt, [rk_b], [lt_b])
        P.op("dve", lambda e: e.memset(Sst, 0.0), [], [Sst_b])
        Fr, Fr_b = A.f32("Fr", 2048)
        for r in range(GSZ - 1):
            wr, wr_b = A.f32("wr%d" % r, SH)
            P.op("dve", lambda e, wr=wr: e.memset(wr, 0.0), [], [wr_b])
            for m in range(r + 1, GSZ - 1):
                STT(wr, Dr[m][0], lt[:, m:m + 1], wr, ALU.mult, ALU.add, [Dr[m][1], lt_b, wr_b], [wr_b])
            ACT(wr, wr, AF.Exp, [wr_b], [wr_b])
            TS("dve", wr, wr, lt[:, r:r + 1], ALU.mult, [wr_b, lt_b], [wr_b])
            gap, gn = st_gs.g_rows(r, 0, 128)
            P.dma("sp", Fr, gap, reads=[dbuf[gn]], writes=[Fr_b])
            F3 = Fr.rearrange("p (h q) -> p h q", q=64)
            TT("dve", F3, F3, bcast_last(wr, 64), ALU.mult, [Fr_b, wr_b], [Fr_b])
            TT("dve", Sst, Sst, Fr, ALU.add, [Sst_b, Fr_b], [Sst_b])
        ssd_pass(True)
        P.barrier()
        A.release(m0_)

    P.dma("sp", rk[:, 0:4], bcast_ap(rank_in, 128, 4), reads=[dbuf["rank"]], writes=[rk_b])
    for r in range(GSZ):
        TS("dve", prevsel[:, r:r + 1], rk[:, 0:1], float(r + 1), ALU.is_equal, [rk_b], [prevsel_b])

    def load_prev_rows(gs, dst3, dst_b, nsub, ncols):
        first = True
        for r in range(GSZ - 1):
            tmp_, tmpb_ = A.f32("prevtmp%d" % r, nsub * ncols)
            tmp3 = tmp_.rearrange("p (m c) -> p m c", c=ncols)
            gap, gn = gs.g_rows(r, 0, 128)
            P.dma("sp", tmp_, gap, reads=[dbuf[gn]], writes=[tmpb_])
            if first:
                TS("dve", dst3, tmp3, prevsel[:, r:r + 1], ALU.mult, [tmpb_, prevsel_b], [dst_b])
                first = False
            else:
                STT(dst3, tmp3, prevsel[:, r:r + 1], dst3, ALU.mult, ALU.add, [tmpb_, prevsel_b, dst_b], [dst_b])

    for l in range(depth):
        A.release(m_x)
        phase_A(l)
        P.barrier()
        A.release(m_x)
        if cfg.stage >= 30:
            spill_x()
            A.release(m_pers)
            phase_B(l)
            A.release(m_pers)
            if cfg.stage >= 40:
                phase_C(l)
            A.release(m_x)
            restore_x()
        if cfg.stage >= 20:
            phase_D(l)
        if cfg.stage >= 21:
            phase_EF(l, None)
    A.release(m_x)
    for name in dbg_copies:
        rows = dram[name].shape[0]
        for r0 in range(0, rows, 512):
            r1 = min(rows, r0 + 512)
            P.dma("sp", dram[name + "_dbg"][r0:r1, :], dram[name][r0:r1, :], reads=[dbuf[name]], writes=[dbuf[name + "_dbg"]])
    P.barrier()

    m0 = A.mark()
    outs = [A.f32("otok%d" % i, D) for i in range(2)]
    for i in range(NT):
        ot, ot_b = outs[i % 2]
        for k in range(KC):
            pt, pb = next_ps()
            P.op("pe", lambda e, pt=pt, k=k, i=i: e.transpose(out=pt[:, 0:128], in_=xT3[:, k, i * 128:(i + 1) * 128],
                                                             identity=cv["ident"]),
                 reads=[xT_b, cst_b], writes=[pb])
            if k % 2 == 0:
                P.op("act", lambda e, pt=pt, k=k, ot=ot: e.copy(out=ot[:, k * 128:(k + 1) * 128], in_=pt[:, 0:128]),
                     reads=[pb], writes=[ot_b])
            else:
                P.op("dve", lambda e, pt=pt, k=k, ot=ot: e.tensor_copy(out=ot[:, k * 128:(k + 1) * 128], in_=pt[:, 0:128]),
                     reads=[pb], writes=[ot_b])
        P.dma("sp", y_out[i * 128:(i + 1) * 128, :], ot, reads=[ot_b], writes=[dbuf["y"]])
    P.barrier()
    A.release(m0)

    P.emit(stack)
    stack.close()
    return nc


def make_in_maps(cfg, inputs):
    T = cfg.T
    depth = cfg.depth
    maps = []
    f = lambda a: np.ascontiguousarray(np.asarray(a, dtype=np.float32))
    shared = {
        "consts": CONST_ARR,
        "w_ada": f(inputs["w_ada"]), "b_ada": f(inputs["b_ada"]).reshape(depth, 48, 128),
        "norm1_g": f(inputs["norm1_g"]).reshape(depth, KC, 128), "w_in": f(inputs["w_in"]),
        "q_norm_g": f(inputs["q_norm_g"]).reshape(depth, 1, 64), "k_norm_g": f(inputs["k_norm_g"]).reshape(depth, 1, 64),
        "ssm_conv_w": f(inputs["ssm_conv_w"]).reshape(depth, 4, 24, 128),
        "ssm_conv_b": f(inputs["ssm_conv_b"]).reshape(depth, 24, 128),
        "dt_bias": f(inputs["dt_bias"]).reshape(depth, 1, SH), "a_log": f(inputs["a_log"]).reshape(depth, 1, SH),
        "d_skip": f(inputs["d_skip"]).reshape(depth, 1, SH), "ssm_norm_g": f(inputs["ssm_norm_g"]).reshape(depth, 16, 128),
        "w_attn_o": f(inputs["w_attn_o"]), "w_ssm_o": f(inputs["w_ssm_o"]), "w_out": f(inputs["w_out"]),
        "norm2_g": f(inputs["norm2_g"]).reshape(depth, KC, 128), "w_up": f(inputs["w_up"]),
        "ffn_conv_w": f(inputs["ffn_conv_w"]).reshape(depth, 3, 44, 128),
        "ffn_conv_b": f(inputs["ffn_conv_b"]).reshape(depth, 44, 128), "w_down": f(inputs["w_down"]),
    }
    x = f(inputs["x"])
    c = f(inputs["c"])
    pos = np.ascontiguousarray(np.asarray(inputs["positions"], dtype=np.int32))
    for r in range(NCORES):
        b, j = divmod(r, GSZ)
        m = dict(shared)
        m["x"] = np.ascontiguousarray(x[b, j * T:(j + 1) * T, :])
        m["c"] = np.ascontiguousarray(c[b].reshape(KC, 128))
        m["positions"] = np.ascontiguousarray(pos[b, j * T:(j + 1) * T].reshape(1, T))
        rk = np.zeros((1, 4), np.float32)
        rk[0, 0] = j
        m["rank"] = rk
        for k_, v_ in (getattr(cfg, "feed_data", None) or {}).items():
            m[k_] = v_[r]
        maps.append(m)
    return maps


_CACHE = {}


def run(cfg, inputs):
    key = (cfg.S, cfg.depth, tuple(sorted(cfg.debug)), cfg.stage, tuple(sorted(cfg.feed)))
    if key not in _CACHE:
        _CACHE[key] = build_program(cfg)
    nc = _CACHE[key]
    maps = make_in_maps(cfg, inputs)
    res = run_bass_kernel_spmd(nc, maps, core_ids=list(range(NCORES)))
    return res.results


def kernel(**inputs):
    cfg = Cfg(seq=int(np.asarray(inputs["x"]).shape[1]), depth=int(np.asarray(inputs["w_in"]).shape[0]))
    results = run(cfg, inputs)
    B = np.asarray(inputs["x"]).shape[0]
    out = np.zeros((B, cfg.S, D), np.float32)
    for r in range(NCORES):
        b, j = divmod(r, GSZ)
        out[b, j * cfg.T:(j + 1) * cfg.T, :] = results[r]["y"]
    return out
```

```python
from contextlib import ExitStack
import numpy as np
import ml_dtypes
import concourse.bass as bass
import concourse.mybir as mybir
from concourse.bass_utils import run_bass_kernel_spmd

F32 = mybir.dt.float32
BF16 = mybir.dt.bfloat16
I32 = mybir.dt.int32
FP8 = mybir.dt.float8e5
AF = mybir.ActivationFunctionType
ALU = mybir.AluOpType
AX = mybir.AxisListType

NCORES = 8
GSZ = 4
D = 1024
KC = D // 128
HEADS = 16
HD = 64
IH = 8
TOPK = 256
DI = 2048
SH = 32
SG = 4
NST = 128
XBC = DI + 2 * SG * NST
DFF = 2816
EPS = 1e-6
C_Q, C_K, C_V, C_IQ, C_IK, C_IW, C_Z, C_XBC, C_DT, C_GA, C_GM = (
    0, 1024, 2048, 3072, 3584, 3648, 3656, 5704, 8776, 8808, 9832)
INW = 10856
NEG = -30000.0


class Buf:
    __slots__ = ("name", "w", "r")

    def __init__(self, name):
        self.name = name
        self.w = None
        self.r = {}


class Prog:
    ENGS = ("pe", "act", "dve", "pool", "sp")

    def __init__(self, nc, n_dma=40):
        self.nc = nc
        self.ops = {e: [] for e in self.ENGS}
        self.cnt = {e: 0 for e in self.ENGS}
        self.known = {e: {} for e in self.ENGS}
        self.dma_val = [0] * n_dma
        self.dma_next = 0
        self.cc_val = 0

    def _need(self, eng, k, v):
        if k == eng and eng == "pe":
            return
        kn = self.known[eng]
        if kn.get(k, 0) >= v:
            return
        kn[k] = v
        self.ops[eng].append(("wait", k, v))

    def _deps(self, eng, reads, writes):
        for b in reads:
            if b.w is not None:
                self._need(eng, *b.w)
        for b in writes:
            if b.w is not None:
                self._need(eng, *b.w)
            for k, v in b.r.items():
                self._need(eng, k, v)

    def _mark(self, tok, reads, writes):
        for b in writes:
            b.w = tok
            b.r = {}
        for b in reads:
            if b in writes:
                continue
            if b.r.get(tok[0], 0) < tok[1]:
                b.r[tok[0]] = tok[1]

    def op(self, eng, fn, reads=(), writes=()):
        self._deps(eng, reads, writes)
        self.cnt[eng] += 1
        tok = (eng, self.cnt[eng])
        self.ops[eng].append(("ins", fn, eng, 1, self._where()))
        self._mark(tok, reads, writes)
        return tok

    DEBUG_WHERE = False

    def _where(self):
        if not Prog.DEBUG_WHERE:
            return None
        import traceback
        return [(f.lineno, f.name) for f in traceback.extract_stack(limit=6)[:-2]]

    def dma(self, q, out, in_, reads=(), writes=()):
        i = self.dma_next
        self.dma_next = (i + 1) % len(self.dma_val)
        key = ("dma", i)
        self._deps(q, reads, writes)
        if self.dma_val[i]:
            self._need(q, key, self.dma_val[i])
        self.dma_val[i] += 16
        tok = (key, self.dma_val[i])
        self.ops[q].append(("ins", lambda e, o=out, s=in_: e.dma_start(out=o, in_=s), key, 16))
        self._mark(tok, reads, writes)
        return tok

    def collective(self, fn, reads=(), writes=()):
        self._deps("pool", reads, writes)
        self.cc_val += 1
        tok = ("cc", self.cc_val)
        self.ops["pool"].append(("ins", fn, "cc", 1))
        self._mark(tok, reads, writes)
        return tok

    def barrier(self):
        for e in self.ENGS:
            for f in ("pe", "act", "dve", "pool"):
                if self.cnt[f]:
                    self._need(e, f, self.cnt[f])
            for i, v in enumerate(self.dma_val):
                if v:
                    self._need(e, ("dma", i), v)
            if self.cc_val:
                self._need(e, "cc", self.cc_val)

    def emit(self, stack):
        nc = self.nc
        sems = {}
        for e in ("pe", "act", "dve", "pool"):
            sems[e] = stack.enter_context(nc.semaphore("s_" + e))
        for i in range(len(self.dma_val)):
            sems[("dma", i)] = stack.enter_context(nc.semaphore("d%d" % i))
        sems["cc"] = stack.enter_context(nc.semaphore("s_cc"))
        block = stack.enter_context(nc.Block())

        def mk(name):
            def body(eng):
                for o in self.ops[name]:
                    if o[0] == "wait":
                        eng.wait_ge(sems[o[1]], o[2])
                    else:
                        ins = o[1](eng)
                        ins.then_inc(sems[o[2]], o[3])
                        if Prog.DEBUG_WHERE and len(o) > 4:
                            print("INS", name, getattr(getattr(ins, "ins", None), "name", None), o[4])
            return body

        block.tensor(mk("pe"))
        block.scalar(mk("act"))
        block.vector(mk("dve"))
        block.gpsimd(mk("pool"))
        block.sync(mk("sp"))


class Arena:
    def __init__(self, big, nwords):
        self.big = big
        self.n = nwords
        self.off = 0

    def mark(self):
        return self.off

    def release(self, m):
        self.off = m

    def f32(self, name, cols):
        a = self.off
        self.off += cols
        assert self.off <= self.n, ("SBUF arena overflow", name, self.off, self.n)
        return self.big[:, a:a + cols], Buf(name)

    def bf16(self, name, cols):
        w = (cols + 1) // 2
        a = self.off
        self.off += w
        assert self.off <= self.n, ("SBUF arena overflow", name, self.off, self.n)
        return self.big[:, a:a + w].bitcast(BF16)[:, 0:cols], Buf(name)


def make_consts():
    c = {}
    c["ident"] = np.eye(128, dtype=np.float32)
    c["ones"] = np.ones((128, 128), np.float32)
    bo = np.zeros((128, 128), np.float32)
    bo[:64, :64] = 1.0
    bo[64:, 64:] = 1.0
    c["blockones"] = bo
    rr = np.zeros((128, 128), np.float32)
    for m in range(128):
        if (m % 64) < 32:
            rr[m + 32, m] = -1.0
        else:
            rr[m - 32, m] = 1.0
    c["rrot"] = rr
    tri = (np.arange(128)[:, None] <= np.arange(128)[None, :]).astype(np.float32)
    c["tri"] = tri
    c["causb"] = np.where(np.arange(128)[None, :] <= np.arange(128)[:, None], 0.0, -1e30).astype(np.float32)
    invf = (1.0 / (10000.0 ** (np.arange(0, 64, 2, dtype=np.float32) / 64.0))).astype(np.float32)
    c["invf"] = np.tile(invf, 4)[:, None].astype(np.float32) * np.ones((1, 128), np.float32)
    c["iotaf"] = np.tile(np.arange(512, dtype=np.float32)[None, :], (128, 1))
    c["pidx"] = np.tile(np.arange(128, dtype=np.float32)[:, None], (1, 128))
    sw = np.zeros((128, 128), np.float32)
    for m in range(128):
        sw[(m + 64) % 128, m] = 1.0
    c["swap"] = sw
    names = ["ident", "ones", "blockones", "rrot", "tri", "causb", "invf", "iotaf", "pidx", "swap"]
    return [(n, c[n].shape[1]) for n in names], np.concatenate([c[n] for n in names], axis=1)


CONST_NAMES, CONST_ARR = make_consts()


class Cfg:
    def __init__(self, seq=8192, depth=4, debug=(), stage=99, feed=()):
        self.stage = stage
        self.feed = set(feed)
        self.S = seq
        self.T = seq // GSZ
        self.depth = depth
        self.debug = set(debug)
        self.NT = self.T // 128
        self.NB = self.T // 512
        self.NCH = self.T // 256
        self.nkeep = min(TOPK, seq // 4)


def build_program(cfg):
    T, NT, NB, depth = cfg.T, cfg.NT, cfg.NB, cfg.depth
    nc = bass.Bass("TRN2", target_bir_lowering=False)
    stack = ExitStack()
    P = Prog(nc)
    dram = {}
    dbuf = {}

    dbg_copies = []

    def dten(name, shape, dtype, kind="Internal"):
        if name in cfg.debug:
            if "_g" in name[-4:]:
                dcp = nc.dram_tensor(name + "_dbg", list(shape), dtype, kind="ExternalOutput").ap()
                dram[name + "_dbg"] = dcp
                dbuf[name + "_dbg"] = Buf(name + "_dbg")
                dbg_copies.append(name)
            else:
                kind = "ExternalOutput"
        t = nc.dram_tensor(name, list(shape), dtype, kind=kind).ap()
        dram[name] = t
        dbuf[name] = Buf(name)
        return t

    x_in = dten("x", [T, D], F32, "ExternalInput")
    c_in = dten("c", [KC, 128], F32, "ExternalInput")
    pos_in = dten("positions", [1, T], I32, "ExternalInput")
    consts_in = dten("consts", [128, CONST_ARR.shape[1]], F32, "ExternalInput")
    rank_in = dten("rank", [1, 4], F32, "ExternalInput")
    W = {}
    for nm, shp in (("w_ada", [depth, D, 6 * D]), ("b_ada", [depth, 48, 128]), ("norm1_g", [depth, KC, 128]),
                    ("w_in", [depth, D, INW]), ("q_norm_g", [depth, 1, 64]), ("k_norm_g", [depth, 1, 64]),
                    ("ssm_conv_w", [depth, 4, 24, 128]), ("ssm_conv_b", [depth, 24, 128]),
                    ("dt_bias", [depth, 1, SH]), ("a_log", [depth, 1, SH]), ("d_skip", [depth, 1, SH]),
                    ("ssm_norm_g", [depth, 16, 128]), ("w_attn_o", [depth, D, D]), ("w_ssm_o", [depth, DI, D]),
                    ("w_out", [depth, D, D]), ("norm2_g", [depth, KC, 128]), ("w_up", [depth, D, 2 * DFF]),
                    ("ffn_conv_w", [depth, 3, 44, 128]), ("ffn_conv_b", [depth, 44, 128]),
                    ("w_down", [depth, DFF, D])):
        W[nm] = dten(nm, shp, F32, "ExternalInput")
    y_out = dten("y", [T, D], F32, "ExternalOutput")

    NWORDS = 52224
    big = stack.enter_context(nc.sbuf_tensor("arena", [128, NWORDS], F32))
    A = Arena(big, NWORDS)
    psum = []
    for i in range(8):
        pt = stack.enter_context(nc.psum_tensor("ps%d" % i, [128, 512], F32))
        psum.append((pt[:, :], Buf("ps%d" % i)))
    ps_rr = [0]

    def next_ps():
        i = ps_rr[0]
        ps_rr[0] = (i + 1) % 8
        return psum[i]

    cst, cst_b = A.f32("consts", CONST_ARR.shape[1])
    cv = {}
    o = 0
    for n, wd in CONST_NAMES:
        cv[n] = cst[:, o:o + wd]
        o += wd
    ident_bf, ident_bf_b = A.bf16("ident_bf", 128)

    class RPool:
        def __init__(self, name, n, cols, kind):
            self.t = [(A.f32 if kind == "f32" else A.bf16)("%s%d" % (name, i), cols) for i in range(n)]
            self.i = 0

        def get(self):
            r = self.t[self.i]
            self.i = (self.i + 1) % len(self.t)
            return r


    lp, lp_b = A.f32("lp", 512)
    dtb_bc, dtb_b = A.f32("dtb_bc", SH)
    alog_bc, alog_b = A.f32("alog_bc", SH)
    iw_tm, iw_b = A.f32("iw_tm", NT * 8)
    dt_tm, dt_b = A.f32("dt_tm", NT * SH)
    a_tm, a_b = A.f32("a_tm", NT * SH)
    halfsel, halfsel_b = A.f32("halfsel", 128)
    lfm_pool = RPool("lfm", 3, 128, "f32")
    WSTG = 2048
    wst_pool = RPool("wst", 2, WSTG, "f32")
    wbf_pool = RPool("wbf", 2, WSTG, "bf16")
    stg_pool = RPool("stg", 8, 512, "f32")
    sbf_pool = RPool("sbf", 4, 512, "bf16")
    epsT, epsT_b = A.f32("epsT", 2)
    oneT, oneT_b = A.f32("oneT", 2)
    rk, rk_b = A.f32("rk", 8)
    prevsel, prevsel_b = A.f32("prevsel", 4)
    class NS:
        pass
    ns = NS()
    m_pers = A.mark()
    xT, xT_b = A.f32("xT", KC * T)
    xT3 = xT.rearrange("p (k t) -> p k t", t=T)
    m_x = A.mark()

    P.dma("sp", cst, consts_in, reads=[dbuf["consts"]], writes=[cst_b])
    P.op("dve", lambda e: e.tensor_copy(out=ident_bf, in_=cv["ident"]), reads=[cst_b], writes=[ident_bf_b])

    def transpose_f32(dst, dst_b, src, src_b, rows, cols, evac="act"):
        pt, pb = next_ps()
        P.op("pe", lambda e: e.transpose(out=pt[0:cols, 0:rows], in_=src, identity=cv["ident"][0:rows, 0:rows]),
             reads=[src_b, cst_b], writes=[pb])
        if evac == "act":
            P.op("act", lambda e: e.copy(out=dst, in_=pt[0:cols, 0:rows]), reads=[pb], writes=[dst_b])
        else:
            P.op("dve", lambda e: e.tensor_copy(out=dst, in_=pt[0:cols, 0:rows]), reads=[pb], writes=[dst_b])

    def load_fm(dst, dst_b, src_ap, src_name, nrow):
        m = A.mark()
        tmp, tmp_b = A.f32("lfm_tmp", 128)
        P.dma("sp", tmp[0:nrow, :], src_ap, reads=[dbuf[src_name]], writes=[tmp_b])
        transpose_f32(dst, dst_b, tmp[0:nrow, :], tmp_b, nrow, 128)
        A.release(m)
        return tmp_b

    m0 = A.mark()
    xtoks = [A.f32("xtok%d" % i, D) for i in range(2)]
    for i in range(NT):
        xt, xt_b = xtoks[i % 2]
        P.dma("sp", xt, x_in[i * 128:(i + 1) * 128, :], reads=[dbuf["x"]], writes=[xt_b])
        for k in range(KC):
            pt, pb = next_ps()
            P.op("pe", lambda e, pt=pt, xt=xt, k=k: e.transpose(out=pt[:, 0:128], in_=xt[:, k * 128:(k + 1) * 128],
                                                               identity=cv["ident"]),
                 reads=[xt_b, cst_b], writes=[pb])
            eng = "act" if k % 2 == 0 else "dve"
            if eng == "act":
                P.op("act", lambda e, pt=pt, k=k, i=i: e.copy(out=xT3[:, k, i * 128:(i + 1) * 128], in_=pt[:, 0:128]),
                     reads=[pb], writes=[xT_b])
            else:
                P.op("dve", lambda e, pt=pt, k=k, i=i: e.tensor_copy(out=xT3[:, k, i * 128:(i + 1) * 128], in_=pt[:, 0:128]),
                     reads=[pb], writes=[xT_b])
    P.barrier()
    A.release(m0)


    def ACT(out, in_, func, reads, writes, **kw):
        P.op("act", lambda e: e.activation(out=out, in_=in_, func=func, **kw), reads, writes)

    def TS(eng, out, in0, s1, op0, reads, writes, s2=None, op1=None, **kw):
        if op1 is None:
            P.op(eng, lambda e: e.tensor_scalar(out=out, in0=in0, scalar1=s1, scalar2=None, op0=op0, **kw), reads, writes)
        else:
            P.op(eng, lambda e: e.tensor_scalar(out=out, in0=in0, scalar1=s1, scalar2=s2, op0=op0, op1=op1, **kw),
                 reads, writes)

    def TT(eng, out, in0, in1, op, reads, writes):
        P.op(eng, lambda e: e.tensor_tensor(out=out, in0=in0, in1=in1, op=op), reads, writes)

    def STT(out, in0, scalar, in1, op0, op1, reads, writes):
        P.op("dve", lambda e: e.scalar_tensor_tensor(out=out, in0=in0, scalar=scalar, in1=in1, op0=op0, op1=op1),
             reads, writes)

    def MM(out, lhsT, rhs, start, stop, reads, writes):
        P.op("pe", lambda e: e.matmul(out, lhsT, rhs, start=start, stop=stop), reads, writes)

    def CP(eng, out, in_, reads, writes):
        if eng == "act":
            P.op("act", lambda e: e.copy(out=out, in_=in_), reads, writes)
        else:
            P.op(eng, lambda e: e.tensor_copy(out=out, in_=in_), reads, writes)

    def bcast_ap(ap2d_row, nparts, ncols, offset_elems=0):
        return bass.AP(ap2d_row.tensor, ap2d_row.offset + offset_elems, [[0, nparts], [1, ncols]])

    qT_d = dten("qT_d", [8 * 128, T], BF16)
    groups = [[0, 1, 2, 3], [4, 5, 6, 7]]

    class GatherSet:
        def __init__(self, name, rows, cols, dtype, esz):
            rpc = rows
            while rpc * cols * esz > (1 << 20):
                rpc //= 2
            assert rows % rpc == 0
            self.name, self.rows, self.cols, self.rpc, self.n = name, rows, cols, rpc, rows // rpc
            self.loc = [dten("%s_loc%d" % (name, c), [rpc, cols], dtype) for c in range(self.n)]
            self.g = [dten("%s_g%d" % (name, c), [GSZ * rpc, cols], dtype) for c in range(self.n)]

        def loc_rows(self, r0, r1):
            c = r0 // self.rpc
            assert (r1 - 1) // self.rpc == c
            return self.loc[c][r0 - c * self.rpc:r1 - c * self.rpc, :], "%s_loc%d" % (self.name, c)

        def g_rows(self, rank, r0, r1):
            c = r0 // self.rpc
            assert (r1 - 1) // self.rpc == c
            base = rank * self.rpc - c * self.rpc
            return self.g[c][base + r0:base + r1, :], "%s_g%d" % (self.name, c)

        def gather(self):
            for c in range(self.n):
                ln, gn = "%s_loc%d" % (self.name, c), "%s_g%d" % (self.name, c)
                P.collective(lambda e, ln=ln, gn=gn: e.collective_compute("AllGather", ALU.bypass, replica_groups=groups,
                                                                          ins=[dram[ln]], outs=[dram[gn]]),
                             reads=[dbuf[ln]], writes=[dbuf[gn]])

    kT_gs = GatherSet("kT", 8 * 128, T, BF16, 2)
    v_gs = GatherSet("v", T, 2048, BF16, 2)
    iqT_d = dten("iqT_d", [4 * 128, T], BF16)
    ik_gs = GatherSet("ik", 128, T, BF16, 2)
    zT_d = dten("zT_d", [16 * 128, T], BF16)
    xbc_raw = dten("xbc_raw", [24 * 128, T], F32)
    halo_gs = GatherSet("halo", 128, 24 * 4, F32, 4)
    gT_d = dten("gT_d", [16 * 128, T], F32)
    dbg_small = dten("dbg_small", [128, NT * 80], F32)

    rope_d = dten("rope_d", [2 * 128, T], F32)
    m0 = A.mark()
    C4, C4_b = A.f32("C4", T)
    S4, S4_b = A.f32("S4", T)
    posi, posi_b = A.f32("posi", T)
    posi_i = posi.bitcast(I32)
    ang, ang_b = A.f32("ang", T)
    t1, t1_b = A.f32("rt1", T)
    t2, t2_b = A.f32("rt2", T)
    t2_i = t2.bitcast(I32)
    P.dma("sp", posi_i, bcast_ap(pos_in, 128, T), reads=[dbuf["positions"]], writes=[posi_b])
    CP("dve", ang, posi_i, [posi_b], [ang_b])
    TS("dve", ang, ang, cv["invf"][:, 0:1], ALU.mult, [ang_b, cst_b], [ang_b])
    TWO_PI = 2.0 * np.pi
    C1 = 6.28125
    C2 = TWO_PI - C1
    for dst, dst_b, shift in ((S4, S4_b, 0.0), (C4, C4_b, np.pi / 2.0)):
        TS("dve", t1, ang, shift, ALU.add, [ang_b], [t1_b], s2=1.0 / TWO_PI, op1=ALU.mult)
        CP("dve", t2_i, t1, [t1_b], [t2_b])
        CP("dve", t1, t2_i, [t2_b], [t1_b])
        TS("dve", t2, ang, shift, ALU.add, [ang_b], [t2_b])
        STT(t2, t1, -C1, t2, ALU.mult, ALU.add, [t1_b, t2_b], [t2_b])
        STT(t2, t1, -C2, t2, ALU.mult, ALU.add, [t1_b, t2_b], [t2_b])
        TS("dve", t1, t2, float(np.pi), ALU.is_gt, [t2_b], [t1_b])
        STT(t2, t1, -TWO_PI, t2, ALU.mult, ALU.add, [t1_b, t2_b], [t2_b])
        TS("dve", t1, t2, float(-np.pi), ALU.is_lt, [t2_b], [t1_b])
        STT(t2, t1, TWO_PI, t2, ALU.mult, ALU.add, [t1_b, t2_b], [t2_b])
        TS("dve", t2, t2, float(np.pi), ALU.min, [t2_b], [t2_b], s2=float(-np.pi), op1=ALU.max)
        ACT(dst, t2, AF.Sin, [t2_b], [dst_b])
    P.dma("sp", rope_d[0:128, :], C4, reads=[C4_b], writes=[dbuf["rope_d"]])
    P.dma("sp", rope_d[128:256, :], S4, reads=[S4_b], writes=[dbuf["rope_d"]])
    P.barrier()
    A.release(m0)

    o_ = [0]

    def lp_alloc(n):
        a = o_[0]
        o_[0] += n
        assert o_[0] <= 512
        return lp[:, a:a + n]

    b_adaT = lp_alloc(48)
    modT = lp_alloc(48)
    n1gT = lp_alloc(8)
    n2gT = lp_alloc(8)
    scwT = lp_alloc(96)
    scbT = lp_alloc(24)
    sngT = lp_alloc(16)
    fcwT = lp_alloc(132)
    fcbT = lp_alloc(44)
    qg2 = lp_alloc(1)
    kg2 = lp_alloc(1)
    A1 = lp_alloc(8)
    A2 = lp_alloc(8)
    cact2 = lp_alloc(16)
    dskT = lp_alloc(16)
    cact2_3 = cact2.rearrange("p (k two) -> p k two", two=2)
    iw3 = iw_tm.rearrange("p (i h) -> p i h", h=8)
    dt3 = dt_tm.rearrange("p (i h) -> p i h", h=SH)
    a3 = a_tm.rearrange("p (i h) -> p i h", h=SH)

    m0 = A.mark()
    tmp, tmp_b = A.f32("ctmp", 128)
    P.dma("sp", tmp[0:KC, :], c_in, reads=[dbuf["c"]], writes=[tmp_b])
    pt, pb = next_ps()
    P.op("pe", lambda e, pt=pt: e.transpose(out=pt[:, 0:KC], in_=tmp[0:KC, :], identity=cv["ident"][0:KC, 0:KC]),
         reads=[tmp_b, cst_b], writes=[pb])
    ACT(cact2_3[:, :, 0], pt[:, 0:KC], AF.Silu, [pb], [lp_b])
    ACT(cact2_3[:, :, 1], pt[:, 0:KC], AF.Silu, [pb], [lp_b])
    P.barrier()
    A.release(m0)

    def load_fm_rows(dst, src_ap, src_name, nrow):
        tmp_, tmpb_ = lfm_pool.get()
        P.dma("sp", tmp_[0:nrow, :], src_ap, reads=[dbuf[src_name]], writes=[tmpb_])
        pt_, pb_ = next_ps()
        P.op("pe", lambda e: e.transpose(out=pt_[:, 0:nrow], in_=tmp_[0:nrow, :], identity=cv["ident"][0:nrow, 0:nrow]),
             reads=[tmpb_, cst_b], writes=[pb_])
        CP("dve", dst, pt_[:, 0:nrow], [pb_], [lp_b])

    def layer_params(l):
        load_fm_rows(b_adaT, W["b_ada"][l], "b_ada", 48)
        load_fm_rows(n1gT, W["norm1_g"][l], "norm1_g", KC)
        load_fm_rows(n2gT, W["norm2_g"][l], "norm2_g", KC)
        for tap in range(4):
            load_fm_rows(scwT[:, tap * 24:(tap + 1) * 24], W["ssm_conv_w"][l, tap], "ssm_conv_w", 24)
        load_fm_rows(scbT, W["ssm_conv_b"][l], "ssm_conv_b", 24)
        load_fm_rows(sngT, W["ssm_norm_g"][l], "ssm_norm_g", 16)
        for tap in range(3):
            load_fm_rows(fcwT[:, tap * 44:(tap + 1) * 44], W["ffn_conv_w"][l, tap], "ffn_conv_w", 44)
        load_fm_rows(fcbT, W["ffn_conv_b"][l], "ffn_conv_b", 44)
        for dst, nm in ((qg2, "q_norm_g"), (kg2, "k_norm_g")):
            tmp_, tmpb_ = lfm_pool.get()
            P.dma("sp", tmp_[0:1, 0:64], W[nm][l], reads=[dbuf[nm]], writes=[tmpb_])
            P.dma("sp", tmp_[0:1, 64:128], W[nm][l], reads=[dbuf[nm]], writes=[tmpb_])
            pt_, pb_ = next_ps()
            P.op("pe", lambda e, pt_=pt_, tmp_=tmp_: e.transpose(out=pt_[:, 0:1], in_=tmp_[0:1, :],
                                                                 identity=cv["ident"][0:1, 0:1]),
                 reads=[tmpb_, cst_b], writes=[pb_])
            CP("dve", dst, pt_[:, 0:1], [pb_], [lp_b])
        tmp_, tmpb_ = lfm_pool.get()
        P.dma("sp", tmp_[0:16, 0:2], W["d_skip"][l].rearrange("o (c two) -> (o c) two", two=2),
              reads=[dbuf["d_skip"]], writes=[tmpb_])
        pt_, pb_ = next_ps()
        P.op("pe", lambda e: e.transpose(out=pt_[0:2, 0:16], in_=tmp_[0:16, 0:2], identity=cv["ident"][0:16, 0:16]),
             reads=[tmpb_, cst_b], writes=[pb_])
        tmp2_, tmp2b_ = lfm_pool.get()
        CP("dve", tmp2_[0:2, 0:16], pt_[0:2, 0:16], [pb_], [tmp2b_])
        pt2_, pb2_ = next_ps()
        MM(pt2_[:, 0:16], halfsel[0:2, :], tmp2_[0:2, 0:16], True, True, [tmp2b_, halfsel_b], [pb2_])
        CP("dve", dskT, pt2_[:, 0:16], [pb2_], [lp_b])
        P.dma("sp", dtb_bc, bcast_ap(W["dt_bias"][l], 128, SH), reads=[dbuf["dt_bias"]], writes=[dtb_b])
        P.dma("sp", alog_bc, bcast_ap(W["a_log"][l], 128, SH), reads=[dbuf["a_log"]], writes=[alog_b])
        ACT(alog_bc, alog_bc, AF.Exp, [alog_b], [alog_b])

    P.dma("sp", halfsel[0:1, :], consts_in[0:1, 256:384], reads=[dbuf["consts"]], writes=[halfsel_b])
    P.dma("sp", halfsel[1:2, :], consts_in[64:65, 256:384], reads=[dbuf["consts"]], writes=[halfsel_b])


    cast_rr = [0]

    def load_w(wname, l, col0, ncols, kc, row0=0, to_bf16=True, dst=None):
        assert kc * ncols <= WSTG
        st, st_b = wst_pool.get()
        st3 = st[:, 0:kc * ncols].rearrange("p (k c) -> p k c", c=ncols)
        src = W[wname][l][row0:row0 + kc * 128, col0:col0 + ncols].rearrange("(k p) c -> p k c", p=128)
        P.dma("sp", st3, src, reads=[dbuf[wname]], writes=[st_b])
        if not to_bf16:
            return st3, st_b
        if dst is not None:
            CP("pool", dst[0], st3, [st_b], [dst[1]])
            return dst
        wb, wb_b = wbf_pool.get()
        wb3 = wb[:, 0:kc * ncols].rearrange("p (k c) -> p k c", c=ncols)
        CP("pool", wb[:, 0:kc * ncols], st[:, 0:kc * ncols], [st_b], [wb_b])
        return wb3, wb_b

    def load_w_resident(name, wname, l, kc, ncols):
        wr, wr_b = A.bf16(name, kc * ncols)
        wr3 = wr.rearrange("p (k c) -> p k c", c=ncols)
        kstep = max(1, WSTG // ncols) if ncols <= WSTG else 1
        cstep = min(ncols, WSTG)
        for k0 in range(0, kc, kstep):
            kk = min(kstep, kc - k0)
            for c0 in range(0, ncols, cstep):
                cc = min(cstep, ncols - c0)
                load_w(wname, l, c0, cc, kk, row0=k0 * 128, dst=(wr3[:, k0:k0 + kk, c0:c0 + cc], wr_b))
        return wr3, wr_b


    def store(dst_ap, dname, src_ap, src_b):
        P.dma("pool", dst_ap, src_ap, reads=[src_b], writes=[dbuf[dname]])

    def compute_mod(l):
        for cb in range(24):
            w3, w_b = load_w("w_ada", l, cb * 256, 256, KC, to_bf16=False)
            pt, pb = next_ps()
            for m in range(2):
                for k in range(KC):
                    MM(pt[:, 2 * m:2 * m + 2], w3[:, k, m * 128:(m + 1) * 128], cact2_3[:, k, :], k == 0, k == KC - 1,
                       [w_b, lp_b], [pb])
            ptv = pt[:, 0:4].rearrange("p (m two) -> p m two", two=2)[:, :, 0]
            TT("dve", modT[:, cb * 2:cb * 2 + 2], ptv, b_adaT[:, cb * 2:cb * 2 + 2], ALU.add, [pb, lp_b], [lp_b])
        STT(A1, modT[:, 8:16], 1.0, n1gT, ALU.add, ALU.mult, [lp_b], [lp_b])
        STT(A2, modT[:, 32:40], 1.0, n2gT, ALU.add, ALU.mult, [lp_b], [lp_b])

    def norm_mod(Avec, Bvec):
        for tb in range(NB):
            sl = slice(tb * 512, (tb + 1) * 512)
            pt, pb = next_ps()
            for k in range(KC):
                sq, sq_b = stg_pool.get()
                ACT(sq, xT3[:, k, sl], AF.Square, [xT_b], [sq_b])
                MM(pt, cv["ones"], sq, k == 0, k == KC - 1, [sq_b, cst_b], [pb])
            rs, rs_b = stg_pool.get()
            ACT(rs, pt, AF.Sqrt, [pb], [rs_b], bias=epsT[:, 0:1], scale=1.0 / D)
            P.op("dve", lambda e, rs=rs: e.reciprocal(out=rs, in_=rs), [rs_b], [rs_b])
            for k in range(KC):
                tq, tq_b = stg_pool.get()
                TT("dve", tq, xT3[:, k, sl], rs, ALU.mult, [xT_b, rs_b], [tq_b])
                TS("dve", ns.hT3[:, k, sl], tq, Avec[:, k:k + 1], ALU.mult, [tq_b, lp_b], [ns.hT_b],
                   s2=Bvec[:, k:k + 1], op1=ALU.add)

    P.op("dve", lambda e: e.memset(epsT, EPS), [], [epsT_b])

    def proj_fm(wname, l, col0, ncols, rhs3, rhs_b, kc, epilogue, sub0=0, row0=0, blk=512):
        per = max(128, (WSTG // kc) // 128 * 128)
        per = min(per, blk)
        c = 0
        while c < ncols:
            n = min(per, ncols - c)
            w3, w_b = load_w(wname, l, col0 + c, n, kc, row0=row0)
            for m in range(n // 128):
                for tb in range(NB):
                    pt, pb = next_ps()
                    for k in range(kc):
                        MM(pt, w3[:, k, m * 128:(m + 1) * 128], rhs3[:, k, tb * 512:(tb + 1) * 512], k == 0, k == kc - 1,
                           [w_b, rhs_b], [pb])
                    epilogue(pt, pb, sub0 + (c // 128) + m, tb)
            c += n

    def dst_rows(dname, r0, r1):
        if isinstance(dname, GatherSet):
            return dname.loc_rows(r0, r1)
        return dram[dname][r0:r1, :], dname

    def rope_epilogue(src_sb, src_b, tb, dst_dram, dname, row):
        sl = slice(tb * 512, (tb + 1) * 512)
        pr, prb = next_ps()
        MM(pr, cv["rrot"], src_sb, True, True, [src_b, cst_b], [prb])
        u1, u1_b = stg_pool.get()
        TT("pool", u1, src_sb, ns.C4[:, sl], ALU.mult, [src_b, ns.C4_b], [u1_b])
        u2, u2_b = stg_pool.get()
        TT("dve", u2, pr, ns.S4[:, sl], ALU.mult, [prb, ns.S4_b], [u2_b])
        ob, ob_b = sbf_pool.get()
        TT("dve", ob, u1, u2, ALU.add, [u1_b, u2_b], [ob_b])
        dap, dn = dst_rows(dname, row * 128, (row + 1) * 128)
        store(dap[:, sl], dn, ob, ob_b)

    def qk_epilogue(gvec, dname):
        def ep(pt, pb, sub, tb):
            sq, sq_b = stg_pool.get()
            ACT(sq, pt, AF.Square, [pb], [sq_b])
            p2, p2b = next_ps()
            MM(p2, cv["blockones"], sq, True, True, [sq_b, cst_b], [p2b])
            rs, rs_b = stg_pool.get()
            ACT(rs, p2, AF.Sqrt, [p2b], [rs_b], bias=epsT[:, 0:1], scale=1.0 / HD)
            P.op("dve", lambda e, rs=rs: e.reciprocal(out=rs, in_=rs), [rs_b], [rs_b])
            qn, qn_b = stg_pool.get()
            STT(qn, pt, gvec, rs, ALU.mult, ALU.mult, [pb, lp_b, rs_b], [qn_b])
            rope_epilogue(qn, qn_b, tb, None, dname, sub)
        return ep

    def iq_epilogue(dname, nsub_real):
        def ep(pt, pb, sub, tb):
            qn, qn_b = stg_pool.get()
            CP("act", qn, pt, [pb], [qn_b])
            rope_epilogue(qn, qn_b, tb, None, dname, sub)
        return ep

    def z_epilogue(pt, pb, sub, tb):
        ob, ob_b = sbf_pool.get()
        ACT(ob, pt, AF.Silu, [pb], [ob_b])
        store(zT_d[sub * 128:(sub + 1) * 128, tb * 512:(tb + 1) * 512], "zT_d", ob, ob_b)

    def xbc_epilogue(pt, pb, sub, tb):
        o32, o32_b = stg_pool.get()
        CP("act" if (sub + tb) % 2 == 0 else "dve", o32, pt, [pb], [o32_b])
        store(xbc_raw[sub * 128:(sub + 1) * 128, tb * 512:(tb + 1) * 512], "xbc_raw", o32, o32_b)
        if tb == NB - 1:
            dap, dn = halo_gs.loc_rows(0, 128)
            store(dap[:, sub * 4:sub * 4 + 3], dn, o32[:, 509:512], o32_b)

    def gate_epilogue(pt, pb, sub, tb):
        o32, o32_b = stg_pool.get()
        ACT(o32, pt, AF.Sigmoid, [pb], [o32_b])
        store(gT_d[sub * 128:(sub + 1) * 128, tb * 512:(tb + 1) * 512], "gT_d", o32, o32_b)


    def v_projection(l):
        for qd in range(4):
            w3, w_b = load_w("w_in", l, C_V + qd * 256, 256, KC)
            for i in range(NT):
                va, va_b = ns.vaug[i % 2]
                va4 = va.rearrange("p (pr two c) -> p pr two c", two=2, c=128)
                pt, pb = next_ps()
                for k in range(KC):
                    MM(pt[:, 0:256], ns.hT3[:, k, i * 128:(i + 1) * 128], w3[:, k, :], k == 0, k == KC - 1, [ns.hT_b, w_b], [pb])
                pt4 = pt[:, 0:256].rearrange("p (pr two c) -> p pr two c", two=2, c=64)
                ev_ = "act" if i % 2 == 0 else "dve"
                CP(ev_, va4[:, :, 0, 0:64], pt4[:, :, 0, :], [pb], [va_b])
                CP(ev_, va4[:, :, 1, 64:128], pt4[:, :, 1, :], [pb], [va_b])
                dap, dn = v_gs.loc_rows(i * 128, (i + 1) * 128)
                store(dap[:, qd * 512:(qd + 1) * 512], dn, va, va_b)

    def small_projection(l):
        st, st_b = wst_pool.get()
        st3 = st[:, 0:KC * 40].rearrange("p (k c) -> p k c", c=40)
        P.dma("sp", st3[:, :, 0:32], W["w_in"][l][:, C_DT:C_DT + 32].rearrange("(k p) c -> p k c", p=128),
              reads=[dbuf["w_in"]], writes=[st_b])
        P.dma("sp", st3[:, :, 32:40], W["w_in"][l][:, C_IW:C_IW + 8].rearrange("(k p) c -> p k c", p=128),
              reads=[dbuf["w_in"]], writes=[st_b])
        wb, wb_b = wbf_pool.get()
        wb3 = wb[:, 0:KC * 40].rearrange("p (k c) -> p k c", c=40)
        CP("pool", wb[:, 0:KC * 40], st[:, 0:KC * 40], [st_b], [wb_b])
        for i in range(NT):
            pt, pb = next_ps()
            for k in range(KC):
                MM(pt[:, 0:40], ns.hT3[:, k, i * 128:(i + 1) * 128], wb3[:, k, :], k == 0, k == KC - 1, [ns.hT_b, wb_b], [pb])
            xx, xx_b = stg_pool.get()
            CP("dve", xx[:, 64:104], pt[:, 0:40], [pb], [xx_b])
            CP("dve", iw3[:, i, :], xx[:, 96:104], [xx_b], [iw_b])
            TT("dve", xx[:, 0:32], xx[:, 64:96], dtb_bc, ALU.add, [xx_b, dtb_b], [xx_b])
            STT(xx[:, 32:64], xx[:, 0:32], -1.0, xx[:, 0:32], ALU.mult, ALU.max, [xx_b], [xx_b])
            ACT(xx[:, 32:64], xx[:, 32:64], AF.Exp, [xx_b], [xx_b], scale=-1.0)
            ACT(xx[:, 32:64], xx[:, 32:64], AF.Ln, [xx_b], [xx_b], bias=oneT[:, 0:1], scale=1.0)
            STT(dt3[:, i, :], xx[:, 0:32], 0.0, xx[:, 32:64], ALU.max, ALU.add, [xx_b], [dt_b])
            STT(a3[:, i, :], dt3[:, i, :], -1.0, alog_bc, ALU.mult, ALU.mult, [dt_b, alog_b], [a_b])

    P.op("dve", lambda e: e.memset(oneT, 1.0), [], [oneT_b])

    def alloc_hT():
        hT, ns.hT_b = A.bf16("hT", KC * T)
        ns.hT3 = hT.rearrange("p (k t) -> p k t", t=T)

    def phase_A(l):
        if cfg.stage < 1:
            return
        alloc_hT()
        ns.C4, ns.C4_b = A.f32("C4", T)
        ns.S4, ns.S4_b = A.f32("S4", T)
        P.dma("sp", ns.C4, rope_d[0:128, :], reads=[dbuf["rope_d"]], writes=[ns.C4_b])
        P.dma("sp", ns.S4, rope_d[128:256, :], reads=[dbuf["rope_d"]], writes=[ns.S4_b])
        ns.vaug = [A.bf16("vaug%d" % i, 512) for i in range(2)]
        for va, va_b in ns.vaug:
            P.op("pool", lambda e, va=va: e.memset(va, 1.0), [], [va_b])
        layer_params(l)
        if cfg.stage < 2:
            return
        compute_mod(l)
        if cfg.stage < 3:
            return
        norm_mod(A1, modT[:, 0:8])
        if cfg.stage < 4:
            return
        proj_fm("w_in", l, C_Q, 1024, ns.hT3, ns.hT_b, KC, qk_epilogue(qg2[:, 0:1], "qT_d"))
        if cfg.stage < 5:
            return
        proj_fm("w_in", l, C_K, 1024, ns.hT3, ns.hT_b, KC, qk_epilogue(kg2[:, 0:1], kT_gs))
        v_projection(l)
        proj_fm("w_in", l, C_IQ, 512, ns.hT3, ns.hT_b, KC, iq_epilogue("iqT_d", 4))
        st, st_b = wst_pool.get()
        st3 = st[:, 0:KC * 128].rearrange("p (k c) -> p k c", c=128)
        for hf in range(2):
            P.dma("sp", st3[:, :, hf * 64:(hf + 1) * 64],
                  W["w_in"][l][:, C_IK:C_IK + 64].rearrange("(k p) c -> p k c", p=128),
                  reads=[dbuf["w_in"]], writes=[st_b])
        wb, wb_b = wbf_pool.get()
        wb3 = wb[:, 0:KC * 128].rearrange("p (k c) -> p k c", c=128)
        CP("pool", wb[:, 0:KC * 128], st[:, 0:KC * 128], [st_b], [wb_b])
        ikep = iq_epilogue(ik_gs, 1)
        for tb in range(NB):
            pt, pb = next_ps()
            for k in range(KC):
                MM(pt, wb3[:, k, :], ns.hT3[:, k, tb * 512:(tb + 1) * 512], k == 0, k == KC - 1, [wb_b, ns.hT_b], [pb])
            ikep(pt, pb, 0, tb)
        if cfg.stage < 6:
            return
        small_projection(l)
        if cfg.stage < 7:
            return
        proj_fm("w_in", l, C_Z, 2048, ns.hT3, ns.hT_b, KC, z_epilogue)
        proj_fm("w_in", l, C_XBC, 3072, ns.hT3, ns.hT_b, KC, xbc_epilogue)
        proj_fm("w_in", l, C_GA, 2048, ns.hT3, ns.hT_b, KC, gate_epilogue)
        if "dbg_small" in cfg.debug:
            dbg3 = dbg_small.rearrange("p (i c) -> p i c", c=80)
            store(dbg3[:, :, 0:8], "dbg_small", iw3, iw_b)
            store(dbg3[:, :, 8:40], "dbg_small", dt3, dt_b)
            store(dbg3[:, :, 40:72], "dbg_small", a3, a_b)
        if cfg.stage < 8:
            return
        kT_gs.gather()
        v_gs.gather()
        ik_gs.gather()
        halo_gs.gather()


    attnT_d = dten("attnT_d", [8 * 128, T], BF16, "ExternalInput" if "attnT_d" in cfg.feed else "Internal")
    ynT_d = dten("ynT_d", [16 * 128, T], BF16, "ExternalInput" if "ynT_d" in cfg.feed else "Internal")
    aT_d = dten("aT_d", [22 * 128, T], BF16)
    uhalo_gs = GatherSet("uhalo", 128, 44 * 2, F32, 4)

    mixT_d = dten("mixT_d", [8 * 128, T], BF16)

    def phase_D(l):
        m0_ = A.mark()
        wao3, wao_b = load_w_resident("wao", "w_attn_o", l, 8, D)
        wso3, wso_b = load_w_resident("wso", "w_ssm_o", l, 16, D)
        at, at_b = A.bf16("attn_blk", 8 * 512)
        at3 = at.rearrange("p (k t) -> p k t", t=512)
        yn, yn_b = A.bf16("yn_blk", 16 * 512)
        yn3 = yn.rearrange("p (k t) -> p k t", t=512)
        gpool = RPool("gate", 4, 512, "f32")
        for tb in range(NB):
            sl = slice(tb * 512, (tb + 1) * 512)
            P.dma("sp", at3, attnT_d.rearrange("(k p) t -> p k t", p=128)[:, :, sl], reads=[dbuf["attnT_d"]], writes=[at_b])
            P.dma("sp", yn3, ynT_d.rearrange("(k p) t -> p k t", p=128)[:, :, sl], reads=[dbuf["ynT_d"]], writes=[yn_b])
            for m in range(8):
                ga, ga_b = gpool.get()
                gm_, gm_b = gpool.get()
                P.dma("sp", ga, gT_d[m * 128:(m + 1) * 128, sl], reads=[dbuf["gT_d"]], writes=[ga_b])
                P.dma("sp", gm_, gT_d[(8 + m) * 128:(9 + m) * 128, sl], reads=[dbuf["gT_d"]], writes=[gm_b])
                pa, pab = next_ps()
                for k in range(8):
                    MM(pa, wao3[:, k, m * 128:(m + 1) * 128], at3[:, k, :], k == 0, k == 7, [wao_b, at_b], [pab])
                psm, psb = next_ps()
                for k in range(16):
                    MM(psm, wso3[:, k, m * 128:(m + 1) * 128], yn3[:, k, :], k == 0, k == 15, [wso_b, yn_b], [psb])
                t1_, t1b_ = stg_pool.get()
                TT("dve", t1_, pa, ga, ALU.mult, [pab, ga_b], [t1b_])
                t2_, t2b_ = stg_pool.get()
                TT("dve", t2_, psm, gm_, ALU.mult, [psb, gm_b], [t2b_])
                ob, ob_b = sbf_pool.get()
                TT("pool", ob, t1_, t2_, ALU.add, [t1b_, t2b_], [ob_b])
                store(mixT_d[m * 128:(m + 1) * 128, sl], "mixT_d", ob, ob_b)
        P.barrier()
        A.release(m0_)
        m0_ = A.mark()
        wout3, wout_b = load_w_resident("wout", "w_out", l, 8, D)
        mxs = [A.bf16("mix_blk%d" % i, 8 * 512) for i in range(2)]
        for tb in range(NB):
            sl = slice(tb * 512, (tb + 1) * 512)
            mx, mx_b = mxs[tb % 2]
            mx3 = mx.rearrange("p (k t) -> p k t", t=512)
            P.dma("sp", mx3, mixT_d.rearrange("(k p) t -> p k t", p=128)[:, :, sl], reads=[dbuf["mixT_d"]], writes=[mx_b])
            for m2 in range(8):
                po, pob = next_ps()
                for k in range(8):
                    MM(po, wout3[:, k, m2 * 128:(m2 + 1) * 128], mx3[:, k, :], k == 0, k == 7, [wout_b, mx_b], [pob])
                STT(xT3[:, m2, sl], po, modT[:, 16 + m2:17 + m2], xT3[:, m2, sl], ALU.mult, ALU.add, [pob, lp_b, xT_b], [xT_b])
        P.barrier()
        A.release(m0_)

    def phase_EF(l, rank_sel):
        m0_ = A.mark()
        alloc_hT()
        norm_mod(A2, modT[:, 24:32])
        uh, uh_b = A.f32("uhalo_sb", 44 * 2)
        uh3 = uh.rearrange("p (m two) -> p m two", two=2)
        for c0 in range(0, 2 * DFF, 256):
            w3, w_b = load_w("w_up", l, c0, 256, KC)
            pt, pb = next_ps()
            for m in range(2):
                for k in range(KC):
                    MM(pt[:, 2 * m:2 * m + 2], w3[:, k, m * 128:(m + 1) * 128], ns.hT3[:, k, T - 2:T], k == 0, k == KC - 1,
                       [w_b, ns.hT_b], [pb])
            CP("dve", uh3[:, (c0 // 128):(c0 // 128) + 2, :], pt[:, 0:4].rearrange("p (m two) -> p m two", two=2), [pb], [uh_b])
        dap, dn = uhalo_gs.loc_rows(0, 128)
        store(dap, dn, uh, uh_b)
        uhalo_gs.gather()
        pv, pv_b = A.f32("uprev", 44 * 2)
        pv3 = pv.rearrange("p (m two) -> p m two", two=2)
        load_prev_rows(uhalo_gs, pv3, pv_b, 44, 2)
        ub = [A.f32("ubuf%d" % i, 516) for i in range(4)]
        for m in range(22):
            wv3, wv_b = load_w("w_up", l, m * 128, 128, KC)
            wg3, wg_b = load_w("w_up", l, DFF + m * 128, 128, KC)
            conv_out = []
            for tb in range(NB):
                sl = slice(tb * 512, (tb + 1) * 512)
                res = []
                for which, (w3, w_b, sub) in enumerate(((wv3, wv_b, m), (wg3, wg_b, 22 + m))):
                    pt, pb = next_ps()
                    for k in range(KC):
                        MM(pt, w3[:, k, :], ns.hT3[:, k, sl], k == 0, k == KC - 1, [w_b, ns.hT_b], [pb])
                    u, u_b = ub[(tb % 2) * 2 + which]
                    if tb == 0:
                        CP("dve", u[:, 0:2], pv3[:, sub, :], [pv_b], [u_b])
                    else:
                        up, up_b = ub[((tb - 1) % 2) * 2 + which]
                        CP("dve", u[:, 0:2], up[:, 512:514], [up_b], [u_b])
                    CP("act", u[:, 2:514], pt, [pb], [u_b])
                    c_, c_b = stg_pool.get()
                    TS("dve", c_, u[:, 0:512], fcwT[:, sub:sub + 1], ALU.mult, [u_b, lp_b], [c_b],
                       s2=fcbT[:, sub:sub + 1], op1=ALU.add)
                    STT(c_, u[:, 1:513], fcwT[:, 44 + sub:45 + sub], c_, ALU.mult, ALU.add, [u_b, lp_b, c_b], [c_b])
                    STT(c_, u[:, 2:514], fcwT[:, 88 + sub:89 + sub], c_, ALU.mult, ALU.add, [u_b, lp_b, c_b], [c_b])
                    res.append((c_, c_b))
                (cv_, cvb_), (cg_, cgb_) = res
                sg, sg_b = stg_pool.get()
                ACT(sg, cg_, AF.Silu, [cgb_], [sg_b])
                ob, ob_b = sbf_pool.get()
                TT("pool", ob, sg, cv_, ALU.mult, [sg_b, cvb_], [ob_b])
                store(aT_d[m * 128:(m + 1) * 128, sl], "aT_d", ob, ob_b)
        P.barrier()
        A.release(m0_)
        m0_ = A.mark()
        wd3, wd_b = load_w_resident("wdown", "w_down", l, 22, D)
        ab = [A.bf16("a_blk%d" % i, 22 * 512) for i in range(1)]
        for tb in range(NB):
            sl = slice(tb * 512, (tb + 1) * 512)
            a_, a_b_ = ab[0]
            a3_ = a_.rearrange("p (k t) -> p k t", t=512)
            P.dma("sp", a3_, aT_d.rearrange("(k p) t -> p k t", p=128)[:, :, sl], reads=[dbuf["aT_d"]], writes=[a_b_])
            for m2 in range(8):
                po, pob = next_ps()
                for k in range(22):
                    MM(po, wd3[:, k, m2 * 128:(m2 + 1) * 128], a3_[:, k, :], k == 0, k == 21, [wd_b, a_b_], [pob])
                STT(xT3[:, m2, sl], po, modT[:, 40 + m2:41 + m2], xT3[:, m2, sl], ALU.mult, ALU.add, [pob, lp_b, xT_b], [xT_b])
        P.barrier()
        A.release(m0_)


    x_save = dten("x_save", [8 * 128, T], F32)

    def spill_x():
        P.dma("sp", x_save.rearrange("(k p) t -> p k t", p=128), xT3, reads=[xT_b], writes=[dbuf["x_save"]])
        P.barrier()

    def restore_x():
        P.barrier()
        P.dma("sp", xT3, x_save.rearrange("(k p) t -> p k t", p=128), reads=[dbuf["x_save"]], writes=[xT_b])

    NKEY = GSZ * T
    NKT = NKEY // 128
    NKB = NKEY // 512
    NIT = 20
    LO0 = -512.0
    BIGP = float(2.0 ** 100)
    NSPL = (int(NKEY * 0.41) // 512) * 512
    NACT = NKEY - NSPL
    SCALE = float(HD ** -0.5)

    def phase_B(l):
        m0_ = A.mark()
        score, score_b = A.f32("score", NKEY)
        mb, mbA_b = A.bf16("mb", NKEY)
        mbB_b = Buf("mbB")
        mbT_raw, mbT_b = A.f32("mbT", NKT * 512 // 4)
        mbT = mbT_raw.bitcast(FP8)
        mbT3 = mbT.rearrange("p (k t) -> p k t", t=512)
        ikT, ikT_b = A.bf16("ikT", NKEY)
        ikT3 = ikT.rearrange("p (r t) -> p r t", t=T)
        iqb, iqb_b = A.bf16("iq_blk", 4 * 512)
        iqb3 = iqb.rearrange("p (k t) -> p k t", t=512)
        qb_, qb_b = A.bf16("q_blk", 8 * 512)
        qb3 = qb_.rearrange("p (k t) -> p k t", t=512)
        kst = [A.bf16("kst%d" % i, 2 * 512) for i in range(3)]
        vst = [A.bf16("vst%d" % i, 4 * 512) for i in range(3)]
        ppool = RPool("pT", 4, 512, "bf16")
        ab_, ab_b = A.bf16("attn_o_blk", 2 * 512)
        ab3 = ab_.rearrange("p (k t) -> p k t", t=512)
        negd0, negd0_b = A.f32("negd0", 512)
        dg, dg_b = A.bf16("diagw", IH * 128)
        dg3 = dg.rearrange("p (h c) -> p h c", c=128)
        rlp = RPool("relu", 8, 256, "f32")
        rls = []
        rel_ctr = [0]
        sm, sm_b = A.f32("bis_small", 16)
        md, md_b = A.f32("bis_mid", 2)
        cn, cnt_b = A.f32("bis_cnt", 2)
        sa, sacc_b = A.f32("bis_sacc", 2)
        Wk, Wk_b = A.f32("bis_w", NIT)
        lo = sm[:, 0:1]
        mid = md[:, 0:1]
        nmid = md[:, 1:2]
        cnt = cn[:, 0:1]
        sacc = sa[:, 0:1]
        tot = sm[:, 5:6]
        ge = sm[:, 6:7]
        hi0 = sm[:, 7:8]
        qp0 = sm[:, 8:9]
        STT(qp0, rk[:, 0:1], float(T), cv["pidx"][:, 0:1], ALU.mult, ALU.add, [rk_b, cst_b], [sm_b])
        TS("dve", negd0, cv["iotaf"], qp0, ALU.subtract, [cst_b, sm_b], [negd0_b], s2=-BIGP, op1=ALU.mult)
        P.dma("sp", ikT3, ik_gs.g[0].rearrange("(r p) t -> p r t", p=128), reads=[dbuf["ik_g0"]], writes=[ikT_b])
        for qb in range(NB):
            qsl = slice(qb * 512, (qb + 1) * 512)
            P.dma("sp", iqb3, iqT_d.rearrange("(k p) t -> p k t", p=128)[:, :, qsl], reads=[dbuf["iqT_d"]], writes=[iqb_b])
            P.dma("sp", qb3, qT_d.rearrange("(k p) t -> p k t", p=128)[:, :, qsl], reads=[dbuf["qT_d"]], writes=[qb_b])
            for qs in range(4):
                qt = qb * 4 + qs
                for h in range(IH):
                    TS("dve", dg3[:, h, :], ident_bf, iw3[:, qt, h:h + 1], ALU.mult, [ident_bf_b, iw_b], [dg_b])
                for kb in range(NKB):
                    ksl = slice(kb * 512, (kb + 1) * 512)
                    madd, madd_b = stg_pool.get()
                    TS("dve", madd, negd0, float((kb * 512 - qt * 128) * (-BIGP)), ALU.add, [negd0_b], [madd_b],
                       s2=0.0, op1=ALU.min)
                    lg = []
                    for h in range(IH):
                        hb = (h % 2) * 64
                        pt, pb = next_ps()
                        MM(pt, iqb3[hb:hb + 64, h // 2, qs * 128:(qs + 1) * 128], ikT[hb:hb + 64, ksl], True, True,
                           [iqb_b, ikT_b], [pb])
                        lg.append((pt, pb))
                        if len(lg) == 4 or h == IH - 1:
                            for (pt_, pb_) in lg:
                                rl32, rl_b = rlp.get()
                                rl = rl32.bitcast(BF16)[:, 0:512]
                                hh_ = rel_ctr[0]
                                rel_ctr[0] += 1
                                if hh_ % 2 == 0:
                                    ACT(rl, pt_, AF.Relu, [pb_], [rl_b])
                                else:
                                    TS("dve", rl, pt_, 0.0, ALU.max, [pb_], [rl_b])
                                rls.append((rl, rl_b))
                            lg = []
                    pacc, paccb = next_ps()
                    for h in range(IH):
                        rl, rl_b = rls[h]
                        MM(pacc, dg3[:, h, :], rl, h == 0, h == IH - 1, [dg_b, rl_b], [paccb])
                    del rls[:]
                    TT("dve", score[:, ksl], pacc, madd, ALU.add, [paccb, madd_b], [score_b])
                P.op("dve", lambda e: e.tensor_reduce(out=hi0, in_=score, axis=AX.X, op=ALU.max), [score_b], [sm_b])
                TS("dve", tot, hi0, 1.0 - LO0, ALU.add, [sm_b], [sm_b])
                for k in range(NIT):
                    TS("dve", Wk[:, k:k + 1], tot, float(2.0 ** -(k + 1)), ALU.mult, [sm_b], [Wk_b])
                P.op("dve", lambda e: e.memset(lo, LO0), [], [sm_b])
                for k in range(NIT):
                    TT("dve", mid, lo, Wk[:, k:k + 1], ALU.add, [sm_b, Wk_b], [md_b])
                    TS("dve", nmid, mid, -1.0, ALU.mult, [md_b], [md_b])
                    ACT(mb[:, NSPL:NKEY], score[:, NSPL:NKEY], AF.Sign, [score_b, md_b], [mbB_b, sacc_b], bias=nmid, scale=1.0,
                        accum_out=sacc)
                    TS("dve", mb[:, 0:NSPL], score[:, 0:NSPL], mid, ALU.is_ge, [score_b, md_b], [mbA_b, cnt_b],
                       s2=0.0, op1=ALU.add, accum_out=cnt)
                    STT(tot, sacc, 0.5, cnt, ALU.mult, ALU.add, [sacc_b, cnt_b], [sm_b])
                    TS("dve", ge, tot, float(cfg.nkeep - NACT / 2.0), ALU.is_ge, [sm_b], [sm_b])
                    STT(lo, ge, Wk[:, k:k + 1], lo, ALU.mult, ALU.add, [sm_b, Wk_b], [sm_b])
                TS("dve", mb, score, lo, ALU.is_ge, [score_b, sm_b], [mbA_b, mbB_b])
                for k4 in range(NKT // 4):
                    pt, pb = next_ps()
                    ptb = pt.bitcast(BF16)
                    for u in range(4):
                        kt = k4 * 4 + u
                        P.op("pe", lambda e, ptb=ptb, u=u, kt=kt: e.transpose(out=ptb[:, u * 128:(u + 1) * 128],
                                                                             in_=mb[:, kt * 128:(kt + 1) * 128],
                                                                             identity=ident_bf),
                             [mbA_b, mbB_b, ident_bf_b], [pb])
                    dst_ = mbT3[:, k4 * 4:(k4 + 1) * 4, qs * 128:(qs + 1) * 128]
                    src_ = ptb[:, 0:512].rearrange("p (u t) -> p u t", t=128)
                    if k4 % 2 == 0:
                        P.op("act", lambda e, dst_=dst_, src_=src_: e.activation(out=dst_, in_=src_, func=AF.Copy, saturate=False),
                             [pb], [mbT_b])
                    else:
                        P.op("dve", lambda e, dst_=dst_, src_=src_: e.tensor_copy(out=dst_, in_=src_, saturate=False),
                             [pb], [mbT_b])
            for hg in range(4):
                accs = [psum[i_] for i_ in range(4)]
                nk4 = NKT // 4
                bufs = {}

                def issue_kv(k4, hg=hg, bufs=bufs):
                    kk, kk_b = kst[k4 % 3]
                    kk3 = kk.rearrange("p (pr t) -> p pr t", t=512)
                    vv, vv_b = vst[k4 % 3]
                    vv3 = vv.rearrange("p (u c) -> p u c", c=512)
                    r_ = (k4 * 512) // T
                    off = (k4 * 512) % T
                    for pr in range(2):
                        gap, gn = kT_gs.g_rows(r_, (hg * 2 + pr) * 128, (hg * 2 + pr + 1) * 128)
                        P.dma("sp", kk3[:, pr, :], gap[:, off:off + 512], reads=[dbuf[gn]], writes=[kk_b])
                    for u in range(4):
                        gap, gn = v_gs.g_rows(r_, off + u * 128, off + (u + 1) * 128)
                        P.dma("sp", vv3[:, u, :], gap[:, hg * 512:(hg + 1) * 512], reads=[dbuf[gn]], writes=[vv_b])
                    bufs[k4] = (kk3, kk_b, vv3, vv_b)

                steps = [(k4, u, hh) for k4 in range(nk4) for u in range(4) for hh in range(4)]
                LA = 3
                pend = {}
                issue_kv(0)
                for si in range(len(steps) + LA):
                    if si < len(steps):
                        k4, u, hh = steps[si]
                        if u == 0 and hh == 0 and k4 + 1 < nk4:
                            issue_kv(k4 + 1)
                        kk3, kk_b, vv3, vv_b = bufs[k4]
                        kt = k4 * 4 + u
                        h = hg * 4 + hh
                        hb = (h % 2) * 64
                        pt, pb = next_ps_s()
                        MM(pt, kk3[hb:hb + 64, hh // 2, u * 128:(u + 1) * 128], qb3[hb:hb + 64, h // 2, :], True, True,
                           [kk_b, qb_b], [pb])
                        pend[si] = (pt, pb)
                    ti = si - LA
                    if ti >= 0:
                        k4, u, hh = steps[ti]
                        kk3, kk_b, vv3, vv_b = bufs[k4]
                        kt = k4 * 4 + u
                        pt, pb = pend.pop(ti)
                        pT, pT_b = ppool.get()
                        ACT(pT, pt, AF.Exp, [pb], [pT_b], scale=SCALE)
                        TT("dve", pT, pT, mbT3[:, kt, :], ALU.mult, [pT_b, mbT_b], [pT_b])
                        acc, acc_b = accs[hh]
                        MM(acc, vv3[:, u, hh * 128:(hh + 1) * 128], pT, kt == 0, kt == NKT - 1, [vv_b, pT_b], [acc_b])
                for hh in range(4):
                    h = hg * 4 + hh
                    hb = (h % 2) * 64
                    acc, acc_b = accs[hh]
                    osb, osb_b = stg_pool.get()
                    CP("act", osb, acc, [acc_b], [osb_b])
                    pw, pw_b = next_ps_s()
                    MM(pw, cv["swap"], osb, True, True, [cst_b, osb_b], [pw_b])
                    rc, rc_b = stg_pool.get()
                    P.op("dve", lambda e, rc=rc, pw=pw: e.reciprocal(out=rc, in_=pw), [pw_b], [rc_b])
                    TT("dve", ab3[hb:hb + 64, hh // 2, :], osb[hb:hb + 64, :], rc[hb:hb + 64, :], ALU.mult, [osb_b, rc_b], [ab_b])
                store(attnT_d.rearrange("(k p) t -> p k t", p=128)[:, hg * 2:hg * 2 + 2, qsl], "attnT_d", ab3, ab_b)
        print("phase B arena top", A.off, "of", A.n)
        P.barrier()
        A.release(m0_)

    ps_s_rr = [0]

    def next_ps_s():
        i = ps_s_rr[0]
        ps_s_rr[0] = (i + 1) % 4
        return psum[4 + i]


    NCH = T // 256
    st_gs = GatherSet("ssdF", 128, 2048, F32, 4)
    dd_gs = GatherSet("ssdD", 128, SH, F32, 4)

    def bcast_last(ap, n):
        return bass.AP(ap.tensor, ap.offset, [list(x) for x in ap.ap] + [[0, n]])

    def phase_C(l):
        m0_ = A.mark()
        rwp = RPool("rw", 3, 260, "f32")
        xsT, xsT_b = A.f32("xsT", 16 * 256)
        xsT3 = xsT.rearrange("p (m t) -> p m t", t=256)
        BT, BT_b = A.bf16("BT", 4 * 256)
        BT3 = BT.rearrange("p (g t) -> p g t", t=256)
        CT, CT_b = A.bf16("CT", 4 * 256)
        CT3 = CT.rearrange("p (g t) -> p g t", t=256)
        xdt, xdt_b = A.bf16("xdt", 2 * 2048)
        xdt3 = xdt.rearrange("p (i c) -> p i c", c=2048)
        xdd, xdd_b = A.bf16("xdd", 2 * 2048)
        xdd3 = xdd.rearrange("p (i c) -> p i c", c=2048)
        Btm, Btm_b = A.bf16("Btm", 2 * 512)
        Btm4 = Btm.rearrange("p (i g n) -> p i g n", g=4, n=128)
        acs, acs_b = A.f32("acs_tm", 2 * SH)
        acs3 = acs.rearrange("p (i h) -> p i h", h=SH)
        totb, totb_b = A.f32("tot_bc", SH)
        etot, etot_b = A.f32("etot", SH)
        ds_, ds_b = A.f32("ds", 2 * SH)
        ds3 = ds_.rearrange("p (i h) -> p i h", h=SH)
        Sst, Sst_b = A.f32("Sstate", 2048)
        Sbf, Sbf_b = A.bf16("Sstate_bf", 2048)
        Dacc, Dacc_b = A.f32("Dacc", SH)
        cbm, cbm_b = A.f32("CBm", 4 * 384)
        cbm3 = cbm.rearrange("p (g t) -> p g t", t=384)
        triL, triL_b = A.f32("triL", 512)
        yg, yg_b = A.f32("yg", 16 * 256)
        yg3 = yg.rearrange("p (m t) -> p m t", t=256)
        prev, prev_b = A.f32("xbc_prev", 24 * 4)
        prev3 = prev.rearrange("p (m c) -> p m c", c=4)
        zp = RPool("zt", 3, 256, "bf16")
        r0p = RPool("c_r0", 6, 512, "f32")
        dpp = RPool("c_d", 8, 384, "f32")
        ebp = RPool("c_eb", 8, 256, "f32")
        bcp = RPool("c_bc", 6, 256, "f32")
        gpp = RPool("c_G", 6, 384, "bf16")
        cep = RPool("c_Ce", 6, 256, "bf16")
        print("phase C arena top", A.off, "of", A.n)
        CP("dve", triL[:, 0:128], cv["tri"], [cst_b], [triL_b])
        CP("dve", triL[:, 128:256], cv["ones"], [cst_b], [triL_b])
        P.op("dve", lambda e: e.memset(triL[:, 256:384], 0.0), [], [triL_b])
        CP("dve", triL[:, 384:512], cv["tri"], [cst_b], [triL_b])
        load_prev_rows(halo_gs, prev3, prev_b, 24, 4)

        def conv_block(blk, c, out_ap, out_b, eng_out="act"):
            rw, rw_b = rwp.get()
            if c == 0:
                P.dma("sp", rw[:, 3:259], xbc_raw[blk * 128:(blk + 1) * 128, 0:256], reads=[dbuf["xbc_raw"]], writes=[rw_b])
                CP("pool", rw[:, 0:3], prev3[:, blk, 0:3], [prev_b], [rw_b])
            else:
                P.dma("sp", rw[:, 0:259], xbc_raw[blk * 128:(blk + 1) * 128, c * 256 - 3:c * 256 + 256],
                      reads=[dbuf["xbc_raw"]], writes=[rw_b])
            ac, ac_b = stg_pool.get()
            TS("dve", ac[:, 0:256], rw[:, 0:256], scwT[:, blk:blk + 1], ALU.mult, [rw_b, lp_b], [ac_b],
               s2=scbT[:, blk:blk + 1], op1=ALU.add)
            for tap in range(1, 4):
                STT(ac[:, 0:256], rw[:, tap:tap + 256], scwT[:, tap * 24 + blk:tap * 24 + blk + 1], ac[:, 0:256], ALU.mult, ALU.add,
                    [rw_b, lp_b, ac_b], [ac_b])
            ACT(out_ap, ac[:, 0:256], AF.Silu, [ac_b], [out_b])

        def ssd_pass(compute_y):
            for c in range(NCH):
                csl = slice(c * 256, (c + 1) * 256)
                for blk in range(16):
                    conv_block(blk, c, xsT3[:, blk, :], xsT_b)
                for g in range(4):
                    conv_block(16 + g, c, BT3[:, g, :], BT_b)
                if compute_y:
                    for g in range(4):
                        conv_block(20 + g, c, CT3[:, g, :], CT_b)
                for i in range(2):
                    ti = c * 2 + i
                    for bank in range(4):
                        pt, pb = next_ps()
                        for u in range(4):
                            blk = bank * 4 + u
                            P.op("pe", lambda e, pt=pt, u=u, blk=blk, i=i: e.transpose(
                                out=pt[:, u * 128:(u + 1) * 128], in_=xsT3[:, blk, i * 128:(i + 1) * 128], identity=cv["ident"]),
                                [xsT_b, cst_b], [pb])
                        TT("dve", xdt3[:, i, bank * 512:(bank + 1) * 512].rearrange("p (h q) -> p h q", q=64),
                           pt.rearrange("p (h q) -> p h q", q=64), bcast_last(dt3[:, ti, bank * 8:(bank + 1) * 8], 64), ALU.mult,
                           [pb, dt_b], [xdt_b])
                    pt, pb = next_ps()
                    ptb = pt.bitcast(BF16)
                    for g in range(4):
                        P.op("pe", lambda e, ptb=ptb, g=g, i=i: e.transpose(out=ptb[:, g * 128:(g + 1) * 128],
                                                                           in_=BT3[:, g, i * 128:(i + 1) * 128], identity=ident_bf),
                             [BT_b, ident_bf_b], [pb])
                    CP("act", Btm4[:, i, :, :], ptb[:, 0:512].rearrange("p (g n) -> p g n", n=128), [pb], [Btm_b])
                a0 = a3[:, c * 2, :]
                a1 = a3[:, c * 2 + 1, :]
                pt, pb = next_ps()
                MM(pt[:, 0:32], cv["tri"], a0, True, True, [cst_b, a_b], [pb])
                MM(pt[:, 32:64], cv["ones"], a0, True, False, [cst_b, a_b], [pb])
                MM(pt[:, 32:64], cv["tri"], a1, False, True, [cst_b, a_b], [pb])
                MM(pt[:, 64:96], cv["ones"], a0, True, False, [cst_b, a_b], [pb])
                MM(pt[:, 64:96], cv["ones"], a1, False, True, [cst_b, a_b], [pb])
                CP("dve", acs, pt[:, 0:64], [pb], [acs_b])
                CP("dve", totb, pt[:, 64:96], [pb], [totb_b])
                ACT(etot, totb, AF.Exp, [totb_b], [etot_b])
                TT("dve", Dacc, Dacc, totb, ALU.add, [Dacc_b, totb_b], [Dacc_b])
                for i in range(2):
                    TT("dve", ds3[:, i, :], totb, acs3[:, i, :], ALU.subtract, [totb_b, acs_b], [ds_b])
                ACT(ds_, ds_, AF.Exp, [ds_b], [ds_b])
                for i in range(2):
                    TT("pool", xdd3[:, i, :].rearrange("p (h q) -> p h q", q=64), xdt3[:, i, :].rearrange("p (h q) -> p h q", q=64),
                       bcast_last(ds3[:, i, :], 64), ALU.mult, [xdt_b, ds_b], [xdd_b])
                if compute_y:
                    CP("dve", Sbf, Sst, [Sst_b], [Sbf_b])
                    for g in range(4):
                        pt, pb = next_ps()
                        MM(pt[:, 0:256], BT3[:, g, 0:128], CT3[:, g, :], True, True, [BT_b, CT_b], [pb])
                        MM(pt[:, 256:384], BT3[:, g, 128:256], CT3[:, g, 128:256], True, True, [BT_b, CT_b], [pb])
                        TT("dve", cbm3[:, g, 0:128], pt[:, 0:128], cv["tri"], ALU.mult, [pb, cst_b], [cbm_b])
                        CP("dve", cbm3[:, g, 128:256], pt[:, 128:256], [pb], [cbm_b])
                        TT("dve", cbm3[:, g, 256:384], pt[:, 256:384], cv["tri"], ALU.mult, [pb, cst_b], [cbm_b])
                    NBQ = 8

                    def front(bq):
                        st_ = []
                        for hq in range(4):
                            h = bq * 4 + hq
                            r0, r0_b = r0p.get()
                            TS("dve", r0[:, 0:256], triL[:, 0:256], a0[:, h:h + 1], ALU.mult, [triL_b, a_b], [r0_b])
                            TS("pool", r0[:, 256:512], triL[:, 256:512], a1[:, h:h + 1], ALU.mult, [triL_b, a_b], [r0_b])
                            st_.append([h, r0, r0_b])
                        for e_ in st_:
                            h, r0, r0_b = e_
                            pbc, pbcb = next_ps()
                            MM(pbc[:, 0:256], cv["ones"], r0[:, 0:256], True, False, [cst_b, r0_b], [pbcb])
                            MM(pbc[:, 0:256], cv["ones"], r0[:, 256:512], False, True, [cst_b, r0_b], [pbcb])
                            e_ += [pbc, pbcb]
                        for e_ in st_:
                            h, r0, r0_b, pbc, pbcb = e_
                            d_, d_b = dpp.get()
                            bcs, bcs_b = bcp.get()
                            CP("act", bcs, pbc[:, 0:256], [pbcb], [bcs_b])
                            TS("dve", d_[:, 0:256], bcs[:, 0:256], acs3[:, 0, h:h + 1], ALU.subtract, [bcs_b, acs_b], [d_b],
                               s2=0.0, op1=ALU.min)
                            TS("dve", d_[:, 256:384], bcs[:, 128:256], acs3[:, 1, h:h + 1], ALU.subtract, [bcs_b, acs_b], [d_b],
                               s2=0.0, op1=ALU.min)
                            eb, eb_b = ebp.get()
                            ACT(eb, bcs, AF.Exp, [bcs_b], [eb_b])
                            e_ += [d_, d_b, eb, eb_b]
                        return st_

                    def back(bq, st_):
                        for e_ in st_:
                            h, r0, r0_b, pbc, pbcb, d_, d_b, eb, eb_b = e_
                            g = h // 8
                            ACT(d_, d_, AF.Exp, [d_b], [d_b])
                            Ce, Ce_b = cep.get()
                            TT("pool", Ce, eb, CT3[:, g, :], ALU.mult, [eb_b, CT_b], [Ce_b])
                            e_ += [Ce, Ce_b]
                        for e_ in st_:
                            h, d_, d_b = e_[0], e_[5], e_[6]
                            g = h // 8
                            G_, G_b = gpp.get()
                            TT("dve", G_, d_, cbm3[:, g, :], ALU.mult, [d_b, cbm_b], [G_b])
                            e_ += [G_, G_b]
                        for pq in range(2):
                            pr = bq * 2 + pq
                            py, pyb = next_ps()
                            for hh in range(2):
                                e_ = st_[pq * 2 + hh]
                                h, Ce, Ce_b, G_, G_b = e_[0], e_[9], e_[10], e_[11], e_[12]
                                hb = hh * 64
                                MM(py[hb:hb + 64, 0:256], xdt3[:, 0, h * 64:(h + 1) * 64], G_[:, 0:256], True, False, [xdt_b, G_b], [pyb])
                                MM(py[hb:hb + 64, 128:256], xdt3[:, 1, h * 64:(h + 1) * 64], G_[:, 256:384], False, False,
                                   [xdt_b, G_b], [pyb])
                                MM(py[hb:hb + 64, 0:256], Sbf[:, h * 64:(h + 1) * 64], Ce, False, True, [Sbf_b, Ce_b], [pyb])
                            zt, zt_b = zp.get()
                            P.dma("sp", zt, zT_d[pr * 128:(pr + 1) * 128, csl], reads=[dbuf["zT_d"]], writes=[zt_b])
                            yv, yv_b = stg_pool.get()
                            STT(yv[:, 0:256], xsT3[:, pr, :], dskT[:, pr:pr + 1], py[:, 0:256], ALU.mult, ALU.add,
                                [xsT_b, lp_b, pyb], [yv_b])
                            TT("pool", yg3[:, pr, :], yv[:, 0:256], zt, ALU.mult, [yv_b, zt_b], [yg_b])

                    prev_st = None
                    for bq in range(NBQ + 1):
                        cur = front(bq) if bq < NBQ else None
                        if prev_st is not None:
                            back(bq - 1, prev_st)
                        prev_st = cur
                    for g in range(4):
                        pss, pssb = next_ps()
                        for q in range(4):
                            sq, sq_b = stg_pool.get()
                            ACT(sq[:, 0:256], yg3[:, g * 4 + q, :], AF.Square, [yg_b], [sq_b])
                            MM(pss[:, 0:256], cv["ones"], sq[:, 0:256], q == 0, q == 3, [cst_b, sq_b], [pssb])
                        rs, rs_b = stg_pool.get()
                        ACT(rs[:, 0:256], pss[:, 0:256], AF.Sqrt, [pssb], [rs_b], bias=epsT[:, 0:1], scale=1.0 / 512.0)
                        P.op("dve", lambda e, rs=rs: e.reciprocal(out=rs[:, 0:256], in_=rs[:, 0:256]), [rs_b], [rs_b])
                        for q in range(4):
                            pr = g * 4 + q
                            ob, ob_b = sbf_pool.get()
                            STT(ob[:, 0:256], yg3[:, pr, :], sngT[:, pr:pr + 1], rs[:, 0:256], ALU.mult, ALU.mult,
                                [yg_b, lp_b, rs_b], [ob_b])
                            store(ynT_d[pr * 128:(pr + 1) * 128, csl], "ynT_d", ob[:, 0:256], ob_b)
                for g in range(4):
                    pst, pstb = next_ps()
                    for i in range(2):
                        MM(pst, Btm4[:, i, g, :], xdd3[:, i, g * 512:(g + 1) * 512], i == 0, i == 1, [Btm_b, xdd_b], [pstb])
                    sg3 = Sst[:, g * 512:(g + 1) * 512].rearrange("p (h q) -> p h q", q=64)
                    TT("dve", sg3, sg3, bcast_last(etot[:, g * 8:(g + 1) * 8], 64), ALU.mult, [Sst_b, etot_b], [Sst_b])
                    TT("dve", Sst[:, g * 512:(g + 1) * 512], Sst[:, g * 512:(g + 1) * 512], pst, ALU.add, [Sst_b, pstb], [Sst_b])

        P.op("dve", lambda e: e.memset(Sst, 0.0), [], [Sst_b])
        P.op("dve", lambda e: e.memset(Dacc, 0.0), [], [Dacc_b])
        ssd_pass(False)
        dap, dn = st_gs.loc_rows(0, 128)
        store(dap, dn, Sst, Sst_b)
        dap, dn = dd_gs.loc_rows(0, 128)
        store(dap, dn, Dacc, Dacc_b)
        st_gs.gather()
        dd_gs.gather()
        Dr = []
        for r in range(GSZ - 1):
            t_, tb_ = A.f32("Dr%d" % r, SH)
            gap, gn = dd_gs.g_rows(r, 0, 128)
            P.dma("sp", t_, gap, reads=[dbuf[gn]], writes=[tb_])
            Dr.append((t_, tb_))
        lt, lt_b = A.f32("ltflag", 4)
        for m in range(GSZ - 1):
            TS("dve", lt[:, m:m + 1], rk[:, 0:1], float(m), ALU.is_gt, [rk_b], [lt_b])
        P.op("dve", lambda e: e.memset(Sst, 0.0), [], [Sst_b])
        Fr, Fr_b = A.f32("Fr", 2048)
        for r in range(GSZ - 1):
            wr, wr_b = A.f32("wr%d" % r, SH)
            P.op("dve", lambda e, wr=wr: e.memset(wr, 0.0), [], [wr_b])
            for m in range(r + 1, GSZ - 1):
                STT(wr, Dr[m][0], lt[:, m:m + 1], wr, ALU.mult, ALU.add, [Dr[m][1], lt_b, wr_b], [wr_b])
            ACT(wr, wr, AF.Exp, [wr_b], [wr_b])
            TS("dve", wr, wr, lt[:, r:r + 1], ALU.mult, [wr_b, lt_b], [wr_b])
            gap, gn = st_gs.g_rows(r, 0, 128)
            P.dma("sp", Fr, gap, reads=[dbuf[gn]], writes=[Fr_b])
            F3 = Fr.rearrange("p (h q) -> p h q", q=64)
            TT("dve", F3, F3, bcast_last(wr, 64), ALU.mult, [Fr_b, wr_b], [Fr_b])
            TT("dve", Sst, Sst, Fr, ALU.add, [Sst_b, Fr_b], [Sst_b])
        ssd_pass(True)
        P.barrier()
        A.release(m0_)

    P.dma("sp", rk[:, 0:4], bcast_ap(rank_in, 128, 4), reads=[dbuf["rank"]], writes=[rk_b])
    for r in range(GSZ):
        TS("dve", prevsel[:, r:r + 1], rk[:, 0:1], float(r + 1), ALU.is_equal, [rk_b], [prevsel_b])

    def load_prev_rows(gs, dst3, dst_b, nsub, ncols):
        first = True
        for r in range(GSZ - 1):
            tmp_, tmpb_ = A.f32("prevtmp%d" % r, nsub * ncols)
            tmp3 = tmp_.rearrange("p (m c) -> p m c", c=ncols)
            gap, gn = gs.g_rows(r, 0, 128)
            P.dma("sp", tmp_, gap, reads=[dbuf[gn]], writes=[tmpb_])
            if first:
                TS("dve", dst3, tmp3, prevsel[:, r:r + 1], ALU.mult, [tmpb_, prevsel_b], [dst_b])
                first = False
            else:
                STT(dst3, tmp3, prevsel[:, r:r + 1], dst3, ALU.mult, ALU.add, [tmpb_, prevsel_b, dst_b], [dst_b])

    for l in range(depth):
        A.release(m_x)
        phase_A(l)
        P.barrier()
        A.release(m_x)
        if cfg.stage >= 30:
            spill_x()
            A.release(m_pers)
            phase_B(l)
            A.release(m_pers)
            if cfg.stage >= 40:
                phase_C(l)
            A.release(m_x)
            restore_x()
        if cfg.stage >= 20:
            phase_D(l)
        if cfg.stage >= 21:
            phase_EF(l, None)
    A.release(m_x)
    for name in dbg_copies:
        rows = dram[name].shape[0]
        for r0 in range(0, rows, 512):
            r1 = min(rows, r0 + 512)
            P.dma("sp", dram[name + "_dbg"][r0:r1, :], dram[name][r0:r1, :], reads=[dbuf[name]], writes=[dbuf[name + "_dbg"]])
    P.barrier()

    m0 = A.mark()
    outs = [A.f32("otok%d" % i, D) for i in range(2)]
    for i in range(NT):
        ot, ot_b = outs[i % 2]
        for k in range(KC):
            pt, pb = next_ps()
            P.op("pe", lambda e, pt=pt, k=k, i=i: e.transpose(out=pt[:, 0:128], in_=xT3[:, k, i * 128:(i + 1) * 128],
                                                             identity=cv["ident"]),
                 reads=[xT_b, cst_b], writes=[pb])
            if k % 2 == 0:
                P.op("act", lambda e, pt=pt, k=k, ot=ot: e.copy(out=ot[:, k * 128:(k + 1) * 128], in_=pt[:, 0:128]),
                     reads=[pb], writes=[ot_b])
            else:
                P.op("dve", lambda e, pt=pt, k=k, ot=ot: e.tensor_copy(out=ot[:, k * 128:(k + 1) * 128], in_=pt[:, 0:128]),
                     reads=[pb], writes=[ot_b])
        P.dma("sp", y_out[i * 128:(i + 1) * 128, :], ot, reads=[ot_b], writes=[dbuf["y"]])
    P.barrier()
    A.release(m0)

    P.emit(stack)
    stack.close()
    return nc


def make_in_maps(cfg, inputs):
    T = cfg.T
    depth = cfg.depth
    maps = []
    f = lambda a: np.ascontiguousarray(np.asarray(a, dtype=np.float32))
    shared = {
        "consts": CONST_ARR,
        "w_ada": f(inputs["w_ada"]), "b_ada": f(inputs["b_ada"]).reshape(depth, 48, 128),
        "norm1_g": f(inputs["norm1_g"]).reshape(depth, KC, 128), "w_in": f(inputs["w_in"]),
        "q_norm_g": f(inputs["q_norm_g"]).reshape(depth, 1, 64), "k_norm_g": f(inputs["k_norm_g"]).reshape(depth, 1, 64),
        "ssm_conv_w": f(inputs["ssm_conv_w"]).reshape(depth, 4, 24, 128),
        "ssm_conv_b": f(inputs["ssm_conv_b"]).reshape(depth, 24, 128),
        "dt_bias": f(inputs["dt_bias"]).reshape(depth, 1, SH), "a_log": f(inputs["a_log"]).reshape(depth, 1, SH),
        "d_skip": f(inputs["d_skip"]).reshape(depth, 1, SH), "ssm_norm_g": f(inputs["ssm_norm_g"]).reshape(depth, 16, 128),
        "w_attn_o": f(inputs["w_attn_o"]), "w_ssm_o": f(inputs["w_ssm_o"]), "w_out": f(inputs["w_out"]),
        "norm2_g": f(inputs["norm2_g"]).reshape(depth, KC, 128), "w_up": f(inputs["w_up"]),
        "ffn_conv_w": f(inputs["ffn_conv_w"]).reshape(depth, 3, 44, 128),
        "ffn_conv_b": f(inputs["ffn_conv_b"]).reshape(depth, 44, 128), "w_down": f(inputs["w_down"]),
    }
    x = f(inputs["x"])
    c = f(inputs["c"])
    pos = np.ascontiguousarray(np.asarray(inputs["positions"], dtype=np.int32))
    for r in range(NCORES):
        b, j = divmod(r, GSZ)
        m = dict(shared)
        m["x"] = np.ascontiguousarray(x[b, j * T:(j + 1) * T, :])
        m["c"] = np.ascontiguousarray(c[b].reshape(KC, 128))
        m["positions"] = np.ascontiguousarray(pos[b, j * T:(j + 1) * T].reshape(1, T))
        rk = np.zeros((1, 4), np.float32)
        rk[0, 0] = j
        m["rank"] = rk
        for k_, v_ in (getattr(cfg, "feed_data", None) or {}).items():
            m[k_] = v_[r]
        maps.append(m)
    return maps


_CACHE = {}


def run(cfg, inputs):
    key = (cfg.S, cfg.depth, tuple(sorted(cfg.debug)), cfg.stage, tuple(sorted(cfg.feed)))
    if key not in _CACHE:
        _CACHE[key] = build_program(cfg)
    nc = _CACHE[key]
    maps = make_in_maps(cfg, inputs)
    res = run_bass_kernel_spmd(nc, maps, core_ids=list(range(NCORES)))
    return res.results


def kernel(**inputs):
    cfg = Cfg(seq=int(np.asarray(inputs["x"]).shape[1]), depth=int(np.asarray(inputs["w_in"]).shape[0]))
    results = run(cfg, inputs)
    B = np.asarray(inputs["x"]).shape[0]
    out = np.zeros((B, cfg.S, D), np.float32)
    for r in range(NCORES):
        b, j = divmod(r, GSZ)
        out[b, j * cfg.T:(j + 1) * cfg.T, :] = results[r]["y"]
    return out
```

```python
from contextlib import ExitStack
import numpy as np
import ml_dtypes
import concourse.bass as bass
import concourse.mybir as mybir
from concourse.bass_utils import run_bass_kernel_spmd

F32 = mybir.dt.float32
BF16 = mybir.dt.bfloat16
I32 = mybir.dt.int32
FP8 = mybir.dt.float8e5
AF = mybir.ActivationFunctionType
ALU = mybir.AluOpType
AX = mybir.AxisListType

NCORES = 8
GSZ = 4
D = 1024
KC = D // 128
HEADS = 16
HD = 64
IH = 8
TOPK = 256
DI = 2048
SH = 32
SG = 4
NST = 128
XBC = DI + 2 * SG * NST
DFF = 2816
EPS = 1e-6
C_Q, C_K, C_V, C_IQ, C_IK, C_IW, C_Z, C_XBC, C_DT, C_GA, C_GM = (
    0, 1024, 2048, 3072, 3584, 3648, 3656, 5704, 8776, 8808, 9832)
INW = 10856
NEG = -30000.0


class Buf:
    __slots__ = ("name", "w", "r")

    def __init__(self, name):
        self.name = name
        self.w = None
        self.r = {}


class Prog:
    ENGS = ("pe", "act", "dve", "pool", "sp")

    def __init__(self, nc, n_dma=40):
        self.nc = nc
        self.ops = {e: [] for e in self.ENGS}
        self.cnt = {e: 0 for e in self.ENGS}
        self.known = {e: {} for e in self.ENGS}
        self.dma_val = [0] * n_dma
        self.dma_next = 0
        self.cc_val = 0

    def _need(self, eng, k, v):
        if k == eng and eng == "pe":
            return
        kn = self.known[eng]
        if kn.get(k, 0) >= v:
            return
        kn[k] = v
        self.ops[eng].append(("wait", k, v))

    def _deps(self, eng, reads, writes):
        for b in reads:
            if b.w is not None:
                self._need(eng, *b.w)
        for b in writes:
            if b.w is not None:
                self._need(eng, *b.w)
            for k, v in b.r.items():
                self._need(eng, k, v)

    def _mark(self, tok, reads, writes):
        for b in writes:
            b.w = tok
            b.r = {}
        for b in reads:
            if b in writes:
                continue
            if b.r.get(tok[0], 0) < tok[1]:
                b.r[tok[0]] = tok[1]

    def op(self, eng, fn, reads=(), writes=()):
        self._deps(eng, reads, writes)
        self.cnt[eng] += 1
        tok = (eng, self.cnt[eng])
        self.ops[eng].append(("ins", fn, eng, 1, self._where()))
        self._mark(tok, reads, writes)
        return tok

    DEBUG_WHERE = False

    def _where(self):
        if not Prog.DEBUG_WHERE:
            return None
        import traceback
        return [(f.lineno, f.name) for f in traceback.extract_stack(limit=6)[:-2]]

    def dma(self, q, out, in_, reads=(), writes=()):
        i = self.dma_next
        self.dma_next = (i + 1) % len(self.dma_val)
        key = ("dma", i)
        self._deps(q, reads, writes)
        if self.dma_val[i]:
            self._need(q, key, self.dma_val[i])
        self.dma_val[i] += 16
        tok = (key, self.dma_val[i])
        self.ops[q].append(("ins", lambda e, o=out, s=in_: e.dma_start(out=o, in_=s), key, 16))
        self._mark(tok, reads, writes)
        return tok

    def collective(self, fn, reads=(), writes=()):
        self._deps("pool", reads, writes)
        self.cc_val += 1
        tok = ("cc", self.cc_val)
        self.ops["pool"].append(("ins", fn, "cc", 1))
        self._mark(tok, reads, writes)
        return tok

    def barrier(self):
        for e in self.ENGS:
            for f in ("pe", "act", "dve", "pool"):
                if self.cnt[f]:
                    self._need(e, f, self.cnt[f])
            for i, v in enumerate(self.dma_val):
                if v:
                    self._need(e, ("dma", i), v)
            if self.cc_val:
                self._need(e, "cc", self.cc_val)

    def emit(self, stack):
        nc = self.nc
        sems = {}
        for e in ("pe", "act", "dve", "pool"):
            sems[e] = stack.enter_context(nc.semaphore("s_" + e))
        for i in range(len(self.dma_val)):
            sems[("dma", i)] = stack.enter_context(nc.semaphore("d%d" % i))
        sems["cc"] = stack.enter_context(nc.semaphore("s_cc"))
        block = stack.enter_context(nc.Block())

        def mk(name):
            def body(eng):
                for o in self.ops[name]:
                    if o[0] == "wait":
                        eng.wait_ge(sems[o[1]], o[2])
                    else:
                        ins = o[1](eng)
                        ins.then_inc(sems[o[2]], o[3])
                        if Prog.DEBUG_WHERE and len(o) > 4:
                            print("INS", name, getattr(getattr(ins, "ins", None), "name", None), o[4])
            return body

        block.tensor(mk("pe"))
        block.scalar(mk("act"))
        block.vector(mk("dve"))
        block.gpsimd(mk("pool"))
        block.sync(mk("sp"))


class Arena:
    def __init__(self, big, nwords):
        self.big = big
        self.n = nwords
        self.off = 0

    def mark(self):
        return self.off

    def release(self, m):
        self.off = m

    def f32(self, name, cols):
        a = self.off
        self.off += cols
        assert self.off <= self.n, ("SBUF arena overflow", name, self.off, self.n)
        return self.big[:, a:a + cols], Buf(name)

    def bf16(self, name, cols):
        w = (cols + 1) // 2
        a = self.off
        self.off += w
        assert self.off <= self.n, ("SBUF arena overflow", name, self.off, self.n)
        return self.big[:, a:a + w].bitcast(BF16)[:, 0:cols], Buf(name)


def make_consts():
    c = {}
    c["ident"] = np.eye(128, dtype=np.float32)
    c["ones"] = np.ones((128, 128), np.float32)
    bo = np.zeros((128, 128), np.float32)
    bo[:64, :64] = 1.0
    bo[64:, 64:] = 1.0
    c["blockones"] = bo
    rr = np.zeros((128, 128), np.float32)
    for m in range(128):
        if (m % 64) < 32:
            rr[m + 32, m] = -1.0
        else:
            rr[m - 32, m] = 1.0
    c["rrot"] = rr
    tri = (np.arange(128)[:, None] <= np.arange(128)[None, :]).astype(np.float32)
    c["tri"] = tri
    c["causb"] = np.where(np.arange(128)[None, :] <= np.arange(128)[:, None], 0.0, -1e30).astype(np.float32)
    invf = (1.0 / (10000.0 ** (np.arange(0, 64, 2, dtype=np.float32) / 64.0))).astype(np.float32)
    c["invf"] = np.tile(invf, 4)[:, None].astype(np.float32) * np.ones((1, 128), np.float32)
    c["iotaf"] = np.tile(np.arange(512, dtype=np.float32)[None, :], (128, 1))
    c["pidx"] = np.tile(np.arange(128, dtype=np.float32)[:, None], (1, 128))
    sw = np.zeros((128, 128), np.float32)
    for m in range(128):
        sw[(m + 64) % 128, m] = 1.0
    c["swap"] = sw
    names = ["ident", "ones", "blockones", "rrot", "tri", "causb", "invf", "iotaf", "pidx", "swap"]
    return [(n, c[n].shape[1]) for n in names], np.concatenate([c[n] for n in names], axis=1)


CONST_NAMES, CONST_ARR = make_consts()


class Cfg:
    def __init__(self, seq=8192, depth=4, debug=(), stage=99, feed=()):
        self.stage = stage
        self.feed = set(feed)
        self.S = seq
        self.T = seq // GSZ
        self.depth = depth
        self.debug = set(debug)
        self.NT = self.T // 128
        self.NB = self.T // 512
        self.NCH = self.T // 256
        self.nkeep = min(TOPK, seq // 4)


def build_program(cfg):
    T, NT, NB, depth = cfg.T, cfg.NT, cfg.NB, cfg.depth
    nc = bass.Bass("TRN2", target_bir_lowering=False)
    stack = ExitStack()
    P = Prog(nc)
    dram = {}
    dbuf = {}

    dbg_copies = []

    def dten(name, shape, dtype, kind="Internal"):
        if name in cfg.debug:
            if "_g" in name[-4:]:
                dcp = nc.dram_tensor(name + "_dbg", list(shape), dtype, kind="ExternalOutput").ap()
                dram[name + "_dbg"] = dcp
                dbuf[name + "_dbg"] = Buf(name + "_dbg")
                dbg_copies.append(name)
            else:
                kind = "ExternalOutput"
        t = nc.dram_tensor(name, list(shape), dtype, kind=kind).ap()
        dram[name] = t
        dbuf[name] = Buf(name)
        return t

    x_in = dten("x", [T, D], F32, "ExternalInput")
    c_in = dten("c", [KC, 128], F32, "ExternalInput")
    pos_in = dten("positions", [1, T], I32, "ExternalInput")
    consts_in = dten("consts", [128, CONST_ARR.shape[1]], F32, "ExternalInput")
    rank_in = dten("rank", [1, 4], F32, "ExternalInput")
    W = {}
    for nm, shp in (("w_ada", [depth, D, 6 * D]), ("b_ada", [depth, 48, 128]), ("norm1_g", [depth, KC, 128]),
                    ("w_in", [depth, D, INW]), ("q_norm_g", [depth, 1, 64]), ("k_norm_g", [depth, 1, 64]),
                    ("ssm_conv_w", [depth, 4, 24, 128]), ("ssm_conv_b", [depth, 24, 128]),
                    ("dt_bias", [depth, 1, SH]), ("a_log", [depth, 1, SH]), ("d_skip", [depth, 1, SH]),
                    ("ssm_norm_g", [depth, 16, 128]), ("w_attn_o", [depth, D, D]), ("w_ssm_o", [depth, DI, D]),
                    ("w_out", [depth, D, D]), ("norm2_g", [depth, KC, 128]), ("w_up", [depth, D, 2 * DFF]),
                    ("ffn_conv_w", [depth, 3, 44, 128]), ("ffn_conv_b", [depth, 44, 128]),
                    ("w_down", [depth, DFF, D])):
        W[nm] = dten(nm, shp, F32, "ExternalInput")
    y_out = dten("y", [T, D], F32, "ExternalOutput")

    NWORDS = 52224
    big = stack.enter_context(nc.sbuf_tensor("arena", [128, NWORDS], F32))
    A = Arena(big, NWORDS)
    psum = []
    for i in range(8):
        pt = stack.enter_context(nc.psum_tensor("ps%d" % i, [128, 512], F32))
        psum.append((pt[:, :], Buf("ps%d" % i)))
    ps_rr = [0]

    def next_ps():
        i = ps_rr[0]
        ps_rr[0] = (i + 1) % 8
        return psum[i]

    cst, cst_b = A.f32("consts", CONST_ARR.shape[1])
    cv = {}
    o = 0
    for n, wd in CONST_NAMES:
        cv[n] = cst[:, o:o + wd]
        o += wd
    ident_bf, ident_bf_b = A.bf16("ident_bf", 128)

    class RPool:
        def __init__(self, name, n, cols, kind):
            self.t = [(A.f32 if kind == "f32" else A.bf16)("%s%d" % (name, i), cols) for i in range(n)]
            self.i = 0

        def get(self):
            r = self.t[self.i]
            self.i = (self.i + 1) % len(self.t)
            return r


    lp, lp_b = A.f32("lp", 512)
    dtb_bc, dtb_b = A.f32("dtb_bc", SH)
    alog_bc, alog_b = A.f32("alog_bc", SH)
    iw_tm, iw_b = A.f32("iw_tm", NT * 8)
    dt_tm, dt_b = A.f32("dt_tm", NT * SH)
    a_tm, a_b = A.f32("a_tm", NT * SH)
    halfsel, halfsel_b = A.f32("halfsel", 128)
    lfm_pool = RPool("lfm", 3, 128, "f32")
    WSTG = 2048
    wst_pool = RPool("wst", 2, WSTG, "f32")
    wbf_pool = RPool("wbf", 2, WSTG, "bf16")
    stg_pool = RPool("stg", 8, 512, "f32")
    sbf_pool = RPool("sbf", 4, 512, "bf16")
    epsT, epsT_b = A.f32("epsT", 2)
    oneT, oneT_b = A.f32("oneT", 2)
    rk, rk_b = A.f32("rk", 8)
    prevsel, prevsel_b = A.f32("prevsel", 4)
    class NS:
        pass
    ns = NS()
    m_pers = A.mark()
    xT, xT_b = A.f32("xT", KC * T)
    xT3 = xT.rearrange("p (k t) -> p k t", t=T)
    m_x = A.mark()

    P.dma("sp", cst, consts_in, reads=[dbuf["consts"]], writes=[cst_b])
    P.op("dve", lambda e: e.tensor_copy(out=ident_bf, in_=cv["ident"]), reads=[cst_b], writes=[ident_bf_b])

    def transpose_f32(dst, dst_b, src, src_b, rows, cols, evac="act"):
        pt, pb = next_ps()
        P.op("pe", lambda e: e.transpose(out=pt[0:cols, 0:rows], in_=src, identity=cv["ident"][0:rows, 0:rows]),
             reads=[src_b, cst_b], writes=[pb])
        if evac == "act":
            P.op("act", lambda e: e.copy(out=dst, in_=pt[0:cols, 0:rows]), reads=[pb], writes=[dst_b])
        else:
            P.op("dve", lambda e: e.tensor_copy(out=dst, in_=pt[0:cols, 0:rows]), reads=[pb], writes=[dst_b])

    def load_fm(dst, dst_b, src_ap, src_name, nrow):
        m = A.mark()
        tmp, tmp_b = A.f32("lfm_tmp", 128)
        P.dma("sp", tmp[0:nrow, :], src_ap, reads=[dbuf[src_name]], writes=[tmp_b])
        transpose_f32(dst, dst_b, tmp[0:nrow, :], tmp_b, nrow, 128)
        A.release(m)
        return tmp_b

    m0 = A.mark()
    xtoks = [A.f32("xtok%d" % i, D) for i in range(2)]
    for i in range(NT):
        xt, xt_b = xtoks[i % 2]
        P.dma("sp", xt, x_in[i * 128:(i + 1) * 128, :], reads=[dbuf["x"]], writes=[xt_b])
        for k in range(KC):
            pt, pb = next_ps()
            P.op("pe", lambda e, pt=pt, xt=xt, k=k: e.transpose(out=pt[:, 0:128], in_=xt[:, k * 128:(k + 1) * 128],
                                                               identity=cv["ident"]),
                 reads=[xt_b, cst_b], writes=[pb])
            eng = "act" if k % 2 == 0 else "dve"
            if eng == "act":
                P.op("act", lambda e, pt=pt, k=k, i=i: e.copy(out=xT3[:, k, i * 128:(i + 1) * 128], in_=pt[:, 0:128]),
                     reads=[pb], writes=[xT_b])
            else:
                P.op("dve", lambda e, pt=pt, k=k, i=i: e.tensor_copy(out=xT3[:, k, i * 128:(i + 1) * 128], in_=pt[:, 0:128]),
                     reads=[pb], writes=[xT_b])
    P.barrier()
    A.release(m0)


    def ACT(out, in_, func, reads, writes, **kw):
        P.op("act", lambda e: e.activation(out=out, in_=in_, func=func, **kw), reads, writes)

    def TS(eng, out, in0, s1, op0, reads, writes, s2=None, op1=None, **kw):
        if op1 is None:
            P.op(eng, lambda e: e.tensor_scalar(out=out, in0=in0, scalar1=s1, scalar2=None, op0=op0, **kw), reads, writes)
        else:
            P.op(eng, lambda e: e.tensor_scalar(out=out, in0=in0, scalar1=s1, scalar2=s2, op0=op0, op1=op1, **kw),
                 reads, writes)

    def TT(eng, out, in0, in1, op, reads, writes):
        P.op(eng, lambda e: e.tensor_tensor(out=out, in0=in0, in1=in1, op=op), reads, writes)

    def STT(out, in0, scalar, in1, op0, op1, reads, writes):
        P.op("dve", lambda e: e.scalar_tensor_tensor(out=out, in0=in0, scalar=scalar, in1=in1, op0=op0, op1=op1),
             reads, writes)

    def MM(out, lhsT, rhs, start, stop, reads, writes):
        P.op("pe", lambda e: e.matmul(out, lhsT, rhs, start=start, stop=stop), reads, writes)

    def CP(eng, out, in_, reads, writes):
        if eng == "act":
            P.op("act", lambda e: e.copy(out=out, in_=in_), reads, writes)
        else:
            P.op(eng, lambda e: e.tensor_copy(out=out, in_=in_), reads, writes)

    def bcast_ap(ap2d_row, nparts, ncols, offset_elems=0):
        return bass.AP(ap2d_row.tensor, ap2d_row.offset + offset_elems, [[0, nparts], [1, ncols]])

    qT_d = dten("qT_d", [8 * 128, T], BF16)
    groups = [[0, 1, 2, 3], [4, 5, 6, 7]]

    class GatherSet:
        def __init__(self, name, rows, cols, dtype, esz):
            rpc = rows
            while rpc * cols * esz > (1 << 20):
                rpc //= 2
            assert rows % rpc == 0
            self.name, self.rows, self.cols, self.rpc, self.n = name, rows, cols, rpc, rows // rpc
            self.loc = [dten("%s_loc%d" % (name, c), [rpc, cols], dtype) for c in range(self.n)]
            self.g = [dten("%s_g%d" % (name, c), [GSZ * rpc, cols], dtype) for c in range(self.n)]

        def loc_rows(self, r0, r1):
            c = r0 // self.rpc
            assert (r1 - 1) // self.rpc == c
            return self.loc[c][r0 - c * self.rpc:r1 - c * self.rpc, :], "%s_loc%d" % (self.name, c)

        def g_rows(self, rank, r0, r1):
            c = r0 // self.rpc
            assert (r1 - 1) // self.rpc == c
            base = rank * self.rpc - c * self.rpc
            return self.g[c][base + r0:base + r1, :], "%s_g%d" % (self.name, c)

        def gather(self):
            for c in range(self.n):
                ln, gn = "%s_loc%d" % (self.name, c), "%s_g%d" % (self.name, c)
                P.collective(lambda e, ln=ln, gn=gn: e.collective_compute("AllGather", ALU.bypass, replica_groups=groups,
                                                                          ins=[dram[ln]], outs=[dram[gn]]),
                             reads=[dbuf[ln]], writes=[dbuf[gn]])

    kT_gs = GatherSet("kT", 8 * 128, T, BF16, 2)
    v_gs = GatherSet("v", T, 2048, BF16, 2)
    iqT_d = dten("iqT_d", [4 * 128, T], BF16)
    ik_gs = GatherSet("ik", 128, T, BF16, 2)
    zT_d = dten("zT_d", [16 * 128, T], BF16)
    xbc_raw = dten("xbc_raw", [24 * 128, T], F32)
    halo_gs = GatherSet("halo", 128, 24 * 4, F32, 4)
    gT_d = dten("gT_d", [16 * 128, T], F32)
    dbg_small = dten("dbg_small", [128, NT * 80], F32)

    rope_d = dten("rope_d", [2 * 128, T], F32)
    m0 = A.mark()
    C4, C4_b = A.f32("C4", T)
    S4, S4_b = A.f32("S4", T)
    posi, posi_b = A.f32("posi", T)
    posi_i = posi.bitcast(I32)
    ang, ang_b = A.f32("ang", T)
    t1, t1_b = A.f32("rt1", T)
    t2, t2_b = A.f32("rt2", T)
    t2_i = t2.bitcast(I32)
    P.dma("sp", posi_i, bcast_ap(pos_in, 128, T), reads=[dbuf["positions"]], writes=[posi_b])
    CP("dve", ang, posi_i, [posi_b], [ang_b])
    TS("dve", ang, ang, cv["invf"][:, 0:1], ALU.mult, [ang_b, cst_b], [ang_b])
    TWO_PI = 2.0 * np.pi
    C1 = 6.28125
    C2 = TWO_PI - C1
    for dst, dst_b, shift in ((S4, S4_b, 0.0), (C4, C4_b, np.pi / 2.0)):
        TS("dve", t1, ang, shift, ALU.add, [ang_b], [t1_b], s2=1.0 / TWO_PI, op1=ALU.mult)
        CP("dve", t2_i, t1, [t1_b], [t2_b])
        CP("dve", t1, t2_i, [t2_b], [t1_b])
        TS("dve", t2, ang, shift, ALU.add, [ang_b], [t2_b])
        STT(t2, t1, -C1, t2, ALU.mult, ALU.add, [t1_b, t2_b], [t2_b])
        STT(t2, t1, -C2, t2, ALU.mult, ALU.add, [t1_b, t2_b], [t2_b])
        TS("dve", t1, t2, float(np.pi), ALU.is_gt, [t2_b], [t1_b])
        STT(t2, t1, -TWO_PI, t2, ALU.mult, ALU.add, [t1_b, t2_b], [t2_b])
        TS("dve", t1, t2, float(-np.pi), ALU.is_lt, [t2_b], [t1_b])
        STT(t2, t1, TWO_PI, t2, ALU.mult, ALU.add, [t1_b, t2_b], [t2_b])
        TS("dve", t2, t2, float(np.pi), ALU.min, [t2_b], [t2_b], s2=float(-np.pi), op1=ALU.max)
        ACT(dst, t2, AF.Sin, [t2_b], [dst_b])
    P.dma("sp", rope_d[0:128, :], C4, reads=[C4_b], writes=[dbuf["rope_d"]])
    P.dma("sp", rope_d[128:256, :], S4, reads=[S4_b], writes=[dbuf["rope_d"]])
    P.barrier()
    A.release(m0)

    o_ = [0]

    def lp_alloc(n):
        a = o_[0]
        o_[0] += n
        assert o_[0] <= 512
        return lp[:, a:a + n]

    b_adaT = lp_alloc(48)
    modT = lp_alloc(48)
    n1gT = lp_alloc(8)
    n2gT = lp_alloc(8)
    scwT = lp_alloc(96)
    scbT = lp_alloc(24)
    sngT = lp_alloc(16)
    fcwT = lp_alloc(132)
    fcbT = lp_alloc(44)
    qg2 = lp_alloc(1)
    kg2 = lp_alloc(1)
    A1 = lp_alloc(8)
    A2 = lp_alloc(8)
    cact2 = lp_alloc(16)
    dskT = lp_alloc(16)
    cact2_3 = cact2.rearrange("p (k two) -> p k two", two=2)
    iw3 = iw_tm.rearrange("p (i h) -> p i h", h=8)
    dt3 = dt_tm.rearrange("p (i h) -> p i h", h=SH)
    a3 = a_tm.rearrange("p (i h) -> p i h", h=SH)

    m0 = A.mark()
    tmp, tmp_b = A.f32("ctmp", 128)
    P.dma("sp", tmp[0:KC, :], c_in, reads=[dbuf["c"]], writes=[tmp_b])
    pt, pb = next_ps()
    P.op("pe", lambda e, pt=pt: e.transpose(out=pt[:, 0:KC], in_=tmp[0:KC, :], identity=cv["ident"][0:KC, 0:KC]),
         reads=[tmp_b, cst_b], writes=[pb])
    ACT(cact2_3[:, :, 0], pt[:, 0:KC], AF.Silu, [pb], [lp_b])
    ACT(cact2_3[:, :, 1], pt[:, 0:KC], AF.Silu, [pb], [lp_b])
    P.barrier()
    A.release(m0)

    def load_fm_rows(dst, src_ap, src_name, nrow):
        tmp_, tmpb_ = lfm_pool.get()
        P.dma("sp", tmp_[0:nrow, :], src_ap, reads=[dbuf[src_name]], writes=[tmpb_])
        pt_, pb_ = next_ps()
        P.op("pe", lambda e: e.transpose(out=pt_[:, 0:nrow], in_=tmp_[0:nrow, :], identity=cv["ident"][0:nrow, 0:nrow]),
             reads=[tmpb_, cst_b], writes=[pb_])
        CP("dve", dst, pt_[:, 0:nrow], [pb_], [lp_b])

    def layer_params(l):
        load_fm_rows(b_adaT, W["b_ada"][l], "b_ada", 48)
        load_fm_rows(n1gT, W["norm1_g"][l], "norm1_g", KC)
        load_fm_rows(n2gT, W["norm2_g"][l], "norm2_g", KC)
        for tap in range(4):
            load_fm_rows(scwT[:, tap * 24:(tap + 1) * 24], W["ssm_conv_w"][l, tap], "ssm_conv_w", 24)
        load_fm_rows(scbT, W["ssm_conv_b"][l], "ssm_conv_b", 24)
        load_fm_rows(sngT, W["ssm_norm_g"][l], "ssm_norm_g", 16)
        for tap in range(3):
            load_fm_rows(fcwT[:, tap * 44:(tap + 1) * 44], W["ffn_conv_w"][l, tap], "ffn_conv_w", 44)
        load_fm_rows(fcbT, W["ffn_conv_b"][l], "ffn_conv_b", 44)
        for dst, nm in ((qg2, "q_norm_g"), (kg2, "k_norm_g")):
            tmp_, tmpb_ = lfm_pool.get()
            P.dma("sp", tmp_[0:1, 0:64], W[nm][l], reads=[dbuf[nm]], writes=[tmpb_])
            P.dma("sp", tmp_[0:1, 64:128], W[nm][l], reads=[dbuf[nm]], writes=[tmpb_])
            pt_, pb_ = next_ps()
            P.op("pe", lambda e, pt_=pt_, tmp_=tmp_: e.transpose(out=pt_[:, 0:1], in_=tmp_[0:1, :],
                                                                 identity=cv["ident"][0:1, 0:1]),
                 reads=[tmpb_, cst_b], writes=[pb_])
            CP("dve", dst, pt_[:, 0:1], [pb_], [lp_b])
        tmp_, tmpb_ = lfm_pool.get()
        P.dma("sp", tmp_[0:16, 0:2], W["d_skip"][l].rearrange("o (c two) -> (o c) two", two=2),
              reads=[dbuf["d_skip"]], writes=[tmpb_])
        pt_, pb_ = next_ps()
        P.op("pe", lambda e: e.transpose(out=pt_[0:2, 0:16], in_=tmp_[0:16, 0:2], identity=cv["ident"][0:16, 0:16]),
             reads=[tmpb_, cst_b], writes=[pb_])
        tmp2_, tmp2b_ = lfm_pool.get()
        CP("dve", tmp2_[0:2, 0:16], pt_[0:2, 0:16], [pb_], [tmp2b_])
        pt2_, pb2_ = next_ps()
        MM(pt2_[:, 0:16], halfsel[0:2, :], tmp2_[0:2, 0:16], True, True, [tmp2b_, halfsel_b], [pb2_])
        CP("dve", dskT, pt2_[:, 0:16], [pb2_], [lp_b])
        P.dma("sp", dtb_bc, bcast_ap(W["dt_bias"][l], 128, SH), reads=[dbuf["dt_bias"]], writes=[dtb_b])
        P.dma("sp", alog_bc, bcast_ap(W["a_log"][l], 128, SH), reads=[dbuf["a_log"]], writes=[alog_b])
        ACT(alog_bc, alog_bc, AF.Exp, [alog_b], [alog_b])

    P.dma("sp", halfsel[0:1, :], consts_in[0:1, 256:384], reads=[dbuf["consts"]], writes=[halfsel_b])
    P.dma("sp", halfsel[1:2, :], consts_in[64:65, 256:384], reads=[dbuf["consts"]], writes=[halfsel_b])


    cast_rr = [0]

    def load_w(wname, l, col0, ncols, kc, row0=0, to_bf16=True, dst=None, pools=None):
        assert kc * ncols <= WSTG
        st, st_b = (pools[0] if pools else wst_pool).get()
        st3 = st[:, 0:kc * ncols].rearrange("p (k c) -> p k c", c=ncols)
        src = W[wname][l][row0:row0 + kc * 128, col0:col0 + ncols].rearrange("(k p) c -> p k c", p=128)
        P.dma("sp", st3, src, reads=[dbuf[wname]], writes=[st_b])
        if not to_bf16:
            return st3, st_b
        if dst is not None:
            CP("pool", dst[0], st3, [st_b], [dst[1]])
            return dst
        wb, wb_b = (pools[1] if pools else wbf_pool).get()
        wb3 = wb[:, 0:kc * ncols].rearrange("p (k c) -> p k c", c=ncols)
        CP("pool", wb[:, 0:kc * ncols], st[:, 0:kc * ncols], [st_b], [wb_b])
        return wb3, wb_b

    def load_w_resident(name, wname, l, kc, ncols):
        wr, wr_b = A.bf16(name, kc * ncols)
        wr3 = wr.rearrange("p (k c) -> p k c", c=ncols)
        kstep = max(1, WSTG // ncols) if ncols <= WSTG else 1
        cstep = min(ncols, WSTG)
        for k0 in range(0, kc, kstep):
            kk = min(kstep, kc - k0)
            for c0 in range(0, ncols, cstep):
                cc = min(cstep, ncols - c0)
                load_w(wname, l, c0, cc, kk, row0=k0 * 128, dst=(wr3[:, k0:k0 + kk, c0:c0 + cc], wr_b))
        return wr3, wr_b


    def store(dst_ap, dname, src_ap, src_b):
        P.dma("pool", dst_ap, src_ap, reads=[src_b], writes=[dbuf[dname]])

    def compute_mod(l):
        for cb in range(24):
            w3, w_b = load_w("w_ada", l, cb * 256, 256, KC, to_bf16=False)
            pt, pb = next_ps()
            for m in range(2):
                for k in range(KC):
                    MM(pt[:, 2 * m:2 * m + 2], w3[:, k, m * 128:(m + 1) * 128], cact2_3[:, k, :], k == 0, k == KC - 1,
                       [w_b, lp_b], [pb])
            ptv = pt[:, 0:4].rearrange("p (m two) -> p m two", two=2)[:, :, 0]
            TT("dve", modT[:, cb * 2:cb * 2 + 2], ptv, b_adaT[:, cb * 2:cb * 2 + 2], ALU.add, [pb, lp_b], [lp_b])
        STT(A1, modT[:, 8:16], 1.0, n1gT, ALU.add, ALU.mult, [lp_b], [lp_b])
        STT(A2, modT[:, 32:40], 1.0, n2gT, ALU.add, ALU.mult, [lp_b], [lp_b])

    def norm_mod(Avec, Bvec):
        for tb in range(NB):
            sl = slice(tb * 512, (tb + 1) * 512)
            pt, pb = next_ps()
            for k in range(KC):
                sq, sq_b = stg_pool.get()
                ACT(sq, xT3[:, k, sl], AF.Square, [xT_b], [sq_b])
                MM(pt, cv["ones"], sq, k == 0, k == KC - 1, [sq_b, cst_b], [pb])
            rs, rs_b = stg_pool.get()
            ACT(rs, pt, AF.Sqrt, [pb], [rs_b], bias=epsT[:, 0:1], scale=1.0 / D)
            P.op("dve", lambda e, rs=rs: e.reciprocal(out=rs, in_=rs), [rs_b], [rs_b])
            for k in range(KC):
                tq, tq_b = stg_pool.get()
                TT("dve", tq, xT3[:, k, sl], rs, ALU.mult, [xT_b, rs_b], [tq_b])
                TS("dve", ns.hT3[:, k, sl], tq, Avec[:, k:k + 1], ALU.mult, [tq_b, lp_b], [ns.hT_b],
                   s2=Bvec[:, k:k + 1], op1=ALU.add)

    P.op("dve", lambda e: e.memset(epsT, EPS), [], [epsT_b])

    def proj_fm(wname, l, col0, ncols, rhs3, rhs_b, kc, epilogue, sub0=0, row0=0, blk=512):
        per = max(128, (WSTG // kc) // 128 * 128)
        per = min(per, blk)
        c = 0
        while c < ncols:
            n = min(per, ncols - c)
            w3, w_b = load_w(wname, l, col0 + c, n, kc, row0=row0)
            for m in range(n // 128):
                for tb in range(NB):
                    pt, pb = next_ps()
                    for k in range(kc):
                        MM(pt, w3[:, k, m * 128:(m + 1) * 128], rhs3[:, k, tb * 512:(tb + 1) * 512], k == 0, k == kc - 1,
                           [w_b, rhs_b], [pb])
                    epilogue(pt, pb, sub0 + (c // 128) + m, tb)
            c += n

    def dst_rows(dname, r0, r1):
        if isinstance(dname, GatherSet):
            return dname.loc_rows(r0, r1)
        return dram[dname][r0:r1, :], dname

    def rope_epilogue(src_sb, src_b, tb, dst_dram, dname, row):
        sl = slice(tb * 512, (tb + 1) * 512)
        pr, prb = next_ps()
        MM(pr, cv["rrot"], src_sb, True, True, [src_b, cst_b], [prb])
        u1, u1_b = stg_pool.get()
        TT("pool", u1, src_sb, ns.C4[:, sl], ALU.mult, [src_b, ns.C4_b], [u1_b])
        u2, u2_b = stg_pool.get()
        TT("dve", u2, pr, ns.S4[:, sl], ALU.mult, [prb, ns.S4_b], [u2_b])
        ob, ob_b = sbf_pool.get()
        TT("dve", ob, u1, u2, ALU.add, [u1_b, u2_b], [ob_b])
        dap, dn = dst_rows(dname, row * 128, (row + 1) * 128)
        store(dap[:, sl], dn, ob, ob_b)

    def qk_epilogue(gvec, dname):
        def ep(pt, pb, sub, tb):
            sq, sq_b = stg_pool.get()
            ACT(sq, pt, AF.Square, [pb], [sq_b])
            p2, p2b = next_ps()
            MM(p2, cv["blockones"], sq, True, True, [sq_b, cst_b], [p2b])
            rs, rs_b = stg_pool.get()
            ACT(rs, p2, AF.Sqrt, [p2b], [rs_b], bias=epsT[:, 0:1], scale=1.0 / HD)
            P.op("dve", lambda e, rs=rs: e.reciprocal(out=rs, in_=rs), [rs_b], [rs_b])
            qn, qn_b = stg_pool.get()
            STT(qn, pt, gvec, rs, ALU.mult, ALU.mult, [pb, lp_b, rs_b], [qn_b])
            rope_epilogue(qn, qn_b, tb, None, dname, sub)
        return ep

    def iq_epilogue(dname, nsub_real):
        def ep(pt, pb, sub, tb):
            qn, qn_b = stg_pool.get()
            CP("act", qn, pt, [pb], [qn_b])
            rope_epilogue(qn, qn_b, tb, None, dname, sub)
        return ep

    def z_epilogue(pt, pb, sub, tb):
        ob, ob_b = sbf_pool.get()
        ACT(ob, pt, AF.Silu, [pb], [ob_b])
        store(zT_d[sub * 128:(sub + 1) * 128, tb * 512:(tb + 1) * 512], "zT_d", ob, ob_b)

    def xbc_epilogue(pt, pb, sub, tb):
        o32, o32_b = stg_pool.get()
        CP("act" if (sub + tb) % 2 == 0 else "dve", o32, pt, [pb], [o32_b])
        store(xbc_raw[sub * 128:(sub + 1) * 128, tb * 512:(tb + 1) * 512], "xbc_raw", o32, o32_b)
        if tb == NB - 1:
            dap, dn = halo_gs.loc_rows(0, 128)
            store(dap[:, sub * 4:sub * 4 + 3], dn, o32[:, 509:512], o32_b)

    def gate_epilogue(pt, pb, sub, tb):
        o32, o32_b = stg_pool.get()
        ACT(o32, pt, AF.Sigmoid, [pb], [o32_b])
        store(gT_d[sub * 128:(sub + 1) * 128, tb * 512:(tb + 1) * 512], "gT_d", o32, o32_b)


    def v_projection(l):
        for qd in range(4):
            w3, w_b = load_w("w_in", l, C_V + qd * 256, 256, KC)
            for i in range(NT):
                va, va_b = ns.vaug[i % 2]
                va4 = va.rearrange("p (pr two c) -> p pr two c", two=2, c=128)
                pt, pb = next_ps()
                for k in range(KC):
                    MM(pt[:, 0:256], ns.hT3[:, k, i * 128:(i + 1) * 128], w3[:, k, :], k == 0, k == KC - 1, [ns.hT_b, w_b], [pb])
                pt4 = pt[:, 0:256].rearrange("p (pr two c) -> p pr two c", two=2, c=64)
                ev_ = "act" if i % 2 == 0 else "dve"
                CP(ev_, va4[:, :, 0, 0:64], pt4[:, :, 0, :], [pb], [va_b])
                CP(ev_, va4[:, :, 1, 64:128], pt4[:, :, 1, :], [pb], [va_b])
                dap, dn = v_gs.loc_rows(i * 128, (i + 1) * 128)
                store(dap[:, qd * 512:(qd + 1) * 512], dn, va, va_b)

    def small_projection(l):
        st, st_b = wst_pool.get()
        st3 = st[:, 0:KC * 40].rearrange("p (k c) -> p k c", c=40)
        P.dma("sp", st3[:, :, 0:32], W["w_in"][l][:, C_DT:C_DT + 32].rearrange("(k p) c -> p k c", p=128),
              reads=[dbuf["w_in"]], writes=[st_b])
        P.dma("sp", st3[:, :, 32:40], W["w_in"][l][:, C_IW:C_IW + 8].rearrange("(k p) c -> p k c", p=128),
              reads=[dbuf["w_in"]], writes=[st_b])
        wb, wb_b = wbf_pool.get()
        wb3 = wb[:, 0:KC * 40].rearrange("p (k c) -> p k c", c=40)
        CP("pool", wb[:, 0:KC * 40], st[:, 0:KC * 40], [st_b], [wb_b])
        for i in range(NT):
            pt, pb = next_ps()
            for k in range(KC):
                MM(pt[:, 0:40], ns.hT3[:, k, i * 128:(i + 1) * 128], wb3[:, k, :], k == 0, k == KC - 1, [ns.hT_b, wb_b], [pb])
            xx, xx_b = stg_pool.get()
            CP("dve", xx[:, 64:104], pt[:, 0:40], [pb], [xx_b])
            CP("dve", iw3[:, i, :], xx[:, 96:104], [xx_b], [iw_b])
            TT("dve", xx[:, 0:32], xx[:, 64:96], dtb_bc, ALU.add, [xx_b, dtb_b], [xx_b])
            STT(xx[:, 32:64], xx[:, 0:32], -1.0, xx[:, 0:32], ALU.mult, ALU.max, [xx_b], [xx_b])
            ACT(xx[:, 32:64], xx[:, 32:64], AF.Exp, [xx_b], [xx_b], scale=-1.0)
            ACT(xx[:, 32:64], xx[:, 32:64], AF.Ln, [xx_b], [xx_b], bias=oneT[:, 0:1], scale=1.0)
            STT(dt3[:, i, :], xx[:, 0:32], 0.0, xx[:, 32:64], ALU.max, ALU.add, [xx_b], [dt_b])
            STT(a3[:, i, :], dt3[:, i, :], -1.0, alog_bc, ALU.mult, ALU.mult, [dt_b, alog_b], [a_b])

    P.op("dve", lambda e: e.memset(oneT, 1.0), [], [oneT_b])

    def alloc_hT():
        hT, ns.hT_b = A.bf16("hT", KC * T)
        ns.hT3 = hT.rearrange("p (k t) -> p k t", t=T)

    def phase_A(l):
        if cfg.stage < 1:
            return
        alloc_hT()
        ns.C4, ns.C4_b = A.f32("C4", T)
        ns.S4, ns.S4_b = A.f32("S4", T)
        P.dma("sp", ns.C4, rope_d[0:128, :], reads=[dbuf["rope_d"]], writes=[ns.C4_b])
        P.dma("sp", ns.S4, rope_d[128:256, :], reads=[dbuf["rope_d"]], writes=[ns.S4_b])
        ns.vaug = [A.bf16("vaug%d" % i, 512) for i in range(2)]
        for va, va_b in ns.vaug:
            P.op("pool", lambda e, va=va: e.memset(va, 1.0), [], [va_b])
        layer_params(l)
        if cfg.stage < 2:
            return
        compute_mod(l)
        if cfg.stage < 3:
            return
        norm_mod(A1, modT[:, 0:8])
        if cfg.stage < 4:
            return
        proj_fm("w_in", l, C_Q, 1024, ns.hT3, ns.hT_b, KC, qk_epilogue(qg2[:, 0:1], "qT_d"))
        if cfg.stage < 5:
            return
        proj_fm("w_in", l, C_K, 1024, ns.hT3, ns.hT_b, KC, qk_epilogue(kg2[:, 0:1], kT_gs))
        v_projection(l)
        proj_fm("w_in", l, C_IQ, 512, ns.hT3, ns.hT_b, KC, iq_epilogue("iqT_d", 4))
        st, st_b = wst_pool.get()
        st3 = st[:, 0:KC * 128].rearrange("p (k c) -> p k c", c=128)
        for hf in range(2):
            P.dma("sp", st3[:, :, hf * 64:(hf + 1) * 64],
                  W["w_in"][l][:, C_IK:C_IK + 64].rearrange("(k p) c -> p k c", p=128),
                  reads=[dbuf["w_in"]], writes=[st_b])
        wb, wb_b = wbf_pool.get()
        wb3 = wb[:, 0:KC * 128].rearrange("p (k c) -> p k c", c=128)
        CP("pool", wb[:, 0:KC * 128], st[:, 0:KC * 128], [st_b], [wb_b])
        ikep = iq_epilogue(ik_gs, 1)
        for tb in range(NB):
            pt, pb = next_ps()
            for k in range(KC):
                MM(pt, wb3[:, k, :], ns.hT3[:, k, tb * 512:(tb + 1) * 512], k == 0, k == KC - 1, [wb_b, ns.hT_b], [pb])
            ikep(pt, pb, 0, tb)
        if cfg.stage < 6:
            return
        small_projection(l)
        if cfg.stage < 7:
            return
        proj_fm("w_in", l, C_Z, 2048, ns.hT3, ns.hT_b, KC, z_epilogue)
        proj_fm("w_in", l, C_XBC, 3072, ns.hT3, ns.hT_b, KC, xbc_epilogue)
        proj_fm("w_in", l, C_GA, 2048, ns.hT3, ns.hT_b, KC, gate_epilogue)
        if "dbg_small" in cfg.debug:
            dbg3 = dbg_small.rearrange("p (i c) -> p i c", c=80)
            store(dbg3[:, :, 0:8], "dbg_small", iw3, iw_b)
            store(dbg3[:, :, 8:40], "dbg_small", dt3, dt_b)
            store(dbg3[:, :, 40:72], "dbg_small", a3, a_b)
        if cfg.stage < 8:
            return
        kT_gs.gather()
        v_gs.gather()
        ik_gs.gather()
        halo_gs.gather()


    attnT_d = dten("attnT_d", [8 * 128, T], BF16, "ExternalInput" if "attnT_d" in cfg.feed else "Internal")
    ynT_d = dten("ynT_d", [16 * 128, T], BF16, "ExternalInput" if "ynT_d" in cfg.feed else "Internal")
    aT_d = dten("aT_d", [22 * 128, T], BF16)
    uhalo_gs = GatherSet("uhalo", 128, 44 * 2, F32, 4)

    mixT_d = dten("mixT_d", [8 * 128, T], BF16)

    def phase_D(l):
        m0_ = A.mark()
        wao3, wao_b = load_w_resident("wao", "w_attn_o", l, 8, D)
        wso3, wso_b = load_w_resident("wso", "w_ssm_o", l, 16, D)
        at, at_b = A.bf16("attn_blk", 8 * 512)
        at3 = at.rearrange("p (k t) -> p k t", t=512)
        yn, yn_b = A.bf16("yn_blk", 16 * 512)
        yn3 = yn.rearrange("p (k t) -> p k t", t=512)
        gpool = RPool("gate", 4, 512, "f32")
        for tb in range(NB):
            sl = slice(tb * 512, (tb + 1) * 512)
            P.dma("sp", at3, attnT_d.rearrange("(k p) t -> p k t", p=128)[:, :, sl], reads=[dbuf["attnT_d"]], writes=[at_b])
            P.dma("sp", yn3, ynT_d.rearrange("(k p) t -> p k t", p=128)[:, :, sl], reads=[dbuf["ynT_d"]], writes=[yn_b])
            for m in range(8):
                ga, ga_b = gpool.get()
                gm_, gm_b = gpool.get()
                P.dma("sp", ga, gT_d[m * 128:(m + 1) * 128, sl], reads=[dbuf["gT_d"]], writes=[ga_b])
                P.dma("sp", gm_, gT_d[(8 + m) * 128:(9 + m) * 128, sl], reads=[dbuf["gT_d"]], writes=[gm_b])
                pa, pab = next_ps()
                for k in range(8):
                    MM(pa, wao3[:, k, m * 128:(m + 1) * 128], at3[:, k, :], k == 0, k == 7, [wao_b, at_b], [pab])
                psm, psb = next_ps()
                for k in range(16):
                    MM(psm, wso3[:, k, m * 128:(m + 1) * 128], yn3[:, k, :], k == 0, k == 15, [wso_b, yn_b], [psb])
                t1_, t1b_ = stg_pool.get()
                TT("dve", t1_, pa, ga, ALU.mult, [pab, ga_b], [t1b_])
                t2_, t2b_ = stg_pool.get()
                TT("dve", t2_, psm, gm_, ALU.mult, [psb, gm_b], [t2b_])
                ob, ob_b = sbf_pool.get()
                TT("pool", ob, t1_, t2_, ALU.add, [t1b_, t2b_], [ob_b])
                store(mixT_d[m * 128:(m + 1) * 128, sl], "mixT_d", ob, ob_b)
        P.barrier()
        A.release(m0_)
        m0_ = A.mark()
        wout3, wout_b = load_w_resident("wout", "w_out", l, 8, D)
        mxs = [A.bf16("mix_blk%d" % i, 8 * 512) for i in range(2)]
        for tb in range(NB):
            sl = slice(tb * 512, (tb + 1) * 512)
            mx, mx_b = mxs[tb % 2]
            mx3 = mx.rearrange("p (k t) -> p k t", t=512)
            P.dma("sp", mx3, mixT_d.rearrange("(k p) t -> p k t", p=128)[:, :, sl], reads=[dbuf["mixT_d"]], writes=[mx_b])
            for m2 in range(8):
                po, pob = next_ps()
                for k in range(8):
                    MM(po, wout3[:, k, m2 * 128:(m2 + 1) * 128], mx3[:, k, :], k == 0, k == 7, [wout_b, mx_b], [pob])
                STT(xT3[:, m2, sl], po, modT[:, 16 + m2:17 + m2], xT3[:, m2, sl], ALU.mult, ALU.add, [pob, lp_b, xT_b], [xT_b])
        P.barrier()
        A.release(m0_)

    def phase_EF(l, rank_sel):
        m0_ = A.mark()
        alloc_hT()
        norm_mod(A2, modT[:, 24:32])
        uh, uh_b = A.f32("uhalo_sb", 44 * 2)
        uh3 = uh.rearrange("p (m two) -> p m two", two=2)
        for c0 in range(0, 2 * DFF, 256):
            w3, w_b = load_w("w_up", l, c0, 256, KC)
            pt, pb = next_ps()
            for m in range(2):
                for k in range(KC):
                    MM(pt[:, 2 * m:2 * m + 2], w3[:, k, m * 128:(m + 1) * 128], ns.hT3[:, k, T - 2:T], k == 0, k == KC - 1,
                       [w_b, ns.hT_b], [pb])
            CP("dve", uh3[:, (c0 // 128):(c0 // 128) + 2, :], pt[:, 0:4].rearrange("p (m two) -> p m two", two=2), [pb], [uh_b])
        dap, dn = uhalo_gs.loc_rows(0, 128)
        store(dap, dn, uh, uh_b)
        uhalo_gs.gather()
        pv, pv_b = A.f32("uprev", 44 * 2)
        pv3 = pv.rearrange("p (m two) -> p m two", two=2)
        load_prev_rows(uhalo_gs, pv3, pv_b, 44, 2)
        ub = [A.f32("ubuf%d" % i, 516) for i in range(4)]
        efp = (RPool("wstL", 4, KC * 128, "f32"), RPool("wbfL", 4, KC * 128, "bf16"))
        for m in range(22):
            wv3, wv_b = load_w("w_up", l, m * 128, 128, KC, pools=efp)
            wg3, wg_b = load_w("w_up", l, DFF + m * 128, 128, KC, pools=efp)
            conv_out = []
            for tb in range(NB):
                sl = slice(tb * 512, (tb + 1) * 512)
                res = []
                for which, (w3, w_b, sub) in enumerate(((wv3, wv_b, m), (wg3, wg_b, 22 + m))):
                    pt, pb = next_ps()
                    for k in range(KC):
                        MM(pt, w3[:, k, :], ns.hT3[:, k, sl], k == 0, k == KC - 1, [w_b, ns.hT_b], [pb])
                    u, u_b = ub[(tb % 2) * 2 + which]
                    if tb == 0:
                        CP("dve", u[:, 0:2], pv3[:, sub, :], [pv_b], [u_b])
                    else:
                        up, up_b = ub[((tb - 1) % 2) * 2 + which]
                        CP("dve", u[:, 0:2], up[:, 512:514], [up_b], [u_b])
                    CP("act", u[:, 2:514], pt, [pb], [u_b])
                    c_, c_b = stg_pool.get()
                    TS("dve", c_, u[:, 0:512], fcwT[:, sub:sub + 1], ALU.mult, [u_b, lp_b], [c_b],
                       s2=fcbT[:, sub:sub + 1], op1=ALU.add)
                    STT(c_, u[:, 1:513], fcwT[:, 44 + sub:45 + sub], c_, ALU.mult, ALU.add, [u_b, lp_b, c_b], [c_b])
                    STT(c_, u[:, 2:514], fcwT[:, 88 + sub:89 + sub], c_, ALU.mult, ALU.add, [u_b, lp_b, c_b], [c_b])
                    res.append((c_, c_b))
                (cv_, cvb_), (cg_, cgb_) = res
                sg, sg_b = stg_pool.get()
                ACT(sg, cg_, AF.Silu, [cgb_], [sg_b])
                ob, ob_b = sbf_pool.get()
                TT("pool", ob, sg, cv_, ALU.mult, [sg_b, cvb_], [ob_b])
                store(aT_d[m * 128:(m + 1) * 128, sl], "aT_d", ob, ob_b)
        P.barrier()
        A.release(m0_)
        m0_ = A.mark()
        wd3, wd_b = load_w_resident("wdown", "w_down", l, 22, D)
        ab = [A.bf16("a_blk%d" % i, 22 * 512) for i in range(1)]
        for tb in range(NB):
            sl = slice(tb * 512, (tb + 1) * 512)
            a_, a_b_ = ab[0]
            a3_ = a_.rearrange("p (k t) -> p k t", t=512)
            P.dma("sp", a3_, aT_d.rearrange("(k p) t -> p k t", p=128)[:, :, sl], reads=[dbuf["aT_d"]], writes=[a_b_])
            for m2 in range(8):
                po, pob = next_ps()
                for k in range(22):
                    MM(po, wd3[:, k, m2 * 128:(m2 + 1) * 128], a3_[:, k, :], k == 0, k == 21, [wd_b, a_b_], [pob])
                STT(xT3[:, m2, sl], po, modT[:, 40 + m2:41 + m2], xT3[:, m2, sl], ALU.mult, ALU.add, [pob, lp_b, xT_b], [xT_b])
        P.barrier()
        A.release(m0_)


    x_save = dten("x_save", [8 * 128, T], F32)

    def spill_x():
        P.dma("sp", x_save.rearrange("(k p) t -> p k t", p=128), xT3, reads=[xT_b], writes=[dbuf["x_save"]])
        P.barrier()

    def restore_x():
        P.barrier()
        P.dma("sp", xT3, x_save.rearrange("(k p) t -> p k t", p=128), reads=[dbuf["x_save"]], writes=[xT_b])

    NKEY = GSZ * T
    NKT = NKEY // 128
    NKB = NKEY // 512
    NIT = 20
    LO0 = -512.0
    BIGP = float(2.0 ** 100)
    NSPL = (int(NKEY * 0.41) // 512) * 512
    NACT = NKEY - NSPL
    SCALE = float(HD ** -0.5)

    def phase_B(l):
        m0_ = A.mark()
        score, score_b = A.f32("score", NKEY)
        mb, mbA_b = A.bf16("mb", NKEY)
        mbB_b = Buf("mbB")
        mbT_raw, mbT_b = A.f32("mbT", NKT * 512 // 4)
        mbT = mbT_raw.bitcast(FP8)
        mbT3 = mbT.rearrange("p (k t) -> p k t", t=512)
        ikT, ikT_b = A.bf16("ikT", NKEY)
        ikT3 = ikT.rearrange("p (r t) -> p r t", t=T)
        iqb, iqb_b = A.bf16("iq_blk", 4 * 512)
        iqb3 = iqb.rearrange("p (k t) -> p k t", t=512)
        qb_, qb_b = A.bf16("q_blk", 8 * 512)
        qb3 = qb_.rearrange("p (k t) -> p k t", t=512)
        kst = [A.bf16("kst%d" % i, 2 * 512) for i in range(3)]
        vst = [A.bf16("vst%d" % i, 4 * 512) for i in range(3)]
        ppool = RPool("pT", 4, 512, "bf16")
        ab_, ab_b = A.bf16("attn_o_blk", 2 * 512)
        ab3 = ab_.rearrange("p (k t) -> p k t", t=512)
        negd0, negd0_b = A.f32("negd0", 512)
        dg, dg_b = A.bf16("diagw", IH * 128)
        dg3 = dg.rearrange("p (h c) -> p h c", c=128)
        rlp = RPool("relu", 8, 256, "f32")
        rls = []
        rel_ctr = [0]
        sm, sm_b = A.f32("bis_small", 16)
        md, md_b = A.f32("bis_mid", 2)
        cn, cnt_b = A.f32("bis_cnt", 2)
        sa, sacc_b = A.f32("bis_sacc", 2)
        Wk, Wk_b = A.f32("bis_w", NIT)
        lo = sm[:, 0:1]
        mid = md[:, 0:1]
        nmid = md[:, 1:2]
        cnt = cn[:, 0:1]
        sacc = sa[:, 0:1]
        tot = sm[:, 5:6]
        ge = sm[:, 6:7]
        hi0 = sm[:, 7:8]
        qp0 = sm[:, 8:9]
        STT(qp0, rk[:, 0:1], float(T), cv["pidx"][:, 0:1], ALU.mult, ALU.add, [rk_b, cst_b], [sm_b])
        TS("dve", negd0, cv["iotaf"], qp0, ALU.subtract, [cst_b, sm_b], [negd0_b], s2=-BIGP, op1=ALU.mult)
        P.dma("sp", ikT3, ik_gs.g[0].rearrange("(r p) t -> p r t", p=128), reads=[dbuf["ik_g0"]], writes=[ikT_b])
        for qb in range(NB):
            qsl = slice(qb * 512, (qb + 1) * 512)
            kbnd = min(NKEY, (GSZ - 1) * T + (qb + 1) * 512)
            nkb_q = kbnd // 512
            nkt_q = kbnd // 128
            nspl_q = max(512, (int(kbnd * 0.41) // 512) * 512)
            nact_q = kbnd - nspl_q
            P.dma("sp", iqb3, iqT_d.rearrange("(k p) t -> p k t", p=128)[:, :, qsl], reads=[dbuf["iqT_d"]], writes=[iqb_b])
            P.dma("sp", qb3, qT_d.rearrange("(k p) t -> p k t", p=128)[:, :, qsl], reads=[dbuf["qT_d"]], writes=[qb_b])
            for qs in range(4):
                qt = qb * 4 + qs
                for h in range(IH):
                    TS("dve", dg3[:, h, :], ident_bf, iw3[:, qt, h:h + 1], ALU.mult, [ident_bf_b, iw_b], [dg_b])
                for kb in range(nkb_q):
                    ksl = slice(kb * 512, (kb + 1) * 512)
                    madd, madd_b = stg_pool.get()
                    TS("dve", madd, negd0, float((kb * 512 - qt * 128) * (-BIGP)), ALU.add, [negd0_b], [madd_b],
                       s2=0.0, op1=ALU.min)
                    lg = []
                    for h in range(IH):
                        hb = (h % 2) * 64
                        pt, pb = next_ps()
                        MM(pt, iqb3[hb:hb + 64, h // 2, qs * 128:(qs + 1) * 128], ikT[hb:hb + 64, ksl], True, True,
                           [iqb_b, ikT_b], [pb])
                        lg.append((pt, pb))
                        if len(lg) == 4 or h == IH - 1:
                            for (pt_, pb_) in lg:
                                rl32, rl_b = rlp.get()
                                rl = rl32.bitcast(BF16)[:, 0:512]
                                hh_ = rel_ctr[0]
                                rel_ctr[0] += 1
                                if hh_ % 2 == 0:
                                    ACT(rl, pt_, AF.Relu, [pb_], [rl_b])
                                else:
                                    TS("dve", rl, pt_, 0.0, ALU.max, [pb_], [rl_b])
                                rls.append((rl, rl_b))
                            lg = []
                    pacc, paccb = next_ps()
                    for h in range(IH):
                        rl, rl_b = rls[h]
                        MM(pacc, dg3[:, h, :], rl, h == 0, h == IH - 1, [dg_b, rl_b], [paccb])
                    del rls[:]
                    TT("dve", score[:, ksl], pacc, madd, ALU.add, [paccb, madd_b], [score_b])
                P.op("dve", lambda e, kbnd=kbnd: e.tensor_reduce(out=hi0, in_=score[:, 0:kbnd], axis=AX.X, op=ALU.max),
                     [score_b], [sm_b])
                TS("dve", tot, hi0, 1.0 - LO0, ALU.add, [sm_b], [sm_b])
                for k in range(NIT):
                    TS("dve", Wk[:, k:k + 1], tot, float(2.0 ** -(k + 1)), ALU.mult, [sm_b], [Wk_b])
                TS("dve", mid, Wk[:, 0:1], LO0, ALU.add, [Wk_b], [md_b])
                kthr = float(cfg.nkeep - nact_q / 2.0)
                for k in range(NIT):
                    ACT(mb[:, nspl_q:kbnd], score[:, nspl_q:kbnd], AF.Sign, [score_b, md_b], [mbB_b, sacc_b], bias=mid, scale=-1.0,
                        accum_out=sacc)
                    TS("dve", mb[:, 0:nspl_q], score[:, 0:nspl_q], mid, ALU.is_ge, [score_b, md_b], [mbA_b, cnt_b],
                       s2=0.0, op1=ALU.add, accum_out=cnt)
                    STT(tot, sacc, -0.5, cnt, ALU.mult, ALU.add, [sacc_b, cnt_b], [sm_b])
                    STT(ge, tot, kthr, Wk[:, k:k + 1], ALU.is_ge, ALU.mult, [sm_b, Wk_b], [sm_b])
                    if k < NIT - 1:
                        STT(mid, ge, Wk[:, k + 1:k + 2], mid, ALU.subtract, ALU.add, [sm_b, Wk_b, md_b], [md_b])
                    else:
                        STT(lo, ge, Wk[:, k:k + 1], mid, ALU.subtract, ALU.add, [sm_b, Wk_b, md_b], [sm_b])
                TS("dve", mb[:, 0:kbnd], score[:, 0:kbnd], lo, ALU.is_ge, [score_b, sm_b], [mbA_b, mbB_b])
                for k4 in range(nkt_q // 4):
                    pt, pb = next_ps()
                    ptb = pt.bitcast(BF16)
                    for u in range(4):
                        kt = k4 * 4 + u
                        P.op("pe", lambda e, ptb=ptb, u=u, kt=kt: e.transpose(out=ptb[:, u * 128:(u + 1) * 128],
                                                                             in_=mb[:, kt * 128:(kt + 1) * 128],
                                                                             identity=ident_bf),
                             [mbA_b, mbB_b, ident_bf_b], [pb])
                    dst_ = mbT3[:, k4 * 4:(k4 + 1) * 4, qs * 128:(qs + 1) * 128]
                    src_ = ptb[:, 0:512].rearrange("p (u t) -> p u t", t=128)
                    if k4 % 2 == 0:
                        P.op("act", lambda e, dst_=dst_, src_=src_: e.activation(out=dst_, in_=src_, func=AF.Copy, saturate=False),
                             [pb], [mbT_b])
                    else:
                        P.op("dve", lambda e, dst_=dst_, src_=src_: e.tensor_copy(out=dst_, in_=src_, saturate=False),
                             [pb], [mbT_b])
            for hg in range(4):
                accs = [psum[i_] for i_ in range(4)]
                nk4 = nkt_q // 4
                bufs = {}

                def issue_kv(k4, hg=hg, bufs=bufs):
                    kk, kk_b = kst[k4 % 3]
                    kk3 = kk.rearrange("p (pr t) -> p pr t", t=512)
                    vv, vv_b = vst[k4 % 3]
                    vv3 = vv.rearrange("p (u c) -> p u c", c=512)
                    r_ = (k4 * 512) // T
                    off = (k4 * 512) % T
                    for pr in range(2):
                        gap, gn = kT_gs.g_rows(r_, (hg * 2 + pr) * 128, (hg * 2 + pr + 1) * 128)
                        P.dma("sp", kk3[:, pr, :], gap[:, off:off + 512], reads=[dbuf[gn]], writes=[kk_b])
                    for u in range(4):
                        gap, gn = v_gs.g_rows(r_, off + u * 128, off + (u + 1) * 128)
                        P.dma("sp", vv3[:, u, :], gap[:, hg * 512:(hg + 1) * 512], reads=[dbuf[gn]], writes=[vv_b])
                    bufs[k4] = (kk3, kk_b, vv3, vv_b)

                steps = [(k4, u, hh) for k4 in range(nk4) for u in range(4) for hh in range(4)]
                LA = 3
                pend = {}
                issue_kv(0)
                for si in range(len(steps) + LA):
                    if si < len(steps):
                        k4, u, hh = steps[si]
                        if u == 0 and hh == 0 and k4 + 1 < nk4:
                            issue_kv(k4 + 1)
                        kk3, kk_b, vv3, vv_b = bufs[k4]
                        kt = k4 * 4 + u
                        h = hg * 4 + hh
                        hb = (h % 2) * 64
                        pt, pb = next_ps_s()
                        MM(pt, kk3[hb:hb + 64, hh // 2, u * 128:(u + 1) * 128], qb3[hb:hb + 64, h // 2, :], True, True,
                           [kk_b, qb_b], [pb])
                        pend[si] = (pt, pb)
                    ti = si - LA
                    if ti >= 0:
                        k4, u, hh = steps[ti]
                        kk3, kk_b, vv3, vv_b = bufs[k4]
                        kt = k4 * 4 + u
                        pt, pb = pend.pop(ti)
                        pT, pT_b = ppool.get()
                        ACT(pT, pt, AF.Exp, [pb], [pT_b], scale=SCALE)
                        TT("dve", pT, pT, mbT3[:, kt, :], ALU.mult, [pT_b, mbT_b], [pT_b])
                        acc, acc_b = accs[hh]
                        MM(acc, vv3[:, u, hh * 128:(hh + 1) * 128], pT, kt == 0, kt == nkt_q - 1, [vv_b, pT_b], [acc_b])
                for hh in range(4):
                    h = hg * 4 + hh
                    hb = (h % 2) * 64
                    acc, acc_b = accs[hh]
                    osb, osb_b = stg_pool.get()
                    CP("act", osb, acc, [acc_b], [osb_b])
                    pw, pw_b = next_ps_s()
                    MM(pw, cv["swap"], osb, True, True, [cst_b, osb_b], [pw_b])
                    rc, rc_b = stg_pool.get()
                    P.op("dve", lambda e, rc=rc, pw=pw: e.reciprocal(out=rc, in_=pw), [pw_b], [rc_b])
                    TT("dve", ab3[hb:hb + 64, hh // 2, :], osb[hb:hb + 64, :], rc[hb:hb + 64, :], ALU.mult, [osb_b, rc_b], [ab_b])
                store(attnT_d.rearrange("(k p) t -> p k t", p=128)[:, hg * 2:hg * 2 + 2, qsl], "attnT_d", ab3, ab_b)
        print("phase B arena top", A.off, "of", A.n)
        P.barrier()
        A.release(m0_)

    ps_s_rr = [0]

    def next_ps_s():
        i = ps_s_rr[0]
        ps_s_rr[0] = (i + 1) % 4
        return psum[4 + i]


    NCH = T // 256
    st_gs = GatherSet("ssdF", 128, 2048, F32, 4)
    dd_gs = GatherSet("ssdD", 128, SH, F32, 4)

    def bcast_last(ap, n):
        return bass.AP(ap.tensor, ap.offset, [list(x) for x in ap.ap] + [[0, n]])

    def phase_C(l):
        m0_ = A.mark()
        rwp = RPool("rw", 3, 260, "f32")
        xsT, xsT_b = A.f32("xsT", 16 * 256)
        xsT3 = xsT.rearrange("p (m t) -> p m t", t=256)
        BT, BT_b = A.bf16("BT", 4 * 256)
        BT3 = BT.rearrange("p (g t) -> p g t", t=256)
        CT, CT_b = A.bf16("CT", 4 * 256)
        CT3 = CT.rearrange("p (g t) -> p g t", t=256)
        xdt, xdt_b = A.bf16("xdt", 2 * 2048)
        xdt3 = xdt.rearrange("p (i c) -> p i c", c=2048)
        xdd, xdd_b = A.bf16("xdd", 2 * 2048)
        xdd3 = xdd.rearrange("p (i c) -> p i c", c=2048)
        Btm, Btm_b = A.bf16("Btm", 2 * 512)
        Btm4 = Btm.rearrange("p (i g n) -> p i g n", g=4, n=128)
        acs, acs_b = A.f32("acs_tm", 2 * SH)
        acs3 = acs.rearrange("p (i h) -> p i h", h=SH)
        totb, totb_b = A.f32("tot_bc", SH)
        etot, etot_b = A.f32("etot", SH)
        ds_, ds_b = A.f32("ds", 2 * SH)
        ds3 = ds_.rearrange("p (i h) -> p i h", h=SH)
        Sst, Sst_b = A.f32("Sstate", 2048)
        Sbf, Sbf_b = A.bf16("Sstate_bf", 2048)
        Dacc, Dacc_b = A.f32("Dacc", SH)
        cbm, cbm_b = A.f32("CBm", 4 * 384)
        cbm3 = cbm.rearrange("p (g t) -> p g t", t=384)
        triL, triL_b = A.f32("triL", 512)
        yg, yg_b = A.f32("yg", 16 * 256)
        yg3 = yg.rearrange("p (m t) -> p m t", t=256)
        prev, prev_b = A.f32("xbc_prev", 24 * 4)
        prev3 = prev.rearrange("p (m c) -> p m c", c=4)
        zp = RPool("zt", 3, 256, "bf16")
        r0p = RPool("c_r0", 6, 512, "f32")
        dpp = RPool("c_d", 8, 384, "f32")
        ebp = RPool("c_eb", 8, 256, "f32")
        bcp = RPool("c_bc", 6, 256, "f32")
        gpp = RPool("c_G", 6, 384, "bf16")
        cep = RPool("c_Ce", 6, 256, "bf16")
        print("phase C arena top", A.off, "of", A.n)
        CP("dve", triL[:, 0:128], cv["tri"], [cst_b], [triL_b])
        CP("dve", triL[:, 128:256], cv["ones"], [cst_b], [triL_b])
        P.op("dve", lambda e: e.memset(triL[:, 256:384], 0.0), [], [triL_b])
        CP("dve", triL[:, 384:512], cv["tri"], [cst_b], [triL_b])
        load_prev_rows(halo_gs, prev3, prev_b, 24, 4)

        def conv_block(blk, c, out_ap, out_b, eng_out="act"):
            rw, rw_b = rwp.get()
            if c == 0:
                P.dma("sp", rw[:, 3:259], xbc_raw[blk * 128:(blk + 1) * 128, 0:256], reads=[dbuf["xbc_raw"]], writes=[rw_b])
                CP("pool", rw[:, 0:3], prev3[:, blk, 0:3], [prev_b], [rw_b])
            else:
                P.dma("sp", rw[:, 0:259], xbc_raw[blk * 128:(blk + 1) * 128, c * 256 - 3:c * 256 + 256],
                      reads=[dbuf["xbc_raw"]], writes=[rw_b])
            ac, ac_b = stg_pool.get()
            TS("dve", ac[:, 0:256], rw[:, 0:256], scwT[:, blk:blk + 1], ALU.mult, [rw_b, lp_b], [ac_b],
               s2=scbT[:, blk:blk + 1], op1=ALU.add)
            for tap in range(1, 4):
                STT(ac[:, 0:256], rw[:, tap:tap + 256], scwT[:, tap * 24 + blk:tap * 24 + blk + 1], ac[:, 0:256], ALU.mult, ALU.add,
                    [rw_b, lp_b, ac_b], [ac_b])
            ACT(out_ap, ac[:, 0:256], AF.Silu, [ac_b], [out_b])

        def ssd_pass(compute_y):
            for c in range(NCH):
                csl = slice(c * 256, (c + 1) * 256)
                for blk in range(16):
                    conv_block(blk, c, xsT3[:, blk, :], xsT_b)
                for g in range(4):
                    conv_block(16 + g, c, BT3[:, g, :], BT_b)
                if compute_y:
                    for g in range(4):
                        conv_block(20 + g, c, CT3[:, g, :], CT_b)
                for i in range(2):
                    ti = c * 2 + i
                    for bank in range(4):
                        pt, pb = next_ps()
                        for u in range(4):
                            blk = bank * 4 + u
                            P.op("pe", lambda e, pt=pt, u=u, blk=blk, i=i: e.transpose(
                                out=pt[:, u * 128:(u + 1) * 128], in_=xsT3[:, blk, i * 128:(i + 1) * 128], identity=cv["ident"]),
                                [xsT_b, cst_b], [pb])
                        TT("dve", xdt3[:, i, bank * 512:(bank + 1) * 512].rearrange("p (h q) -> p h q", q=64),
                           pt.rearrange("p (h q) -> p h q", q=64), bcast_last(dt3[:, ti, bank * 8:(bank + 1) * 8], 64), ALU.mult,
                           [pb, dt_b], [xdt_b])
                    pt, pb = next_ps()
                    ptb = pt.bitcast(BF16)
                    for g in range(4):
                        P.op("pe", lambda e, ptb=ptb, g=g, i=i: e.transpose(out=ptb[:, g * 128:(g + 1) * 128],
                                                                           in_=BT3[:, g, i * 128:(i + 1) * 128], identity=ident_bf),
                             [BT_b, ident_bf_b], [pb])
                    CP("act", Btm4[:, i, :, :], ptb[:, 0:512].rearrange("p (g n) -> p g n", n=128), [pb], [Btm_b])
                a0 = a3[:, c * 2, :]
                a1 = a3[:, c * 2 + 1, :]
                pt, pb = next_ps()
                MM(pt[:, 0:32], cv["tri"], a0, True, True, [cst_b, a_b], [pb])
                MM(pt[:, 32:64], cv["ones"], a0, True, False, [cst_b, a_b], [pb])
                MM(pt[:, 32:64], cv["tri"], a1, False, True, [cst_b, a_b], [pb])
                MM(pt[:, 64:96], cv["ones"], a0, True, False, [cst_b, a_b], [pb])
                MM(pt[:, 64:96], cv["ones"], a1, False, True, [cst_b, a_b], [pb])
                CP("dve", acs, pt[:, 0:64], [pb], [acs_b])
                CP("dve", totb, pt[:, 64:96], [pb], [totb_b])
                ACT(etot, totb, AF.Exp, [totb_b], [etot_b])
                TT("dve", Dacc, Dacc, totb, ALU.add, [Dacc_b, totb_b], [Dacc_b])
                for i in range(2):
                    TT("dve", ds3[:, i, :], totb, acs3[:, i, :], ALU.subtract, [totb_b, acs_b], [ds_b])
                ACT(ds_, ds_, AF.Exp, [ds_b], [ds_b])
                for i in range(2):
                    TT("pool", xdd3[:, i, :].rearrange("p (h q) -> p h q", q=64), xdt3[:, i, :].rearrange("p (h q) -> p h q", q=64),
                       bcast_last(ds3[:, i, :], 64), ALU.mult, [xdt_b, ds_b], [xdd_b])
                if compute_y:
                    CP("dve", Sbf, Sst, [Sst_b], [Sbf_b])
                    for g in range(4):
                        pt, pb = next_ps()
                        MM(pt[:, 0:256], BT3[:, g, 0:128], CT3[:, g, :], True, True, [BT_b, CT_b], [pb])
                        MM(pt[:, 256:384], BT3[:, g, 128:256], CT3[:, g, 128:256], True, True, [BT_b, CT_b], [pb])
                        TT("dve", cbm3[:, g, 0:128], pt[:, 0:128], cv["tri"], ALU.mult, [pb, cst_b], [cbm_b])
                        CP("dve", cbm3[:, g, 128:256], pt[:, 128:256], [pb], [cbm_b])
                        TT("dve", cbm3[:, g, 256:384], pt[:, 256:384], cv["tri"], ALU.mult, [pb, cst_b], [cbm_b])
                    NBQ = 8

                    def front(bq):
                        st_ = []
                        for hq in range(4):
                            h = bq * 4 + hq
                            r0, r0_b = r0p.get()
                            TS("dve", r0[:, 0:256], triL[:, 0:256], a0[:, h:h + 1], ALU.mult, [triL_b, a_b], [r0_b])
                            TS("pool", r0[:, 256:512], triL[:, 256:512], a1[:, h:h + 1], ALU.mult, [triL_b, a_b], [r0_b])
                            st_.append([h, r0, r0_b])
                        for e_ in st_:
                            h, r0, r0_b = e_
                            pbc, pbcb = next_ps()
                            MM(pbc[:, 0:256], cv["ones"], r0[:, 0:256], True, False, [cst_b, r0_b], [pbcb])
                            MM(pbc[:, 0:256], cv["ones"], r0[:, 256:512], False, True, [cst_b, r0_b], [pbcb])
                            e_ += [pbc, pbcb]
                        for e_ in st_:
                            h, r0, r0_b, pbc, pbcb = e_
                            d_, d_b = dpp.get()
                            bcs, bcs_b = bcp.get()
                            CP("act", bcs, pbc[:, 0:256], [pbcb], [bcs_b])
                            TS("dve", d_[:, 0:256], bcs[:, 0:256], acs3[:, 0, h:h + 1], ALU.subtract, [bcs_b, acs_b], [d_b],
                               s2=0.0, op1=ALU.min)
                            TS("dve", d_[:, 256:384], bcs[:, 128:256], acs3[:, 1, h:h + 1], ALU.subtract, [bcs_b, acs_b], [d_b],
                               s2=0.0, op1=ALU.min)
                            eb, eb_b = ebp.get()
                            ACT(eb, bcs, AF.Exp, [bcs_b], [eb_b])
                            e_ += [d_, d_b, eb, eb_b]
                        return st_

                    def back(bq, st_):
                        for e_ in st_:
                            h, r0, r0_b, pbc, pbcb, d_, d_b, eb, eb_b = e_
                            g = h // 8
                            ACT(d_, d_, AF.Exp, [d_b], [d_b])
                            Ce, Ce_b = cep.get()
                            TT("pool", Ce, eb, CT3[:, g, :], ALU.mult, [eb_b, CT_b], [Ce_b])
                            e_ += [Ce, Ce_b]
                        for e_ in st_:
                            h, d_, d_b = e_[0], e_[5], e_[6]
                            g = h // 8
                            G_, G_b = gpp.get()
                            TT("dve", G_, d_, cbm3[:, g, :], ALU.mult, [d_b, cbm_b], [G_b])
                            e_ += [G_, G_b]
                        for pq in range(2):
                            pr = bq * 2 + pq
                            py, pyb = next_ps()
                            for hh in range(2):
                                e_ = st_[pq * 2 + hh]
                                h, Ce, Ce_b, G_, G_b = e_[0], e_[9], e_[10], e_[11], e_[12]
                                hb = hh * 64
                                MM(py[hb:hb + 64, 0:256], xdt3[:, 0, h * 64:(h + 1) * 64], G_[:, 0:256], True, False, [xdt_b, G_b], [pyb])
                                MM(py[hb:hb + 64, 128:256], xdt3[:, 1, h * 64:(h + 1) * 64], G_[:, 256:384], False, False,
                                   [xdt_b, G_b], [pyb])
                                MM(py[hb:hb + 64, 0:256], Sbf[:, h * 64:(h + 1) * 64], Ce, False, True, [Sbf_b, Ce_b], [pyb])
                            zt, zt_b = zp.get()
                            P.dma("sp", zt, zT_d[pr * 128:(pr + 1) * 128, csl], reads=[dbuf["zT_d"]], writes=[zt_b])
                            yv, yv_b = stg_pool.get()
                            STT(yv[:, 0:256], xsT3[:, pr, :], dskT[:, pr:pr + 1], py[:, 0:256], ALU.mult, ALU.add,
                                [xsT_b, lp_b, pyb], [yv_b])
                            TT("pool", yg3[:, pr, :], yv[:, 0:256], zt, ALU.mult, [yv_b, zt_b], [yg_b])

                    prev_st = None
                    for bq in range(NBQ + 1):
                        cur = front(bq) if bq < NBQ else None
                        if prev_st is not None:
                            back(bq - 1, prev_st)
                        prev_st = cur
                    for g in range(4):
                        pss, pssb = next_ps()
                        for q in range(4):
                            sq, sq_b = stg_pool.get()
                            ACT(sq[:, 0:256], yg3[:, g * 4 + q, :], AF.Square, [yg_b], [sq_b])
                            MM(pss[:, 0:256], cv["ones"], sq[:, 0:256], q == 0, q == 3, [cst_b, sq_b], [pssb])
                        rs, rs_b = stg_pool.get()
                        ACT(rs[:, 0:256], pss[:, 0:256], AF.Sqrt, [pssb], [rs_b], bias=epsT[:, 0:1], scale=1.0 / 512.0)
                        P.op("dve", lambda e, rs=rs: e.reciprocal(out=rs[:, 0:256], in_=rs[:, 0:256]), [rs_b], [rs_b])
                        for q in range(4):
                            pr = g * 4 + q
                            ob, ob_b = sbf_pool.get()
                            STT(ob[:, 0:256], yg3[:, pr, :], sngT[:, pr:pr + 1], rs[:, 0:256], ALU.mult, ALU.mult,
                                [yg_b, lp_b, rs_b], [ob_b])
                            store(ynT_d[pr * 128:(pr + 1) * 128, csl], "ynT_d", ob[:, 0:256], ob_b)
                for g in range(4):
                    pst, pstb = next_ps()
                    for i in range(2):
                        MM(pst, Btm4[:, i, g, :], xdd3[:, i, g * 512:(g + 1) * 512], i == 0, i == 1, [Btm_b, xdd_b], [pstb])
                    sg3 = Sst[:, g * 512:(g + 1) * 512].rearrange("p (h q) -> p h q", q=64)
                    TT("dve", sg3, sg3, bcast_last(etot[:, g * 8:(g + 1) * 8], 64), ALU.mult, [Sst_b, etot_b], [Sst_b])
                    TT("dve", Sst[:, g * 512:(g + 1) * 512], Sst[:, g * 512:(g + 1) * 512], pst, ALU.add, [Sst_b, pstb], [Sst_b])

        P.op("dve", lambda e: e.memset(Sst, 0.0), [], [Sst_b])
        P.op("dve", lambda e: e.memset(Dacc, 0.0), [], [Dacc_b])
        ssd_pass(False)
        dap, dn = st_gs.loc_rows(0, 128)
        store(dap, dn, Sst, Sst_b)
        dap, dn = dd_gs.loc_rows(0, 128)
        store(dap, dn, Dacc, Dacc_b)
        st_gs.gather()
        dd_gs.gather()
        Dr = []
        for r in range(GSZ - 1):
            t_, tb_ = A.f32("Dr%d" % r, SH)
            gap, gn = dd_gs.g_rows(r, 0, 128)
            P.dma("sp", t_, gap, reads=[dbuf[gn]], writes=[tb_])
            Dr.append((t_, tb_))
        lt, lt_b = A.f32("ltflag", 4)
        for m in range(GSZ - 1):
            TS("dve", lt[:, m:m + 1], rk[:, 0:1], float(m), ALU.is_gt, [rk_b], [lt_b])
        P.op("dve", lambda e: e.memset(Sst, 0.0), [], [Sst_b])
        Fr, Fr_b = A.f32("Fr", 2048)
        for r in range(GSZ - 1):
            wr, wr_b = A.f32("wr%d" % r, SH)
            P.op("dve", lambda e, wr=wr: e.memset(wr, 0.0), [], [wr_b])
            for m in range(r + 1, GSZ - 1):
                STT(wr, Dr[m][0], lt[:, m:m + 1], wr, ALU.mult, ALU.add, [Dr[m][1], lt_b, wr_b], [wr_b])
            ACT(wr, wr, AF.Exp, [wr_b], [wr_b])
            TS("dve", wr, wr, lt[:, r:r + 1], ALU.mult, [wr_b, lt_b], [wr_b])
            gap, gn = st_gs.g_rows(r, 0, 128)
            P.dma("sp", Fr, gap, reads=[dbuf[gn]], writes=[Fr_b])
            F3 = Fr.rearrange("p (h q) -> p h q", q=64)
            TT("dve", F3, F3, bcast_last(wr, 64), ALU.mult, [Fr_b, wr_b], [Fr_b])
            TT("dve", Sst, Sst, Fr, ALU.add, [Sst_b, Fr_b], [Sst_b])
        ssd_pass(True)
        P.barrier()
        A.release(m0_)

    P.dma("sp", rk[:, 0:4], bcast_ap(rank_in, 128, 4), reads=[dbuf["rank"]], writes=[rk_b])
    for r in range(GSZ):
        TS("dve", prevsel[:, r:r + 1], rk[:, 0:1], float(r + 1), ALU.is_equal, [rk_b], [prevsel_b])

    def load_prev_rows(gs, dst3, dst_b, nsub, ncols):
        first = True
        for r in range(GSZ - 1):
            tmp_, tmpb_ = A.f32("prevtmp%d" % r, nsub * ncols)
            tmp3 = tmp_.rearrange("p (m c) -> p m c", c=ncols)
            gap, gn = gs.g_rows(r, 0, 128)
            P.dma("sp", tmp_, gap, reads=[dbuf[gn]], writes=[tmpb_])
            if first:
                TS("dve", dst3, tmp3, prevsel[:, r:r + 1], ALU.mult, [tmpb_, prevsel_b], [dst_b])
                first = False
            else:
                STT(dst3, tmp3, prevsel[:, r:r + 1], dst3, ALU.mult, ALU.add, [tmpb_, prevsel_b, dst_b], [dst_b])

    for l in range(depth):
        A.release(m_x)
        phase_A(l)
        P.barrier()
        A.release(m_x)
        if cfg.stage >= 30:
            spill_x()
            A.release(m_pers)
            phase_B(l)
            A.release(m_pers)
            if cfg.stage >= 40:
                phase_C(l)
            A.release(m_x)
            restore_x()
        if cfg.stage >= 20:
            phase_D(l)
        if cfg.stage >= 21:
            phase_EF(l, None)
    A.release(m_x)
    for name in dbg_copies:
        rows = dram[name].shape[0]
        for r0 in range(0, rows, 512):
            r1 = min(rows, r0 + 512)
            P.dma("sp", dram[name + "_dbg"][r0:r1, :], dram[name][r0:r1, :], reads=[dbuf[name]], writes=[dbuf[name + "_dbg"]])
    P.barrier()

    m0 = A.mark()
    outs = [A.f32("otok%d" % i, D) for i in range(2)]
    for i in range(NT):
        ot, ot_b = outs[i % 2]
        for k in range(KC):
            pt, pb = next_ps()
            P.op("pe", lambda e, pt=pt, k=k, i=i: e.transpose(out=pt[:, 0:128], in_=xT3[:, k, i * 128:(i + 1) * 128],
                                                             identity=cv["ident"]),
                 reads=[xT_b, cst_b], writes=[pb])
            if k % 2 == 0:
                P.op("act", lambda e, pt=pt, k=k, ot=ot: e.copy(out=ot[:, k * 128:(k + 1) * 128], in_=pt[:, 0:128]),
                     reads=[pb], writes=[ot_b])
            else:
                P.op("dve", lambda e, pt=pt, k=k, ot=ot: e.tensor_copy(out=ot[:, k * 128:(k + 1) * 128], in_=pt[:, 0:128]),
                     reads=[pb], writes=[ot_b])
        P.dma("sp", y_out[i * 128:(i + 1) * 128, :], ot, reads=[ot_b], writes=[dbuf["y"]])
    P.barrier()
    A.release(m0)

    P.emit(stack)
    stack.close()
    return nc


def make_in_maps(cfg, inputs):
    T = cfg.T
    depth = cfg.depth
    maps = []
    f = lambda a: np.ascontiguousarray(np.asarray(a, dtype=np.float32))
    shared = {
        "consts": CONST_ARR,
        "w_ada": f(inputs["w_ada"]), "b_ada": f(inputs["b_ada"]).reshape(depth, 48, 128),
        "norm1_g": f(inputs["norm1_g"]).reshape(depth, KC, 128), "w_in": f(inputs["w_in"]),
        "q_norm_g": f(inputs["q_norm_g"]).reshape(depth, 1, 64), "k_norm_g": f(inputs["k_norm_g"]).reshape(depth, 1, 64),
        "ssm_conv_w": f(inputs["ssm_conv_w"]).reshape(depth, 4, 24, 128),
        "ssm_conv_b": f(inputs["ssm_conv_b"]).reshape(depth, 24, 128),
        "dt_bias": f(inputs["dt_bias"]).reshape(depth, 1, SH), "a_log": f(inputs["a_log"]).reshape(depth, 1, SH),
        "d_skip": f(inputs["d_skip"]).reshape(depth, 1, SH), "ssm_norm_g": f(inputs["ssm_norm_g"]).reshape(depth, 16, 128),
        "w_attn_o": f(inputs["w_attn_o"]), "w_ssm_o": f(inputs["w_ssm_o"]), "w_out": f(inputs["w_out"]),
        "norm2_g": f(inputs["norm2_g"]).reshape(depth, KC, 128), "w_up": f(inputs["w_up"]),
        "ffn_conv_w": f(inputs["ffn_conv_w"]).reshape(depth, 3, 44, 128),
        "ffn_conv_b": f(inputs["ffn_conv_b"]).reshape(depth, 44, 128), "w_down": f(inputs["w_down"]),
    }
    x = f(inputs["x"])
    c = f(inputs["c"])
    pos = np.ascontiguousarray(np.asarray(inputs["positions"], dtype=np.int32))
    for r in range(NCORES):
        b, j = divmod(r, GSZ)
        m = dict(shared)
        m["x"] = np.ascontiguousarray(x[b, j * T:(j + 1) * T, :])
        m["c"] = np.ascontiguousarray(c[b].reshape(KC, 128))
        m["positions"] = np.ascontiguousarray(pos[b, j * T:(j + 1) * T].reshape(1, T))
        rk = np.zeros((1, 4), np.float32)
        rk[0, 0] = j
        m["rank"] = rk
        for k_, v_ in (getattr(cfg, "feed_data", None) or {}).items():
            m[k_] = v_[r]
        maps.append(m)
    return maps


_CACHE = {}


def run(cfg, inputs):
    key = (cfg.S, cfg.depth, tuple(sorted(cfg.debug)), cfg.stage, tuple(sorted(cfg.feed)))
    if key not in _CACHE:
        _CACHE[key] = build_program(cfg)
    nc = _CACHE[key]
    maps = make_in_maps(cfg, inputs)
    res = run_bass_kernel_spmd(nc, maps, core_ids=list(range(NCORES)))
    return res.results


def kernel(**inputs):
    cfg = Cfg(seq=int(np.asarray(inputs["x"]).shape[1]), depth=int(np.asarray(inputs["w_in"]).shape[0]))
    results = run(cfg, inputs)
    B = np.asarray(inputs["x"]).shape[0]
    out = np.zeros((B, cfg.S, D), np.float32)
    for r in range(NCORES):
        b, j = divmod(r, GSZ)
        out[b, j * cfg.T:(j + 1) * cfg.T, :] = results[r]["y"]
    return out
```

```python
from contextlib import ExitStack
import numpy as np
import ml_dtypes
import concourse.bass as bass
import concourse.mybir as mybir
from concourse.bass_utils import run_bass_kernel_spmd

F32 = mybir.dt.float32
BF16 = mybir.dt.bfloat16
I32 = mybir.dt.int32
FP8 = mybir.dt.float8e5
AF = mybir.ActivationFunctionType
ALU = mybir.AluOpType
AX = mybir.AxisListType

NCORES = 8
GSZ = 4
D = 1024
KC = D // 128
HEADS = 16
HD = 64
IH = 8
TOPK = 256
DI = 2048
SH = 32
SG = 4
NST = 128
XBC = DI + 2 * SG * NST
DFF = 2816
EPS = 1e-6
C_Q, C_K, C_V, C_IQ, C_IK, C_IW, C_Z, C_XBC, C_DT, C_GA, C_GM = (
    0, 1024, 2048, 3072, 3584, 3648, 3656, 5704, 8776, 8808, 9832)
INW = 10856
NEG = -30000.0


class Buf:
    __slots__ = ("name", "w", "r")

    def __init__(self, name):
        self.name = name
        self.w = None
        self.r = {}


class Prog:
    ENGS = ("pe", "act", "dve", "pool", "sp")

    def __init__(self, nc, n_dma=40):
        self.nc = nc
        self.ops = {e: [] for e in self.ENGS}
        self.cnt = {e: 0 for e in self.ENGS}
        self.known = {e: {} for e in self.ENGS}
        self.dma_val = [0] * n_dma
        self.dma_next = 0
        self.cc_val = 0

    def _need(self, eng, k, v):
        if k == eng and eng == "pe":
            return
        kn = self.known[eng]
        if kn.get(k, 0) >= v:
            return
        kn[k] = v
        self.ops[eng].append(("wait", k, v))

    def _deps(self, eng, reads, writes):
        for b in reads:
            if b.w is not None:
                self._need(eng, *b.w)
        for b in writes:
            if b.w is not None:
                self._need(eng, *b.w)
            for k, v in b.r.items():
                self._need(eng, k, v)

    def _mark(self, tok, reads, writes):
        for b in writes:
            b.w = tok
            b.r = {}
        for b in reads:
            if b in writes:
                continue
            if b.r.get(tok[0], 0) < tok[1]:
                b.r[tok[0]] = tok[1]

    def op(self, eng, fn, reads=(), writes=()):
        self._deps(eng, reads, writes)
        self.cnt[eng] += 1
        tok = (eng, self.cnt[eng])
        self.ops[eng].append(("ins", fn, eng, 1, self._where()))
        self._mark(tok, reads, writes)
        return tok

    DEBUG_WHERE = False

    def _where(self):
        if not Prog.DEBUG_WHERE:
            return None
        import traceback
        return [(f.lineno, f.name) for f in traceback.extract_stack(limit=6)[:-2]]

    def dma(self, q, out, in_, reads=(), writes=()):
        i = self.dma_next
        self.dma_next = (i + 1) % len(self.dma_val)
        key = ("dma", i)
        self._deps(q, reads, writes)
        if self.dma_val[i]:
            self._need(q, key, self.dma_val[i])
        self.dma_val[i] += 16
        tok = (key, self.dma_val[i])
        self.ops[q].append(("ins", lambda e, o=out, s=in_: e.dma_start(out=o, in_=s), key, 16))
        self._mark(tok, reads, writes)
        return tok

    def collective(self, fn, reads=(), writes=()):
        self._deps("pool", reads, writes)
        self.cc_val += 1
        tok = ("cc", self.cc_val)
        self.ops["pool"].append(("ins", fn, "cc", 1))
        self._mark(tok, reads, writes)
        return tok

    def barrier(self):
        for e in self.ENGS:
            for f in ("pe", "act", "dve", "pool"):
                if self.cnt[f]:
                    self._need(e, f, self.cnt[f])
            for i, v in enumerate(self.dma_val):
                if v:
                    self._need(e, ("dma", i), v)
            if self.cc_val:
                self._need(e, "cc", self.cc_val)

    def emit(self, stack):
        nc = self.nc
        sems = {}
        for e in ("pe", "act", "dve", "pool"):
            sems[e] = stack.enter_context(nc.semaphore("s_" + e))
        for i in range(len(self.dma_val)):
            sems[("dma", i)] = stack.enter_context(nc.semaphore("d%d" % i))
        sems["cc"] = stack.enter_context(nc.semaphore("s_cc"))
        block = stack.enter_context(nc.Block())

        def mk(name):
            def body(eng):
                for o in self.ops[name]:
                    if o[0] == "wait":
                        eng.wait_ge(sems[o[1]], o[2])
                    else:
                        ins = o[1](eng)
                        ins.then_inc(sems[o[2]], o[3])
                        if Prog.DEBUG_WHERE and len(o) > 4:
                            print("INS", name, getattr(getattr(ins, "ins", None), "name", None), o[4])
            return body

        block.tensor(mk("pe"))
        block.scalar(mk("act"))
        block.vector(mk("dve"))
        block.gpsimd(mk("pool"))
        block.sync(mk("sp"))


class Arena:
    def __init__(self, big, nwords):
        self.big = big
        self.n = nwords
        self.off = 0

    def mark(self):
        return self.off

    def release(self, m):
        self.off = m

    def f32(self, name, cols):
        a = self.off
        self.off += cols
        assert self.off <= self.n, ("SBUF arena overflow", name, self.off, self.n)
        return self.big[:, a:a + cols], Buf(name)

    def bf16(self, name, cols):
        w = (cols + 1) // 2
        a = self.off
        self.off += w
        assert self.off <= self.n, ("SBUF arena overflow", name, self.off, self.n)
        return self.big[:, a:a + w].bitcast(BF16)[:, 0:cols], Buf(name)


def make_consts():
    c = {}
    c["ident"] = np.eye(128, dtype=np.float32)
    c["ones"] = np.ones((128, 128), np.float32)
    bo = np.zeros((128, 128), np.float32)
    bo[:64, :64] = 1.0
    bo[64:, 64:] = 1.0
    c["blockones"] = bo
    rr = np.zeros((128, 128), np.float32)
    for m in range(128):
        if (m % 64) < 32:
            rr[m + 32, m] = -1.0
        else:
            rr[m - 32, m] = 1.0
    c["rrot"] = rr
    tri = (np.arange(128)[:, None] <= np.arange(128)[None, :]).astype(np.float32)
    c["tri"] = tri
    c["causb"] = np.where(np.arange(128)[None, :] <= np.arange(128)[:, None], 0.0, -1e30).astype(np.float32)
    invf = (1.0 / (10000.0 ** (np.arange(0, 64, 2, dtype=np.float32) / 64.0))).astype(np.float32)
    c["invf"] = np.tile(invf, 4)[:, None].astype(np.float32) * np.ones((1, 128), np.float32)
    c["iotaf"] = np.tile(np.arange(512, dtype=np.float32)[None, :], (128, 1))
    c["pidx"] = np.tile(np.arange(128, dtype=np.float32)[:, None], (1, 128))
    sw = np.zeros((128, 128), np.float32)
    for m in range(128):
        sw[(m + 64) % 128, m] = 1.0
    c["swap"] = sw
    names = ["ident", "ones", "blockones", "rrot", "tri", "causb", "invf", "iotaf", "pidx", "swap"]
    return [(n, c[n].shape[1]) for n in names], np.concatenate([c[n] for n in names], axis=1)


CONST_NAMES, CONST_ARR = make_consts()


class Cfg:
    def __init__(self, seq=8192, depth=4, debug=(), stage=99, feed=()):
        self.stage = stage
        self.feed = set(feed)
        self.S = seq
        self.T = seq // GSZ
        self.depth = depth
        self.debug = set(debug)
        self.NT = self.T // 128
        self.NB = self.T // 512
        self.NCH = self.T // 256
        self.nkeep = min(TOPK, seq // 4)


def build_program(cfg):
    T, NT, NB, depth = cfg.T, cfg.NT, cfg.NB, cfg.depth
    nc = bass.Bass("TRN2", target_bir_lowering=False)
    stack = ExitStack()
    P = Prog(nc)
    dram = {}
    dbuf = {}

    dbg_copies = []

    def dten(name, shape, dtype, kind="Internal"):
        if name in cfg.debug:
            if "_g" in name[-4:]:
                dcp = nc.dram_tensor(name + "_dbg", list(shape), dtype, kind="ExternalOutput").ap()
                dram[name + "_dbg"] = dcp
                dbuf[name + "_dbg"] = Buf(name + "_dbg")
                dbg_copies.append(name)
            else:
                kind = "ExternalOutput"
        t = nc.dram_tensor(name, list(shape), dtype, kind=kind).ap()
        dram[name] = t
        dbuf[name] = Buf(name)
        return t

    x_in = dten("x", [T, D], F32, "ExternalInput")
    c_in = dten("c", [KC, 128], F32, "ExternalInput")
    pos_in = dten("positions", [1, T], I32, "ExternalInput")
    consts_in = dten("consts", [128, CONST_ARR.shape[1]], F32, "ExternalInput")
    rank_in = dten("rank", [1, 4], F32, "ExternalInput")
    W = {}
    for nm, shp in (("w_ada", [depth, D, 6 * D]), ("b_ada", [depth, 48, 128]), ("norm1_g", [depth, KC, 128]),
                    ("w_in", [depth, D, INW]), ("q_norm_g", [depth, 1, 64]), ("k_norm_g", [depth, 1, 64]),
                    ("ssm_conv_w", [depth, 4, 24, 128]), ("ssm_conv_b", [depth, 24, 128]),
                    ("dt_bias", [depth, 1, SH]), ("a_log", [depth, 1, SH]), ("d_skip", [depth, 1, SH]),
                    ("ssm_norm_g", [depth, 16, 128]), ("w_attn_o", [depth, D, D]), ("w_ssm_o", [depth, DI, D]),
                    ("w_out", [depth, D, D]), ("norm2_g", [depth, KC, 128]), ("w_up", [depth, D, 2 * DFF]),
                    ("ffn_conv_w", [depth, 3, 44, 128]), ("ffn_conv_b", [depth, 44, 128]),
                    ("w_down", [depth, DFF, D])):
        W[nm] = dten(nm, shp, F32, "ExternalInput")
    y_out = dten("y", [T, D], F32, "ExternalOutput")

    NWORDS = 52224
    big = stack.enter_context(nc.sbuf_tensor("arena", [128, NWORDS], F32))
    A = Arena(big, NWORDS)
    psum = []
    for i in range(8):
        pt = stack.enter_context(nc.psum_tensor("ps%d" % i, [128, 512], F32))
        psum.append((pt[:, :], Buf("ps%d" % i)))
    ps_rr = [0]

    def next_ps():
        i = ps_rr[0]
        ps_rr[0] = (i + 1) % 8
        return psum[i]

    cst, cst_b = A.f32("consts", CONST_ARR.shape[1])
    cv = {}
    o = 0
    for n, wd in CONST_NAMES:
        cv[n] = cst[:, o:o + wd]
        o += wd
    ident_bf, ident_bf_b = A.bf16("ident_bf", 128)

    class RPool:
        def __init__(self, name, n, cols, kind):
            self.t = [(A.f32 if kind == "f32" else A.bf16)("%s%d" % (name, i), cols) for i in range(n)]
            self.i = 0

        def get(self):
            r = self.t[self.i]
            self.i = (self.i + 1) % len(self.t)
            return r


    lp, lp_b = A.f32("lp", 512)
    dtb_bc, dtb_b = A.f32("dtb_bc", SH)
    alog_bc, alog_b = A.f32("alog_bc", SH)
    iw_tm, iw_b = A.f32("iw_tm", NT * 8)
    dt_tm, dt_b = A.f32("dt_tm", NT * SH)
    a_tm, a_b = A.f32("a_tm", NT * SH)
    halfsel, halfsel_b = A.f32("halfsel", 128)
    lfm_pool = RPool("lfm", 3, 128, "f32")
    WSTG = 2048
    wst_pool = RPool("wst", 2, WSTG, "f32")
    wbf_pool = RPool("wbf", 2, WSTG, "bf16")
    stg_pool = RPool("stg", 8, 512, "f32")
    sbf_pool = RPool("sbf", 4, 512, "bf16")
    epsT, epsT_b = A.f32("epsT", 2)
    oneT, oneT_b = A.f32("oneT", 2)
    rk, rk_b = A.f32("rk", 8)
    prevsel, prevsel_b = A.f32("prevsel", 4)
    class NS:
        pass
    ns = NS()
    m_pers = A.mark()
    xT, xT_b = A.f32("xT", KC * T)
    xT3 = xT.rearrange("p (k t) -> p k t", t=T)
    m_x = A.mark()

    P.dma("sp", cst, consts_in, reads=[dbuf["consts"]], writes=[cst_b])
    P.op("dve", lambda e: e.tensor_copy(out=ident_bf, in_=cv["ident"]), reads=[cst_b], writes=[ident_bf_b])

    def transpose_f32(dst, dst_b, src, src_b, rows, cols, evac="act"):
        pt, pb = next_ps()
        P.op("pe", lambda e: e.transpose(out=pt[0:cols, 0:rows], in_=src, identity=cv["ident"][0:rows, 0:rows]),
             reads=[src_b, cst_b], writes=[pb])
        if evac == "act":
            P.op("act", lambda e: e.copy(out=dst, in_=pt[0:cols, 0:rows]), reads=[pb], writes=[dst_b])
        else:
            P.op("dve", lambda e: e.tensor_copy(out=dst, in_=pt[0:cols, 0:rows]), reads=[pb], writes=[dst_b])

    def load_fm(dst, dst_b, src_ap, src_name, nrow):
        m = A.mark()
        tmp, tmp_b = A.f32("lfm_tmp", 128)
        P.dma("sp", tmp[0:nrow, :], src_ap, reads=[dbuf[src_name]], writes=[tmp_b])
        transpose_f32(dst, dst_b, tmp[0:nrow, :], tmp_b, nrow, 128)
        A.release(m)
        return tmp_b

    m0 = A.mark()
    xtoks = [A.f32("xtok%d" % i, D) for i in range(2)]
    for i in range(NT):
        xt, xt_b = xtoks[i % 2]
        P.dma("sp", xt, x_in[i * 128:(i + 1) * 128, :], reads=[dbuf["x"]], writes=[xt_b])
        for k in range(KC):
            pt, pb = next_ps()
            P.op("pe", lambda e, pt=pt, xt=xt, k=k: e.transpose(out=pt[:, 0:128], in_=xt[:, k * 128:(k + 1) * 128],
                                                               identity=cv["ident"]),
                 reads=[xt_b, cst_b], writes=[pb])
            eng = "act" if k % 2 == 0 else "dve"
            if eng == "act":
                P.op("act", lambda e, pt=pt, k=k, i=i: e.copy(out=xT3[:, k, i * 128:(i + 1) * 128], in_=pt[:, 0:128]),
                     reads=[pb], writes=[xT_b])
            else:
                P.op("dve", lambda e, pt=pt, k=k, i=i: e.tensor_copy(out=xT3[:, k, i * 128:(i + 1) * 128], in_=pt[:, 0:128]),
                     reads=[pb], writes=[xT_b])
    P.barrier()
    A.release(m0)


    def ACT(out, in_, func, reads, writes, **kw):
        P.op("act", lambda e: e.activation(out=out, in_=in_, func=func, **kw), reads, writes)

    def TS(eng, out, in0, s1, op0, reads, writes, s2=None, op1=None, **kw):
        if op1 is None:
            P.op(eng, lambda e: e.tensor_scalar(out=out, in0=in0, scalar1=s1, scalar2=None, op0=op0, **kw), reads, writes)
        else:
            P.op(eng, lambda e: e.tensor_scalar(out=out, in0=in0, scalar1=s1, scalar2=s2, op0=op0, op1=op1, **kw),
                 reads, writes)

    def TT(eng, out, in0, in1, op, reads, writes):
        P.op(eng, lambda e: e.tensor_tensor(out=out, in0=in0, in1=in1, op=op), reads, writes)

    def STT(out, in0, scalar, in1, op0, op1, reads, writes):
        P.op("dve", lambda e: e.scalar_tensor_tensor(out=out, in0=in0, scalar=scalar, in1=in1, op0=op0, op1=op1),
             reads, writes)

    def MM(out, lhsT, rhs, start, stop, reads, writes):
        P.op("pe", lambda e: e.matmul(out, lhsT, rhs, start=start, stop=stop), reads, writes)

    def CP(eng, out, in_, reads, writes):
        if eng == "act":
            P.op("act", lambda e: e.copy(out=out, in_=in_), reads, writes)
        else:
            P.op(eng, lambda e: e.tensor_copy(out=out, in_=in_), reads, writes)

    def bcast_ap(ap2d_row, nparts, ncols, offset_elems=0):
        return bass.AP(ap2d_row.tensor, ap2d_row.offset + offset_elems, [[0, nparts], [1, ncols]])

    qT_d = dten("qT_d", [8 * 128, T], BF16)
    groups = [[0, 1, 2, 3], [4, 5, 6, 7]]

    class GatherSet:
        def __init__(self, name, rows, cols, dtype, esz):
            rpc = rows
            while rpc * cols * esz > (1 << 20):
                rpc //= 2
            assert rows % rpc == 0
            self.name, self.rows, self.cols, self.rpc, self.n = name, rows, cols, rpc, rows // rpc
            self.loc = [dten("%s_loc%d" % (name, c), [rpc, cols], dtype) for c in range(self.n)]
            self.g = [dten("%s_g%d" % (name, c), [GSZ * rpc, cols], dtype) for c in range(self.n)]

        def loc_rows(self, r0, r1):
            c = r0 // self.rpc
            assert (r1 - 1) // self.rpc == c
            return self.loc[c][r0 - c * self.rpc:r1 - c * self.rpc, :], "%s_loc%d" % (self.name, c)

        def g_rows(self, rank, r0, r1):
            c = r0 // self.rpc
            assert (r1 - 1) // self.rpc == c
            base = rank * self.rpc - c * self.rpc
            return self.g[c][base + r0:base + r1, :], "%s_g%d" % (self.name, c)

        def gather(self):
            for c in range(self.n):
                ln, gn = "%s_loc%d" % (self.name, c), "%s_g%d" % (self.name, c)
                P.collective(lambda e, ln=ln, gn=gn: e.collective_compute("AllGather", ALU.bypass, replica_groups=groups,
                                                                          ins=[dram[ln]], outs=[dram[gn]]),
                             reads=[dbuf[ln]], writes=[dbuf[gn]])

    kT_gs = GatherSet("kT", 8 * 128, T, BF16, 2)
    v_gs = GatherSet("v", T, 2048, BF16, 2)
    iqT_d = dten("iqT_d", [4 * 128, T], BF16)
    ik_gs = GatherSet("ik", 128, T, BF16, 2)
    zT_d = dten("zT_d", [16 * 128, T], BF16)
    xbc_raw = dten("xbc_raw", [24 * 128, T], F32)
    halo_gs = GatherSet("halo", 128, 24 * 4, F32, 4)
    gT_d = dten("gT_d", [16 * 128, T], F32)
    dbg_small = dten("dbg_small", [128, NT * 80], F32)

    rope_d = dten("rope_d", [2 * 128, T], F32)
    m0 = A.mark()
    C4, C4_b = A.f32("C4", T)
    S4, S4_b = A.f32("S4", T)
    posi, posi_b = A.f32("posi", T)
    posi_i = posi.bitcast(I32)
    ang, ang_b = A.f32("ang", T)
    t1, t1_b = A.f32("rt1", T)
    t2, t2_b = A.f32("rt2", T)
    t2_i = t2.bitcast(I32)
    P.dma("sp", posi_i, bcast_ap(pos_in, 128, T), reads=[dbuf["positions"]], writes=[posi_b])
    CP("dve", ang, posi_i, [posi_b], [ang_b])
    TS("dve", ang, ang, cv["invf"][:, 0:1], ALU.mult, [ang_b, cst_b], [ang_b])
    TWO_PI = 2.0 * np.pi
    C1 = 6.28125
    C2 = TWO_PI - C1
    for dst, dst_b, shift in ((S4, S4_b, 0.0), (C4, C4_b, np.pi / 2.0)):
        TS("dve", t1, ang, shift, ALU.add, [ang_b], [t1_b], s2=1.0 / TWO_PI, op1=ALU.mult)
        CP("dve", t2_i, t1, [t1_b], [t2_b])
        CP("dve", t1, t2_i, [t2_b], [t1_b])
        TS("dve", t2, ang, shift, ALU.add, [ang_b], [t2_b])
        STT(t2, t1, -C1, t2, ALU.mult, ALU.add, [t1_b, t2_b], [t2_b])
        STT(t2, t1, -C2, t2, ALU.mult, ALU.add, [t1_b, t2_b], [t2_b])
        TS("dve", t1, t2, float(np.pi), ALU.is_gt, [t2_b], [t1_b])
        STT(t2, t1, -TWO_PI, t2, ALU.mult, ALU.add, [t1_b, t2_b], [t2_b])
        TS("dve", t1, t2, float(-np.pi), ALU.is_lt, [t2_b], [t1_b])
        STT(t2, t1, TWO_PI, t2, ALU.mult, ALU.add, [t1_b, t2_b], [t2_b])
        TS("dve", t2, t2, float(np.pi), ALU.min, [t2_b], [t2_b], s2=float(-np.pi), op1=ALU.max)
        ACT(dst, t2, AF.Sin, [t2_b], [dst_b])
    P.dma("sp", rope_d[0:128, :], C4, reads=[C4_b], writes=[dbuf["rope_d"]])
    P.dma("sp", rope_d[128:256, :], S4, reads=[S4_b], writes=[dbuf["rope_d"]])
    P.barrier()
    A.release(m0)

    o_ = [0]

    def lp_alloc(n):
        a = o_[0]
        o_[0] += n
        assert o_[0] <= 512
        return lp[:, a:a + n]

    b_adaT = lp_alloc(48)
    modT = lp_alloc(48)
    n1gT = lp_alloc(8)
    n2gT = lp_alloc(8)
    scwT = lp_alloc(96)
    scbT = lp_alloc(24)
    sngT = lp_alloc(16)
    fcwT = lp_alloc(132)
    fcbT = lp_alloc(44)
    qg2 = lp_alloc(1)
    kg2 = lp_alloc(1)
    A1 = lp_alloc(8)
    A2 = lp_alloc(8)
    cact2 = lp_alloc(16)
    dskT = lp_alloc(16)
    cact2_3 = cact2.rearrange("p (k two) -> p k two", two=2)
    iw3 = iw_tm.rearrange("p (i h) -> p i h", h=8)
    dt3 = dt_tm.rearrange("p (i h) -> p i h", h=SH)
    a3 = a_tm.rearrange("p (i h) -> p i h", h=SH)

    m0 = A.mark()
    tmp, tmp_b = A.f32("ctmp", 128)
    P.dma("sp", tmp[0:KC, :], c_in, reads=[dbuf["c"]], writes=[tmp_b])
    pt, pb = next_ps()
    P.op("pe", lambda e, pt=pt: e.transpose(out=pt[:, 0:KC], in_=tmp[0:KC, :], identity=cv["ident"][0:KC, 0:KC]),
         reads=[tmp_b, cst_b], writes=[pb])
    ACT(cact2_3[:, :, 0], pt[:, 0:KC], AF.Silu, [pb], [lp_b])
    ACT(cact2_3[:, :, 1], pt[:, 0:KC], AF.Silu, [pb], [lp_b])
    P.barrier()
    A.release(m0)

    def load_fm_rows(dst, src_ap, src_name, nrow):
        tmp_, tmpb_ = lfm_pool.get()
        P.dma("sp", tmp_[0:nrow, :], src_ap, reads=[dbuf[src_name]], writes=[tmpb_])
        pt_, pb_ = next_ps()
        P.op("pe", lambda e: e.transpose(out=pt_[:, 0:nrow], in_=tmp_[0:nrow, :], identity=cv["ident"][0:nrow, 0:nrow]),
             reads=[tmpb_, cst_b], writes=[pb_])
        CP("dve", dst, pt_[:, 0:nrow], [pb_], [lp_b])

    def layer_params(l):
        load_fm_rows(b_adaT, W["b_ada"][l], "b_ada", 48)
        load_fm_rows(n1gT, W["norm1_g"][l], "norm1_g", KC)
        load_fm_rows(n2gT, W["norm2_g"][l], "norm2_g", KC)
        for tap in range(4):
            load_fm_rows(scwT[:, tap * 24:(tap + 1) * 24], W["ssm_conv_w"][l, tap], "ssm_conv_w", 24)
        load_fm_rows(scbT, W["ssm_conv_b"][l], "ssm_conv_b", 24)
        load_fm_rows(sngT, W["ssm_norm_g"][l], "ssm_norm_g", 16)
        for tap in range(3):
            load_fm_rows(fcwT[:, tap * 44:(tap + 1) * 44], W["ffn_conv_w"][l, tap], "ffn_conv_w", 44)
        load_fm_rows(fcbT, W["ffn_conv_b"][l], "ffn_conv_b", 44)
        for dst, nm in ((qg2, "q_norm_g"), (kg2, "k_norm_g")):
            tmp_, tmpb_ = lfm_pool.get()
            P.dma("sp", tmp_[0:1, 0:64], W[nm][l], reads=[dbuf[nm]], writes=[tmpb_])
            P.dma("sp", tmp_[0:1, 64:128], W[nm][l], reads=[dbuf[nm]], writes=[tmpb_])
            pt_, pb_ = next_ps()
            P.op("pe", lambda e, pt_=pt_, tmp_=tmp_: e.transpose(out=pt_[:, 0:1], in_=tmp_[0:1, :],
                                                                 identity=cv["ident"][0:1, 0:1]),
                 reads=[tmpb_, cst_b], writes=[pb_])
            CP("dve", dst, pt_[:, 0:1], [pb_], [lp_b])
        tmp_, tmpb_ = lfm_pool.get()
        P.dma("sp", tmp_[0:16, 0:2], W["d_skip"][l].rearrange("o (c two) -> (o c) two", two=2),
              reads=[dbuf["d_skip"]], writes=[tmpb_])
        pt_, pb_ = next_ps()
        P.op("pe", lambda e: e.transpose(out=pt_[0:2, 0:16], in_=tmp_[0:16, 0:2], identity=cv["ident"][0:16, 0:16]),
             reads=[tmpb_, cst_b], writes=[pb_])
        tmp2_, tmp2b_ = lfm_pool.get()
        CP("dve", tmp2_[0:2, 0:16], pt_[0:2, 0:16], [pb_], [tmp2b_])
        pt2_, pb2_ = next_ps()
        MM(pt2_[:, 0:16], halfsel[0:2, :], tmp2_[0:2, 0:16], True, True, [tmp2b_, halfsel_b], [pb2_])
        CP("dve", dskT, pt2_[:, 0:16], [pb2_], [lp_b])
        P.dma("sp", dtb_bc, bcast_ap(W["dt_bias"][l], 128, SH), reads=[dbuf["dt_bias"]], writes=[dtb_b])
        P.dma("sp", alog_bc, bcast_ap(W["a_log"][l], 128, SH), reads=[dbuf["a_log"]], writes=[alog_b])
        ACT(alog_bc, alog_bc, AF.Exp, [alog_b], [alog_b])

    P.dma("sp", halfsel[0:1, :], consts_in[0:1, 256:384], reads=[dbuf["consts"]], writes=[halfsel_b])
    P.dma("sp", halfsel[1:2, :], consts_in[64:65, 256:384], reads=[dbuf["consts"]], writes=[halfsel_b])


    cast_rr = [0]

    def load_w(wname, l, col0, ncols, kc, row0=0, to_bf16=True, dst=None, pools=None):
        assert kc * ncols <= WSTG
        st, st_b = (pools[0] if pools else wst_pool).get()
        st3 = st[:, 0:kc * ncols].rearrange("p (k c) -> p k c", c=ncols)
        src = W[wname][l][row0:row0 + kc * 128, col0:col0 + ncols].rearrange("(k p) c -> p k c", p=128)
        P.dma("sp", st3, src, reads=[dbuf[wname]], writes=[st_b])
        if not to_bf16:
            return st3, st_b
        if dst is not None:
            CP("pool", dst[0], st3, [st_b], [dst[1]])
            return dst
        wb, wb_b = (pools[1] if pools else wbf_pool).get()
        wb3 = wb[:, 0:kc * ncols].rearrange("p (k c) -> p k c", c=ncols)
        CP("pool", wb[:, 0:kc * ncols], st[:, 0:kc * ncols], [st_b], [wb_b])
        return wb3, wb_b

    def load_w_resident(name, wname, l, kc, ncols):
        wr, wr_b = A.bf16(name, kc * ncols)
        wr3 = wr.rearrange("p (k c) -> p k c", c=ncols)
        kstep = max(1, WSTG // ncols) if ncols <= WSTG else 1
        cstep = min(ncols, WSTG)
        for k0 in range(0, kc, kstep):
            kk = min(kstep, kc - k0)
            for c0 in range(0, ncols, cstep):
                cc = min(cstep, ncols - c0)
                load_w(wname, l, c0, cc, kk, row0=k0 * 128, dst=(wr3[:, k0:k0 + kk, c0:c0 + cc], wr_b))
        return wr3, wr_b


    def store(dst_ap, dname, src_ap, src_b):
        P.dma("pool", dst_ap, src_ap, reads=[src_b], writes=[dbuf[dname]])

    def compute_mod(l):
        for cb in range(24):
            w3, w_b = load_w("w_ada", l, cb * 256, 256, KC, to_bf16=False)
            pt, pb = next_ps()
            for m in range(2):
                for k in range(KC):
                    MM(pt[:, 2 * m:2 * m + 2], w3[:, k, m * 128:(m + 1) * 128], cact2_3[:, k, :], k == 0, k == KC - 1,
                       [w_b, lp_b], [pb])
            ptv = pt[:, 0:4].rearrange("p (m two) -> p m two", two=2)[:, :, 0]
            TT("dve", modT[:, cb * 2:cb * 2 + 2], ptv, b_adaT[:, cb * 2:cb * 2 + 2], ALU.add, [pb, lp_b], [lp_b])
        STT(A1, modT[:, 8:16], 1.0, n1gT, ALU.add, ALU.mult, [lp_b], [lp_b])
        STT(A2, modT[:, 32:40], 1.0, n2gT, ALU.add, ALU.mult, [lp_b], [lp_b])

    def norm_mod(Avec, Bvec):
        for tb in range(NB):
            sl = slice(tb * 512, (tb + 1) * 512)
            pt, pb = next_ps()
            for k in range(KC):
                sq, sq_b = stg_pool.get()
                ACT(sq, xT3[:, k, sl], AF.Square, [xT_b], [sq_b])
                MM(pt, cv["ones"], sq, k == 0, k == KC - 1, [sq_b, cst_b], [pb])
            rs, rs_b = stg_pool.get()
            ACT(rs, pt, AF.Sqrt, [pb], [rs_b], bias=epsT[:, 0:1], scale=1.0 / D)
            P.op("dve", lambda e, rs=rs: e.reciprocal(out=rs, in_=rs), [rs_b], [rs_b])
            for k in range(KC):
                tq, tq_b = stg_pool.get()
                TT("dve", tq, xT3[:, k, sl], rs, ALU.mult, [xT_b, rs_b], [tq_b])
                TS("dve", ns.hT3[:, k, sl], tq, Avec[:, k:k + 1], ALU.mult, [tq_b, lp_b], [ns.hT_b],
                   s2=Bvec[:, k:k + 1], op1=ALU.add)

    P.op("dve", lambda e: e.memset(epsT, EPS), [], [epsT_b])

    def proj_fm(wname, l, col0, ncols, rhs3, rhs_b, kc, epilogue, sub0=0, row0=0, blk=512):
        per = max(128, (WSTG // kc) // 128 * 128)
        per = min(per, blk)
        c = 0
        while c < ncols:
            n = min(per, ncols - c)
            w3, w_b = load_w(wname, l, col0 + c, n, kc, row0=row0)
            for m in range(n // 128):
                for tb in range(NB):
                    pt, pb = next_ps()
                    for k in range(kc):
                        MM(pt, w3[:, k, m * 128:(m + 1) * 128], rhs3[:, k, tb * 512:(tb + 1) * 512], k == 0, k == kc - 1,
                           [w_b, rhs_b], [pb])
                    epilogue(pt, pb, sub0 + (c // 128) + m, tb)
            c += n

    def dst_rows(dname, r0, r1):
        if isinstance(dname, GatherSet):
            return dname.loc_rows(r0, r1)
        return dram[dname][r0:r1, :], dname

    def rope_epilogue(src_sb, src_b, tb, dst_dram, dname, row):
        sl = slice(tb * 512, (tb + 1) * 512)
        pr, prb = next_ps()
        MM(pr, cv["rrot"], src_sb, True, True, [src_b, cst_b], [prb])
        u1, u1_b = stg_pool.get()
        TT("pool", u1, src_sb, ns.C4[:, sl], ALU.mult, [src_b, ns.C4_b], [u1_b])
        u2, u2_b = stg_pool.get()
        TT("dve", u2, pr, ns.S4[:, sl], ALU.mult, [prb, ns.S4_b], [u2_b])
        ob, ob_b = sbf_pool.get()
        TT("dve", ob, u1, u2, ALU.add, [u1_b, u2_b], [ob_b])
        dap, dn = dst_rows(dname, row * 128, (row + 1) * 128)
        store(dap[:, sl], dn, ob, ob_b)

    def qk_epilogue(gvec, dname):
        def ep(pt, pb, sub, tb):
            sq, sq_b = stg_pool.get()
            ACT(sq, pt, AF.Square, [pb], [sq_b])
            p2, p2b = next_ps()
            MM(p2, cv["blockones"], sq, True, True, [sq_b, cst_b], [p2b])
            rs, rs_b = stg_pool.get()
            ACT(rs, p2, AF.Sqrt, [p2b], [rs_b], bias=epsT[:, 0:1], scale=1.0 / HD)
            P.op("dve", lambda e, rs=rs: e.reciprocal(out=rs, in_=rs), [rs_b], [rs_b])
            qn, qn_b = stg_pool.get()
            STT(qn, pt, gvec, rs, ALU.mult, ALU.mult, [pb, lp_b, rs_b], [qn_b])
            rope_epilogue(qn, qn_b, tb, None, dname, sub)
        return ep

    def iq_epilogue(dname, nsub_real):
        def ep(pt, pb, sub, tb):
            qn, qn_b = stg_pool.get()
            CP("act", qn, pt, [pb], [qn_b])
            rope_epilogue(qn, qn_b, tb, None, dname, sub)
        return ep

    def z_epilogue(pt, pb, sub, tb):
        ob, ob_b = sbf_pool.get()
        ACT(ob, pt, AF.Silu, [pb], [ob_b])
        store(zT_d[sub * 128:(sub + 1) * 128, tb * 512:(tb + 1) * 512], "zT_d", ob, ob_b)

    def xbc_epilogue(pt, pb, sub, tb):
        o32, o32_b = stg_pool.get()
        CP("act" if (sub + tb) % 2 == 0 else "dve", o32, pt, [pb], [o32_b])
        store(xbc_raw[sub * 128:(sub + 1) * 128, tb * 512:(tb + 1) * 512], "xbc_raw", o32, o32_b)
        if tb == NB - 1:
            dap, dn = halo_gs.loc_rows(0, 128)
            store(dap[:, sub * 4:sub * 4 + 3], dn, o32[:, 509:512], o32_b)

    def gate_epilogue(pt, pb, sub, tb):
        o32, o32_b = stg_pool.get()
        ACT(o32, pt, AF.Sigmoid, [pb], [o32_b])
        store(gT_d[sub * 128:(sub + 1) * 128, tb * 512:(tb + 1) * 512], "gT_d", o32, o32_b)


    def v_projection(l):
        for qd in range(4):
            w3, w_b = load_w("w_in", l, C_V + qd * 256, 256, KC)
            for i in range(NT):
                va, va_b = ns.vaug[i % 2]
                va4 = va.rearrange("p (pr two c) -> p pr two c", two=2, c=128)
                pt, pb = next_ps()
                for k in range(KC):
                    MM(pt[:, 0:256], ns.hT3[:, k, i * 128:(i + 1) * 128], w3[:, k, :], k == 0, k == KC - 1, [ns.hT_b, w_b], [pb])
                pt4 = pt[:, 0:256].rearrange("p (pr two c) -> p pr two c", two=2, c=64)
                ev_ = "act" if i % 2 == 0 else "dve"
                CP(ev_, va4[:, :, 0, 0:64], pt4[:, :, 0, :], [pb], [va_b])
                CP(ev_, va4[:, :, 1, 64:128], pt4[:, :, 1, :], [pb], [va_b])
                dap, dn = v_gs.loc_rows(i * 128, (i + 1) * 128)
                store(dap[:, qd * 512:(qd + 1) * 512], dn, va, va_b)

    def small_projection(l):
        st, st_b = wst_pool.get()
        st3 = st[:, 0:KC * 40].rearrange("p (k c) -> p k c", c=40)
        P.dma("sp", st3[:, :, 0:32], W["w_in"][l][:, C_DT:C_DT + 32].rearrange("(k p) c -> p k c", p=128),
              reads=[dbuf["w_in"]], writes=[st_b])
        P.dma("sp", st3[:, :, 32:40], W["w_in"][l][:, C_IW:C_IW + 8].rearrange("(k p) c -> p k c", p=128),
              reads=[dbuf["w_in"]], writes=[st_b])
        wb, wb_b = wbf_pool.get()
        wb3 = wb[:, 0:KC * 40].rearrange("p (k c) -> p k c", c=40)
        CP("pool", wb[:, 0:KC * 40], st[:, 0:KC * 40], [st_b], [wb_b])
        for i in range(NT):
            pt, pb = next_ps()
            for k in range(KC):
                MM(pt[:, 0:40], ns.hT3[:, k, i * 128:(i + 1) * 128], wb3[:, k, :], k == 0, k == KC - 1, [ns.hT_b, wb_b], [pb])
            xx, xx_b = stg_pool.get()
            CP("dve", xx[:, 64:104], pt[:, 0:40], [pb], [xx_b])
            CP("dve", iw3[:, i, :], xx[:, 96:104], [xx_b], [iw_b])
            TT("dve", xx[:, 0:32], xx[:, 64:96], dtb_bc, ALU.add, [xx_b, dtb_b], [xx_b])
            STT(xx[:, 32:64], xx[:, 0:32], -1.0, xx[:, 0:32], ALU.mult, ALU.max, [xx_b], [xx_b])
            ACT(xx[:, 32:64], xx[:, 32:64], AF.Exp, [xx_b], [xx_b], scale=-1.0)
            ACT(xx[:, 32:64], xx[:, 32:64], AF.Ln, [xx_b], [xx_b], bias=oneT[:, 0:1], scale=1.0)
            STT(dt3[:, i, :], xx[:, 0:32], 0.0, xx[:, 32:64], ALU.max, ALU.add, [xx_b], [dt_b])
            STT(a3[:, i, :], dt3[:, i, :], -1.0, alog_bc, ALU.mult, ALU.mult, [dt_b, alog_b], [a_b])

    P.op("dve", lambda e: e.memset(oneT, 1.0), [], [oneT_b])

    def alloc_hT():
        hT, ns.hT_b = A.bf16("hT", KC * T)
        ns.hT3 = hT.rearrange("p (k t) -> p k t", t=T)

    def phase_A(l):
        if cfg.stage < 1:
            return
        alloc_hT()
        ns.C4, ns.C4_b = A.f32("C4", T)
        ns.S4, ns.S4_b = A.f32("S4", T)
        P.dma("sp", ns.C4, rope_d[0:128, :], reads=[dbuf["rope_d"]], writes=[ns.C4_b])
        P.dma("sp", ns.S4, rope_d[128:256, :], reads=[dbuf["rope_d"]], writes=[ns.S4_b])
        ns.vaug = [A.bf16("vaug%d" % i, 512) for i in range(2)]
        for va, va_b in ns.vaug:
            P.op("pool", lambda e, va=va: e.memset(va, 1.0), [], [va_b])
        layer_params(l)
        if cfg.stage < 2:
            return
        compute_mod(l)
        if cfg.stage < 3:
            return
        norm_mod(A1, modT[:, 0:8])
        if cfg.stage < 4:
            return
        proj_fm("w_in", l, C_Q, 1024, ns.hT3, ns.hT_b, KC, qk_epilogue(qg2[:, 0:1], "qT_d"))
        if cfg.stage < 5:
            return
        proj_fm("w_in", l, C_K, 1024, ns.hT3, ns.hT_b, KC, qk_epilogue(kg2[:, 0:1], kT_gs))
        v_projection(l)
        proj_fm("w_in", l, C_IQ, 512, ns.hT3, ns.hT_b, KC, iq_epilogue("iqT_d", 4))
        st, st_b = wst_pool.get()
        st3 = st[:, 0:KC * 128].rearrange("p (k c) -> p k c", c=128)
        for hf in range(2):
            P.dma("sp", st3[:, :, hf * 64:(hf + 1) * 64],
                  W["w_in"][l][:, C_IK:C_IK + 64].rearrange("(k p) c -> p k c", p=128),
                  reads=[dbuf["w_in"]], writes=[st_b])
        wb, wb_b = wbf_pool.get()
        wb3 = wb[:, 0:KC * 128].rearrange("p (k c) -> p k c", c=128)
        CP("pool", wb[:, 0:KC * 128], st[:, 0:KC * 128], [st_b], [wb_b])
        ikep = iq_epilogue(ik_gs, 1)
        for tb in range(NB):
            pt, pb = next_ps()
            for k in range(KC):
                MM(pt, wb3[:, k, :], ns.hT3[:, k, tb * 512:(tb + 1) * 512], k == 0, k == KC - 1, [wb_b, ns.hT_b], [pb])
            ikep(pt, pb, 0, tb)
        if cfg.stage < 6:
            return
        small_projection(l)
        if cfg.stage < 7:
            return
        proj_fm("w_in", l, C_Z, 2048, ns.hT3, ns.hT_b, KC, z_epilogue)
        proj_fm("w_in", l, C_XBC, 3072, ns.hT3, ns.hT_b, KC, xbc_epilogue)
        proj_fm("w_in", l, C_GA, 2048, ns.hT3, ns.hT_b, KC, gate_epilogue)
        if "dbg_small" in cfg.debug:
            dbg3 = dbg_small.rearrange("p (i c) -> p i c", c=80)
            store(dbg3[:, :, 0:8], "dbg_small", iw3, iw_b)
            store(dbg3[:, :, 8:40], "dbg_small", dt3, dt_b)
            store(dbg3[:, :, 40:72], "dbg_small", a3, a_b)
        if cfg.stage < 8:
            return
        kT_gs.gather()
        v_gs.gather()
        ik_gs.gather()
        halo_gs.gather()


    attnT_d = dten("attnT_d", [8 * 128, T], BF16, "ExternalInput" if "attnT_d" in cfg.feed else "Internal")
    ynT_d = dten("ynT_d", [16 * 128, T], BF16, "ExternalInput" if "ynT_d" in cfg.feed else "Internal")
    aT_d = dten("aT_d", [22 * 128, T], BF16)
    uhalo_gs = GatherSet("uhalo", 128, 44 * 2, F32, 4)

    mixT_d = dten("mixT_d", [8 * 128, T], BF16)

    def phase_D(l):
        m0_ = A.mark()
        wao3, wao_b = load_w_resident("wao", "w_attn_o", l, 8, D)
        wso3, wso_b = load_w_resident("wso", "w_ssm_o", l, 16, D)
        at, at_b = A.bf16("attn_blk", 8 * 512)
        at3 = at.rearrange("p (k t) -> p k t", t=512)
        yn, yn_b = A.bf16("yn_blk", 16 * 512)
        yn3 = yn.rearrange("p (k t) -> p k t", t=512)
        gpool = RPool("gate", 4, 512, "f32")
        for tb in range(NB):
            sl = slice(tb * 512, (tb + 1) * 512)
            P.dma("sp", at3, attnT_d.rearrange("(k p) t -> p k t", p=128)[:, :, sl], reads=[dbuf["attnT_d"]], writes=[at_b])
            P.dma("sp", yn3, ynT_d.rearrange("(k p) t -> p k t", p=128)[:, :, sl], reads=[dbuf["ynT_d"]], writes=[yn_b])
            for m in range(8):
                ga, ga_b = gpool.get()
                gm_, gm_b = gpool.get()
                P.dma("sp", ga, gT_d[m * 128:(m + 1) * 128, sl], reads=[dbuf["gT_d"]], writes=[ga_b])
                P.dma("sp", gm_, gT_d[(8 + m) * 128:(9 + m) * 128, sl], reads=[dbuf["gT_d"]], writes=[gm_b])
                pa, pab = next_ps()
                for k in range(8):
                    MM(pa, wao3[:, k, m * 128:(m + 1) * 128], at3[:, k, :], k == 0, k == 7, [wao_b, at_b], [pab])
                psm, psb = next_ps()
                for k in range(16):
                    MM(psm, wso3[:, k, m * 128:(m + 1) * 128], yn3[:, k, :], k == 0, k == 15, [wso_b, yn_b], [psb])
                t1_, t1b_ = stg_pool.get()
                TT("dve", t1_, pa, ga, ALU.mult, [pab, ga_b], [t1b_])
                t2_, t2b_ = stg_pool.get()
                TT("dve", t2_, psm, gm_, ALU.mult, [psb, gm_b], [t2b_])
                ob, ob_b = sbf_pool.get()
                TT("pool", ob, t1_, t2_, ALU.add, [t1b_, t2b_], [ob_b])
                store(mixT_d[m * 128:(m + 1) * 128, sl], "mixT_d", ob, ob_b)
        P.barrier()
        A.release(m0_)
        m0_ = A.mark()
        wout3, wout_b = load_w_resident("wout", "w_out", l, 8, D)
        mxs = [A.bf16("mix_blk%d" % i, 8 * 512) for i in range(2)]
        for tb in range(NB):
            sl = slice(tb * 512, (tb + 1) * 512)
            mx, mx_b = mxs[tb % 2]
            mx3 = mx.rearrange("p (k t) -> p k t", t=512)
            P.dma("sp", mx3, mixT_d.rearrange("(k p) t -> p k t", p=128)[:, :, sl], reads=[dbuf["mixT_d"]], writes=[mx_b])
            for m2 in range(8):
                po, pob = next_ps()
                for k in range(8):
                    MM(po, wout3[:, k, m2 * 128:(m2 + 1) * 128], mx3[:, k, :], k == 0, k == 7, [wout_b, mx_b], [pob])
                STT(xT3[:, m2, sl], po, modT[:, 16 + m2:17 + m2], xT3[:, m2, sl], ALU.mult, ALU.add, [pob, lp_b, xT_b], [xT_b])
        P.barrier()
        A.release(m0_)

    def phase_EF(l, rank_sel):
        m0_ = A.mark()
        alloc_hT()
        norm_mod(A2, modT[:, 24:32])
        uh, uh_b = A.f32("uhalo_sb", 44 * 2)
        uh3 = uh.rearrange("p (m two) -> p m two", two=2)
        for c0 in range(0, 2 * DFF, 256):
            w3, w_b = load_w("w_up", l, c0, 256, KC)
            pt, pb = next_ps()
            for m in range(2):
                for k in range(KC):
                    MM(pt[:, 2 * m:2 * m + 2], w3[:, k, m * 128:(m + 1) * 128], ns.hT3[:, k, T - 2:T], k == 0, k == KC - 1,
                       [w_b, ns.hT_b], [pb])
            CP("dve", uh3[:, (c0 // 128):(c0 // 128) + 2, :], pt[:, 0:4].rearrange("p (m two) -> p m two", two=2), [pb], [uh_b])
        dap, dn = uhalo_gs.loc_rows(0, 128)
        store(dap, dn, uh, uh_b)
        uhalo_gs.gather()
        pv, pv_b = A.f32("uprev", 44 * 2)
        pv3 = pv.rearrange("p (m two) -> p m two", two=2)
        load_prev_rows(uhalo_gs, pv3, pv_b, 44, 2)
        ub = [A.f32("ubuf%d" % i, 516) for i in range(4)]
        efp = (RPool("wstL", 4, KC * 128, "f32"), RPool("wbfL", 4, KC * 128, "bf16"))
        for m in range(22):
            wv3, wv_b = load_w("w_up", l, m * 128, 128, KC, pools=efp)
            wg3, wg_b = load_w("w_up", l, DFF + m * 128, 128, KC, pools=efp)
            conv_out = []
            for tb in range(NB):
                sl = slice(tb * 512, (tb + 1) * 512)
                res = []
                for which, (w3, w_b, sub) in enumerate(((wv3, wv_b, m), (wg3, wg_b, 22 + m))):
                    pt, pb = next_ps()
                    for k in range(KC):
                        MM(pt, w3[:, k, :], ns.hT3[:, k, sl], k == 0, k == KC - 1, [w_b, ns.hT_b], [pb])
                    u, u_b = ub[(tb % 2) * 2 + which]
                    if tb == 0:
                        CP("dve", u[:, 0:2], pv3[:, sub, :], [pv_b], [u_b])
                    else:
                        up, up_b = ub[((tb - 1) % 2) * 2 + which]
                        CP("dve", u[:, 0:2], up[:, 512:514], [up_b], [u_b])
                    CP("act", u[:, 2:514], pt, [pb], [u_b])
                    c_, c_b = stg_pool.get()
                    TS("dve", c_, u[:, 0:512], fcwT[:, sub:sub + 1], ALU.mult, [u_b, lp_b], [c_b],
                       s2=fcbT[:, sub:sub + 1], op1=ALU.add)
                    STT(c_, u[:, 1:513], fcwT[:, 44 + sub:45 + sub], c_, ALU.mult, ALU.add, [u_b, lp_b, c_b], [c_b])
                    STT(c_, u[:, 2:514], fcwT[:, 88 + sub:89 + sub], c_, ALU.mult, ALU.add, [u_b, lp_b, c_b], [c_b])
                    res.append((c_, c_b))
                (cv_, cvb_), (cg_, cgb_) = res
                sg, sg_b = stg_pool.get()
                ACT(sg, cg_, AF.Silu, [cgb_], [sg_b])
                ob, ob_b = sbf_pool.get()
                TT("pool", ob, sg, cv_, ALU.mult, [sg_b, cvb_], [ob_b])
                store(aT_d[m * 128:(m + 1) * 128, sl], "aT_d", ob, ob_b)
        P.barrier()
        A.release(m0_)
        m0_ = A.mark()
        wd3, wd_b = load_w_resident("wdown", "w_down", l, 22, D)
        ab = [A.bf16("a_blk%d" % i, 22 * 512) for i in range(1)]
        for tb in range(NB):
            sl = slice(tb * 512, (tb + 1) * 512)
            a_, a_b_ = ab[0]
            a3_ = a_.rearrange("p (k t) -> p k t", t=512)
            P.dma("sp", a3_, aT_d.rearrange("(k p) t -> p k t", p=128)[:, :, sl], reads=[dbuf["aT_d"]], writes=[a_b_])
            for m2 in range(8):
                po, pob = next_ps()
                for k in range(22):
                    MM(po, wd3[:, k, m2 * 128:(m2 + 1) * 128], a3_[:, k, :], k == 0, k == 21, [wd_b, a_b_], [pob])
                STT(xT3[:, m2, sl], po, modT[:, 40 + m2:41 + m2], xT3[:, m2, sl], ALU.mult, ALU.add, [pob, lp_b, xT_b], [xT_b])
        P.barrier()
        A.release(m0_)


    x_save = dten("x_save", [8 * 128, T], F32)

    def spill_x():
        P.dma("sp", x_save.rearrange("(k p) t -> p k t", p=128), xT3, reads=[xT_b], writes=[dbuf["x_save"]])
        P.barrier()

    def restore_x():
        P.barrier()
        P.dma("sp", xT3, x_save.rearrange("(k p) t -> p k t", p=128), reads=[dbuf["x_save"]], writes=[xT_b])

    NKEY = GSZ * T
    NKT = NKEY // 128
    NKB = NKEY // 512
    NIT = 20
    LO0 = -512.0
    BIGP = float(2.0 ** 100)
    NSPL = (int(NKEY * 0.41) // 512) * 512
    NACT = NKEY - NSPL
    SCALE = float(HD ** -0.5)

    def phase_B(l):
        m0_ = A.mark()
        score, score_b = A.f32("score", NKEY)
        mb, mbA_b = A.bf16("mb", NKEY)
        mbB_b = Buf("mbB")
        mbT_raw, mbT_b = A.f32("mbT", NKT * 512 // 4)
        mbT = mbT_raw.bitcast(FP8)
        mbT3 = mbT.rearrange("p (k t) -> p k t", t=512)
        ikT, ikT_b = A.bf16("ikT", NKEY)
        ikT3 = ikT.rearrange("p (r t) -> p r t", t=T)
        iqb, iqb_b = A.bf16("iq_blk", 4 * 512)
        iqb3 = iqb.rearrange("p (k t) -> p k t", t=512)
        qb_, qb_b = A.bf16("q_blk", 8 * 512)
        qb3 = qb_.rearrange("p (k t) -> p k t", t=512)
        kst = [A.bf16("kst%d" % i, 2 * 512) for i in range(3)]
        vst = [A.bf16("vst%d" % i, 4 * 512) for i in range(3)]
        ppool = RPool("pT", 4, 512, "bf16")
        ab_, ab_b = A.bf16("attn_o_blk", 2 * 512)
        ab3 = ab_.rearrange("p (k t) -> p k t", t=512)
        negd0, negd0_b = A.f32("negd0", 512)
        dg, dg_b = A.bf16("diagw", IH * 128)
        dg3 = dg.rearrange("p (h c) -> p h c", c=128)
        rlp = RPool("relu", 8, 256, "f32")
        rls = []
        rel_ctr = [0]
        sm, sm_b = A.f32("bis_small", 16)
        md, md_b = A.f32("bis_mid", 2)
        cn, cnt_b = A.f32("bis_cnt", 2)
        sa, sacc_b = A.f32("bis_sacc", 2)
        Wk, Wk_b = A.f32("bis_w", NIT)
        lo = sm[:, 0:1]
        mid = md[:, 0:1]
        nmid = md[:, 1:2]
        cnt = cn[:, 0:1]
        sacc = sa[:, 0:1]
        tot = sm[:, 5:6]
        ge = sm[:, 6:7]
        hi0 = sm[:, 7:8]
        qp0 = sm[:, 8:9]
        STT(qp0, rk[:, 0:1], float(T), cv["pidx"][:, 0:1], ALU.mult, ALU.add, [rk_b, cst_b], [sm_b])
        TS("dve", negd0, cv["iotaf"], qp0, ALU.subtract, [cst_b, sm_b], [negd0_b], s2=-BIGP, op1=ALU.mult)
        P.dma("sp", ikT3, ik_gs.g[0].rearrange("(r p) t -> p r t", p=128), reads=[dbuf["ik_g0"]], writes=[ikT_b])
        for qb in range(NB):
            qsl = slice(qb * 512, (qb + 1) * 512)
            kbnd = min(NKEY, (GSZ - 1) * T + (qb + 1) * 512)
            nkb_q = kbnd // 512
            nkt_q = kbnd // 128
            nspl_q = max(512, (int(kbnd * 0.41) // 512) * 512)
            nact_q = kbnd - nspl_q
            P.dma("sp", iqb3, iqT_d.rearrange("(k p) t -> p k t", p=128)[:, :, qsl], reads=[dbuf["iqT_d"]], writes=[iqb_b])
            P.dma("sp", qb3, qT_d.rearrange("(k p) t -> p k t", p=128)[:, :, qsl], reads=[dbuf["qT_d"]], writes=[qb_b])
            for qs in range(4):
                qt = qb * 4 + qs
                for h in range(IH):
                    TS("dve", dg3[:, h, :], ident_bf, iw3[:, qt, h:h + 1], ALU.mult, [ident_bf_b, iw_b], [dg_b])
                for kb in range(nkb_q):
                    ksl = slice(kb * 512, (kb + 1) * 512)
                    madd, madd_b = stg_pool.get()
                    TS("dve", madd, negd0, float((kb * 512 - qt * 128) * (-BIGP)), ALU.add, [negd0_b], [madd_b],
                       s2=0.0, op1=ALU.min)
                    lg = []
                    for h in range(IH):
                        hb = (h % 2) * 64
                        pt, pb = next_ps()
                        MM(pt, iqb3[hb:hb + 64, h // 2, qs * 128:(qs + 1) * 128], ikT[hb:hb + 64, ksl], True, True,
                           [iqb_b, ikT_b], [pb])
                        lg.append((pt, pb))
                        if len(lg) == 4 or h == IH - 1:
                            for (pt_, pb_) in lg:
                                rl32, rl_b = rlp.get()
                                rl = rl32.bitcast(BF16)[:, 0:512]
                                hh_ = rel_ctr[0]
                                rel_ctr[0] += 1
                                if hh_ % 4 != 3:
                                    ACT(rl, pt_, AF.Relu, [pb_], [rl_b])
                                else:
                                    TS("dve", rl, pt_, 0.0, ALU.max, [pb_], [rl_b])
                                rls.append((rl, rl_b))
                            lg = []
                    pacc, paccb = next_ps()
                    for h in range(IH):
                        rl, rl_b = rls[h]
                        MM(pacc, dg3[:, h, :], rl, h == 0, h == IH - 1, [dg_b, rl_b], [paccb])
                    del rls[:]
                    TT("dve", score[:, ksl], pacc, madd, ALU.add, [paccb, madd_b], [score_b])
                P.op("dve", lambda e, kbnd=kbnd: e.tensor_reduce(out=hi0, in_=score[:, 0:kbnd], axis=AX.X, op=ALU.max),
                     [score_b], [sm_b])
                TS("dve", tot, hi0, 1.0 - LO0, ALU.add, [sm_b], [sm_b])
                for k in range(NIT):
                    TS("dve", Wk[:, k:k + 1], tot, float(2.0 ** -(k + 1)), ALU.mult, [sm_b], [Wk_b])
                TS("dve", mid, Wk[:, 0:1], LO0, ALU.add, [Wk_b], [md_b])
                kthr = float(cfg.nkeep - nact_q / 2.0)
                for k in range(NIT):
                    ACT(mb[:, nspl_q:kbnd], score[:, nspl_q:kbnd], AF.Sign, [score_b, md_b], [mbB_b, sacc_b], bias=mid, scale=-1.0,
                        accum_out=sacc)
                    TS("dve", mb[:, 0:nspl_q], score[:, 0:nspl_q], mid, ALU.is_ge, [score_b, md_b], [mbA_b, cnt_b],
                       s2=0.0, op1=ALU.add, accum_out=cnt)
                    STT(tot, sacc, -0.5, cnt, ALU.mult, ALU.add, [sacc_b, cnt_b], [sm_b])
                    STT(ge, tot, kthr, Wk[:, k:k + 1], ALU.is_ge, ALU.mult, [sm_b, Wk_b], [sm_b])
                    if k < NIT - 1:
                        STT(mid, ge, Wk[:, k + 1:k + 2], mid, ALU.subtract, ALU.add, [sm_b, Wk_b, md_b], [md_b])
                    else:
                        STT(lo, ge, Wk[:, k:k + 1], mid, ALU.subtract, ALU.add, [sm_b, Wk_b, md_b], [sm_b])
                TS("dve", mb[:, 0:kbnd], score[:, 0:kbnd], lo, ALU.is_ge, [score_b, sm_b], [mbA_b, mbB_b])
                for k4 in range(nkt_q // 4):
                    pt, pb = next_ps()
                    ptb = pt.bitcast(BF16)
                    for u in range(4):
                        kt = k4 * 4 + u
                        P.op("pe", lambda e, ptb=ptb, u=u, kt=kt: e.transpose(out=ptb[:, u * 128:(u + 1) * 128],
                                                                             in_=mb[:, kt * 128:(kt + 1) * 128],
                                                                             identity=ident_bf),
                             [mbA_b, mbB_b, ident_bf_b], [pb])
                    dst_ = mbT3[:, k4 * 4:(k4 + 1) * 4, qs * 128:(qs + 1) * 128]
                    src_ = ptb[:, 0:512].rearrange("p (u t) -> p u t", t=128)
                    if k4 % 2 == 0:
                        P.op("act", lambda e, dst_=dst_, src_=src_: e.activation(out=dst_, in_=src_, func=AF.Copy, saturate=False),
                             [pb], [mbT_b])
                    else:
                        P.op("dve", lambda e, dst_=dst_, src_=src_: e.tensor_copy(out=dst_, in_=src_, saturate=False),
                             [pb], [mbT_b])
            for hg in range(4):
                accs = [psum[i_] for i_ in range(4)]
                nk4 = nkt_q // 4
                bufs = {}

                def issue_kv(k4, hg=hg, bufs=bufs):
                    kk, kk_b = kst[k4 % 3]
                    kk3 = kk.rearrange("p (pr t) -> p pr t", t=512)
                    vv, vv_b = vst[k4 % 3]
                    vv3 = vv.rearrange("p (u c) -> p u c", c=512)
                    r_ = (k4 * 512) // T
                    off = (k4 * 512) % T
                    for pr in range(2):
                        gap, gn = kT_gs.g_rows(r_, (hg * 2 + pr) * 128, (hg * 2 + pr + 1) * 128)
                        P.dma("sp", kk3[:, pr, :], gap[:, off:off + 512], reads=[dbuf[gn]], writes=[kk_b])
                    for u in range(4):
                        gap, gn = v_gs.g_rows(r_, off + u * 128, off + (u + 1) * 128)
                        P.dma("sp", vv3[:, u, :], gap[:, hg * 512:(hg + 1) * 512], reads=[dbuf[gn]], writes=[vv_b])
                    bufs[k4] = (kk3, kk_b, vv3, vv_b)

                steps = [(k4, u, hh) for k4 in range(nk4) for u in range(4) for hh in range(4)]
                LA = 2
                pend = {}
                issue_kv(0)

                def emit_qk(si):
                    k4, u, hh = steps[si]
                    if u == 0 and hh == 0 and k4 + 1 < nk4:
                        issue_kv(k4 + 1)
                    kk3, kk_b, vv3, vv_b = bufs[k4]
                    h = hg * 4 + hh
                    hb = (h % 2) * 64
                    pt, pb = next_ps_s()
                    MM(pt, kk3[hb:hb + 64, hh // 2, u * 128:(u + 1) * 128], qb3[hb:hb + 64, h // 2, :], True, True,
                       [kk_b, qb_b], [pb])
                    pend[si] = (pt, pb)

                def emit_pv(ti):
                    k4, u, hh = steps[ti]
                    kk3, kk_b, vv3, vv_b = bufs[k4]
                    kt = k4 * 4 + u
                    pt, pb = pend.pop(ti)
                    pT, pT_b = ppool.get()
                    ACT(pT, pt, AF.Exp, [pb], [pT_b], scale=SCALE)
                    TT("dve", pT, pT, mbT3[:, kt, :], ALU.mult, [pT_b, mbT_b], [pT_b])
                    acc, acc_b = accs[hh]
                    MM(acc, vv3[:, u, hh * 128:(hh + 1) * 128], pT, kt == 0, kt == nkt_q - 1, [vv_b, pT_b], [acc_b])

                for si in range(0, len(steps) + LA, 2):
                    if si < len(steps):
                        emit_qk(si)
                        emit_qk(si + 1)
                    ti = si - LA
                    if ti >= 0:
                        emit_pv(ti)
                        emit_pv(ti + 1)
                for hh in range(4):
                    h = hg * 4 + hh
                    hb = (h % 2) * 64
                    acc, acc_b = accs[hh]
                    osb, osb_b = stg_pool.get()
                    CP("act", osb, acc, [acc_b], [osb_b])
                    pw, pw_b = next_ps_s()
                    MM(pw, cv["swap"], osb, True, True, [cst_b, osb_b], [pw_b])
                    rc, rc_b = stg_pool.get()
                    P.op("dve", lambda e, rc=rc, pw=pw: e.reciprocal(out=rc, in_=pw), [pw_b], [rc_b])
                    TT("dve", ab3[hb:hb + 64, hh // 2, :], osb[hb:hb + 64, :], rc[hb:hb + 64, :], ALU.mult, [osb_b, rc_b], [ab_b])
                store(attnT_d.rearrange("(k p) t -> p k t", p=128)[:, hg * 2:hg * 2 + 2, qsl], "attnT_d", ab3, ab_b)
        print("phase B arena top", A.off, "of", A.n)
        P.barrier()
        A.release(m0_)

    ps_s_rr = [0]

    def next_ps_s():
        i = ps_s_rr[0]
        ps_s_rr[0] = (i + 1) % 4
        return psum[4 + i]


    NCH = T // 256
    st_gs = GatherSet("ssdF", 128, 2048, F32, 4)
    dd_gs = GatherSet("ssdD", 128, SH, F32, 4)

    def bcast_last(ap, n):
        return bass.AP(ap.tensor, ap.offset, [list(x) for x in ap.ap] + [[0, n]])

    def phase_C(l):
        m0_ = A.mark()
        rwp = RPool("rw", 3, 260, "f32")
        xsT, xsT_b = A.f32("xsT", 16 * 256)
        xsT3 = xsT.rearrange("p (m t) -> p m t", t=256)
        BT, BT_b = A.bf16("BT", 4 * 256)
        BT3 = BT.rearrange("p (g t) -> p g t", t=256)
        CT, CT_b = A.bf16("CT", 4 * 256)
        CT3 = CT.rearrange("p (g t) -> p g t", t=256)
        xdt, xdt_b = A.bf16("xdt", 2 * 2048)
        xdt3 = xdt.rearrange("p (i c) -> p i c", c=2048)
        xdd, xdd_b = A.bf16("xdd", 2 * 2048)
        xdd3 = xdd.rearrange("p (i c) -> p i c", c=2048)
        Btm, Btm_b = A.bf16("Btm", 2 * 512)
        Btm4 = Btm.rearrange("p (i g n) -> p i g n", g=4, n=128)
        acs, acs_b = A.f32("acs_tm", 2 * SH)
        acs3 = acs.rearrange("p (i h) -> p i h", h=SH)
        totb, totb_b = A.f32("tot_bc", SH)
        etot, etot_b = A.f32("etot", SH)
        ds_, ds_b = A.f32("ds", 2 * SH)
        ds3 = ds_.rearrange("p (i h) -> p i h", h=SH)
        Sst, Sst_b = A.f32("Sstate", 2048)
        Sbf, Sbf_b = A.bf16("Sstate_bf", 2048)
        Dacc, Dacc_b = A.f32("Dacc", SH)
        cbm, cbm_b = A.f32("CBm", 4 * 384)
        cbm3 = cbm.rearrange("p (g t) -> p g t", t=384)
        triL, triL_b = A.f32("triL", 512)
        yg, yg_b = A.f32("yg", 16 * 256)
        yg3 = yg.rearrange("p (m t) -> p m t", t=256)
        prev, prev_b = A.f32("xbc_prev", 24 * 4)
        prev3 = prev.rearrange("p (m c) -> p m c", c=4)
        zp = RPool("zt", 3, 256, "bf16")
        r0p = RPool("c_r0", 6, 512, "f32")
        dpp = RPool("c_d", 8, 384, "f32")
        ebp = RPool("c_eb", 8, 256, "f32")
        bcp = RPool("c_bc", 6, 256, "f32")
        gpp = RPool("c_G", 6, 384, "bf16")
        cep = RPool("c_Ce", 6, 256, "bf16")
        print("phase C arena top", A.off, "of", A.n)
        CP("dve", triL[:, 0:128], cv["tri"], [cst_b], [triL_b])
        CP("dve", triL[:, 128:256], cv["ones"], [cst_b], [triL_b])
        P.op("dve", lambda e: e.memset(triL[:, 256:384], 0.0), [], [triL_b])
        CP("dve", triL[:, 384:512], cv["tri"], [cst_b], [triL_b])
        load_prev_rows(halo_gs, prev3, prev_b, 24, 4)

        def conv_block(blk, c, out_ap, out_b, eng_out="act"):
            rw, rw_b = rwp.get()
            if c == 0:
                P.dma("sp", rw[:, 3:259], xbc_raw[blk * 128:(blk + 1) * 128, 0:256], reads=[dbuf["xbc_raw"]], writes=[rw_b])
                CP("pool", rw[:, 0:3], prev3[:, blk, 0:3], [prev_b], [rw_b])
            else:
                P.dma("sp", rw[:, 0:259], xbc_raw[blk * 128:(blk + 1) * 128, c * 256 - 3:c * 256 + 256],
                      reads=[dbuf["xbc_raw"]], writes=[rw_b])
            ac, ac_b = stg_pool.get()
            TS("dve", ac[:, 0:256], rw[:, 0:256], scwT[:, blk:blk + 1], ALU.mult, [rw_b, lp_b], [ac_b],
               s2=scbT[:, blk:blk + 1], op1=ALU.add)
            for tap in range(1, 4):
                STT(ac[:, 0:256], rw[:, tap:tap + 256], scwT[:, tap * 24 + blk:tap * 24 + blk + 1], ac[:, 0:256], ALU.mult, ALU.add,
                    [rw_b, lp_b, ac_b], [ac_b])
            ACT(out_ap, ac[:, 0:256], AF.Silu, [ac_b], [out_b])

        def ssd_pass(compute_y):
            for c in range(NCH):
                csl = slice(c * 256, (c + 1) * 256)
                for blk in range(16):
                    conv_block(blk, c, xsT3[:, blk, :], xsT_b)
                for g in range(4):
                    conv_block(16 + g, c, BT3[:, g, :], BT_b)
                if compute_y:
                    for g in range(4):
                        conv_block(20 + g, c, CT3[:, g, :], CT_b)
                for i in range(2):
                    ti = c * 2 + i
                    for bank in range(4):
                        pt, pb = next_ps()
                        for u in range(4):
                            blk = bank * 4 + u
                            P.op("pe", lambda e, pt=pt, u=u, blk=blk, i=i: e.transpose(
                                out=pt[:, u * 128:(u + 1) * 128], in_=xsT3[:, blk, i * 128:(i + 1) * 128], identity=cv["ident"]),
                                [xsT_b, cst_b], [pb])
                        TT("dve", xdt3[:, i, bank * 512:(bank + 1) * 512].rearrange("p (h q) -> p h q", q=64),
                           pt.rearrange("p (h q) -> p h q", q=64), bcast_last(dt3[:, ti, bank * 8:(bank + 1) * 8], 64), ALU.mult,
                           [pb, dt_b], [xdt_b])
                    pt, pb = next_ps()
                    ptb = pt.bitcast(BF16)
                    for g in range(4):
                        P.op("pe", lambda e, ptb=ptb, g=g, i=i: e.transpose(out=ptb[:, g * 128:(g + 1) * 128],
                                                                           in_=BT3[:, g, i * 128:(i + 1) * 128], identity=ident_bf),
                             [BT_b, ident_bf_b], [pb])
                    CP("act", Btm4[:, i, :, :], ptb[:, 0:512].rearrange("p (g n) -> p g n", n=128), [pb], [Btm_b])
                a0 = a3[:, c * 2, :]
                a1 = a3[:, c * 2 + 1, :]
                pt, pb = next_ps()
                MM(pt[:, 0:32], cv["tri"], a0, True, True, [cst_b, a_b], [pb])
                MM(pt[:, 32:64], cv["ones"], a0, True, False, [cst_b, a_b], [pb])
                MM(pt[:, 32:64], cv["tri"], a1, False, True, [cst_b, a_b], [pb])
                MM(pt[:, 64:96], cv["ones"], a0, True, False, [cst_b, a_b], [pb])
                MM(pt[:, 64:96], cv["ones"], a1, False, True, [cst_b, a_b], [pb])
                CP("dve", acs, pt[:, 0:64], [pb], [acs_b])
                CP("dve", totb, pt[:, 64:96], [pb], [totb_b])
                ACT(etot, totb, AF.Exp, [totb_b], [etot_b])
                TT("dve", Dacc, Dacc, totb, ALU.add, [Dacc_b, totb_b], [Dacc_b])
                for i in range(2):
                    TT("dve", ds3[:, i, :], totb, acs3[:, i, :], ALU.subtract, [totb_b, acs_b], [ds_b])
                ACT(ds_, ds_, AF.Exp, [ds_b], [ds_b])
                for i in range(2):
                    TT("pool", xdd3[:, i, :].rearrange("p (h q) -> p h q", q=64), xdt3[:, i, :].rearrange("p (h q) -> p h q", q=64),
                       bcast_last(ds3[:, i, :], 64), ALU.mult, [xdt_b, ds_b], [xdd_b])
                if compute_y:
                    CP("dve", Sbf, Sst, [Sst_b], [Sbf_b])
                    for g in range(4):
                        pt, pb = next_ps()
                        MM(pt[:, 0:256], BT3[:, g, 0:128], CT3[:, g, :], True, True, [BT_b, CT_b], [pb])
                        MM(pt[:, 256:384], BT3[:, g, 128:256], CT3[:, g, 128:256], True, True, [BT_b, CT_b], [pb])
                        TT("dve", cbm3[:, g, 0:128], pt[:, 0:128], cv["tri"], ALU.mult, [pb, cst_b], [cbm_b])
                        CP("dve", cbm3[:, g, 128:256], pt[:, 128:256], [pb], [cbm_b])
                        TT("dve", cbm3[:, g, 256:384], pt[:, 256:384], cv["tri"], ALU.mult, [pb, cst_b], [cbm_b])
                    NBQ = 8

                    def front(bq):
                        st_ = []
                        for hq in range(4):
                            h = bq * 4 + hq
                            r0, r0_b = r0p.get()
                            TS("dve", r0[:, 0:256], triL[:, 0:256], a0[:, h:h + 1], ALU.mult, [triL_b, a_b], [r0_b])
                            TS("pool", r0[:, 256:512], triL[:, 256:512], a1[:, h:h + 1], ALU.mult, [triL_b, a_b], [r0_b])
                            st_.append([h, r0, r0_b])
                        for e_ in st_:
                            h, r0, r0_b = e_
                            pbc, pbcb = next_ps()
                            MM(pbc[:, 0:256], cv["ones"], r0[:, 0:256], True, False, [cst_b, r0_b], [pbcb])
                            MM(pbc[:, 0:256], cv["ones"], r0[:, 256:512], False, True, [cst_b, r0_b], [pbcb])
                            e_ += [pbc, pbcb]
                        for e_ in st_:
                            h, r0, r0_b, pbc, pbcb = e_
                            d_, d_b = dpp.get()
                            bcs, bcs_b = bcp.get()
                            CP("act", bcs, pbc[:, 0:256], [pbcb], [bcs_b])
                            TS("dve", d_[:, 0:256], bcs[:, 0:256], acs3[:, 0, h:h + 1], ALU.subtract, [bcs_b, acs_b], [d_b],
                               s2=0.0, op1=ALU.min)
                            TS("dve", d_[:, 256:384], bcs[:, 128:256], acs3[:, 1, h:h + 1], ALU.subtract, [bcs_b, acs_b], [d_b],
                               s2=0.0, op1=ALU.min)
                            eb, eb_b = ebp.get()
                            ACT(eb, bcs, AF.Exp, [bcs_b], [eb_b])
                            e_ += [d_, d_b, eb, eb_b]
                        return st_

                    def back(bq, st_):
                        for e_ in st_:
                            h, r0, r0_b, pbc, pbcb, d_, d_b, eb, eb_b = e_
                            g = h // 8
                            ACT(d_, d_, AF.Exp, [d_b], [d_b])
                            Ce, Ce_b = cep.get()
                            TT("pool", Ce, eb, CT3[:, g, :], ALU.mult, [eb_b, CT_b], [Ce_b])
                            e_ += [Ce, Ce_b]
                        for e_ in st_:
                            h, d_, d_b = e_[0], e_[5], e_[6]
                            g = h // 8
                            G_, G_b = gpp.get()
                            TT("dve", G_, d_, cbm3[:, g, :], ALU.mult, [d_b, cbm_b], [G_b])
                            e_ += [G_, G_b]
                        for pq in range(2):
                            pr = bq * 2 + pq
                            py, pyb = next_ps()
                            for hh in range(2):
                                e_ = st_[pq * 2 + hh]
                                h, Ce, Ce_b, G_, G_b = e_[0], e_[9], e_[10], e_[11], e_[12]
                                hb = hh * 64
                                MM(py[hb:hb + 64, 0:256], xdt3[:, 0, h * 64:(h + 1) * 64], G_[:, 0:256], True, False, [xdt_b, G_b], [pyb])
                                MM(py[hb:hb + 64, 128:256], xdt3[:, 1, h * 64:(h + 1) * 64], G_[:, 256:384], False, False,
                                   [xdt_b, G_b], [pyb])
                                MM(py[hb:hb + 64, 0:256], Sbf[:, h * 64:(h + 1) * 64], Ce, False, True, [Sbf_b, Ce_b], [pyb])
                            zt, zt_b = zp.get()
                            P.dma("sp", zt, zT_d[pr * 128:(pr + 1) * 128, csl], reads=[dbuf["zT_d"]], writes=[zt_b])
                            yv, yv_b = stg_pool.get()
                            STT(yv[:, 0:256], xsT3[:, pr, :], dskT[:, pr:pr + 1], py[:, 0:256], ALU.mult, ALU.add,
                                [xsT_b, lp_b, pyb], [yv_b])
                            TT("pool", yg3[:, pr, :], yv[:, 0:256], zt, ALU.mult, [yv_b, zt_b], [yg_b])

                    prev_st = None
                    for bq in range(NBQ + 1):
                        cur = front(bq) if bq < NBQ else None
                        if prev_st is not None:
                            back(bq - 1, prev_st)
                        prev_st = cur
                    for g in range(4):
                        pss, pssb = next_ps()
                        for q in range(4):
                            sq, sq_b = stg_pool.get()
                            ACT(sq[:, 0:256], yg3[:, g * 4 + q, :], AF.Square, [yg_b], [sq_b])
                            MM(pss[:, 0:256], cv["ones"], sq[:, 0:256], q == 0, q == 3, [cst_b, sq_b], [pssb])
                        rs, rs_b = stg_pool.get()
                        ACT(rs[:, 0:256], pss[:, 0:256], AF.Sqrt, [pssb], [rs_b], bias=epsT[:, 0:1], scale=1.0 / 512.0)
                        P.op("dve", lambda e, rs=rs: e.reciprocal(out=rs[:, 0:256], in_=rs[:, 0:256]), [rs_b], [rs_b])
                        for q in range(4):
                            pr = g * 4 + q
                            ob, ob_b = sbf_pool.get()
                            STT(ob[:, 0:256], yg3[:, pr, :], sngT[:, pr:pr + 1], rs[:, 0:256], ALU.mult, ALU.mult,
                                [yg_b, lp_b, rs_b], [ob_b])
                            store(ynT_d[pr * 128:(pr + 1) * 128, csl], "ynT_d", ob[:, 0:256], ob_b)
                for g in range(4):
                    pst, pstb = next_ps()
                    for i in range(2):
                        MM(pst, Btm4[:, i, g, :], xdd3[:, i, g * 512:(g + 1) * 512], i == 0, i == 1, [Btm_b, xdd_b], [pstb])
                    sg3 = Sst[:, g * 512:(g + 1) * 512].rearrange("p (h q) -> p h q", q=64)
                    TT("dve", sg3, sg3, bcast_last(etot[:, g * 8:(g + 1) * 8], 64), ALU.mult, [Sst_b, etot_b], [Sst_b])
                    TT("dve", Sst[:, g * 512:(g + 1) * 512], Sst[:, g * 512:(g + 1) * 512], pst, ALU.add, [Sst_b, pstb], [Sst_b])

        P.op("dve", lambda e: e.memset(Sst, 0.0), [], [Sst_b])
        P.op("dve", lambda e: e.memset(Dacc, 0.0), [], [Dacc_b])
        ssd_pass(False)
        dap, dn = st_gs.loc_rows(0, 128)
        store(dap, dn, Sst, Sst_b)
        dap, dn = dd_gs.loc_rows(0, 128)
        store(dap, dn, Dacc, Dacc_b)
        st_gs.gather()
        dd_gs.gather()
        Dr = []
        for r in range(GSZ - 1):
            t_, tb_ = A.f32("Dr%d" % r, SH)
            gap, gn = dd_gs.g_rows(r, 0, 128)
            P.dma("sp", t_, gap, reads=[dbuf[gn]], writes=[tb_])
            Dr.append((t_, tb_))
        lt, lt_b = A.f32("ltflag", 4)
        for m in range(GSZ - 1):
            TS("dve", lt[:, m:m + 1], rk[:, 0:1], float(m), ALU.is_gt, [rk_b], [lt_b])
        P.op("dve", lambda e: e.memset(Sst, 0.0), [], [Sst_b])
        Fr, Fr_b = A.f32("Fr", 2048)
        for r in range(GSZ - 1):
            wr, wr_b = A.f32("wr%d" % r, SH)
            P.op("dve", lambda e, wr=wr: e.memset(wr, 0.0), [], [wr_b])
            for m in range(r + 1, GSZ - 1):
                STT(wr, Dr[m][0], lt[:, m:m + 1], wr, ALU.mult, ALU.add, [Dr[m][1], lt_b, wr_b], [wr_b])
            ACT(wr, wr, AF.Exp, [wr_b], [wr_b])
            TS("dve", wr, wr, lt[:, r:r + 1], ALU.mult, [wr_b, lt_b], [wr_b])
            gap, gn = st_gs.g_rows(r, 0, 128)
            P.dma("sp", Fr, gap, reads=[dbuf[gn]], writes=[Fr_b])
            F3 = Fr.rearrange("p (h q) -> p h q", q=64)
            TT("dve", F3, F3, bcast_last(wr, 64), ALU.mult, [Fr_b, wr_b], [Fr_b])
            TT("dve", Sst, Sst, Fr, ALU.add, [Sst_b, Fr_b], [Sst_b])
        ssd_pass(True)
        P.barrier()
        A.release(m0_)

    P.dma("sp", rk[:, 0:4], bcast_ap(rank_in, 128, 4), reads=[dbuf["rank"]], writes=[rk_b])
    for r in range(GSZ):
        TS("dve", prevsel[:, r:r + 1], rk[:, 0:1], float(r + 1), ALU.is_equal, [rk_b], [prevsel_b])

    def load_prev_rows(gs, dst3, dst_b, nsub, ncols):
        first = True
        for r in range(GSZ - 1):
            tmp_, tmpb_ = A.f32("prevtmp%d" % r, nsub * ncols)
            tmp3 = tmp_.rearrange("p (m c) -> p m c", c=ncols)
            gap, gn = gs.g_rows(r, 0, 128)
            P.dma("sp", tmp_, gap, reads=[dbuf[gn]], writes=[tmpb_])
            if first:
                TS("dve", dst3, tmp3, prevsel[:, r:r + 1], ALU.mult, [tmpb_, prevsel_b], [dst_b])
                first = False
            else:
                STT(dst3, tmp3, prevsel[:, r:r + 1], dst3, ALU.mult, ALU.add, [tmpb_, prevsel_b, dst_b], [dst_b])

    for l in range(depth):
        A.release(m_x)
        phase_A(l)
        P.barrier()
        A.release(m_x)
        if cfg.stage >= 30:
            spill_x()
            A.release(m_pers)
            phase_B(l)
            A.release(m_pers)
            if cfg.stage >= 40:
                phase_C(l)
            A.release(m_x)
            restore_x()
        if cfg.stage >= 20:
            phase_D(l)
        if cfg.stage >= 21:
            phase_EF(l, None)
    A.release(m_x)
    for name in dbg_copies:
        rows = dram[name].shape[0]
        for r0 in range(0, rows, 512):
            r1 = min(rows, r0 + 512)
            P.dma("sp", dram[name + "_dbg"][r0:r1, :], dram[name][r0:r1, :], reads=[dbuf[name]], writes=[dbuf[name + "_dbg"]])
    P.barrier()

    m0 = A.mark()
    outs = [A.f32("otok%d" % i, D) for i in range(2)]
    for i in range(NT):
        ot, ot_b = outs[i % 2]
        for k in range(KC):
            pt, pb = next_ps()
            P.op("pe", lambda e, pt=pt, k=k, i=i: e.transpose(out=pt[:, 0:128], in_=xT3[:, k, i * 128:(i + 1) * 128],
                                                             identity=cv["ident"]),
                 reads=[xT_b, cst_b], writes=[pb])
            if k % 2 == 0:
                P.op("act", lambda e, pt=pt, k=k, ot=ot: e.copy(out=ot[:, k * 128:(k + 1) * 128], in_=pt[:, 0:128]),
                     reads=[pb], writes=[ot_b])
            else:
                P.op("dve", lambda e, pt=pt, k=k, ot=ot: e.tensor_copy(out=ot[:, k * 128:(k + 1) * 128], in_=pt[:, 0:128]),
                     reads=[pb], writes=[ot_b])
        P.dma("sp", y_out[i * 128:(i + 1) * 128, :], ot, reads=[ot_b], writes=[dbuf["y"]])
    P.barrier()
    A.release(m0)

    P.emit(stack)
    stack.close()
    return nc


def make_in_maps(cfg, inputs):
    T = cfg.T
    depth = cfg.depth
    maps = []
    f = lambda a: np.ascontiguousarray(np.asarray(a, dtype=np.float32))
    shared = {
        "consts": CONST_ARR,
        "w_ada": f(inputs["w_ada"]), "b_ada": f(inputs["b_ada"]).reshape(depth, 48, 128),
        "norm1_g": f(inputs["norm1_g"]).reshape(depth, KC, 128), "w_in": f(inputs["w_in"]),
        "q_norm_g": f(inputs["q_norm_g"]).reshape(depth, 1, 64), "k_norm_g": f(inputs["k_norm_g"]).reshape(depth, 1, 64),
        "ssm_conv_w": f(inputs["ssm_conv_w"]).reshape(depth, 4, 24, 128),
        "ssm_conv_b": f(inputs["ssm_conv_b"]).reshape(depth, 24, 128),
        "dt_bias": f(inputs["dt_bias"]).reshape(depth, 1, SH), "a_log": f(inputs["a_log"]).reshape(depth, 1, SH),
        "d_skip": f(inputs["d_skip"]).reshape(depth, 1, SH), "ssm_norm_g": f(inputs["ssm_norm_g"]).reshape(depth, 16, 128),
        "w_attn_o": f(inputs["w_attn_o"]), "w_ssm_o": f(inputs["w_ssm_o"]), "w_out": f(inputs["w_out"]),
        "norm2_g": f(inputs["norm2_g"]).reshape(depth, KC, 128), "w_up": f(inputs["w_up"]),
        "ffn_conv_w": f(inputs["ffn_conv_w"]).reshape(depth, 3, 44, 128),
        "ffn_conv_b": f(inputs["ffn_conv_b"]).reshape(depth, 44, 128), "w_down": f(inputs["w_down"]),
    }
    x = f(inputs["x"])
    c = f(inputs["c"])
    pos = np.ascontiguousarray(np.asarray(inputs["positions"], dtype=np.int32))
    for r in range(NCORES):
        b, j = divmod(r, GSZ)
        m = dict(shared)
        m["x"] = np.ascontiguousarray(x[b, j * T:(j + 1) * T, :])
        m["c"] = np.ascontiguousarray(c[b].reshape(KC, 128))
        m["positions"] = np.ascontiguousarray(pos[b, j * T:(j + 1) * T].reshape(1, T))
        rk = np.zeros((1, 4), np.float32)
        rk[0, 0] = j
        m["rank"] = rk
        for k_, v_ in (getattr(cfg, "feed_data", None) or {}).items():
            m[k_] = v_[r]
        maps.append(m)
    return maps


_CACHE = {}


def run(cfg, inputs):
    key = (cfg.S, cfg.depth, tuple(sorted(cfg.debug)), cfg.stage, tuple(sorted(cfg.feed)))
    if key not in _CACHE:
        _CACHE[key] = build_program(cfg)
    nc = _CACHE[key]
    maps = make_in_maps(cfg, inputs)
    res = run_bass_kernel_spmd(nc, maps, core_ids=list(range(NCORES)))
    return res.results


def kernel(**inputs):
    cfg = Cfg(seq=int(np.asarray(inputs["x"]).shape[1]), depth=int(np.asarray(inputs["w_in"]).shape[0]))
    results = run(cfg, inputs)
    B = np.asarray(inputs["x"]).shape[0]
    out = np.zeros((B, cfg.S, D), np.float32)
    for r in range(NCORES):
        b, j = divmod(r, GSZ)
        out[b, j * cfg.T:(j + 1) * cfg.T, :] = results[r]["y"]
    return out
```

```python
from contextlib import ExitStack
import numpy as np
import ml_dtypes
import concourse.bass as bass
import concourse.mybir as mybir
from concourse.bass_utils import run_bass_kernel_spmd

F32 = mybir.dt.float32
BF16 = mybir.dt.bfloat16
I32 = mybir.dt.int32
FP8 = mybir.dt.float8e5
AF = mybir.ActivationFunctionType
ALU = mybir.AluOpType
AX = mybir.AxisListType

NCORES = 8
GSZ = 4
D = 1024
KC = D // 128
HEADS = 16
HD = 64
IH = 8
TOPK = 256
DI = 2048
SH = 32
SG = 4
NST = 128
XBC = DI + 2 * SG * NST
DFF = 2816
EPS = 1e-6
C_Q, C_K, C_V, C_IQ, C_IK, C_IW, C_Z, C_XBC, C_DT, C_GA, C_GM = (
    0, 1024, 2048, 3072, 3584, 3648, 3656, 5704, 8776, 8808, 9832)
INW = 10856
NEG = -30000.0


class Buf:
    __slots__ = ("name", "w", "r")

    def __init__(self, name):
        self.name = name
        self.w = None
        self.r = {}


class Prog:
    ENGS = ("pe", "act", "dve", "pool", "sp")

    def __init__(self, nc, n_dma=40):
        self.nc = nc
        self.ops = {e: [] for e in self.ENGS}
        self.cnt = {e: 0 for e in self.ENGS}
        self.known = {e: {} for e in self.ENGS}
        self.dma_val = [0] * n_dma
        self.dma_next = 0
        self.cc_val = 0

    def _need(self, eng, k, v):
        if k == eng and eng == "pe":
            return
        kn = self.known[eng]
        if kn.get(k, 0) >= v:
            return
        kn[k] = v
        self.ops[eng].append(("wait", k, v))

    def _deps(self, eng, reads, writes):
        for b in reads:
            if b.w is not None:
                self._need(eng, *b.w)
        for b in writes:
            if b.w is not None:
                self._need(eng, *b.w)
            for k, v in b.r.items():
                self._need(eng, k, v)

    def _mark(self, tok, reads, writes):
        for b in writes:
            b.w = tok
            b.r = {}
        for b in reads:
            if b in writes:
                continue
            if b.r.get(tok[0], 0) < tok[1]:
                b.r[tok[0]] = tok[1]

    def op(self, eng, fn, reads=(), writes=()):
        self._deps(eng, reads, writes)
        self.cnt[eng] += 1
        tok = (eng, self.cnt[eng])
        self.ops[eng].append(("ins", fn, eng, 1, self._where()))
        self._mark(tok, reads, writes)
        return tok

    DEBUG_WHERE = False

    def _where(self):
        if not Prog.DEBUG_WHERE:
            return None
        import traceback
        return [(f.lineno, f.name) for f in traceback.extract_stack(limit=6)[:-2]]

    def dma(self, q, out, in_, reads=(), writes=()):
        i = self.dma_next
        self.dma_next = (i + 1) % len(self.dma_val)
        key = ("dma", i)
        self._deps(q, reads, writes)
        if self.dma_val[i]:
            self._need(q, key, self.dma_val[i])
        self.dma_val[i] += 16
        tok = (key, self.dma_val[i])
        self.ops[q].append(("ins", lambda e, o=out, s=in_: e.dma_start(out=o, in_=s), key, 16))
        self._mark(tok, reads, writes)
        return tok

    def collective(self, fn, reads=(), writes=()):
        self._deps("pool", reads, writes)
        self.cc_val += 1
        tok = ("cc", self.cc_val)
        self.ops["pool"].append(("ins", fn, "cc", 1))
        self._mark(tok, reads, writes)
        return tok

    def barrier(self):
        for e in self.ENGS:
            for f in ("pe", "act", "dve", "pool"):
                if self.cnt[f]:
                    self._need(e, f, self.cnt[f])
            for i, v in enumerate(self.dma_val):
                if v:
                    self._need(e, ("dma", i), v)
            if self.cc_val:
                self._need(e, "cc", self.cc_val)

    def emit(self, stack):
        nc = self.nc
        sems = {}
        for e in ("pe", "act", "dve", "pool"):
            sems[e] = stack.enter_context(nc.semaphore("s_" + e))
        for i in range(len(self.dma_val)):
            sems[("dma", i)] = stack.enter_context(nc.semaphore("d%d" % i))
        sems["cc"] = stack.enter_context(nc.semaphore("s_cc"))
        block = stack.enter_context(nc.Block())

        def mk(name):
            def body(eng):
                for o in self.ops[name]:
                    if o[0] == "wait":
                        eng.wait_ge(sems[o[1]], o[2])
                    else:
                        ins = o[1](eng)
                        ins.then_inc(sems[o[2]], o[3])
                        if Prog.DEBUG_WHERE and len(o) > 4:
                            print("INS", name, getattr(getattr(ins, "ins", None), "name", None), o[4])
            return body

        block.tensor(mk("pe"))
        block.scalar(mk("act"))
        block.vector(mk("dve"))
        block.gpsimd(mk("pool"))
        block.sync(mk("sp"))


class Arena:
    def __init__(self, big, nwords):
        self.big = big
        self.n = nwords
        self.off = 0

    def mark(self):
        return self.off

    def release(self, m):
        self.off = m

    def f32(self, name, cols):
        a = self.off
        self.off += cols
        assert self.off <= self.n, ("SBUF arena overflow", name, self.off, self.n)
        return self.big[:, a:a + cols], Buf(name)

    def bf16(self, name, cols):
        w = (cols + 1) // 2
        a = self.off
        self.off += w
        assert self.off <= self.n, ("SBUF arena overflow", name, self.off, self.n)
        return self.big[:, a:a + w].bitcast(BF16)[:, 0:cols], Buf(name)


def make_consts():
    c = {}
    c["ident"] = np.eye(128, dtype=np.float32)
    c["ones"] = np.ones((128, 128), np.float32)
    bo = np.zeros((128, 128), np.float32)
    bo[:64, :64] = 1.0
    bo[64:, 64:] = 1.0
    c["blockones"] = bo
    rr = np.zeros((128, 128), np.float32)
    for m in range(128):
        if (m % 64) < 32:
            rr[m + 32, m] = -1.0
        else:
            rr[m - 32, m] = 1.0
    c["rrot"] = rr
    tri = (np.arange(128)[:, None] <= np.arange(128)[None, :]).astype(np.float32)
    c["tri"] = tri
    c["causb"] = np.where(np.arange(128)[None, :] <= np.arange(128)[:, None], 0.0, -1e30).astype(np.float32)
    invf = (1.0 / (10000.0 ** (np.arange(0, 64, 2, dtype=np.float32) / 64.0))).astype(np.float32)
    c["invf"] = np.tile(invf, 4)[:, None].astype(np.float32) * np.ones((1, 128), np.float32)
    c["iotaf"] = np.tile(np.arange(512, dtype=np.float32)[None, :], (128, 1))
    c["pidx"] = np.tile(np.arange(128, dtype=np.float32)[:, None], (1, 128))
    sw = np.zeros((128, 128), np.float32)
    for m in range(128):
        sw[(m + 64) % 128, m] = 1.0
    c["swap"] = sw
    names = ["ident", "ones", "blockones", "rrot", "tri", "causb", "invf", "iotaf", "pidx", "swap"]
    return [(n, c[n].shape[1]) for n in names], np.concatenate([c[n] for n in names], axis=1)


CONST_NAMES, CONST_ARR = make_consts()


class Cfg:
    def __init__(self, seq=8192, depth=4, debug=(), stage=99, feed=()):
        self.stage = stage
        self.feed = set(feed)
        self.S = seq
        self.T = seq // GSZ
        self.depth = depth
        self.debug = set(debug)
        self.NT = self.T // 128
        self.NB = self.T // 512
        self.NCH = self.T // 256
        self.nkeep = min(TOPK, seq // 4)


def build_program(cfg):
    T, NT, NB, depth = cfg.T, cfg.NT, cfg.NB, cfg.depth
    nc = bass.Bass("TRN2", target_bir_lowering=False)
    stack = ExitStack()
    P = Prog(nc)
    dram = {}
    dbuf = {}

    dbg_copies = []

    def dten(name, shape, dtype, kind="Internal"):
        if name in cfg.debug:
            if "_g" in name[-4:]:
                dcp = nc.dram_tensor(name + "_dbg", list(shape), dtype, kind="ExternalOutput").ap()
                dram[name + "_dbg"] = dcp
                dbuf[name + "_dbg"] = Buf(name + "_dbg")
                dbg_copies.append(name)
            else:
                kind = "ExternalOutput"
        t = nc.dram_tensor(name, list(shape), dtype, kind=kind).ap()
        dram[name] = t
        dbuf[name] = Buf(name)
        return t

    x_in = dten("x", [T, D], F32, "ExternalInput")
    c_in = dten("c", [KC, 128], F32, "ExternalInput")
    pos_in = dten("positions", [1, T], I32, "ExternalInput")
    consts_in = dten("consts", [128, CONST_ARR.shape[1]], F32, "ExternalInput")
    rank_in = dten("rank", [1, 4], F32, "ExternalInput")
    W = {}
    for nm, shp in (("w_ada", [depth, D, 6 * D]), ("b_ada", [depth, 48, 128]), ("norm1_g", [depth, KC, 128]),
                    ("w_in", [depth, D, INW]), ("q_norm_g", [depth, 1, 64]), ("k_norm_g", [depth, 1, 64]),
                    ("ssm_conv_w", [depth, 4, 24, 128]), ("ssm_conv_b", [depth, 24, 128]),
                    ("dt_bias", [depth, 1, SH]), ("a_log", [depth, 1, SH]), ("d_skip", [depth, 1, SH]),
                    ("ssm_norm_g", [depth, 16, 128]), ("w_attn_o", [depth, D, D]), ("w_ssm_o", [depth, DI, D]),
                    ("w_out", [depth, D, D]), ("norm2_g", [depth, KC, 128]), ("w_up", [depth, D, 2 * DFF]),
                    ("ffn_conv_w", [depth, 3, 44, 128]), ("ffn_conv_b", [depth, 44, 128]),
                    ("w_down", [depth, DFF, D])):
        W[nm] = dten(nm, shp, F32, "ExternalInput")
    y_out = dten("y", [T, D], F32, "ExternalOutput")

    NWORDS = 52224
    big = stack.enter_context(nc.sbuf_tensor("arena", [128, NWORDS], F32))
    A = Arena(big, NWORDS)
    psum = []
    for i in range(8):
        pt = stack.enter_context(nc.psum_tensor("ps%d" % i, [128, 512], F32))
        psum.append((pt[:, :], Buf("ps%d" % i)))
    ps_rr = [0]

    def next_ps():
        i = ps_rr[0]
        ps_rr[0] = (i + 1) % 8
        return psum[i]

    cst, cst_b = A.f32("consts", CONST_ARR.shape[1])
    cv = {}
    o = 0
    for n, wd in CONST_NAMES:
        cv[n] = cst[:, o:o + wd]
        o += wd
    ident_bf, ident_bf_b = A.bf16("ident_bf", 128)

    class RPool:
        def __init__(self, name, n, cols, kind):
            self.t = [(A.f32 if kind == "f32" else A.bf16)("%s%d" % (name, i), cols) for i in range(n)]
            self.i = 0

        def get(self):
            r = self.t[self.i]
            self.i = (self.i + 1) % len(self.t)
            return r


    lp, lp_b = A.f32("lp", 512)
    dtb_bc, dtb_b = A.f32("dtb_bc", SH)
    alog_bc, alog_b = A.f32("alog_bc", SH)
    iw_tm, iw_b = A.f32("iw_tm", NT * 8)
    dt_tm, dt_b = A.f32("dt_tm", NT * SH)
    a_tm, a_b = A.f32("a_tm", NT * SH)
    halfsel, halfsel_b = A.f32("halfsel", 128)
    lfm_pool = RPool("lfm", 3, 128, "f32")
    WSTG = 2048
    wst_pool = RPool("wst", 2, WSTG, "f32")
    wbf_pool = RPool("wbf", 2, WSTG, "bf16")
    stg_pool = RPool("stg", 8, 512, "f32")
    sbf_pool = RPool("sbf", 4, 512, "bf16")
    epsT, epsT_b = A.f32("epsT", 2)
    oneT, oneT_b = A.f32("oneT", 2)
    rk, rk_b = A.f32("rk", 8)
    prevsel, prevsel_b = A.f32("prevsel", 4)
    class NS:
        pass
    ns = NS()
    m_pers = A.mark()
    xT, xT_b = A.f32("xT", KC * T)
    xT3 = xT.rearrange("p (k t) -> p k t", t=T)
    m_x = A.mark()

    P.dma("sp", cst, consts_in, reads=[dbuf["consts"]], writes=[cst_b])
    P.op("dve", lambda e: e.tensor_copy(out=ident_bf, in_=cv["ident"]), reads=[cst_b], writes=[ident_bf_b])

    def transpose_f32(dst, dst_b, src, src_b, rows, cols, evac="act"):
        pt, pb = next_ps()
        P.op("pe", lambda e: e.transpose(out=pt[0:cols, 0:rows], in_=src, identity=cv["ident"][0:rows, 0:rows]),
             reads=[src_b, cst_b], writes=[pb])
        if evac == "act":
            P.op("act", lambda e: e.copy(out=dst, in_=pt[0:cols, 0:rows]), reads=[pb], writes=[dst_b])
        else:
            P.op("dve", lambda e: e.tensor_copy(out=dst, in_=pt[0:cols, 0:rows]), reads=[pb], writes=[dst_b])

    def load_fm(dst, dst_b, src_ap, src_name, nrow):
        m = A.mark()
        tmp, tmp_b = A.f32("lfm_tmp", 128)
        P.dma("sp", tmp[0:nrow, :], src_ap, reads=[dbuf[src_name]], writes=[tmp_b])
        transpose_f32(dst, dst_b, tmp[0:nrow, :], tmp_b, nrow, 128)
        A.release(m)
        return tmp_b

    m0 = A.mark()
    xtoks = [A.f32("xtok%d" % i, D) for i in range(2)]
    for i in range(NT):
        xt, xt_b = xtoks[i % 2]
        P.dma("sp", xt, x_in[i * 128:(i + 1) * 128, :], reads=[dbuf["x"]], writes=[xt_b])
        for k in range(KC):
            pt, pb = next_ps()
            P.op("pe", lambda e, pt=pt, xt=xt, k=k: e.transpose(out=pt[:, 0:128], in_=xt[:, k * 128:(k + 1) * 128],
                                                               identity=cv["ident"]),
                 reads=[xt_b, cst_b], writes=[pb])
            eng = "act" if k % 2 == 0 else "dve"
            if eng == "act":
                P.op("act", lambda e, pt=pt, k=k, i=i: e.copy(out=xT3[:, k, i * 128:(i + 1) * 128], in_=pt[:, 0:128]),
                     reads=[pb], writes=[xT_b])
            else:
                P.op("dve", lambda e, pt=pt, k=k, i=i: e.tensor_copy(out=xT3[:, k, i * 128:(i + 1) * 128], in_=pt[:, 0:128]),
                     reads=[pb], writes=[xT_b])
    P.barrier()
    A.release(m0)


    def ACT(out, in_, func, reads, writes, **kw):
        P.op("act", lambda e: e.activation(out=out, in_=in_, func=func, **kw), reads, writes)

    def TS(eng, out, in0, s1, op0, reads, writes, s2=None, op1=None, **kw):
        if op1 is None:
            P.op(eng, lambda e: e.tensor_scalar(out=out, in0=in0, scalar1=s1, scalar2=None, op0=op0, **kw), reads, writes)
        else:
            P.op(eng, lambda e: e.tensor_scalar(out=out, in0=in0, scalar1=s1, scalar2=s2, op0=op0, op1=op1, **kw),
                 reads, writes)

    def TT(eng, out, in0, in1, op, reads, writes):
        P.op(eng, lambda e: e.tensor_tensor(out=out, in0=in0, in1=in1, op=op), reads, writes)

    def STT(out, in0, scalar, in1, op0, op1, reads, writes):
        P.op("dve", lambda e: e.scalar_tensor_tensor(out=out, in0=in0, scalar=scalar, in1=in1, op0=op0, op1=op1),
             reads, writes)

    def MM(out, lhsT, rhs, start, stop, reads, writes):
        P.op("pe", lambda e: e.matmul(out, lhsT, rhs, start=start, stop=stop), reads, writes)

    def CP(eng, out, in_, reads, writes):
        if eng == "act":
            P.op("act", lambda e: e.copy(out=out, in_=in_), reads, writes)
        else:
            P.op(eng, lambda e: e.tensor_copy(out=out, in_=in_), reads, writes)

    def bcast_ap(ap2d_row, nparts, ncols, offset_elems=0):
        return bass.AP(ap2d_row.tensor, ap2d_row.offset + offset_elems, [[0, nparts], [1, ncols]])

    qT_d = dten("qT_d", [8 * 128, T], BF16)
    groups = [[0, 1, 2, 3], [4, 5, 6, 7]]

    class GatherSet:
        def __init__(self, name, rows, cols, dtype, esz):
            rpc = rows
            while rpc * cols * esz > (1 << 20):
                rpc //= 2
            assert rows % rpc == 0
            self.name, self.rows, self.cols, self.rpc, self.n = name, rows, cols, rpc, rows // rpc
            self.loc = [dten("%s_loc%d" % (name, c), [rpc, cols], dtype) for c in range(self.n)]
            self.g = [dten("%s_g%d" % (name, c), [GSZ * rpc, cols], dtype) for c in range(self.n)]

        def loc_rows(self, r0, r1):
            c = r0 // self.rpc
            assert (r1 - 1) // self.rpc == c
            return self.loc[c][r0 - c * self.rpc:r1 - c * self.rpc, :], "%s_loc%d" % (self.name, c)

        def g_rows(self, rank, r0, r1):
            c = r0 // self.rpc
            assert (r1 - 1) // self.rpc == c
            base = rank * self.rpc - c * self.rpc
            return self.g[c][base + r0:base + r1, :], "%s_g%d" % (self.name, c)

        def gather(self):
            for c in range(self.n):
                ln, gn = "%s_loc%d" % (self.name, c), "%s_g%d" % (self.name, c)
                P.collective(lambda e, ln=ln, gn=gn: e.collective_compute("AllGather", ALU.bypass, replica_groups=groups,
                                                                          ins=[dram[ln]], outs=[dram[gn]]),
                             reads=[dbuf[ln]], writes=[dbuf[gn]])

    kT_gs = GatherSet("kT", 8 * 128, T, BF16, 2)
    v_gs = GatherSet("v", T, 2048, BF16, 2)
    iqT_d = dten("iqT_d", [4 * 128, T], BF16)
    ik_gs = GatherSet("ik", 128, T, BF16, 2)
    zT_d = dten("zT_d", [16 * 128, T], BF16)
    xbc_raw = dten("xbc_raw", [24 * 128, T], F32)
    halo_gs = GatherSet("halo", 128, 24 * 4, F32, 4)
    gT_d = dten("gT_d", [16 * 128, T], F32)
    dbg_small = dten("dbg_small", [128, NT * 80], F32)

    rope_d = dten("rope_d", [2 * 128, T], F32)
    m0 = A.mark()
    C4, C4_b = A.f32("C4", T)
    S4, S4_b = A.f32("S4", T)
    posi, posi_b = A.f32("posi", T)
    posi_i = posi.bitcast(I32)
    ang, ang_b = A.f32("ang", T)
    t1, t1_b = A.f32("rt1", T)
    t2, t2_b = A.f32("rt2", T)
    t2_i = t2.bitcast(I32)
    P.dma("sp", posi_i, bcast_ap(pos_in, 128, T), reads=[dbuf["positions"]], writes=[posi_b])
    CP("dve", ang, posi_i, [posi_b], [ang_b])
    TS("dve", ang, ang, cv["invf"][:, 0:1], ALU.mult, [ang_b, cst_b], [ang_b])
    TWO_PI = 2.0 * np.pi
    C1 = 6.28125
    C2 = TWO_PI - C1
    for dst, dst_b, shift in ((S4, S4_b, 0.0), (C4, C4_b, np.pi / 2.0)):
        TS("dve", t1, ang, shift, ALU.add, [ang_b], [t1_b], s2=1.0 / TWO_PI, op1=ALU.mult)
        CP("dve", t2_i, t1, [t1_b], [t2_b])
        CP("dve", t1, t2_i, [t2_b], [t1_b])
        TS("dve", t2, ang, shift, ALU.add, [ang_b], [t2_b])
        STT(t2, t1, -C1, t2, ALU.mult, ALU.add, [t1_b, t2_b], [t2_b])
        STT(t2, t1, -C2, t2, ALU.mult, ALU.add, [t1_b, t2_b], [t2_b])
        TS("dve", t1, t2, float(np.pi), ALU.is_gt, [t2_b], [t1_b])
        STT(t2, t1, -TWO_PI, t2, ALU.mult, ALU.add, [t1_b, t2_b], [t2_b])
        TS("dve", t1, t2, float(-np.pi), ALU.is_lt, [t2_b], [t1_b])
        STT(t2, t1, TWO_PI, t2, ALU.mult, ALU.add, [t1_b, t2_b], [t2_b])
        TS("dve", t2, t2, float(np.pi), ALU.min, [t2_b], [t2_b], s2=float(-np.pi), op1=ALU.max)
        ACT(dst, t2, AF.Sin, [t2_b], [dst_b])
    P.dma("sp", rope_d[0:128, :], C4, reads=[C4_b], writes=[dbuf["rope_d"]])
    P.dma("sp", rope_d[128:256, :], S4, reads=[S4_b], writes=[dbuf["rope_d"]])
    P.barrier()
    A.release(m0)

    o_ = [0]

    def lp_alloc(n):
        a = o_[0]
        o_[0] += n
        assert o_[0] <= 512
        return lp[:, a:a + n]

    b_adaT = lp_alloc(48)
    modT = lp_alloc(48)
    n1gT = lp_alloc(8)
    n2gT = lp_alloc(8)
    scwT = lp_alloc(96)
    scbT = lp_alloc(24)
    sngT = lp_alloc(16)
    fcwT = lp_alloc(132)
    fcbT = lp_alloc(44)
    qg2 = lp_alloc(1)
    kg2 = lp_alloc(1)
    A1 = lp_alloc(8)
    A2 = lp_alloc(8)
    cact2 = lp_alloc(16)
    dskT = lp_alloc(16)
    cact2_3 = cact2.rearrange("p (k two) -> p k two", two=2)
    iw3 = iw_tm.rearrange("p (i h) -> p i h", h=8)
    dt3 = dt_tm.rearrange("p (i h) -> p i h", h=SH)
    a3 = a_tm.rearrange("p (i h) -> p i h", h=SH)

    m0 = A.mark()
    tmp, tmp_b = A.f32("ctmp", 128)
    P.dma("sp", tmp[0:KC, :], c_in, reads=[dbuf["c"]], writes=[tmp_b])
    pt, pb = next_ps()
    P.op("pe", lambda e, pt=pt: e.transpose(out=pt[:, 0:KC], in_=tmp[0:KC, :], identity=cv["ident"][0:KC, 0:KC]),
         reads=[tmp_b, cst_b], writes=[pb])
    ACT(cact2_3[:, :, 0], pt[:, 0:KC], AF.Silu, [pb], [lp_b])
    ACT(cact2_3[:, :, 1], pt[:, 0:KC], AF.Silu, [pb], [lp_b])
    P.barrier()
    A.release(m0)

    def load_fm_rows(dst, src_ap, src_name, nrow):
        tmp_, tmpb_ = lfm_pool.get()
        P.dma("sp", tmp_[0:nrow, :], src_ap, reads=[dbuf[src_name]], writes=[tmpb_])
        pt_, pb_ = next_ps()
        P.op("pe", lambda e: e.transpose(out=pt_[:, 0:nrow], in_=tmp_[0:nrow, :], identity=cv["ident"][0:nrow, 0:nrow]),
             reads=[tmpb_, cst_b], writes=[pb_])
        CP("dve", dst, pt_[:, 0:nrow], [pb_], [lp_b])

    def layer_params(l):
        load_fm_rows(b_adaT, W["b_ada"][l], "b_ada", 48)
        load_fm_rows(n1gT, W["norm1_g"][l], "norm1_g", KC)
        load_fm_rows(n2gT, W["norm2_g"][l], "norm2_g", KC)
        for tap in range(4):
            load_fm_rows(scwT[:, tap * 24:(tap + 1) * 24], W["ssm_conv_w"][l, tap], "ssm_conv_w", 24)
        load_fm_rows(scbT, W["ssm_conv_b"][l], "ssm_conv_b", 24)
        load_fm_rows(sngT, W["ssm_norm_g"][l], "ssm_norm_g", 16)
        for tap in range(3):
            load_fm_rows(fcwT[:, tap * 44:(tap + 1) * 44], W["ffn_conv_w"][l, tap], "ffn_conv_w", 44)
        load_fm_rows(fcbT, W["ffn_conv_b"][l], "ffn_conv_b", 44)
        for dst, nm in ((qg2, "q_norm_g"), (kg2, "k_norm_g")):
            tmp_, tmpb_ = lfm_pool.get()
            P.dma("sp", tmp_[0:1, 0:64], W[nm][l], reads=[dbuf[nm]], writes=[tmpb_])
            P.dma("sp", tmp_[0:1, 64:128], W[nm][l], reads=[dbuf[nm]], writes=[tmpb_])
            pt_, pb_ = next_ps()
            P.op("pe", lambda e, pt_=pt_, tmp_=tmp_: e.transpose(out=pt_[:, 0:1], in_=tmp_[0:1, :],
                                                                 identity=cv["ident"][0:1, 0:1]),
                 reads=[tmpb_, cst_b], writes=[pb_])
            CP("dve", dst, pt_[:, 0:1], [pb_], [lp_b])
        tmp_, tmpb_ = lfm_pool.get()
        P.dma("sp", tmp_[0:16, 0:2], W["d_skip"][l].rearrange("o (c two) -> (o c) two", two=2),
              reads=[dbuf["d_skip"]], writes=[tmpb_])
        pt_, pb_ = next_ps()
        P.op("pe", lambda e: e.transpose(out=pt_[0:2, 0:16], in_=tmp_[0:16, 0:2], identity=cv["ident"][0:16, 0:16]),
             reads=[tmpb_, cst_b], writes=[pb_])
        tmp2_, tmp2b_ = lfm_pool.get()
        CP("dve", tmp2_[0:2, 0:16], pt_[0:2, 0:16], [pb_], [tmp2b_])
        pt2_, pb2_ = next_ps()
        MM(pt2_[:, 0:16], halfsel[0:2, :], tmp2_[0:2, 0:16], True, True, [tmp2b_, halfsel_b], [pb2_])
        CP("dve", dskT, pt2_[:, 0:16], [pb2_], [lp_b])
        P.dma("sp", dtb_bc, bcast_ap(W["dt_bias"][l], 128, SH), reads=[dbuf["dt_bias"]], writes=[dtb_b])
        P.dma("sp", alog_bc, bcast_ap(W["a_log"][l], 128, SH), reads=[dbuf["a_log"]], writes=[alog_b])
        ACT(alog_bc, alog_bc, AF.Exp, [alog_b], [alog_b])

    P.dma("sp", halfsel[0:1, :], consts_in[0:1, 256:384], reads=[dbuf["consts"]], writes=[halfsel_b])
    P.dma("sp", halfsel[1:2, :], consts_in[64:65, 256:384], reads=[dbuf["consts"]], writes=[halfsel_b])


    cast_rr = [0]

    def load_w(wname, l, col0, ncols, kc, row0=0, to_bf16=True, dst=None, pools=None):
        assert kc * ncols <= WSTG
        st, st_b = (pools[0] if pools else wst_pool).get()
        st3 = st[:, 0:kc * ncols].rearrange("p (k c) -> p k c", c=ncols)
        src = W[wname][l][row0:row0 + kc * 128, col0:col0 + ncols].rearrange("(k p) c -> p k c", p=128)
        P.dma("sp", st3, src, reads=[dbuf[wname]], writes=[st_b])
        if not to_bf16:
            return st3, st_b
        if dst is not None:
            CP("pool", dst[0], st3, [st_b], [dst[1]])
            return dst
        wb, wb_b = (pools[1] if pools else wbf_pool).get()
        wb3 = wb[:, 0:kc * ncols].rearrange("p (k c) -> p k c", c=ncols)
        CP("pool", wb[:, 0:kc * ncols], st[:, 0:kc * ncols], [st_b], [wb_b])
        return wb3, wb_b

    def load_w_resident(name, wname, l, kc, ncols):
        wr, wr_b = A.bf16(name, kc * ncols)
        wr3 = wr.rearrange("p (k c) -> p k c", c=ncols)
        kstep = max(1, WSTG // ncols) if ncols <= WSTG else 1
        cstep = min(ncols, WSTG)
        for k0 in range(0, kc, kstep):
            kk = min(kstep, kc - k0)
            for c0 in range(0, ncols, cstep):
                cc = min(cstep, ncols - c0)
                load_w(wname, l, c0, cc, kk, row0=k0 * 128, dst=(wr3[:, k0:k0 + kk, c0:c0 + cc], wr_b))
        return wr3, wr_b


    def store(dst_ap, dname, src_ap, src_b):
        P.dma("pool", dst_ap, src_ap, reads=[src_b], writes=[dbuf[dname]])

    def compute_mod(l):
        for cb in range(24):
            w3, w_b = load_w("w_ada", l, cb * 256, 256, KC, to_bf16=False)
            pt, pb = next_ps()
            for m in range(2):
                for k in range(KC):
                    MM(pt[:, 2 * m:2 * m + 2], w3[:, k, m * 128:(m + 1) * 128], cact2_3[:, k, :], k == 0, k == KC - 1,
                       [w_b, lp_b], [pb])
            ptv = pt[:, 0:4].rearrange("p (m two) -> p m two", two=2)[:, :, 0]
            TT("dve", modT[:, cb * 2:cb * 2 + 2], ptv, b_adaT[:, cb * 2:cb * 2 + 2], ALU.add, [pb, lp_b], [lp_b])
        STT(A1, modT[:, 8:16], 1.0, n1gT, ALU.add, ALU.mult, [lp_b], [lp_b])
        STT(A2, modT[:, 32:40], 1.0, n2gT, ALU.add, ALU.mult, [lp_b], [lp_b])

    def norm_mod(Avec, Bvec):
        for tb in range(NB):
            sl = slice(tb * 512, (tb + 1) * 512)
            pt, pb = next_ps()
            for k in range(KC):
                sq, sq_b = stg_pool.get()
                ACT(sq, xT3[:, k, sl], AF.Square, [xT_b], [sq_b])
                MM(pt, cv["ones"], sq, k == 0, k == KC - 1, [sq_b, cst_b], [pb])
            rs, rs_b = stg_pool.get()
            ACT(rs, pt, AF.Sqrt, [pb], [rs_b], bias=epsT[:, 0:1], scale=1.0 / D)
            P.op("dve", lambda e, rs=rs: e.reciprocal(out=rs, in_=rs), [rs_b], [rs_b])
            for k in range(KC):
                tq, tq_b = stg_pool.get()
                TT("dve", tq, xT3[:, k, sl], rs, ALU.mult, [xT_b, rs_b], [tq_b])
                TS("dve", ns.hT3[:, k, sl], tq, Avec[:, k:k + 1], ALU.mult, [tq_b, lp_b], [ns.hT_b],
                   s2=Bvec[:, k:k + 1], op1=ALU.add)

    P.op("dve", lambda e: e.memset(epsT, EPS), [], [epsT_b])

    def proj_fm(wname, l, col0, ncols, rhs3, rhs_b, kc, epilogue, sub0=0, row0=0, blk=512):
        per = max(128, (WSTG // kc) // 128 * 128)
        per = min(per, blk)
        c = 0
        while c < ncols:
            n = min(per, ncols - c)
            w3, w_b = load_w(wname, l, col0 + c, n, kc, row0=row0)
            for m in range(n // 128):
                for tb in range(NB):
                    pt, pb = next_ps()
                    for k in range(kc):
                        MM(pt, w3[:, k, m * 128:(m + 1) * 128], rhs3[:, k, tb * 512:(tb + 1) * 512], k == 0, k == kc - 1,
                           [w_b, rhs_b], [pb])
                    epilogue(pt, pb, sub0 + (c // 128) + m, tb)
            c += n

    def dst_rows(dname, r0, r1):
        if isinstance(dname, GatherSet):
            return dname.loc_rows(r0, r1)
        return dram[dname][r0:r1, :], dname

    def rope_epilogue(src_sb, src_b, tb, dst_dram, dname, row):
        sl = slice(tb * 512, (tb + 1) * 512)
        pr, prb = next_ps()
        MM(pr, cv["rrot"], src_sb, True, True, [src_b, cst_b], [prb])
        u1, u1_b = stg_pool.get()
        TT("pool", u1, src_sb, ns.C4[:, sl], ALU.mult, [src_b, ns.C4_b], [u1_b])
        u2, u2_b = stg_pool.get()
        TT("dve", u2, pr, ns.S4[:, sl], ALU.mult, [prb, ns.S4_b], [u2_b])
        ob, ob_b = sbf_pool.get()
        TT("dve", ob, u1, u2, ALU.add, [u1_b, u2_b], [ob_b])
        dap, dn = dst_rows(dname, row * 128, (row + 1) * 128)
        store(dap[:, sl], dn, ob, ob_b)

    def qk_epilogue(gvec, dname):
        def ep(pt, pb, sub, tb):
            sq, sq_b = stg_pool.get()
            ACT(sq, pt, AF.Square, [pb], [sq_b])
            p2, p2b = next_ps()
            MM(p2, cv["blockones"], sq, True, True, [sq_b, cst_b], [p2b])
            rs, rs_b = stg_pool.get()
            ACT(rs, p2, AF.Sqrt, [p2b], [rs_b], bias=epsT[:, 0:1], scale=1.0 / HD)
            P.op("dve", lambda e, rs=rs: e.reciprocal(out=rs, in_=rs), [rs_b], [rs_b])
            qn, qn_b = stg_pool.get()
            STT(qn, pt, gvec, rs, ALU.mult, ALU.mult, [pb, lp_b, rs_b], [qn_b])
            rope_epilogue(qn, qn_b, tb, None, dname, sub)
        return ep

    def iq_epilogue(dname, nsub_real):
        def ep(pt, pb, sub, tb):
            qn, qn_b = stg_pool.get()
            CP("act", qn, pt, [pb], [qn_b])
            rope_epilogue(qn, qn_b, tb, None, dname, sub)
        return ep

    def z_epilogue(pt, pb, sub, tb):
        ob, ob_b = sbf_pool.get()
        ACT(ob, pt, AF.Silu, [pb], [ob_b])
        store(zT_d[sub * 128:(sub + 1) * 128, tb * 512:(tb + 1) * 512], "zT_d", ob, ob_b)

    def xbc_epilogue(pt, pb, sub, tb):
        o32, o32_b = stg_pool.get()
        CP("act" if (sub + tb) % 2 == 0 else "dve", o32, pt, [pb], [o32_b])
        store(xbc_raw[sub * 128:(sub + 1) * 128, tb * 512:(tb + 1) * 512], "xbc_raw", o32, o32_b)
        if tb == NB - 1:
            dap, dn = halo_gs.loc_rows(0, 128)
            store(dap[:, sub * 4:sub * 4 + 3], dn, o32[:, 509:512], o32_b)

    def gate_epilogue(pt, pb, sub, tb):
        o32, o32_b = stg_pool.get()
        ACT(o32, pt, AF.Sigmoid, [pb], [o32_b])
        store(gT_d[sub * 128:(sub + 1) * 128, tb * 512:(tb + 1) * 512], "gT_d", o32, o32_b)


    def v_projection(l):
        for qd in range(4):
            w3, w_b = load_w("w_in", l, C_V + qd * 256, 256, KC)
            for i in range(NT):
                va, va_b = ns.vaug[i % 2]
                va4 = va.rearrange("p (pr two c) -> p pr two c", two=2, c=128)
                pt, pb = next_ps()
                for k in range(KC):
                    MM(pt[:, 0:256], ns.hT3[:, k, i * 128:(i + 1) * 128], w3[:, k, :], k == 0, k == KC - 1, [ns.hT_b, w_b], [pb])
                pt4 = pt[:, 0:256].rearrange("p (pr two c) -> p pr two c", two=2, c=64)
                ev_ = "act" if i % 2 == 0 else "dve"
                CP(ev_, va4[:, :, 0, 0:64], pt4[:, :, 0, :], [pb], [va_b])
                CP(ev_, va4[:, :, 1, 64:128], pt4[:, :, 1, :], [pb], [va_b])
                dap, dn = v_gs.loc_rows(i * 128, (i + 1) * 128)
                store(dap[:, qd * 512:(qd + 1) * 512], dn, va, va_b)

    def small_projection(l):
        st, st_b = wst_pool.get()
        st3 = st[:, 0:KC * 40].rearrange("p (k c) -> p k c", c=40)
        P.dma("sp", st3[:, :, 0:32], W["w_in"][l][:, C_DT:C_DT + 32].rearrange("(k p) c -> p k c", p=128),
              reads=[dbuf["w_in"]], writes=[st_b])
        P.dma("sp", st3[:, :, 32:40], W["w_in"][l][:, C_IW:C_IW + 8].rearrange("(k p) c -> p k c", p=128),
              reads=[dbuf["w_in"]], writes=[st_b])
        wb, wb_b = wbf_pool.get()
        wb3 = wb[:, 0:KC * 40].rearrange("p (k c) -> p k c", c=40)
        CP("pool", wb[:, 0:KC * 40], st[:, 0:KC * 40], [st_b], [wb_b])
        for i in range(NT):
            pt, pb = next_ps()
            for k in range(KC):
                MM(pt[:, 0:40], ns.hT3[:, k, i * 128:(i + 1) * 128], wb3[:, k, :], k == 0, k == KC - 1, [ns.hT_b, wb_b], [pb])
            xx, xx_b = stg_pool.get()
            CP("dve", xx[:, 64:104], pt[:, 0:40], [pb], [xx_b])
            CP("dve", iw3[:, i, :], xx[:, 96:104], [xx_b], [iw_b])
            TT("dve", xx[:, 0:32], xx[:, 64:96], dtb_bc, ALU.add, [xx_b, dtb_b], [xx_b])
            STT(xx[:, 32:64], xx[:, 0:32], -1.0, xx[:, 0:32], ALU.mult, ALU.max, [xx_b], [xx_b])
            ACT(xx[:, 32:64], xx[:, 32:64], AF.Exp, [xx_b], [xx_b], scale=-1.0)
            ACT(xx[:, 32:64], xx[:, 32:64], AF.Ln, [xx_b], [xx_b], bias=oneT[:, 0:1], scale=1.0)
            STT(dt3[:, i, :], xx[:, 0:32], 0.0, xx[:, 32:64], ALU.max, ALU.add, [xx_b], [dt_b])
            STT(a3[:, i, :], dt3[:, i, :], -1.0, alog_bc, ALU.mult, ALU.mult, [dt_b, alog_b], [a_b])

    P.op("dve", lambda e: e.memset(oneT, 1.0), [], [oneT_b])

    def alloc_hT():
        hT, ns.hT_b = A.bf16("hT", KC * T)
        ns.hT3 = hT.rearrange("p (k t) -> p k t", t=T)

    def phase_A(l):
        if cfg.stage < 1:
            return
        alloc_hT()
        ns.C4, ns.C4_b = A.f32("C4", T)
        ns.S4, ns.S4_b = A.f32("S4", T)
        P.dma("sp", ns.C4, rope_d[0:128, :], reads=[dbuf["rope_d"]], writes=[ns.C4_b])
        P.dma("sp", ns.S4, rope_d[128:256, :], reads=[dbuf["rope_d"]], writes=[ns.S4_b])
        ns.vaug = [A.bf16("vaug%d" % i, 512) for i in range(2)]
        for va, va_b in ns.vaug:
            P.op("pool", lambda e, va=va: e.memset(va, 1.0), [], [va_b])
        layer_params(l)
        if cfg.stage < 2:
            return
        compute_mod(l)
        if cfg.stage < 3:
            return
        norm_mod(A1, modT[:, 0:8])
        if cfg.stage < 4:
            return
        proj_fm("w_in", l, C_Q, 1024, ns.hT3, ns.hT_b, KC, qk_epilogue(qg2[:, 0:1], "qT_d"))
        if cfg.stage < 5:
            return
        proj_fm("w_in", l, C_K, 1024, ns.hT3, ns.hT_b, KC, qk_epilogue(kg2[:, 0:1], kT_gs))
        if cfg.stage >= 8:
            kT_gs.gather()
        v_projection(l)
        if cfg.stage >= 8:
            v_gs.gather()
        proj_fm("w_in", l, C_IQ, 512, ns.hT3, ns.hT_b, KC, iq_epilogue("iqT_d", 4))
        st, st_b = wst_pool.get()
        st3 = st[:, 0:KC * 128].rearrange("p (k c) -> p k c", c=128)
        for hf in range(2):
            P.dma("sp", st3[:, :, hf * 64:(hf + 1) * 64],
                  W["w_in"][l][:, C_IK:C_IK + 64].rearrange("(k p) c -> p k c", p=128),
                  reads=[dbuf["w_in"]], writes=[st_b])
        wb, wb_b = wbf_pool.get()
        wb3 = wb[:, 0:KC * 128].rearrange("p (k c) -> p k c", c=128)
        CP("pool", wb[:, 0:KC * 128], st[:, 0:KC * 128], [st_b], [wb_b])
        ikep = iq_epilogue(ik_gs, 1)
        for tb in range(NB):
            pt, pb = next_ps()
            for k in range(KC):
                MM(pt, wb3[:, k, :], ns.hT3[:, k, tb * 512:(tb + 1) * 512], k == 0, k == KC - 1, [wb_b, ns.hT_b], [pb])
            ikep(pt, pb, 0, tb)
        if cfg.stage >= 8:
            ik_gs.gather()
        if cfg.stage < 6:
            return
        small_projection(l)
        if cfg.stage < 7:
            return
        proj_fm("w_in", l, C_Z, 2048, ns.hT3, ns.hT_b, KC, z_epilogue)
        proj_fm("w_in", l, C_XBC, 3072, ns.hT3, ns.hT_b, KC, xbc_epilogue)
        if cfg.stage >= 8:
            halo_gs.gather()
        proj_fm("w_in", l, C_GA, 2048, ns.hT3, ns.hT_b, KC, gate_epilogue)
        if "dbg_small" in cfg.debug:
            dbg3 = dbg_small.rearrange("p (i c) -> p i c", c=80)
            store(dbg3[:, :, 0:8], "dbg_small", iw3, iw_b)
            store(dbg3[:, :, 8:40], "dbg_small", dt3, dt_b)
            store(dbg3[:, :, 40:72], "dbg_small", a3, a_b)
        if cfg.stage < 8:
            return


    attnT_d = dten("attnT_d", [8 * 128, T], BF16, "ExternalInput" if "attnT_d" in cfg.feed else "Internal")
    ynT_d = dten("ynT_d", [16 * 128, T], BF16, "ExternalInput" if "ynT_d" in cfg.feed else "Internal")
    aT_d = dten("aT_d", [22 * 128, T], BF16)
    uhalo_gs = GatherSet("uhalo", 128, 44 * 2, F32, 4)

    mixT_d = dten("mixT_d", [8 * 128, T], BF16)

    def phase_D(l):
        m0_ = A.mark()
        wao3, wao_b = load_w_resident("wao", "w_attn_o", l, 8, D)
        wso3, wso_b = load_w_resident("wso", "w_ssm_o", l, 16, D)
        at, at_b = A.bf16("attn_blk", 8 * 512)
        at3 = at.rearrange("p (k t) -> p k t", t=512)
        yn, yn_b = A.bf16("yn_blk", 16 * 512)
        yn3 = yn.rearrange("p (k t) -> p k t", t=512)
        gpool = RPool("gate", 4, 512, "f32")
        for tb in range(NB):
            sl = slice(tb * 512, (tb + 1) * 512)
            P.dma("sp", at3, attnT_d.rearrange("(k p) t -> p k t", p=128)[:, :, sl], reads=[dbuf["attnT_d"]], writes=[at_b])
            P.dma("sp", yn3, ynT_d.rearrange("(k p) t -> p k t", p=128)[:, :, sl], reads=[dbuf["ynT_d"]], writes=[yn_b])
            for m in range(8):
                ga, ga_b = gpool.get()
                gm_, gm_b = gpool.get()
                P.dma("sp", ga, gT_d[m * 128:(m + 1) * 128, sl], reads=[dbuf["gT_d"]], writes=[ga_b])
                P.dma("sp", gm_, gT_d[(8 + m) * 128:(9 + m) * 128, sl], reads=[dbuf["gT_d"]], writes=[gm_b])
                pa, pab = next_ps()
                for k in range(8):
                    MM(pa, wao3[:, k, m * 128:(m + 1) * 128], at3[:, k, :], k == 0, k == 7, [wao_b, at_b], [pab])
                psm, psb = next_ps()
                for k in range(16):
                    MM(psm, wso3[:, k, m * 128:(m + 1) * 128], yn3[:, k, :], k == 0, k == 15, [wso_b, yn_b], [psb])
                t1_, t1b_ = stg_pool.get()
                TT("dve", t1_, pa, ga, ALU.mult, [pab, ga_b], [t1b_])
                t2_, t2b_ = stg_pool.get()
                TT("dve", t2_, psm, gm_, ALU.mult, [psb, gm_b], [t2b_])
                ob, ob_b = sbf_pool.get()
                TT("pool", ob, t1_, t2_, ALU.add, [t1b_, t2b_], [ob_b])
                store(mixT_d[m * 128:(m + 1) * 128, sl], "mixT_d", ob, ob_b)
        P.barrier()
        A.release(m0_)
        m0_ = A.mark()
        wout3, wout_b = load_w_resident("wout", "w_out", l, 8, D)
        mxs = [A.bf16("mix_blk%d" % i, 8 * 512) for i in range(2)]
        for tb in range(NB):
            sl = slice(tb * 512, (tb + 1) * 512)
            mx, mx_b = mxs[tb % 2]
            mx3 = mx.rearrange("p (k t) -> p k t", t=512)
            P.dma("sp", mx3, mixT_d.rearrange("(k p) t -> p k t", p=128)[:, :, sl], reads=[dbuf["mixT_d"]], writes=[mx_b])
            for m2 in range(8):
                po, pob = next_ps()
                for k in range(8):
                    MM(po, wout3[:, k, m2 * 128:(m2 + 1) * 128], mx3[:, k, :], k == 0, k == 7, [wout_b, mx_b], [pob])
                STT(xT3[:, m2, sl], po, modT[:, 16 + m2:17 + m2], xT3[:, m2, sl], ALU.mult, ALU.add, [pob, lp_b, xT_b], [xT_b])
        P.barrier()
        A.release(m0_)

    def phase_EF(l, rank_sel):
        m0_ = A.mark()
        alloc_hT()
        norm_mod(A2, modT[:, 24:32])
        uh, uh_b = A.f32("uhalo_sb", 44 * 2)
        uh3 = uh.rearrange("p (m two) -> p m two", two=2)
        for c0 in range(0, 2 * DFF, 256):
            w3, w_b = load_w("w_up", l, c0, 256, KC)
            pt, pb = next_ps()
            for m in range(2):
                for k in range(KC):
                    MM(pt[:, 2 * m:2 * m + 2], w3[:, k, m * 128:(m + 1) * 128], ns.hT3[:, k, T - 2:T], k == 0, k == KC - 1,
                       [w_b, ns.hT_b], [pb])
            CP("dve", uh3[:, (c0 // 128):(c0 // 128) + 2, :], pt[:, 0:4].rearrange("p (m two) -> p m two", two=2), [pb], [uh_b])
        dap, dn = uhalo_gs.loc_rows(0, 128)
        store(dap, dn, uh, uh_b)
        uhalo_gs.gather()
        pv, pv_b = A.f32("uprev", 44 * 2)
        pv3 = pv.rearrange("p (m two) -> p m two", two=2)
        load_prev_rows(uhalo_gs, pv3, pv_b, 44, 2)
        ub = [A.f32("ubuf%d" % i, 516) for i in range(4)]
        efp = (RPool("wstL", 4, KC * 128, "f32"), RPool("wbfL", 4, KC * 128, "bf16"))
        for m in range(22):
            wv3, wv_b = load_w("w_up", l, m * 128, 128, KC, pools=efp)
            wg3, wg_b = load_w("w_up", l, DFF + m * 128, 128, KC, pools=efp)
            conv_out = []
            for tb in range(NB):
                sl = slice(tb * 512, (tb + 1) * 512)
                res = []
                for which, (w3, w_b, sub) in enumerate(((wv3, wv_b, m), (wg3, wg_b, 22 + m))):
                    pt, pb = next_ps()
                    for k in range(KC):
                        MM(pt, w3[:, k, :], ns.hT3[:, k, sl], k == 0, k == KC - 1, [w_b, ns.hT_b], [pb])
                    u, u_b = ub[(tb % 2) * 2 + which]
                    if tb == 0:
                        CP("dve", u[:, 0:2], pv3[:, sub, :], [pv_b], [u_b])
                    else:
                        up, up_b = ub[((tb - 1) % 2) * 2 + which]
                        CP("dve", u[:, 0:2], up[:, 512:514], [up_b], [u_b])
                    CP("act", u[:, 2:514], pt, [pb], [u_b])
                    c_, c_b = stg_pool.get()
                    TS("dve", c_, u[:, 0:512], fcwT[:, sub:sub + 1], ALU.mult, [u_b, lp_b], [c_b],
                       s2=fcbT[:, sub:sub + 1], op1=ALU.add)
                    STT(c_, u[:, 1:513], fcwT[:, 44 + sub:45 + sub], c_, ALU.mult, ALU.add, [u_b, lp_b, c_b], [c_b])
                    STT(c_, u[:, 2:514], fcwT[:, 88 + sub:89 + sub], c_, ALU.mult, ALU.add, [u_b, lp_b, c_b], [c_b])
                    res.append((c_, c_b))
                (cv_, cvb_), (cg_, cgb_) = res
                sg, sg_b = stg_pool.get()
                ACT(sg, cg_, AF.Silu, [cgb_], [sg_b])
                ob, ob_b = sbf_pool.get()
                TT("pool", ob, sg, cv_, ALU.mult, [sg_b, cvb_], [ob_b])
                store(aT_d[m * 128:(m + 1) * 128, sl], "aT_d", ob, ob_b)
        P.barrier()
        A.release(m0_)
        m0_ = A.mark()
        wd3, wd_b = load_w_resident("wdown", "w_down", l, 22, D)
        ab = [A.bf16("a_blk%d" % i, 22 * 512) for i in range(1)]
        for tb in range(NB):
            sl = slice(tb * 512, (tb + 1) * 512)
            a_, a_b_ = ab[0]
            a3_ = a_.rearrange("p (k t) -> p k t", t=512)
            P.dma("sp", a3_, aT_d.rearrange("(k p) t -> p k t", p=128)[:, :, sl], reads=[dbuf["aT_d"]], writes=[a_b_])
            for m2 in range(8):
                po, pob = next_ps()
                for k in range(22):
                    MM(po, wd3[:, k, m2 * 128:(m2 + 1) * 128], a3_[:, k, :], k == 0, k == 21, [wd_b, a_b_], [pob])
                STT(xT3[:, m2, sl], po, modT[:, 40 + m2:41 + m2], xT3[:, m2, sl], ALU.mult, ALU.add, [pob, lp_b, xT_b], [xT_b])
        P.barrier()
        A.release(m0_)


    x_save = dten("x_save", [8 * 128, T], F32)

    def spill_x():
        P.dma("sp", x_save.rearrange("(k p) t -> p k t", p=128), xT3, reads=[xT_b], writes=[dbuf["x_save"]])
        P.barrier()

    def restore_x():
        P.barrier()
        P.dma("sp", xT3, x_save.rearrange("(k p) t -> p k t", p=128), reads=[dbuf["x_save"]], writes=[xT_b])

    NKEY = GSZ * T
    NKT = NKEY // 128
    NKB = NKEY // 512
    NIT = 18
    LO0 = -512.0
    BIGP = float(2.0 ** 100)
    NSPL = (int(NKEY * 0.41) // 512) * 512
    NACT = NKEY - NSPL
    SCALE = float(HD ** -0.5)

    def phase_B(l):
        m0_ = A.mark()
        score, score_b = A.f32("score", NKEY)
        mb, mbA_b = A.bf16("mb", NKEY)
        mbB_b = Buf("mbB")
        mbT_raw, mbT_b = A.f32("mbT", NKT * 512 // 4)
        mbT = mbT_raw.bitcast(FP8)
        mbT3 = mbT.rearrange("p (k t) -> p k t", t=512)
        ikT, ikT_b = A.bf16("ikT", NKEY)
        ikT3 = ikT.rearrange("p (r t) -> p r t", t=T)
        iqb, iqb_b = A.bf16("iq_blk", 4 * 512)
        iqb3 = iqb.rearrange("p (k t) -> p k t", t=512)
        qb_, qb_b = A.bf16("q_blk", 8 * 512)
        qb3 = qb_.rearrange("p (k t) -> p k t", t=512)
        kst = [A.bf16("kst%d" % i, 2 * 512) for i in range(3)]
        vst = [A.bf16("vst%d" % i, 4 * 512) for i in range(3)]
        ppool = RPool("pT", 4, 512, "bf16")
        ab_, ab_b = A.bf16("attn_o_blk", 2 * 512)
        ab3 = ab_.rearrange("p (k t) -> p k t", t=512)
        negd0, negd0_b = A.f32("negd0", 512)
        dg, dg_b = A.bf16("diagw", IH * 128)
        dg3 = dg.rearrange("p (h c) -> p h c", c=128)
        rlp = RPool("relu", 8, 256, "f32")
        rls = []
        rel_ctr = [0]
        sm, sm_b = A.f32("bis_small", 16)
        md, md_b = A.f32("bis_mid", 2)
        cn, cnt_b = A.f32("bis_cnt", 2)
        sa, sacc_b = A.f32("bis_sacc", 2)
        Wk, Wk_b = A.f32("bis_w", NIT)
        lo = sm[:, 0:1]
        mid = md[:, 0:1]
        nmid = md[:, 1:2]
        cnt = cn[:, 0:1]
        sacc = sa[:, 0:1]
        tot = sm[:, 5:6]
        ge = sm[:, 6:7]
        hi0 = sm[:, 7:8]
        qp0 = sm[:, 8:9]
        STT(qp0, rk[:, 0:1], float(T), cv["pidx"][:, 0:1], ALU.mult, ALU.add, [rk_b, cst_b], [sm_b])
        TS("dve", negd0, cv["iotaf"], qp0, ALU.subtract, [cst_b, sm_b], [negd0_b], s2=-BIGP, op1=ALU.mult)
        P.dma("sp", ikT3, ik_gs.g[0].rearrange("(r p) t -> p r t", p=128), reads=[dbuf["ik_g0"]], writes=[ikT_b])
        for qb in range(NB):
            qsl = slice(qb * 512, (qb + 1) * 512)
            kbnd = min(NKEY, (GSZ - 1) * T + (qb + 1) * 512)
            nkb_q = kbnd // 512
            nkt_q = kbnd // 128
            nspl_q = max(512, (int(kbnd * 0.41) // 512) * 512)
            nact_q = kbnd - nspl_q
            P.dma("sp", iqb3, iqT_d.rearrange("(k p) t -> p k t", p=128)[:, :, qsl], reads=[dbuf["iqT_d"]], writes=[iqb_b])
            P.dma("sp", qb3, qT_d.rearrange("(k p) t -> p k t", p=128)[:, :, qsl], reads=[dbuf["qT_d"]], writes=[qb_b])
            for qs in range(4):
                qt = qb * 4 + qs
                for h in range(IH):
                    TS("dve", dg3[:, h, :], ident_bf, iw3[:, qt, h:h + 1], ALU.mult, [ident_bf_b, iw_b], [dg_b])
                for kb in range(nkb_q):
                    ksl = slice(kb * 512, (kb + 1) * 512)
                    madd, madd_b = stg_pool.get()
                    TS("dve", madd, negd0, float((kb * 512 - qt * 128) * (-BIGP)), ALU.add, [negd0_b], [madd_b],
                       s2=0.0, op1=ALU.min)
                    lg = []
                    for h in range(IH):
                        hb = (h % 2) * 64
                        pt, pb = next_ps()
                        MM(pt, iqb3[hb:hb + 64, h // 2, qs * 128:(qs + 1) * 128], ikT[hb:hb + 64, ksl], True, True,
                           [iqb_b, ikT_b], [pb])
                        lg.append((pt, pb))
                        if len(lg) == 4 or h == IH - 1:
                            for (pt_, pb_) in lg:
                                rl32, rl_b = rlp.get()
                                rl = rl32.bitcast(BF16)[:, 0:512]
                                hh_ = rel_ctr[0]
                                rel_ctr[0] += 1
                                if hh_ % 4 != 3:
                                    ACT(rl, pt_, AF.Relu, [pb_], [rl_b])
                                else:
                                    TS("dve", rl, pt_, 0.0, ALU.max, [pb_], [rl_b])
                                rls.append((rl, rl_b))
                            lg = []
                    pacc, paccb = next_ps()
                    for h in range(IH):
                        rl, rl_b = rls[h]
                        MM(pacc, dg3[:, h, :], rl, h == 0, h == IH - 1, [dg_b, rl_b], [paccb])
                    del rls[:]
                    TT("dve", score[:, ksl], pacc, madd, ALU.add, [paccb, madd_b], [score_b])
                P.op("dve", lambda e, kbnd=kbnd: e.tensor_reduce(out=hi0, in_=score[:, 0:kbnd], axis=AX.X, op=ALU.max),
                     [score_b], [sm_b])
                TS("dve", tot, hi0, 1.0 - LO0, ALU.add, [sm_b], [sm_b])
                for k in range(NIT):
                    TS("dve", Wk[:, k:k + 1], tot, float(2.0 ** -(k + 1)), ALU.mult, [sm_b], [Wk_b])
                TS("dve", mid, Wk[:, 0:1], LO0, ALU.add, [Wk_b], [md_b])
                kthr = float(cfg.nkeep - nact_q / 2.0)
                for k in range(NIT):
                    ACT(mb[:, nspl_q:kbnd], score[:, nspl_q:kbnd], AF.Sign, [score_b, md_b], [mbB_b, sacc_b], bias=mid, scale=-1.0,
                        accum_out=sacc)
                    TS("dve", mb[:, 0:nspl_q], score[:, 0:nspl_q], mid, ALU.is_ge, [score_b, md_b], [mbA_b, cnt_b],
                       s2=0.0, op1=ALU.add, accum_out=cnt)
                    STT(tot, sacc, -0.5, cnt, ALU.mult, ALU.add, [sacc_b, cnt_b], [sm_b])
                    STT(ge, tot, kthr, Wk[:, k:k + 1], ALU.is_ge, ALU.mult, [sm_b, Wk_b], [sm_b])
                    if k < NIT - 1:
                        STT(mid, ge, Wk[:, k + 1:k + 2], mid, ALU.subtract, ALU.add, [sm_b, Wk_b, md_b], [md_b])
                    else:
                        STT(lo, ge, Wk[:, k:k + 1], mid, ALU.subtract, ALU.add, [sm_b, Wk_b, md_b], [sm_b])
                TS("dve", mb[:, 0:kbnd], score[:, 0:kbnd], lo, ALU.is_ge, [score_b, sm_b], [mbA_b, mbB_b])
                for k4 in range(nkt_q // 4):
                    pt, pb = next_ps()
                    ptb = pt.bitcast(BF16)
                    for u in range(4):
                        kt = k4 * 4 + u
                        P.op("pe", lambda e, ptb=ptb, u=u, kt=kt: e.transpose(out=ptb[:, u * 128:(u + 1) * 128],
                                                                             in_=mb[:, kt * 128:(kt + 1) * 128],
                                                                             identity=ident_bf),
                             [mbA_b, mbB_b, ident_bf_b], [pb])
                    dst_ = mbT3[:, k4 * 4:(k4 + 1) * 4, qs * 128:(qs + 1) * 128]
                    src_ = ptb[:, 0:512].rearrange("p (u t) -> p u t", t=128)
                    if k4 % 2 == 0:
                        P.op("act", lambda e, dst_=dst_, src_=src_: e.activation(out=dst_, in_=src_, func=AF.Copy, saturate=False),
                             [pb], [mbT_b])
                    else:
                        P.op("dve", lambda e, dst_=dst_, src_=src_: e.tensor_copy(out=dst_, in_=src_, saturate=False),
                             [pb], [mbT_b])
            for hg in range(4):
                accs = [psum[i_] for i_ in range(4)]
                nk4 = nkt_q // 4
                bufs = {}

                def issue_kv(k4, hg=hg, bufs=bufs):
                    kk, kk_b = kst[k4 % 3]
                    kk3 = kk.rearrange("p (pr t) -> p pr t", t=512)
                    vv, vv_b = vst[k4 % 3]
                    vv3 = vv.rearrange("p (u c) -> p u c", c=512)
                    r_ = (k4 * 512) // T
                    off = (k4 * 512) % T
                    for pr in range(2):
                        gap, gn = kT_gs.g_rows(r_, (hg * 2 + pr) * 128, (hg * 2 + pr + 1) * 128)
                        P.dma("sp", kk3[:, pr, :], gap[:, off:off + 512], reads=[dbuf[gn]], writes=[kk_b])
                    for u in range(4):
                        gap, gn = v_gs.g_rows(r_, off + u * 128, off + (u + 1) * 128)
                        P.dma("sp", vv3[:, u, :], gap[:, hg * 512:(hg + 1) * 512], reads=[dbuf[gn]], writes=[vv_b])
                    bufs[k4] = (kk3, kk_b, vv3, vv_b)

                steps = [(k4, u, hh) for k4 in range(nk4) for u in range(4) for hh in range(4)]
                LA = 2
                pend = {}
                issue_kv(0)

                def emit_qk(si):
                    k4, u, hh = steps[si]
                    if u == 0 and hh == 0 and k4 + 1 < nk4:
                        issue_kv(k4 + 1)
                    kk3, kk_b, vv3, vv_b = bufs[k4]
                    h = hg * 4 + hh
                    hb = (h % 2) * 64
                    pt, pb = next_ps_s()
                    MM(pt, kk3[hb:hb + 64, hh // 2, u * 128:(u + 1) * 128], qb3[hb:hb + 64, h // 2, :], True, True,
                       [kk_b, qb_b], [pb])
                    pend[si] = (pt, pb)

                def emit_pv(ti):
                    k4, u, hh = steps[ti]
                    kk3, kk_b, vv3, vv_b = bufs[k4]
                    kt = k4 * 4 + u
                    pt, pb = pend.pop(ti)
                    pT, pT_b = ppool.get()
                    ACT(pT, pt, AF.Exp, [pb], [pT_b], scale=SCALE)
                    TT("dve", pT, pT, mbT3[:, kt, :], ALU.mult, [pT_b, mbT_b], [pT_b])
                    acc, acc_b = accs[hh]
                    MM(acc, vv3[:, u, hh * 128:(hh + 1) * 128], pT, kt == 0, kt == nkt_q - 1, [vv_b, pT_b], [acc_b])

                for si in range(0, len(steps) + LA, 2):
                    if si < len(steps):
                        emit_qk(si)
                        emit_qk(si + 1)
                    ti = si - LA
                    if ti >= 0:
                        emit_pv(ti)
                        emit_pv(ti + 1)
                for hh in range(4):
                    h = hg * 4 + hh
                    hb = (h % 2) * 64
                    acc, acc_b = accs[hh]
                    osb, osb_b = stg_pool.get()
                    CP("act", osb, acc, [acc_b], [osb_b])
                    pw, pw_b = next_ps_s()
                    MM(pw, cv["swap"], osb, True, True, [cst_b, osb_b], [pw_b])
                    rc, rc_b = stg_pool.get()
                    P.op("dve", lambda e, rc=rc, pw=pw: e.reciprocal(out=rc, in_=pw), [pw_b], [rc_b])
                    TT("dve", ab3[hb:hb + 64, hh // 2, :], osb[hb:hb + 64, :], rc[hb:hb + 64, :], ALU.mult, [osb_b, rc_b], [ab_b])
                store(attnT_d.rearrange("(k p) t -> p k t", p=128)[:, hg * 2:hg * 2 + 2, qsl], "attnT_d", ab3, ab_b)
        print("phase B arena top", A.off, "of", A.n)
        P.barrier()
        A.release(m0_)

    ps_s_rr = [0]

    def next_ps_s():
        i = ps_s_rr[0]
        ps_s_rr[0] = (i + 1) % 4
        return psum[4 + i]


    NCH = T // 256
    st_gs = GatherSet("ssdF", 128, 2048, F32, 4)
    dd_gs = GatherSet("ssdD", 128, SH, F32, 4)

    def bcast_last(ap, n):
        return bass.AP(ap.tensor, ap.offset, [list(x) for x in ap.ap] + [[0, n]])

    def phase_C(l):
        m0_ = A.mark()
        rwp = RPool("rw", 3, 260, "f32")
        xsT, xsT_b = A.f32("xsT", 16 * 256)
        xsT3 = xsT.rearrange("p (m t) -> p m t", t=256)
        BT, BT_b = A.bf16("BT", 4 * 256)
        BT3 = BT.rearrange("p (g t) -> p g t", t=256)
        CT, CT_b = A.bf16("CT", 4 * 256)
        CT3 = CT.rearrange("p (g t) -> p g t", t=256)
        xdt, xdt_b = A.bf16("xdt", 2 * 2048)
        xdt3 = xdt.rearrange("p (i c) -> p i c", c=2048)
        xdd, xdd_b = A.bf16("xdd", 2 * 2048)
        xdd3 = xdd.rearrange("p (i c) -> p i c", c=2048)
        Btm, Btm_b = A.bf16("Btm", 2 * 512)
        Btm4 = Btm.rearrange("p (i g n) -> p i g n", g=4, n=128)
        acs, acs_b = A.f32("acs_tm", 2 * SH)
        acs3 = acs.rearrange("p (i h) -> p i h", h=SH)
        totb, totb_b = A.f32("tot_bc", SH)
        etot, etot_b = A.f32("etot", SH)
        ds_, ds_b = A.f32("ds", 2 * SH)
        ds3 = ds_.rearrange("p (i h) -> p i h", h=SH)
        Sst, Sst_b = A.f32("Sstate", 2048)
        Sbf, Sbf_b = A.bf16("Sstate_bf", 2048)
        Dacc, Dacc_b = A.f32("Dacc", SH)
        cbm, cbm_b = A.f32("CBm", 4 * 384)
        cbm3 = cbm.rearrange("p (g t) -> p g t", t=384)
        triL, triL_b = A.f32("triL", 512)
        yg, yg_b = A.f32("yg", 16 * 256)
        yg3 = yg.rearrange("p (m t) -> p m t", t=256)
        prev, prev_b = A.f32("xbc_prev", 24 * 4)
        prev3 = prev.rearrange("p (m c) -> p m c", c=4)
        zp = RPool("zt", 3, 256, "bf16")
        r0p = RPool("c_r0", 6, 512, "f32")
        dpp = RPool("c_d", 8, 384, "f32")
        ebp = RPool("c_eb", 8, 256, "f32")
        bcp = RPool("c_bc", 6, 256, "f32")
        gpp = RPool("c_G", 6, 384, "bf16")
        cep = RPool("c_Ce", 6, 256, "bf16")
        print("phase C arena top", A.off, "of", A.n)
        CP("dve", triL[:, 0:128], cv["tri"], [cst_b], [triL_b])
        CP("dve", triL[:, 128:256], cv["ones"], [cst_b], [triL_b])
        P.op("dve", lambda e: e.memset(triL[:, 256:384], 0.0), [], [triL_b])
        CP("dve", triL[:, 384:512], cv["tri"], [cst_b], [triL_b])
        load_prev_rows(halo_gs, prev3, prev_b, 24, 4)

        def conv_block(blk, c, out_ap, out_b, eng_out="act"):
            rw, rw_b = rwp.get()
            if c == 0:
                P.dma("sp", rw[:, 3:259], xbc_raw[blk * 128:(blk + 1) * 128, 0:256], reads=[dbuf["xbc_raw"]], writes=[rw_b])
                CP("pool", rw[:, 0:3], prev3[:, blk, 0:3], [prev_b], [rw_b])
            else:
                P.dma("sp", rw[:, 0:259], xbc_raw[blk * 128:(blk + 1) * 128, c * 256 - 3:c * 256 + 256],
                      reads=[dbuf["xbc_raw"]], writes=[rw_b])
            ac, ac_b = stg_pool.get()
            TS("dve", ac[:, 0:256], rw[:, 0:256], scwT[:, blk:blk + 1], ALU.mult, [rw_b, lp_b], [ac_b],
               s2=scbT[:, blk:blk + 1], op1=ALU.add)
            for tap in range(1, 4):
                STT(ac[:, 0:256], rw[:, tap:tap + 256], scwT[:, tap * 24 + blk:tap * 24 + blk + 1], ac[:, 0:256], ALU.mult, ALU.add,
                    [rw_b, lp_b, ac_b], [ac_b])
            ACT(out_ap, ac[:, 0:256], AF.Silu, [ac_b], [out_b])

        def ssd_pass(compute_y):
            for c in range(NCH):
                csl = slice(c * 256, (c + 1) * 256)
                for blk in range(16):
                    conv_block(blk, c, xsT3[:, blk, :], xsT_b)
                for g in range(4):
                    conv_block(16 + g, c, BT3[:, g, :], BT_b)
                if compute_y:
                    for g in range(4):
                        conv_block(20 + g, c, CT3[:, g, :], CT_b)
                for i in range(2):
                    ti = c * 2 + i
                    for bank in range(4):
                        pt, pb = next_ps()
                        for u in range(4):
                            blk = bank * 4 + u
                            P.op("pe", lambda e, pt=pt, u=u, blk=blk, i=i: e.transpose(
                                out=pt[:, u * 128:(u + 1) * 128], in_=xsT3[:, blk, i * 128:(i + 1) * 128], identity=cv["ident"]),
                                [xsT_b, cst_b], [pb])
                        TT("dve", xdt3[:, i, bank * 512:(bank + 1) * 512].rearrange("p (h q) -> p h q", q=64),
                           pt.rearrange("p (h q) -> p h q", q=64), bcast_last(dt3[:, ti, bank * 8:(bank + 1) * 8], 64), ALU.mult,
                           [pb, dt_b], [xdt_b])
                    pt, pb = next_ps()
                    ptb = pt.bitcast(BF16)
                    for g in range(4):
                        P.op("pe", lambda e, ptb=ptb, g=g, i=i: e.transpose(out=ptb[:, g * 128:(g + 1) * 128],
                                                                           in_=BT3[:, g, i * 128:(i + 1) * 128], identity=ident_bf),
                             [BT_b, ident_bf_b], [pb])
                    CP("act", Btm4[:, i, :, :], ptb[:, 0:512].rearrange("p (g n) -> p g n", n=128), [pb], [Btm_b])
                a0 = a3[:, c * 2, :]
                a1 = a3[:, c * 2 + 1, :]
                pt, pb = next_ps()
                MM(pt[:, 0:32], cv["tri"], a0, True, True, [cst_b, a_b], [pb])
                MM(pt[:, 32:64], cv["ones"], a0, True, False, [cst_b, a_b], [pb])
                MM(pt[:, 32:64], cv["tri"], a1, False, True, [cst_b, a_b], [pb])
                MM(pt[:, 64:96], cv["ones"], a0, True, False, [cst_b, a_b], [pb])
                MM(pt[:, 64:96], cv["ones"], a1, False, True, [cst_b, a_b], [pb])
                CP("dve", acs, pt[:, 0:64], [pb], [acs_b])
                CP("dve", totb, pt[:, 64:96], [pb], [totb_b])
                ACT(etot, totb, AF.Exp, [totb_b], [etot_b])
                TT("dve", Dacc, Dacc, totb, ALU.add, [Dacc_b, totb_b], [Dacc_b])
                for i in range(2):
                    TT("dve", ds3[:, i, :], totb, acs3[:, i, :], ALU.subtract, [totb_b, acs_b], [ds_b])
                ACT(ds_, ds_, AF.Exp, [ds_b], [ds_b])
                for i in range(2):
                    TT("pool", xdd3[:, i, :].rearrange("p (h q) -> p h q", q=64), xdt3[:, i, :].rearrange("p (h q) -> p h q", q=64),
                       bcast_last(ds3[:, i, :], 64), ALU.mult, [xdt_b, ds_b], [xdd_b])
                if compute_y:
                    CP("dve", Sbf, Sst, [Sst_b], [Sbf_b])
                    for g in range(4):
                        pt, pb = next_ps()
                        MM(pt[:, 0:256], BT3[:, g, 0:128], CT3[:, g, :], True, True, [BT_b, CT_b], [pb])
                        MM(pt[:, 256:384], BT3[:, g, 128:256], CT3[:, g, 128:256], True, True, [BT_b, CT_b], [pb])
                        TT("dve", cbm3[:, g, 0:128], pt[:, 0:128], cv["tri"], ALU.mult, [pb, cst_b], [cbm_b])
                        CP("dve", cbm3[:, g, 128:256], pt[:, 128:256], [pb], [cbm_b])
                        TT("dve", cbm3[:, g, 256:384], pt[:, 256:384], cv["tri"], ALU.mult, [pb, cst_b], [cbm_b])
                    NBQ = 8

                    def front(bq):
                        st_ = []
                        for hq in range(4):
                            h = bq * 4 + hq
                            r0, r0_b = r0p.get()
                            TS("dve", r0[:, 0:256], triL[:, 0:256], a0[:, h:h + 1], ALU.mult, [triL_b, a_b], [r0_b])
                            TS("pool", r0[:, 256:512], triL[:, 256:512], a1[:, h:h + 1], ALU.mult, [triL_b, a_b], [r0_b])
                            st_.append([h, r0, r0_b])
                        for e_ in st_:
                            h, r0, r0_b = e_
                            pbc, pbcb = next_ps()
                            MM(pbc[:, 0:256], cv["ones"], r0[:, 0:256], True, False, [cst_b, r0_b], [pbcb])
                            MM(pbc[:, 0:256], cv["ones"], r0[:, 256:512], False, True, [cst_b, r0_b], [pbcb])
                            e_ += [pbc, pbcb]
                        for e_ in st_:
                            h, r0, r0_b, pbc, pbcb = e_
                            d_, d_b = dpp.get()
                            bcs, bcs_b = bcp.get()
                            CP("act", bcs, pbc[:, 0:256], [pbcb], [bcs_b])
                            TS("dve", d_[:, 0:256], bcs[:, 0:256], acs3[:, 0, h:h + 1], ALU.subtract, [bcs_b, acs_b], [d_b],
                               s2=0.0, op1=ALU.min)
                            TS("dve", d_[:, 256:384], bcs[:, 128:256], acs3[:, 1, h:h + 1], ALU.subtract, [bcs_b, acs_b], [d_b],
                               s2=0.0, op1=ALU.min)
                            eb, eb_b = ebp.get()
                            ACT(eb, bcs, AF.Exp, [bcs_b], [eb_b])
                            e_ += [d_, d_b, eb, eb_b]
                        return st_

                    def back(bq, st_):
                        for e_ in st_:
                            h, r0, r0_b, pbc, pbcb, d_, d_b, eb, eb_b = e_
                            g = h // 8
                            ACT(d_, d_, AF.Exp, [d_b], [d_b])
                            Ce, Ce_b = cep.get()
                            TT("pool", Ce, eb, CT3[:, g, :], ALU.mult, [eb_b, CT_b], [Ce_b])
                            e_ += [Ce, Ce_b]
                        for e_ in st_:
                            h, d_, d_b = e_[0], e_[5], e_[6]
                            g = h // 8
                            G_, G_b = gpp.get()
                            TT("dve", G_, d_, cbm3[:, g, :], ALU.mult, [d_b, cbm_b], [G_b])
                            e_ += [G_, G_b]
                        for pq in range(2):
                            pr = bq * 2 + pq
                            py, pyb = next_ps()
                            for hh in range(2):
                                e_ = st_[pq * 2 + hh]
                                h, Ce, Ce_b, G_, G_b = e_[0], e_[9], e_[10], e_[11], e_[12]
                                hb = hh * 64
                                MM(py[hb:hb + 64, 0:256], xdt3[:, 0, h * 64:(h + 1) * 64], G_[:, 0:256], True, False, [xdt_b, G_b], [pyb])
                                MM(py[hb:hb + 64, 128:256], xdt3[:, 1, h * 64:(h + 1) * 64], G_[:, 256:384], False, False,
                                   [xdt_b, G_b], [pyb])
                                MM(py[hb:hb + 64, 0:256], Sbf[:, h * 64:(h + 1) * 64], Ce, False, True, [Sbf_b, Ce_b], [pyb])
                            zt, zt_b = zp.get()
                            P.dma("sp", zt, zT_d[pr * 128:(pr + 1) * 128, csl], reads=[dbuf["zT_d"]], writes=[zt_b])
                            yv, yv_b = stg_pool.get()
                            STT(yv[:, 0:256], xsT3[:, pr, :], dskT[:, pr:pr + 1], py[:, 0:256], ALU.mult, ALU.add,
                                [xsT_b, lp_b, pyb], [yv_b])
                            TT("pool", yg3[:, pr, :], yv[:, 0:256], zt, ALU.mult, [yv_b, zt_b], [yg_b])

                    prev_st = None
                    for bq in range(NBQ + 1):
                        cur = front(bq) if bq < NBQ else None
                        if prev_st is not None:
                            back(bq - 1, prev_st)
                        prev_st = cur
                    for g in range(4):
                        pss, pssb = next_ps()
                        for q in range(4):
                            sq, sq_b = stg_pool.get()
                            ACT(sq[:, 0:256], yg3[:, g * 4 + q, :], AF.Square, [yg_b], [sq_b])
                            MM(pss[:, 0:256], cv["ones"], sq[:, 0:256], q == 0, q == 3, [cst_b, sq_b], [pssb])
                        rs, rs_b = stg_pool.get()
                        ACT(rs[:, 0:256], pss[:, 0:256], AF.Sqrt, [pssb], [rs_b], bias=epsT[:, 0:1], scale=1.0 / 512.0)
                        P.op("dve", lambda e, rs=rs: e.reciprocal(out=rs[:, 0:256], in_=rs[:, 0:256]), [rs_b], [rs_b])
                        for q in range(4):
                            pr = g * 4 + q
                            ob, ob_b = sbf_pool.get()
                            STT(ob[:, 0:256], yg3[:, pr, :], sngT[:, pr:pr + 1], rs[:, 0:256], ALU.mult, ALU.mult,
                                [yg_b, lp_b, rs_b], [ob_b])
                            store(ynT_d[pr * 128:(pr + 1) * 128, csl], "ynT_d", ob[:, 0:256], ob_b)
                for g in range(4):
                    pst, pstb = next_ps()
                    for i in range(2):
                        MM(pst, Btm4[:, i, g, :], xdd3[:, i, g * 512:(g + 1) * 512], i == 0, i == 1, [Btm_b, xdd_b], [pstb])
                    sg3 = Sst[:, g * 512:(g + 1) * 512].rearrange("p (h q) -> p h q", q=64)
                    TT("dve", sg3, sg3, bcast_last(etot[:, g * 8:(g + 1) * 8], 64), ALU.mult, [Sst_b, etot_b], [Sst_b])
                    TT("dve", Sst[:, g * 512:(g + 1) * 512], Sst[:, g * 512:(g + 1) * 512], pst, ALU.add, [Sst_b, pstb], [Sst_b])

        P.op("dve", lambda e: e.memset(Sst, 0.0), [], [Sst_b])
        P.op("dve", lambda e: e.memset(Dacc, 0.0), [], [Dacc_b])
        ssd_pass(False)
        dap, dn = st_gs.loc_rows(0, 128)
        store(dap, dn, Sst, Sst_b)
        dap, dn = dd_gs.loc_rows(0, 128)
        store(dap, dn, Dacc, Dacc_b)
        st_gs.gather()
        dd_gs.gather()
        Dr = []
        for r in range(GSZ - 1):
            t_, tb_ = A.f32("Dr%d" % r, SH)
            gap, gn = dd_gs.g_rows(r, 0, 128)
            P.dma("sp", t_, gap, reads=[dbuf[gn]], writes=[tb_])
            Dr.append((t_, tb_))
        lt, lt_b = A.f32("ltflag", 4)
        for m in range(GSZ - 1):
            TS("dve", lt[:, m:m + 1], rk[:, 0:1], float(m), ALU.is_gt, [rk_b], [lt_b])
        P.op("dve", lambda e: e.memset(Sst, 0.0), [], [Sst_b])
        Fr, Fr_b = A.f32("Fr", 2048)
        for r in range(GSZ - 1):
            wr, wr_b = A.f32("wr%d" % r, SH)
            P.op("dve", lambda e, wr=wr: e.memset(wr, 0.0), [], [wr_b])
            for m in range(r + 1, GSZ - 1):
                STT(wr, Dr[m][0], lt[:, m:m + 1], wr, ALU.mult, ALU.add, [Dr[m][1], lt_b, wr_b], [wr_b])
            ACT(wr, wr, AF.Exp, [wr_b], [wr_b])
            TS("dve", wr, wr, lt[:, r:r + 1], ALU.mult, [wr_b, lt_b], [wr_b])
            gap, gn = st_gs.g_rows(r, 0, 128)
            P.dma("sp", Fr, gap, reads=[dbuf[gn]], writes=[Fr_b])
            F3 = Fr.rearrange("p (h q) -> p h q", q=64)
            TT("dve", F3, F3, bcast_last(wr, 64), ALU.mult, [Fr_b, wr_b], [Fr_b])
            TT("dve", Sst, Sst, Fr, ALU.add, [Sst_b, Fr_b], [Sst_b])
        ssd_pass(True)
        P.barrier()
        A.release(m0_)

    P.dma("sp", rk[:, 0:4], bcast_ap(rank_in, 128, 4), reads=[dbuf["rank"]], writes=[rk_b])
    for r in range(GSZ):
        TS("dve", prevsel[:, r:r + 1], rk[:, 0:1], float(r + 1), ALU.is_equal, [rk_b], [prevsel_b])

    def load_prev_rows(gs, dst3, dst_b, nsub, ncols):
        first = True
        for r in range(GSZ - 1):
            tmp_, tmpb_ = A.f32("prevtmp%d" % r, nsub * ncols)
            tmp3 = tmp_.rearrange("p (m c) -> p m c", c=ncols)
            gap, gn = gs.g_rows(r, 0, 128)
            P.dma("sp", tmp_, gap, reads=[dbuf[gn]], writes=[tmpb_])
            if first:
                TS("dve", dst3, tmp3, prevsel[:, r:r + 1], ALU.mult, [tmpb_, prevsel_b], [dst_b])
                first = False
            else:
                STT(dst3, tmp3, prevsel[:, r:r + 1], dst3, ALU.mult, ALU.add, [tmpb_, prevsel_b, dst_b], [dst_b])

    for l in range(depth):
        A.release(m_x)
        phase_A(l)
        P.barrier()
        A.release(m_x)
        if cfg.stage >= 30:
            spill_x()
            A.release(m_pers)
            phase_B(l)
            A.release(m_pers)
            if cfg.stage >= 40:
                phase_C(l)
            A.release(m_x)
            restore_x()
        if cfg.stage >= 20:
            phase_D(l)
        if cfg.stage >= 21:
            phase_EF(l, None)
    A.release(m_x)
    for name in dbg_copies:
        rows = dram[name].shape[0]
        for r0 in range(0, rows, 512):
            r1 = min(rows, r0 + 512)
            P.dma("sp", dram[name + "_dbg"][r0:r1, :], dram[name][r0:r1, :], reads=[dbuf[name]], writes=[dbuf[name + "_dbg"]])
    P.barrier()

    m0 = A.mark()
    outs = [A.f32("otok%d" % i, D) for i in range(2)]
    for i in range(NT):
        ot, ot_b = outs[i % 2]
        for k in range(KC):
            pt, pb = next_ps()
            P.op("pe", lambda e, pt=pt, k=k, i=i: e.transpose(out=pt[:, 0:128], in_=xT3[:, k, i * 128:(i + 1) * 128],
                                                             identity=cv["ident"]),
                 reads=[xT_b, cst_b], writes=[pb])
            if k % 2 == 0:
                P.op("act", lambda e, pt=pt, k=k, ot=ot: e.copy(out=ot[:, k * 128:(k + 1) * 128], in_=pt[:, 0:128]),
                     reads=[pb], writes=[ot_b])
            else:
                P.op("dve", lambda e, pt=pt, k=k, ot=ot: e.tensor_copy(out=ot[:, k * 128:(k + 1) * 128], in_=pt[:, 0:128]),
                     reads=[pb], writes=[ot_b])
        P.dma("sp", y_out[i * 128:(i + 1) * 128, :], ot, reads=[ot_b], writes=[dbuf["y"]])
    P.barrier()
    A.release(m0)

    P.emit(stack)
    stack.close()
    return nc


def make_in_maps(cfg, inputs):
    T = cfg.T
    depth = cfg.depth
    maps = []
    f = lambda a: np.ascontiguousarray(np.asarray(a, dtype=np.float32))
    shared = {
        "consts": CONST_ARR,
        "w_ada": f(inputs["w_ada"]), "b_ada": f(inputs["b_ada"]).reshape(depth, 48, 128),
        "norm1_g": f(inputs["norm1_g"]).reshape(depth, KC, 128), "w_in": f(inputs["w_in"]),
        "q_norm_g": f(inputs["q_norm_g"]).reshape(depth, 1, 64), "k_norm_g": f(inputs["k_norm_g"]).reshape(depth, 1, 64),
        "ssm_conv_w": f(inputs["ssm_conv_w"]).reshape(depth, 4, 24, 128),
        "ssm_conv_b": f(inputs["ssm_conv_b"]).reshape(depth, 24, 128),
        "dt_bias": f(inputs["dt_bias"]).reshape(depth, 1, SH), "a_log": f(inputs["a_log"]).reshape(depth, 1, SH),
        "d_skip": f(inputs["d_skip"]).reshape(depth, 1, SH), "ssm_norm_g": f(inputs["ssm_norm_g"]).reshape(depth, 16, 128),
        "w_attn_o": f(inputs["w_attn_o"]), "w_ssm_o": f(inputs["w_ssm_o"]), "w_out": f(inputs["w_out"]),
        "norm2_g": f(inputs["norm2_g"]).reshape(depth, KC, 128), "w_up": f(inputs["w_up"]),
        "ffn_conv_w": f(inputs["ffn_conv_w"]).reshape(depth, 3, 44, 128),
        "ffn_conv_b": f(inputs["ffn_conv_b"]).reshape(depth, 44, 128), "w_down": f(inputs["w_down"]),
    }
    x = f(inputs["x"])
    c = f(inputs["c"])
    pos = np.ascontiguousarray(np.asarray(inputs["positions"], dtype=np.int32))
    for r in range(NCORES):
        b, j = divmod(r, GSZ)
        m = dict(shared)
        m["x"] = np.ascontiguousarray(x[b, j * T:(j + 1) * T, :])
        m["c"] = np.ascontiguousarray(c[b].reshape(KC, 128))
        m["positions"] = np.ascontiguousarray(pos[b, j * T:(j + 1) * T].reshape(1, T))
        rk = np.zeros((1, 4), np.float32)
        rk[0, 0] = j
        m["rank"] = rk
        for k_, v_ in (getattr(cfg, "feed_data", None) or {}).items():
            m[k_] = v_[r]
        maps.append(m)
    return maps


_CACHE = {}


def run(cfg, inputs):
    key = (cfg.S, cfg.depth, tuple(sorted(cfg.debug)), cfg.stage, tuple(sorted(cfg.feed)))
    if key not in _CACHE:
        _CACHE[key] = build_program(cfg)
    nc = _CACHE[key]
    maps = make_in_maps(cfg, inputs)
    res = run_bass_kernel_spmd(nc, maps, core_ids=list(range(NCORES)))
    return res.results


def kernel(**inputs):
    cfg = Cfg(seq=int(np.asarray(inputs["x"]).shape[1]), depth=int(np.asarray(inputs["w_in"]).shape[0]))
    results = run(cfg, inputs)
    B = np.asarray(inputs["x"]).shape[0]
    out = np.zeros((B, cfg.S, D), np.float32)
    for r in range(NCORES):
        b, j = divmod(r, GSZ)
        out[b, j * cfg.T:(j + 1) * cfg.T, :] = results[r]["y"]
    return out
```
